# Optimizing a Trainium2 kernel written in Bass

```python
import math
import jax
import jax.numpy as jnp
from jax import lax
import numpy as np

D_MODEL = 1024
BATCH = 2
SEQ = 8192
DEPTH = 2

GRID_W = 64
CTX_LEN = 256
EPS = 1e-6
F32 = jnp.float32

F_GROUPS = 4
F_GROUP_DIM = D_MODEL // 16
F_WIDTH = F_GROUPS * F_GROUP_DIM
DA_HEADS = 4
DA_QK_DIM = D_MODEL // 16
DA_V_DIM = 2 * DA_QK_DIM
DA_QK_WIDTH = DA_HEADS * 2 * DA_QK_DIM
DA_WIDTH = DA_HEADS * DA_V_DIM
ATT_SCALE = DA_QK_DIM ** -0.5
QBLOCK = 128
ROPE_BASE = 10000.0
CV_WIDTH = D_MODEL // 4
CONV_TAPS = 31
POOL_WINDOWS = (2, 4, 8, 16)
P_GROUP_DIM = D_MODEL // 16
P_WIDTH = len(POOL_WINDOWS) * P_GROUP_DIM
N_BRANCH = 4
D_FF = ((8 * D_MODEL // 3 + 127) // 128) * 128
FFN_TAPS = 3

OFF_F = 0
OFF_Q = OFF_F + F_WIDTH
OFF_K = OFF_Q + DA_QK_WIDTH
OFF_V = OFF_K + DA_QK_WIDTH
OFF_C = OFF_V + DA_WIDTH
OFF_P = OFF_C + 2 * CV_WIDTH
OFF_G = OFF_P + P_WIDTH
IN_WIDTH = OFF_G + N_BRANCH * D_MODEL

kernel_name = "hybrid_fnet_diffattn_conformer_pool_dit"


def rmsnorm(x, g):
    xf = x.astype(F32)
    y = xf * lax.rsqrt(jnp.mean(xf * xf, axis=-1, keepdims=True) + EPS)
    return (y * g.astype(F32)).astype(x.dtype)


def layernorm(x, g, b):
    xf = x.astype(F32)
    mu = jnp.mean(xf, axis=-1, keepdims=True)
    var = jnp.mean(jnp.square(xf - mu), axis=-1, keepdims=True)
    return ((xf - mu) * lax.rsqrt(var + EPS) * g.astype(F32) + b.astype(F32)).astype(x.dtype)


def dwconv(x, w, b):
    taps = w.shape[0]
    pad = (taps - 1) // 2
    y = lax.conv_general_dilated(x, w[:, None, :].astype(x.dtype), (1,), [(pad, taps - 1 - pad)],
                                 dimension_numbers=('NWC', 'WIO', 'NWC'),
                                 feature_group_count=x.shape[-1])
    return y + b


def axial_tables(rows, dtype):
    nf = DA_QK_DIM // 4
    inv = ROPE_BASE ** (-jnp.arange(nf, dtype=F32) / nf)
    r = jnp.repeat(jnp.arange(rows, dtype=F32), GRID_W)
    col = jnp.tile(jnp.arange(GRID_W, dtype=F32), rows)
    ar = r[:, None] * inv
    ac = col[:, None] * inv
    sh = (rows * GRID_W, 1, 1, nf)
    return (jnp.cos(ar).reshape(sh).astype(dtype), jnp.sin(ar).reshape(sh).astype(dtype),
            jnp.cos(ac).reshape(sh).astype(dtype), jnp.sin(ac).reshape(sh).astype(dtype))


def rot_half(x, cos, sin):
    m = x.shape[-1] // 2
    x1, x2 = x[..., :m], x[..., m:]
    return jnp.concatenate([x1 * cos - x2 * sin, x2 * cos + x1 * sin], axis=-1)


def axial_rope(x, rope):
    cos_r, sin_r, cos_c, sin_c = rope
    n = x.shape[-1] // 2
    return jnp.concatenate([rot_half(x[..., :n], cos_r, sin_r), rot_half(x[..., n:], cos_c, sin_c)], axis=-1)


def diff_attention(q, k, v, lam):
    b, lq = q.shape[0], q.shape[1]
    nb = lq // QBLOCK
    qb = (q * ATT_SCALE).reshape(b, nb, QBLOCK, DA_HEADS, 2, DA_QK_DIM).swapaxes(0, 1)

    def block(qi):
        s = jnp.einsum('bqhmd,bkhmd->bhmqk', qi, k).astype(F32)
        pr = jax.nn.softmax(s, axis=-1)
        w = pr[:, :, 0] - lam * pr[:, :, 1]
        return jnp.einsum('bhqk,bkhd->bqhd', w.astype(v.dtype), v)

    o = lax.map(block, qb)
    return o.swapaxes(0, 1).reshape(b, lq, DA_HEADS, DA_V_DIM)


def fourier_mix(u):
    b, l, _ = u.shape
    ug = u.astype(F32).reshape(b, l, F_GROUPS, F_GROUP_DIM)
    y = jnp.real(jnp.fft.fft2(ug, axes=(1, 3), norm='ortho'))
    return y.reshape(b, l, F_WIDTH).astype(u.dtype)


def conformer_conv(u, dw_w, dw_b, ln_g, ln_b):
    a, gt = jnp.split(u, 2, axis=-1)
    z = a * jax.nn.sigmoid(gt)
    z = dwconv(z, dw_w, dw_b)
    z = layernorm(z, ln_g, ln_b)
    return jax.nn.silu(z)


def multiscale_pool(u, pool_w, pool_scale):
    b, l, _ = u.shape
    uf = u.astype(F32)
    csum = jnp.concatenate([jnp.zeros((b, 1, P_WIDTH), F32), jnp.cumsum(uf, axis=1)], axis=1)
    t = jnp.arange(l)
    means = []
    for g, win in enumerate(POOL_WINDOWS):
        lo = jnp.clip(t - win // 2, 0, l - 1)
        hi = jnp.clip(t + win - win // 2 - 1, 0, l - 1)
        cs = csum[..., g * P_GROUP_DIM:(g + 1) * P_GROUP_DIM]
        cnt = (hi - lo + 1).astype(F32)[None, :, None]
        means.append((cs[:, hi + 1] - cs[:, lo]) / cnt)
    d = (jnp.concatenate(means, axis=-1) - uf).astype(u.dtype)
    d = d.reshape(b, l, len(POOL_WINDOWS), P_GROUP_DIM)
    y = jnp.einsum('blgc,gcd->blgd', d, pool_w).reshape(b, l, P_WIDTH)
    return y * pool_scale


def token_mixer(p, k_ctx, v_ctx, rope, lam, lam_init, subln_g, conv_dw_w, conv_dw_b, conv_ln_g, conv_ln_b,
                pool_w, pool_scale, wo_f, wo_a, wo_c, wo_p, w_out):
    b, l, _ = p.shape
    u_f = p[..., OFF_F:OFF_Q]
    q = p[..., OFF_Q:OFF_K].reshape(b, l, DA_HEADS, 2, DA_QK_DIM)
    k = p[..., OFF_K:OFF_V].reshape(b, l, DA_HEADS, 2, DA_QK_DIM)
    v = p[..., OFF_V:OFF_C].reshape(b, l, DA_HEADS, DA_V_DIM)
    u_c = p[..., OFF_C:OFF_P]
    u_p = p[..., OFF_P:OFF_G]
    gates = jax.nn.sigmoid(p[..., OFF_G:].astype(F32)).astype(p.dtype).reshape(b, l, N_BRANCH, D_MODEL)
    if rope is None:
        keys, vals = k, v
    else:
        q = axial_rope(q, rope)
        keys = jnp.concatenate([k_ctx, axial_rope(k, rope)], axis=1)
        vals = jnp.concatenate([v_ctx, v], axis=1)
    o_a = diff_attention(q, keys, vals, lam)
    o_a = (rmsnorm(o_a, subln_g) * (1.0 - lam_init)).reshape(b, l, DA_WIDTH)
    y = (gates[:, :, 0] * (fourier_mix(u_f) @ wo_f)
         + gates[:, :, 1] * (o_a @ wo_a)
         + gates[:, :, 2] * (conformer_conv(u_c, conv_dw_w, conv_dw_b, conv_ln_g, conv_ln_b) @ wo_c)
         + gates[:, :, 3] * (multiscale_pool(u_p, pool_w, pool_scale) @ wo_p))
    return y @ w_out


def conv_ffn(h, w_up, dw_w, dw_b, w_down):
    val, gt = jnp.split(h @ w_up, 2, axis=-1)
    gt = jax.nn.gelu(dwconv(gt, dw_w, dw_b), approximate=False)
    return (val * gt) @ w_down


def setup_inputs(seed: int = 0) -> dict:
    key = jax.random.key(seed)
    ks = jax.random.split(key, 30)

    def nrm(i, shape, s):
        return jax.random.normal(ks[i], shape, F32) * s

    L, D = DEPTH, D_MODEL
    return {
        "x": nrm(0, (BATCH, SEQ, D), 1.0),
        "c": nrm(1, (BATCH, D), 1.0),
        "ctx": nrm(2, (BATCH, CTX_LEN, D), 1.0),
        "c_ctx": nrm(3, (D,), 1.0),
        "norm1_g": 1.0 + nrm(4, (L, D), 0.02),
        "norm2_g": 1.0 + nrm(5, (L, D), 0.02),
        "ada_w": nrm(6, (L, D, 6 * D), 0.5 * D ** -0.5),
        "ada_b": nrm(7, (L, 6 * D), 0.02),
        "w_in": nrm(8, (L, D, IN_WIDTH), D ** -0.5),
        "lam_q1": nrm(9, (L, DA_QK_DIM), 0.1),
        "lam_k1": nrm(10, (L, DA_QK_DIM), 0.1),
        "lam_q2": nrm(11, (L, DA_QK_DIM), 0.1),
        "lam_k2": nrm(12, (L, DA_QK_DIM), 0.1),
        "subln_g": 1.0 + nrm(13, (L, DA_V_DIM), 0.02),
        "conv_dw_w": nrm(14, (L, CONV_TAPS, CV_WIDTH), CONV_TAPS ** -0.5),
        "conv_dw_b": nrm(15, (L, CV_WIDTH), 0.02),
        "conv_ln_g": 1.0 + nrm(16, (L, CV_WIDTH), 0.02),
        "conv_ln_b": nrm(17, (L, CV_WIDTH), 0.02),
        "pool_w": nrm(18, (L, len(POOL_WINDOWS), P_GROUP_DIM, P_GROUP_DIM), P_GROUP_DIM ** -0.5),
        "pool_scale": 1.0 + nrm(19, (L, P_WIDTH), 0.02),
        "wo_f": nrm(20, (L, F_WIDTH, D), F_WIDTH ** -0.5),
        "wo_a": nrm(21, (L, DA_WIDTH, D), DA_WIDTH ** -0.5),
        "wo_c": nrm(22, (L, CV_WIDTH, D), CV_WIDTH ** -0.5),
        "wo_p": nrm(23, (L, P_WIDTH, D), P_WIDTH ** -0.5),
        "w_out": nrm(24, (L, D, D), D ** -0.5),
        "w_up": nrm(25, (L, D, 2 * D_FF), D ** -0.5),
        "ffn_dw_w": nrm(26, (L, FFN_TAPS, D_FF), FFN_TAPS ** -0.5),
        "ffn_dw_b": nrm(27, (L, D_FF), 0.02),
        "w_down": nrm(28, (L, D_FF, D), D_FF ** -0.5),
        "final_g": 1.0 + nrm(29, (D,), 0.02),
    }


def reference(x, c, ctx, c_ctx, norm1_g, norm2_g, ada_w, ada_b, w_in, lam_q1, lam_k1, lam_q2, lam_k2,
              subln_g, conv_dw_w, conv_dw_b, conv_ln_g, conv_ln_b, pool_w, pool_scale, wo_f, wo_a, wo_c,
              wo_p, w_out, w_up, ffn_dw_w, ffn_dw_b, w_down, final_g):
    b = x.shape[0]
    n_ctx = ctx.shape[1]
    rows = x.shape[1] // GRID_W
    rope = axial_tables(rows, x.dtype)
    for l in range(DEPTH):
        last = l == DEPTH - 1
        lam_init = 0.8 - 0.6 * math.exp(-0.3 * l)
        lam = (jnp.exp(jnp.sum(lam_q1[l].astype(F32) * lam_k1[l].astype(F32)))
               - jnp.exp(jnp.sum(lam_q2[l].astype(F32) * lam_k2[l].astype(F32))) + lam_init)
        mod_x = (jax.nn.silu(c) @ ada_w[l] + ada_b[l])[:, None, :]
        mod_c = (jax.nn.silu(c_ctx) @ ada_w[l] + ada_b[l])[None, None, :]
        sh1, sc1, g1, sh2, sc2, g2 = jnp.split(mod_x, 6, axis=-1)
        csh1, csc1, cg1, csh2, csc2, cg2 = jnp.split(mod_c, 6, axis=-1)

        hc = rmsnorm(ctx, norm1_g[l]) * (1.0 + csc1) + csh1
        if last:
            kv_c = hc @ w_in[l][:, OFF_K:OFF_C]
        else:
            pc = hc @ w_in[l]
            kv_c = pc[..., OFF_K:OFF_C]
        k_c = kv_c[..., :DA_QK_WIDTH].reshape(b, n_ctx, DA_HEADS, 2, DA_QK_DIM)
        v_c = kv_c[..., DA_QK_WIDTH:].reshape(b, n_ctx, DA_HEADS, DA_V_DIM)

        hx = rmsnorm(x, norm1_g[l]) * (1.0 + sc1) + sh1
        x = x + g1 * token_mixer(hx @ w_in[l], k_c, v_c, rope, lam, lam_init, subln_g[l], conv_dw_w[l],
                                 conv_dw_b[l], conv_ln_g[l], conv_ln_b[l], pool_w[l], pool_scale[l],
                                 wo_f[l], wo_a[l], wo_c[l], wo_p[l], w_out[l])
        hx2 = rmsnorm(x, norm2_g[l]) * (1.0 + sc2) + sh2
        x = x + g2 * conv_ffn(hx2, w_up[l], ffn_dw_w[l], ffn_dw_b[l], w_down[l])

        if not last:
            ctx = ctx + cg1 * token_mixer(pc, None, None, None, lam, lam_init, subln_g[l], conv_dw_w[l],
                                          conv_dw_b[l], conv_ln_g[l], conv_ln_b[l], pool_w[l], pool_scale[l],
                                          wo_f[l], wo_a[l], wo_c[l], wo_p[l], w_out[l])
            hc2 = rmsnorm(ctx, norm2_g[l]) * (1.0 + csc2) + csh2
            ctx = ctx + cg2 * conv_ffn(hc2, w_up[l], ffn_dw_w[l], ffn_dw_b[l], w_down[l])
    return rmsnorm(x, final_g)
```

```python
import math
from contextlib import ExitStack
import numpy as np
import ml_dtypes
import concourse.bass as bass
import concourse.mybir as mybir
from concourse.bass_utils import run_bass_kernel_spmd

F32 = mybir.dt.float32
BF16 = mybir.dt.bfloat16
AF = mybir.ActivationFunctionType
ALU = mybir.AluOpType
NPBF = ml_dtypes.bfloat16

D = 1024
KC = 8
T = 2048
NCTX = 256
SEQ = 8192
HALO = 16
DFF = 2816
EPS = 1e-6
NCORES = 8
SAME_ENGINE_SYNC = True


class Buf:
    def __init__(self, name=""):
        self.name = name
        self.w = []
        self.r = []


class Sched:
    def __init__(self, nc, es, ndma=20):
        self.nc = nc
        self.E = {"pe": nc.tensor, "act": nc.scalar, "dve": nc.vector, "pool": nc.gpsimd, "sp": nc.sync}
        self.sem = {e: es.enter_context(nc.semaphore("sem_" + e)) for e in self.E}
        self.cnt = {e: 0 for e in self.E}
        self.seen = {e: {} for e in self.E}
        self.pend = {e: [] for e in self.E}
        self.dsems = [es.enter_context(nc.semaphore("dsem%d" % i)) for i in range(ndma)]
        self.dcnt = [0] * ndma
        self.dnext = 0
        self.nwaits = 0
        self.cc_sem = es.enter_context(nc.semaphore("cc_sem"))
        self.cc_cnt = 0

    def _wait(self, e, st):
        key, sem, val = st
        if key == e and (e == "pe" or not SAME_ENGINE_SYNC):
            return
        if self.seen[e].get(key, 0) >= val:
            return
        self.E[e].wait_ge(sem, val)
        self.nwaits += 1
        self.seen[e][key] = val

    def _deps(self, e, reads, writes):
        for oe, pl in self.pend.items():
            if oe == e:
                continue
            for (R, W) in pl:
                for b in list(reads) + list(writes):
                    if b in W or (b in R and b in writes):
                        raise RuntimeError("dependency on un-stamped access of %s by %s (buf %s)" % (oe, e, b.name))
        for b in reads:
            for st in b.w:
                self._wait(e, st)
        for b in writes:
            for st in b.w:
                self._wait(e, st)
            for st in b.r:
                self._wait(e, st)

    def op(self, e, fn, reads=(), writes=(), inc=True):
        self._deps(e, reads, writes)
        ins = fn(self.E[e])
        self.pend[e].append((tuple(reads), tuple(writes)))
        if inc:
            self.cnt[e] += 1
            ins.then_inc(self.sem[e], 1)
            st = (e, self.sem[e], self.cnt[e])
            for (R, W) in self.pend[e]:
                for b in R:
                    b.r.append(st)
                for b in W:
                    b.w = [st]
                    b.r = []
            self.pend[e] = []
        return ins

    def dma(self, q, out, in_, reads=(), writes=(), also=False, slow=False):
        self._deps(q, reads, writes)
        i = self.dnext
        self.dnext = (self.dnext + 1) % len(self.dsems)
        key = "d%d" % i
        if self.dcnt[i] > 0:
            self._wait(q, (key, self.dsems[i], self.dcnt[i]))
        ins = self.E[q].dma_start(out=out, in_=in_, allow_slow_non_contiguous=True) if slow else self.E[q].dma_start(out=out, in_=in_)
        self.dcnt[i] += 16
        ins.then_inc(self.dsems[i], 16)
        st = (key, self.dsems[i], self.dcnt[i])
        for b in reads:
            b.r.append(st)
        for b in writes:
            if also:
                b.w = b.w + [st]
            else:
                b.w = [st]
                b.r = []
        return ins

    def barrier(self):
        for e, pl in self.pend.items():
            if pl:
                raise RuntimeError("barrier with un-stamped accesses on " + e)
        for e in self.E:
            for oe in self.E:
                if oe != e and self.cnt[oe] > 0:
                    self._wait(e, (oe, self.sem[oe], self.cnt[oe]))
            for i in range(len(self.dsems)):
                if self.dcnt[i] > 0:
                    self._wait(e, ("d%d" % i, self.dsems[i], self.dcnt[i]))
            if self.cc_cnt > 0:
                self._wait(e, ("cc", self.cc_sem, self.cc_cnt))

    def allgather(self, in_ap, out_ap):
        ins = self.nc.gpsimd.collective_compute("AllGather", ALU.bypass, replica_groups=[[0, 1, 2, 3], [4, 5, 6, 7]],
                                                ins=[in_ap.opt()], outs=[out_ap.opt()])
        self.cc_cnt += 1
        ins.then_inc(self.cc_sem)
        st = ("cc", self.cc_sem, self.cc_cnt)
        self._wait("pool", st)
        return st

    def finish(self, bufs):
        for i in range(len(self.dsems)):
            if self.dcnt[i] > 0:
                self._wait("sp", ("d%d" % i, self.dsems[i], self.dcnt[i]))


class Ctx:
    def __init__(self, name, fused=False):
        self.nc = bass.Bass("TRN2", target_bir_lowering=False)
        self.es_root = ExitStack()
        self.es = ExitStack()
        self.S = Sched(self.nc, self.es_root)
        self.ins = {}
        self.outs = {}
        self.fused = fused
        self.prefix = ""
        self.links = {}
        self.produced = {}
        self.ext_out = {}
        self.dram_bufs = {}
        self.after_blocks = None
        self.after_fft_tables = None

    def begin_phase(self, prefix, links=None, ext_out=None):
        self.prefix = prefix
        self.links = dict(links or {})
        self.produced = {}
        self.ext_out = dict(ext_out or {})
        self.es = ExitStack()

    def end_phase(self):
        self.S.barrier()
        self.es.close()
        self.es = ExitStack()
        return self.produced

    def inp(self, name, shape, dt=F32):
        if name in self.links:
            return self.links[name]
        t = self.nc.dram_tensor(self.prefix + name, list(shape), dt, kind="ExternalInput").ap()
        self.ins[self.prefix + name] = t
        return t

    def out(self, name, shape, dt=F32):
        if self.fused and name not in self.ext_out:
            t = self.nc.dram_tensor(self.prefix + name, list(shape), dt).ap()
            self.produced[name] = t
            return t
        oname = self.ext_out.get(name, self.prefix + name)
        t = self.nc.dram_tensor(oname, list(shape), dt, kind="ExternalOutput").ap()
        self.outs[oname] = t
        self.produced[name] = t
        return t

    def sb(self, name, shape, dt=F32, stack=None):
        t = (stack or self.es).enter_context(self.nc.sbuf_tensor(self.prefix + name, list(shape), dt))
        return t, Buf(name)

    def ps(self, name, shape, dt=F32, stack=None):
        t = (stack or self.es).enter_context(self.nc.psum_tensor(self.prefix + name, list(shape), dt))
        return t, Buf(name)


class PsumRing:
    def __init__(self, cx, n, name="ps", stack=None):
        self.tiles = [cx.ps("%s%d" % (name, i), [128, 512], F32, stack=stack) for i in range(n)]
        self.i = 0

    def next(self):
        t = self.tiles[self.i]
        self.i = (self.i + 1) % len(self.tiles)
        return t


def load_const(cx, name, dram_ap, shape, dt=F32, q="sp"):
    t, b = cx.sb(name, shape, dt)
    cx.S.dma(q, t[:], dram_ap, writes=[b])
    return t, b


def rope_tables(tok0, n):
    nf = 16
    inv = (10000.0 ** (-np.arange(nf, dtype=np.float32) / nf)).astype(np.float32)
    t = np.arange(tok0, tok0 + n)
    r = (t // 64).astype(np.float32)
    col = (t % 64).astype(np.float32)
    ar = r[:, None] * inv
    ac = col[:, None] * inv
    cr, sr, cc, sc = np.cos(ar), np.sin(ar), np.cos(ac), np.sin(ac)
    C = np.concatenate([cr, cr, cc, cc], axis=1).T
    Ssg = np.concatenate([-sr, sr, -sc, sc], axis=1).T
    return (np.ascontiguousarray(np.concatenate([C, C], 0), dtype=np.float32),
            np.ascontiguousarray(np.concatenate([Ssg, Ssg], 0), dtype=np.float32))


SWAP64 = np.concatenate([np.arange(16, 32), np.arange(0, 16), np.arange(48, 64), np.arange(32, 48)])


def pool_invcnt(tok0, n, L):
    out = np.zeros((256, n), np.float32)
    t = np.arange(tok0, tok0 + n)
    for g, win in enumerate((2, 4, 8, 16)):
        lo = np.clip(t - win // 2, 0, L - 1)
        hi = np.clip(t + win - win // 2 - 1, 0, L - 1)
        out[g * 64:(g + 1) * 64, :] = (1.0 / (hi - lo + 1).astype(np.float32))[None, :]
    return out.reshape(2, 128, n).transpose(1, 0, 2).copy()


def chan_dft_table():
    a = 2 * np.pi * np.outer(np.arange(64), np.arange(64)) / 64.0
    C, Sn = np.cos(a), np.sin(a)
    Cb = np.zeros((128, 128)); Sb = np.zeros((128, 128))
    for g in range(2):
        Cb[g * 64:(g + 1) * 64, g * 64:(g + 1) * 64] = C
        Sb[g * 64:(g + 1) * 64, g * 64:(g + 1) * 64] = Sn
    return np.concatenate([Cb, Sb], axis=1).astype(np.float32)


NWA = 24


def _adaln_mod(cx, cT_t, cT_b, ada_w, adab_t, adab_b, mod_t, mod_b, psr, ngroups, chunk0):
    S = cx.S
    sil_t, sil_b = cx.sb("sil_t", [128, KC, 2], F32)
    S.op("act", lambda e: e.activation(out=sil_t[:], in_=cT_t[:], func=AF.Silu), reads=[cT_b], writes=[sil_b])
    with ExitStack() as st:
        aw = [cx.sb("aw%d" % i, [128, KC, 512], F32, stack=st) for i in range(2)]
        for g in range(ngroups):
            awt, awb = aw[g % 2]
            for kc in range(KC):
                S.dma("sp", awt[:, kc, :], ada_w[kc * 128:(kc + 1) * 128, g * 512:(g + 1) * 512], writes=[awb], also=(kc > 0))
            pt, pb = psr.next()
            for mm in range(4):
                for kc in range(KC):
                    S.op("pe", lambda e, mm=mm, kc=kc: e.matmul(pt[:, mm * 2:mm * 2 + 2], awt[:, kc, mm * 128:(mm + 1) * 128], sil_t[:, kc, :],
                                                                 start=(kc == 0), stop=(kc == KC - 1)),
                         reads=[awb, sil_b], writes=[pb], inc=(mm == 3 and kc == KC - 1))
            c_ = chunk0 + g * 4
            S.op("dve", lambda e, g=g, c_=c_: e.tensor_tensor(out=mod_t[:, c_:c_ + 4, :],
                                                              in0=pt[:, 0:8].rearrange("p (a b) -> p a b", b=2),
                                                              in1=adab_t[:, c_:c_ + 4].unsqueeze(2).to_broadcast([128, 4, 2]), op=ALU.add),
                 reads=[pb, adab_b], writes=[mod_b])
        S.barrier()


def build_A(full_ctx, cx=None):
    standalone = cx is None
    if standalone:
        cx = Ctx("A")
    nc, S = cx.nc, cx.S
    xT = cx.inp("xT", [D, T]); xTh = cx.inp("xTh", [D, 2 * HALO]); cxT = cx.inp("cxT", [D, NCTX])
    premod = "mod_in" in cx.links
    if not premod:
        cT = cx.inp("cT", [128, KC, 2]); ada_w = cx.inp("ada_w", [D, 6 * D]); ada_b = cx.inp("ada_b", [128, 48])
    n1g = cx.inp("n1g", [128, KC])
    wA = cx.inp("wA", [D, NWA * 128]); wV = cx.inp("wV", [D, 512])
    ropeC = cx.inp("ropeC", [128, T]); ropeS = cx.inp("ropeS", [128, T])
    hmask = cx.inp("hmask", [128, 2 * HALO])
    cw = cx.inp("cw", [128, 2, 31]); cvec = cx.inp("cvec", [128, 2, 4])
    pinv = cx.inp("pinv", [128, 2, T]); pinvc = cx.inp("pinvc", [128, 2, NCTX])
    pw = cx.inp("pw", [128, 2, 128]); cdft = cx.inp("cdft", [128, 256]); identA = cx.inp("identA", [128, 128])

    if not premod:
        o_mod = cx.out("mod", [128, 48, 2])
    o_QT = cx.out("QT", [128, 4, T], BF16)
    o_KTh = [cx.out("KT%d" % h, [128, T], BF16) for h in range(4)]
    o_Vh = [cx.out("V%d" % h, [T, 128], BF16) for h in range(4)]
    o_Zq = [cx.out("Z%d" % q, [T, 128], BF16) for q in range(4)]
    o_yc = cx.out("ycT", [128, 2, T], BF16); o_yp = cx.out("ypT", [128, 2, T], BF16)
    o_hx = cx.out("hxT", [128, KC, T], BF16)
    o_KTc = cx.out("KTc", [128, 4, NCTX], BF16); o_Vc = cx.out("Vc", [NCTX, 512], BF16)
    if full_ctx:
        o_QTc = cx.out("QTc", [128, 4, NCTX], BF16); o_Zc = cx.out("Zc", [NCTX, 512], BF16)
        o_ycc = cx.out("ycTc", [128, 2, NCTX], BF16); o_ypc = cx.out("ypTc", [128, 2, NCTX], BF16)
        o_hxc = cx.out("hxTc", [128, KC, NCTX], BF16)

    wA_t, wA_b = cx.sb("wA_t", [128, KC, NWA * 128], BF16)
    for kc in range(KC):
        S.dma("pool", wA_t[:, kc, :], wA[kc * 128:(kc + 1) * 128, :], writes=[wA_b], also=True)
    wV_t, wV_b = cx.sb("wV_t", [128, KC, 512], BF16)
    for kc in range(KC):
        S.dma("pool", wV_t[:, kc, :], wV[kc * 128:(kc + 1) * 128, :], writes=[wV_b], also=True)
    if not premod:
        cT_t, cT_b = load_const(cx, "cT_t", cT[:, :, :], [128, KC, 2])
        adab_t, adab_b = load_const(cx, "adab_t", ada_b[:, :], [128, 48])
    n1g_t, n1g_b = load_const(cx, "n1g_t", n1g[:, :], [128, KC])
    hmask_t, hmask_b = load_const(cx, "hmask_t", hmask[:, :], [128, 2 * HALO])
    cw_t, cw_b = load_const(cx, "cw_t", cw[:, :, :], [128, 2, 31])
    cvec_t, cvec_b = load_const(cx, "cvec_t", cvec[:, :, :], [128, 2, 4])
    pw_t, pw_b = cx.sb("pw_t", [128, 2, 128], BF16)
    S.dma("pool", pw_t[:], pw[:, :, :], writes=[pw_b])
    cdft_t, cdft_b = cx.sb("cdft_t", [128, 256], BF16)
    S.dma("pool", cdft_t[:], cdft[:, :], writes=[cdft_b])
    ones_t, ones_b = cx.sb("ones_t", [128, 128], F32)
    S.op("dve", lambda e: e.memset(ones_t[:], 1.0), writes=[ones_b])
    idA_t, idA_b = load_const(cx, "idA_t", identA[:, :], [128, 128])
    dg_t, dg_b = cx.sb("dg_t", [128, 2, 31, 128], BF16)
    for c in range(2):
        for k in range(31):
            S.op("dve", lambda e, c=c, k=k: e.tensor_scalar(out=dg_t[:, c, k, :], in0=idA_t[:, :], scalar1=cw_t[:, c, k:k + 1], scalar2=None, op0=ALU.mult),
                 reads=[idA_b, cw_b], writes=[dg_b])

    psr = PsumRing(cx, 6)
    pss = PsumRing(cx, 2, "pss")

    mod_t, mod_b = cx.sb("mod_t", [128, 48, 2], F32)
    if "mod_in" in cx.links:
        S.dma("sp", mod_t[:], cx.links["mod_in"][:, :, :], writes=[mod_b])
    else:
        _adaln_mod(cx, cT_t, cT_b, ada_w, adab_t, adab_b, mod_t, mod_b, psr, 12, 0)
        S.barrier()
        S.dma("sp", o_mod[:, :, :], mod_t[:], reads=[mod_b])
    gs_t, gs_b = cx.sb("gs_t", [128, KC, 2], F32)
    S.op("dve", lambda e: e.tensor_scalar(out=gs_t[:], in0=mod_t[:, 8:16, :], scalar1=1.0, scalar2=None, op0=ALU.add),
         reads=[mod_b], writes=[gs_b])
    S.op("dve", lambda e: e.tensor_tensor(out=gs_t[:], in0=gs_t[:], in1=n1g_t[:].unsqueeze(2).to_broadcast([128, KC, 2]), op=ALU.mult),
         reads=[gs_b, n1g_b], writes=[gs_b])

    WZ = T + 2 * HALO
    zb_t, zb_b = cx.sb("zb_t", [128, 2, WZ], BF16)
    ub_t, ub_b = cx.sb("ub_t", [128, 2, WZ], F32)
    WZC = NCTX + 2 * HALO
    zc_t, zc_b = cx.sb("zc_t", [128, 2, WZC], BF16)
    uc_t, uc_b = cx.sb("uc_t", [128, 2, WZC], F32)
    if full_ctx:
        S.op("pool", lambda e: e.memset(zc_t[:], 0.0), writes=[zc_b])
        S.op("pool", lambda e: e.memset(uc_t[:], 0.0), writes=[uc_b])

    with ExitStack() as st:
        xb = [cx.sb("xb%d" % i, [128, KC, 512], F32, stack=st) for i in range(2)]
        sqs = [cx.sb("sq%d" % i, [128, 512], F32, stack=st) for i in range(2)]
        rC = [cx.sb("rC%d" % i, [128, 512], F32, stack=st) for i in range(2)]
        rS = [cx.sb("rS%d" % i, [128, 512], F32, stack=st) for i in range(2)]
        rs_t, rs_b = cx.sb("rs_t", [128, 512], F32, stack=st)
        tmp = [cx.sb("tmp%d" % i, [128, 512], F32, stack=st) for i in range(2)]
        hx = [cx.sb("hx%d" % i, [128, KC, 512], BF16, stack=st) for i in range(2)]
        t1 = [cx.sb("t1_%d" % i, [128, 512], F32, stack=st) for i in range(2)]
        t2 = [cx.sb("t2_%d" % i, [128, 512], F32, stack=st) for i in range(2)]
        qo = [cx.sb("qo%d" % i, [128, 512], BF16, stack=st) for i in range(2)]
        sg = [cx.sb("sg%d" % i, [128, 512], F32, stack=st) for i in range(2)]
        uf = [cx.sb("uf%d" % i, [128, 2, 512], BF16, stack=st) for i in range(1)]
        vo = [cx.sb("vo%d" % i, [128, 512], BF16, stack=st) for i in range(2)]
        zo = [cx.sb("zo%d" % i, [128, 512], BF16, stack=st) for i in range(2)]

        blocks = [("main", i * 512, 512) for i in range(4)] + [("halo", 0, 2 * HALO), ("ctx", 0, NCTX)]
        for bi, (kind, c0, n) in enumerate(blocks):
            xt, xbb = xb[bi % 2]
            src = {"main": xT, "halo": xTh, "ctx": cxT}[kind]
            j = 1 if kind == "ctx" else 0
            for kc in range(KC):
                S.dma("sp", xt[:, kc, 0:n], src[kc * 128:(kc + 1) * 128, c0:c0 + n], writes=[xbb], also=(kc > 0))
            pst, psb = pss.next()
            if kind == "main":
                ropeC_t, ropeC_b = rC[bi % 2]
                ropeS_t, ropeS_b = rS[bi % 2]
                S.dma("sp", ropeC_t[:, :], ropeC[:, c0:c0 + n], writes=[ropeC_b])
                S.dma("sp", ropeS_t[:, :], ropeS[:, c0:c0 + n], writes=[ropeS_b])
            for kc in range(KC):
                sq_t, sq_b = sqs[kc % 2]
                S.op("act", lambda e, kc=kc: e.activation(out=sq_t[:, 0:n], in_=xt[:, kc, 0:n], func=AF.Square), reads=[xbb], writes=[sq_b])
                S.op("pe", lambda e, kc=kc: e.matmul(pst[:, 0:n], ones_t[:, :], sq_t[:, 0:n], start=(kc == 0), stop=(kc == KC - 1)),
                     reads=[ones_b, sq_b], writes=[psb], inc=True)
            S.op("dve", lambda e: e.tensor_scalar(out=rs_t[:, 0:n], in0=pst[:, 0:n], scalar1=1.0 / D, scalar2=EPS, op0=ALU.mult, op1=ALU.add),
                 reads=[psb], writes=[rs_b])
            S.op("act", lambda e: e.activation(out=rs_t[:, 0:n], in_=rs_t[:, 0:n], func=AF.Sqrt), reads=[rs_b], writes=[rs_b])
            S.op("dve", lambda e: e.reciprocal(out=rs_t[:, 0:n], in_=rs_t[:, 0:n]), reads=[rs_b], writes=[rs_b])
            hxt, hxb = hx[bi % 2]
            for kc in range(KC):
                tt, tb = tmp[kc % 2]
                S.op("dve", lambda e, kc=kc, tt=tt: e.tensor_tensor(out=tt[:, 0:n], in0=xt[:, kc, 0:n], in1=rs_t[:, 0:n], op=ALU.mult),
                     reads=[xbb, rs_b], writes=[tb])
                S.op("act", lambda e, kc=kc, tt=tt: e.activation(out=hxt[:, kc, 0:n], in_=tt[:, 0:n], func=AF.Identity,
                                                                 bias=mod_t[:, kc, j:j + 1], scale=gs_t[:, kc, j:j + 1]),
                     reads=[tb, mod_b, gs_b], writes=[hxb])
            if kind == "main":
                S.dma("act", o_hx[:, :, c0:c0 + n], hxt[:, :, 0:n], reads=[hxb])
            elif kind == "ctx" and full_ctx:
                S.dma("act", o_hxc[:, :, :], hxt[:, :, 0:n], reads=[hxb])

            def fm(ci):
                pt, pb = psr.next()
                for kc in range(KC):
                    S.op("pe", lambda e, kc=kc: e.matmul(pt[:, 0:n], wA_t[:, kc, ci * 128:(ci + 1) * 128], hxt[:, kc, 0:n],
                                                         start=(kc == 0), stop=(kc == KC - 1)),
                         reads=[wA_b, hxb], writes=[pb], inc=(kc == KC - 1))
                return pt, pb

            if kind != "halo":
                for qk in range(2):
                    if qk == 0 and kind == "ctx" and not full_ctx:
                        continue
                    for h in range(4):
                        if kind == "main":
                            dsl = o_QT[:, h, c0:c0 + n] if qk == 0 else o_KTh[h][:, c0:c0 + n]
                        else:
                            dsl = (o_QTc if qk == 0 else o_KTc)[:, h, 0:n]
                        pt, pb = fm(qk * 8 + h)
                        qt, qb = qo[h % 2]
                        if kind == "main":
                            p2, p2b = fm(qk * 8 + 4 + h)
                            a1, a1b = t1[h % 2]
                            a2, a2b = t2[h % 2]
                            S.op("dve", lambda e: e.tensor_tensor(out=a1[:, 0:n], in0=pt[:, 0:n], in1=ropeC_t[:, 0:n], op=ALU.mult),
                                 reads=[pb, ropeC_b], writes=[a1b])
                            S.op("dve", lambda e: e.tensor_tensor(out=a2[:, 0:n], in0=p2[:, 0:n], in1=ropeS_t[:, 0:n], op=ALU.mult),
                                 reads=[p2b, ropeS_b], writes=[a2b])
                            S.op("pool", lambda e: e.tensor_tensor(out=qt[:, 0:n], in0=a1[:, 0:n], in1=a2[:, 0:n], op=ALU.add),
                                 reads=[a1b, a2b], writes=[qb])
                            S.dma("pool", dsl, qt[:, 0:n], reads=[qb])
                        else:
                            S.op("act", lambda e: e.activation(out=qt[:, 0:n], in_=pt[:, 0:n], func=AF.Copy), reads=[pb], writes=[qb])
                            S.dma("act", dsl, qt[:, 0:n], reads=[qb])
            do_cp = (kind != "ctx") or full_ctx
            if do_cp:
                if kind == "main":
                    zt, zbb, ut, ubb, col = zb_t, zb_b, ub_t, ub_b, HALO + c0
                elif kind == "ctx":
                    zt, zbb, ut, ubb, col = zc_t, zc_b, uc_t, uc_b, HALO
                for c in range(2):
                    pa, pab = fm(16 + c)
                    pg, pgb = fm(18 + c)
                    s_t, s_b = sg[c % 2]
                    S.op("act", lambda e: e.activation(out=s_t[:, 0:n], in_=pg[:, 0:n], func=AF.Sigmoid), reads=[pgb], writes=[s_b])
                    pu, pub = fm(20 + c)
                    if kind == "halo":
                        for (lo, dcol) in ((0, 0), (HALO, HALO + T)):
                            S.op("dve", lambda e, lo=lo, dcol=dcol: e.tensor_tensor(out=zb_t[:, c, dcol:dcol + HALO], in0=pa[:, lo:lo + HALO],
                                                                                     in1=s_t[:, lo:lo + HALO], op=ALU.mult),
                                 reads=[pab, s_b], writes=[zb_b])
                            S.op("dve", lambda e, lo=lo, dcol=dcol: e.tensor_tensor(out=zb_t[:, c, dcol:dcol + HALO], in0=zb_t[:, c, dcol:dcol + HALO],
                                                                                     in1=hmask_t[:, lo:lo + HALO], op=ALU.mult),
                                 reads=[zb_b, hmask_b], writes=[zb_b])
                            S.op("dve", lambda e, lo=lo, dcol=dcol: e.tensor_tensor(out=ub_t[:, c, dcol:dcol + HALO], in0=pu[:, lo:lo + HALO],
                                                                                     in1=hmask_t[:, lo:lo + HALO], op=ALU.mult),
                                 reads=[pub, hmask_b], writes=[ub_b])
                    else:
                        S.op("dve", lambda e: e.tensor_tensor(out=zt[:, c, col:col + n], in0=pa[:, 0:n], in1=s_t[:, 0:n], op=ALU.mult),
                             reads=[pab, s_b], writes=[zbb])
                        S.op("act", lambda e: e.activation(out=ut[:, c, col:col + n], in_=pu[:, 0:n], func=AF.Copy), reads=[pub], writes=[ubb])
            if kind == "main" or (kind == "ctx" and full_ctx):
                uft, ufb = uf[0]
                for c in range(2):
                    pt, pb = fm(22 + c)
                    S.op("act", lambda e, c=c: e.activation(out=uft[:, c, 0:n], in_=pt[:, 0:n], func=AF.Copy), reads=[pb], writes=[ufb])
                for tt_ in range(n // 128):
                    pt, pb = psr.next()
                    for c in range(2):
                        S.op("pe", lambda e, c=c: e.matmul(pt[:, c * 256:(c + 1) * 256], uft[:, c, tt_ * 128:(tt_ + 1) * 128], cdft_t[:, :],
                                                           start=True, stop=True),
                             reads=[ufb, cdft_b], writes=[pb], inc=(c == 1))
                    z_t, z_b = zo[tt_ % 2]
                    S.op("act", lambda e: e.activation(out=z_t[:, :].rearrange("p (r c k) -> p r c k", r=2, c=2),
                                                       in_=pt[:, :].rearrange("p (c r k) -> p r c k", c=2, r=2), func=AF.Copy),
                         reads=[pb], writes=[z_b])
                    if kind == "main":
                        for q4 in range(4):
                            S.dma("act", o_Zq[q4][c0 + tt_ * 128:c0 + (tt_ + 1) * 128, :], z_t[:, q4 * 128:(q4 + 1) * 128], reads=[z_b])
                    else:
                        S.dma("act", o_Zc[c0 + tt_ * 128:c0 + (tt_ + 1) * 128, :], z_t[:, :], reads=[z_b])
            if kind != "halo":
                for tt_ in range(n // 128):
                    pt, pb = psr.next()
                    for kc in range(KC):
                        S.op("pe", lambda e, kc=kc: e.matmul(pt[:, :], hxt[:, kc, tt_ * 128:(tt_ + 1) * 128], wV_t[:, kc, :],
                                                             start=(kc == 0), stop=(kc == KC - 1)),
                             reads=[hxb, wV_b], writes=[pb], inc=(kc == KC - 1))
                    v_t, v_b = vo[tt_ % 2]
                    S.op("dve", lambda e: e.tensor_copy(out=v_t[:, :], in_=pt[:, :]), reads=[pb], writes=[v_b])
                    if kind == "main":
                        for h in range(4):
                            S.dma("sp", o_Vh[h][c0 + tt_ * 128:c0 + (tt_ + 1) * 128, :], v_t[:, h * 128:(h + 1) * 128], reads=[v_b])
                    else:
                        S.dma("sp", o_Vc[c0 + tt_ * 128:c0 + (tt_ + 1) * 128, :], v_t[:, :], reads=[v_b])

    S.barrier()
    if getattr(cx, "after_blocks", None) is not None:
        cx.after_blocks()
    segs = [(zb_t, zb_b, ub_t, ub_b, T, pinv, o_yc, o_yp)]
    if full_ctx:
        segs.append((zc_t, zc_b, uc_t, uc_b, NCTX, pinvc, o_ycc, o_ypc))
    with ExitStack() as st:
        acc_t, acc_b = cx.sb("acc_t", [128, 2, T], F32, stack=st)
        pin_t, pin_b = cx.sb("pin_t", [128, 2, T], F32, stack=st)
        w2_t, w2_b = cx.sb("w2_t", [128, 2, T + 2 * HALO], F32, stack=st)
        w4_t, w4_b = cx.sb("w4_t", [128, 2, T + 2 * HALO], F32, stack=st)
        w8_t, w8_b = cx.sb("w8_t", [128, T + 2 * HALO], F32, stack=st)
        s1 = [cx.sb("s1_%d" % i, [128, 512], F32, stack=st) for i in range(2)]
        s2 = [cx.sb("s2_%d" % i, [128, 512], F32, stack=st) for i in range(2)]
        s3 = [cx.sb("s3_%d" % i, [128, 512], F32, stack=st) for i in range(2)]
        yo = [cx.sb("yo%d" % i, [128, 2, 512], BF16, stack=st) for i in range(2)]
        dd = [cx.sb("dd%d" % i, [128, 2, 512], BF16, stack=st) for i in range(2)]
        po = [cx.sb("po%d" % i, [128, 2, 512], BF16, stack=st) for i in range(2)]
        for (zt, zbb, ut, ubb, n, pinv_d, oyc, oyp) in segs:
            off = HALO - 15
            for blk in range((n + 511) // 512):
                b0 = blk * 512
                nb = min(512, n - b0)
                for c in range(2):
                    pt, pb = psr.next()
                    for k in range(31):
                        S.op("pe", lambda e, c=c, k=k: e.matmul(pt[:, 0:nb], dg_t[:, c, k, :], zt[:, c, off + k + b0:off + k + b0 + nb],
                                                                start=(k == 0), stop=(k == 30)),
                             reads=[dg_b, zbb], writes=[pb], inc=(k == 30))
                    S.op("act", lambda e, c=c: e.activation(out=acc_t[:, c, b0:b0 + nb], in_=pt[:, 0:nb], func=AF.Identity, bias=cvec_t[:, c, 0:1]),
                         reads=[pb, cvec_b], writes=[acc_b])
            for blk in range((n + 511) // 512):
                b0 = blk * 512
                nb = min(512, n - b0)
                sq_t2, sq_b2 = s1[blk % 2]
                psum_, psumb = pss.next()
                pssq, pssqb = pss.next()
                for c in range(2):
                    S.op("pe", lambda e, c=c: e.matmul(psum_[:, 0:nb], ones_t[:, :], acc_t[:, c, b0:b0 + nb], start=(c == 0), stop=(c == 1)),
                         reads=[ones_b, acc_b], writes=[psumb], inc=(c == 1))
                for c in range(2):
                    S.op("act", lambda e, c=c: e.activation(out=sq_t2[:, 0:nb], in_=acc_t[:, c, b0:b0 + nb], func=AF.Square),
                         reads=[acc_b], writes=[sq_b2])
                    S.op("pe", lambda e, c=c: e.matmul(pssq[:, 0:nb], ones_t[:, :], sq_t2[:, 0:nb], start=(c == 0), stop=(c == 1)),
                         reads=[ones_b, sq_b2], writes=[pssqb], inc=True)
                mean_t, mean_b = s2[blk % 2]
                var_t, var_b = s3[blk % 2]
                S.op("dve", lambda e: e.tensor_scalar(out=mean_t[:, 0:nb], in0=psum_[:, 0:nb], scalar1=1.0 / 256, scalar2=None, op0=ALU.mult),
                     reads=[psumb], writes=[mean_b])
                S.op("dve", lambda e: e.tensor_tensor(out=var_t[:, 0:nb], in0=mean_t[:, 0:nb], in1=mean_t[:, 0:nb], op=ALU.mult),
                     reads=[mean_b], writes=[var_b])
                S.op("dve", lambda e: e.scalar_tensor_tensor(out=var_t[:, 0:nb], in0=pssq[:, 0:nb], scalar=1.0 / 256, in1=var_t[:, 0:nb],
                                                             op0=ALU.mult, op1=ALU.subtract),
                     reads=[pssqb, var_b], writes=[var_b])
                S.op("dve", lambda e: e.tensor_scalar(out=var_t[:, 0:nb], in0=var_t[:, 0:nb], scalar1=EPS, scalar2=None, op0=ALU.add),
                     reads=[var_b], writes=[var_b])
                S.op("act", lambda e: e.activation(out=var_t[:, 0:nb], in_=var_t[:, 0:nb], func=AF.Sqrt), reads=[var_b], writes=[var_b])
                S.op("dve", lambda e: e.reciprocal(out=var_t[:, 0:nb], in_=var_t[:, 0:nb]), reads=[var_b], writes=[var_b])
                y_t, y_b = yo[blk % 2]
                for c in range(2):
                    S.op("dve", lambda e, c=c: e.tensor_tensor(out=sq_t2[:, 0:nb], in0=acc_t[:, c, b0:b0 + nb], in1=mean_t[:, 0:nb], op=ALU.subtract),
                         reads=[acc_b, mean_b], writes=[sq_b2])
                    S.op("dve", lambda e, c=c: e.tensor_tensor(out=sq_t2[:, 0:nb], in0=sq_t2[:, 0:nb], in1=var_t[:, 0:nb], op=ALU.mult),
                         reads=[sq_b2, var_b], writes=[sq_b2])
                    S.op("act", lambda e, c=c: e.activation(out=y_t[:, c, 0:nb], in_=sq_t2[:, 0:nb], func=AF.Silu,
                                                            bias=cvec_t[:, c, 2:3], scale=cvec_t[:, c, 1:2]),
                         reads=[sq_b2, cvec_b], writes=[y_b])
                S.dma("act", oyc[:, :, b0:b0 + nb], y_t[:, :, 0:nb], reads=[y_b])
            S.dma("sp", pin_t[:, :, 0:n], pinv_d[:, :, :], writes=[pin_b])
            W = n + 2 * HALO
            S.op("dve", lambda e: e.tensor_tensor(out=w2_t[:, :, 1:W], in0=ut[:, :, 0:W - 1], in1=ut[:, :, 1:W], op=ALU.add),
                 reads=[ubb], writes=[w2_b])
            S.op("dve", lambda e: e.tensor_tensor(out=w4_t[:, :, 2:W - 1], in0=w2_t[:, :, 1:W - 2], in1=w2_t[:, :, 3:W], op=ALU.add),
                 reads=[w2_b], writes=[w4_b])
            S.op("dve", lambda e: e.tensor_tensor(out=w8_t[:, 4:W - 3], in0=w4_t[:, 1, 2:W - 5], in1=w4_t[:, 1, 6:W - 1], op=ALU.add),
                 reads=[w4_b], writes=[w8_b])
            H = HALO
            S.op("dve", lambda e: e.tensor_copy(out=acc_t[0:64, 0, 0:n], in_=w2_t[0:64, 0, H:H + n]), reads=[w2_b, acc_b], writes=[acc_b])
            S.op("dve", lambda e: e.tensor_copy(out=acc_t[64:128, 0, 0:n], in_=w4_t[64:128, 0, H:H + n]), reads=[w4_b, acc_b], writes=[acc_b])
            S.op("dve", lambda e: e.tensor_copy(out=acc_t[0:64, 1, 0:n], in_=w8_t[0:64, H:H + n]), reads=[w8_b, acc_b], writes=[acc_b])
            S.op("dve", lambda e: e.tensor_tensor(out=acc_t[64:128, 1, 0:n], in0=w8_t[64:128, H - 4:H - 4 + n], in1=w8_t[64:128, H + 4:H + 4 + n], op=ALU.add),
                 reads=[w8_b, acc_b], writes=[acc_b])
            S.op("dve", lambda e: e.tensor_tensor(out=acc_t[:, :, 0:n], in0=acc_t[:, :, 0:n], in1=pin_t[:, :, 0:n], op=ALU.mult),
                 reads=[acc_b, pin_b], writes=[acc_b])
            for blk in range((n + 511) // 512):
                b0 = blk * 512
                nb = min(512, n - b0)
                d_t, d_b = dd[blk % 2]
                S.op("dve", lambda e: e.tensor_tensor(out=d_t[:, :, 0:nb], in0=acc_t[:, :, b0:b0 + nb], in1=ut[:, :, H + b0:H + b0 + nb], op=ALU.subtract),
                     reads=[acc_b, ubb], writes=[d_b])
                p_t, p_b = po[blk % 2]
                for c in range(2):
                    pt, pb = psr.next()
                    S.op("pe", lambda e, c=c: e.matmul(pt[:, 0:nb], pw_t[:, c, :], d_t[:, c, 0:nb], start=True, stop=True),
                         reads=[pw_b, d_b], writes=[pb])
                    S.op("act", lambda e, c=c: e.activation(out=p_t[:, c, 0:nb], in_=pt[:, 0:nb], func=AF.Identity, scale=cvec_t[:, c, 3:4]),
                         reads=[pb, cvec_b], writes=[p_b])
                S.dma("act", oyp[:, :, b0:b0 + nb], p_t[:, :, 0:nb], reads=[p_b])
    if standalone:
        S.finish(None)
    return cx


def fm_vec(v):
    v = np.asarray(v, np.float32)
    return np.ascontiguousarray(v.reshape(-1, 128).T)


OFF_F, OFF_Q, OFF_K, OFF_V, OFF_C, OFF_P, OFF_G = 0, 256, 768, 1280, 1792, 2304, 2560


def wA_layout(w_in):
    cols = []
    swap128 = np.concatenate([SWAP64, 64 + SWAP64])
    for base in (OFF_Q, OFF_K):
        for h in range(4):
            cols.append(base + h * 128 + np.arange(128))
        for h in range(4):
            cols.append(base + h * 128 + swap128)
    cols.append(OFF_C + np.arange(512))
    cols.append(OFF_P + np.arange(256))
    cols.append(OFF_F + np.arange(256))
    cols = np.concatenate(cols)
    return np.ascontiguousarray(w_in[:, cols])


def host_A(inp, l, xT_all, ctxT_all):
    w_in = np.asarray(inp["w_in"][l])
    wA = wA_layout(w_in)
    wV = np.ascontiguousarray(w_in[:, OFF_V:OFF_C])
    ada_w = np.ascontiguousarray(inp["ada_w"][l])
    ada_b = fm_vec(inp["ada_b"][l])
    n1g = fm_vec(inp["norm1_g"][l])
    cw = np.ascontiguousarray(np.asarray(inp["conv_dw_w"][l]).T.reshape(2, 128, 31).transpose(1, 0, 2))
    cvec = np.stack([fm_vec(inp["conv_dw_b"][l]), fm_vec(inp["conv_ln_g"][l]), fm_vec(inp["conv_ln_b"][l]),
                     fm_vec(inp["pool_scale"][l])], axis=2).astype(np.float32)
    pw_in = np.asarray(inp["pool_w"][l])
    pw = np.zeros((128, 2, 128), np.float32)
    for c in range(2):
        for g in range(2):
            pw[g * 64:(g + 1) * 64, c, g * 64:(g + 1) * 64] = pw_in[2 * c + g]
    cdft = chan_dft_table()
    pinvc = pool_invcnt(0, NCTX, NCTX)
    maps = []
    for i in range(NCORES):
        b, j = i // 4, i % 4
        t0 = j * T
        hm = np.zeros((128, 2 * HALO), np.float32)
        if j > 0:
            hm[:, 0:HALO] = 1.0
        if j < 3:
            hm[:, HALO:] = 1.0
        cT = np.stack([fm_vec(inp["c"][b]), fm_vec(inp["c_ctx"])], axis=2)
        rC, rS = rope_tables(t0, T)
        m = dict(cT=np.ascontiguousarray(cT), ada_w=ada_w, ada_b=ada_b, identA=np.eye(128, dtype=np.float32),
                 n1g=n1g, wA=wA, wV=wV, ropeC=rC, ropeS=rS, hmask=hm, cw=cw, cvec=cvec,
                 pinv=pool_invcnt(t0, T, SEQ), pinvc=pinvc, pw=pw, cdft=cdft)
        if xT_all is not None:
            xTh = np.zeros((D, 2 * HALO), np.float32)
            if j > 0:
                xTh[:, 0:HALO] = xT_all[b][:, t0 - HALO:t0]
            if j < 3:
                xTh[:, HALO:] = xT_all[b][:, t0 + T:t0 + T + HALO]
            m.update(xT=np.ascontiguousarray(xT_all[b][:, t0:t0 + T]), xTh=xTh, cxT=np.ascontiguousarray(ctxT_all[b]))
        maps.append(m)
    return maps


_PROGS = {}


def get_prog(key, builder, *args):
    if key not in _PROGS:
        _PROGS[key] = builder(*args)
    return _PROGS[key]


def run_prog(cx, maps):
    res = run_bass_kernel_spmd(cx.nc, maps, core_ids=list(range(NCORES)))
    return res.results


NKEY = NCTX + SEQ
NKC = NKEY // 128
SCALE_F = 1.0 / math.sqrt(SEQ * 64.0)
SCALE_FC = 1.0 / math.sqrt(NCTX * 64.0)


def fft_tables(j):
    n1 = np.arange(128)
    ph = 2 * np.pi * np.outer(n1, n1) / 128.0
    C, Sn = np.cos(ph), np.sin(ph)
    tabA = np.stack([np.concatenate([C, Sn], 1), np.concatenate([-Sn, C], 1)], axis=1).astype(np.float32)
    n2 = np.arange(64)[:, None, None]
    k1 = np.arange(128)[None, :, None]
    k2 = (16 * j + np.arange(16))[None, None, :]
    th = 2 * np.pi * n2 * (k1 + 128 * k2) / float(SEQ)
    M = np.stack([np.cos(th) * SCALE_F, -np.sin(th) * SCALE_F], axis=2)
    tabC = np.concatenate([M, M], axis=0).astype(np.float32)
    l = np.arange(NCTX)
    a = 2 * np.pi * np.outer(l, l) / float(NCTX)
    Cc = (np.cos(a) * SCALE_FC).reshape(2, 128, NCTX).transpose(1, 0, 2)
    Sc = (-np.sin(a) * SCALE_FC).reshape(2, 128, NCTX).transpose(1, 0, 2)
    tabX = np.stack([Cc, Sc], axis=1).astype(np.float32)
    return tabA, tabC, np.ascontiguousarray(tabX)


def build_B(full_ctx, lam_init, dbg=False, only=None, cx=None):
    standalone = cx is None
    if standalone:
        cx = Ctx("B")
    nc, S = cx.nc, cx.S
    QT = cx.inp("QT", [128, 4, T], BF16)
    KTgh = [cx.inp("KTg%d" % h, [512, T], BF16) for h in range(4)]
    Vgh = [cx.inp("Vg%d" % h, [SEQ, 128], BF16) for h in range(4)]
    Zgq = [cx.inp("Zg%d" % q, [SEQ, 128], BF16) for q in range(4)]
    KTc = cx.inp("KTc", [128, 4, NCTX], BF16); Vc = cx.inp("Vc", [NCTX, 512], BF16)
    ycT = cx.inp("ycT", [128, 2, T], BF16); ypT = cx.inp("ypT", [128, 2, T], BF16); hxT = cx.inp("hxT", [128, KC, T], BF16)
    xT = cx.inp("xT", [D, T]); mod = cx.inp("mod", [128, 48, 2])
    wg = cx.inp("wg", [D, 4, D]); wo = cx.inp("wo", [1280, D]); wout = cx.inp("wout", [D, D])
    lamv = cx.inp("lamv", [4, 64]); subg = cx.inp("subg", [128, 1]); n2g = cx.inp("n2g", [128, KC])
    tabA = cx.inp("tabA", [128, 2, 256]); tabC = cx.inp("tabC", [128, 128, 2, 16]); ident = cx.inp("ident", [128, 128])
    o_xm = cx.out("xmT", [D, T]); o_h2 = cx.out("hx2T", [128, KC, T], BF16)
    if full_ctx:
        QTc = cx.inp("QTc", [128, 4, NCTX], BF16); Zc = cx.inp("Zc", [NCTX, 512], BF16)
        ycTc = cx.inp("ycTc", [128, 2, NCTX], BF16); ypTc = cx.inp("ypTc", [128, 2, NCTX], BF16); hxTc = cx.inp("hxTc", [128, KC, NCTX], BF16)
        cxT = cx.inp("cxT", [D, NCTX]); tabX = cx.inp("tabX", [128, 2, 2, NCTX])
        o_cxm = cx.out("cxmT", [D, NCTX]); o_ch2 = cx.out("chx2T", [128, KC, NCTX], BF16)

    mod_t, mod_b = load_const(cx, "mod_t", mod[:, :, :], [128, 48, 2])
    n2g_t, n2g_b = load_const(cx, "n2g_t", n2g[:, :], [128, KC])
    subg_t, subg_b = load_const(cx, "subg_t", subg[:, :], [128, 1])
    ones_t, ones_b = cx.sb("ones_t", [128, 128], F32)
    S.op("dve", lambda e: e.memset(ones_t[:], 1.0), writes=[ones_b])
    id_t, id_b = cx.sb("id_t", [128, 128], BF16)
    S.dma("pool", id_t[:], ident[:, :], writes=[id_b])
    lv_t, lv_b = cx.sb("lv_t", [128, 4, 64], F32)
    for r in range(4):
        S.dma("sp", lv_t[:, r, :], lamv[r:r + 1, :].partition_broadcast(128), writes=[lv_b], also=(r > 0))
    lp_t, lp_b = cx.sb("lp_t", [128, 2, 64], F32)
    S.op("dve", lambda e: e.tensor_tensor(out=lp_t[:], in0=lv_t[:, 0:4:2, :], in1=lv_t[:, 1:4:2, :], op=ALU.mult), reads=[lv_b], writes=[lp_b])
    ls_t, ls_b = cx.sb("ls_t", [128, 2], F32)
    S.op("dve", lambda e: e.reduce_sum(out=ls_t[:], in_=lp_t[:], axis=mybir.AxisListType.X), reads=[lp_b], writes=[ls_b])
    S.op("act", lambda e: e.activation(out=ls_t[:], in_=ls_t[:], func=AF.Exp), reads=[ls_b], writes=[ls_b])
    nlam_t, nlam_b = cx.sb("nlam_t", [128, 1], F32)
    S.op("dve", lambda e: e.tensor_tensor(out=nlam_t[:], in0=ls_t[:, 1:2], in1=ls_t[:, 0:1], op=ALU.subtract), reads=[ls_b], writes=[nlam_b])
    S.op("dve", lambda e: e.tensor_scalar(out=nlam_t[:], in0=nlam_t[:], scalar1=-float(lam_init), scalar2=None, op0=ALU.add),
         reads=[nlam_b], writes=[nlam_b])
    S.op("dve", lambda e: e.tensor_scalar(out=subg_t[:], in0=subg_t[:], scalar1=float(1.0 - lam_init), scalar2=None, op0=ALU.mult),
         reads=[subg_b], writes=[subg_b])
    gs_t, gs_b = cx.sb("gs_t", [128, KC, 2], F32)
    S.op("dve", lambda e: e.tensor_scalar(out=gs_t[:], in0=mod_t[:, 32:40, :], scalar1=1.0, scalar2=None, op0=ALU.add), reads=[mod_b], writes=[gs_b])
    S.op("dve", lambda e: e.tensor_tensor(out=gs_t[:], in0=gs_t[:], in1=n2g_t[:].unsqueeze(2).to_broadcast([128, KC, 2]), op=ALU.mult),
         reads=[gs_b, n2g_b], writes=[gs_b])

    yf_t, yf_b = cx.sb("yf_t", [128, 2, T], BF16)

    oa_t, oa_b = cx.sb("oa_t", [128, 4, T], BF16)
    if full_ctx:
        yfc_t, yfc_b = cx.sb("yfc_t", [128, 2, NCTX], BF16)
        oac_t, oac_b = cx.sb("oac_t", [128, 4, NCTX], BF16)

    with ExitStack() as st:
        zs_t, zs_b = cx.sb("zs_t", [128, 64 * 512], BF16, stack=st)
        tt_t, tt_b = cx.sb("tt_t", [128, 128 * 256], BF16, stack=st)
        tA_t, tA_b = cx.sb("tA_t", [128, 2, 256], BF16, stack=st)
        tC_t, tC_b = cx.sb("tC_t", [128, 128 * 32], BF16, stack=st)
        S.dma("pool", tA_t[:], tabA[:, :, :], writes=[tA_b])
        S.dma("pool", tC_t[:], tabC[:, :, :, :].rearrange("p a b c -> p (a b c)"), writes=[tC_b])
        if cx.after_fft_tables is not None:
            cx.after_fft_tables()
        zdst = zs_t[:, :].rearrange("p (n c) -> p n c", c=512)
        for q4 in range(4):
            S.dma("sp", zdst[:, :, q4 * 128:(q4 + 1) * 128], Zgq[q4][:, :].rearrange("(a b) c -> a b c", b=64), writes=[zs_b], also=(q4 > 0))
        psA = PsumRing(cx, 3, "psA", stack=st)
        psC = [cx.ps("psC%d" % i, [128, 512], F32, stack=st) for i in range(4)]
        zv = zs_t[:, :].rearrange("p (n r c k) -> p r c n k", n=64, r=2, c=2)
        ttv = tt_t[:, :].rearrange("p (k x) -> p k x", x=256)
        for cp2 in range(64):
            pt, pb = psA.next()
            for s_ in range(2):
                cp = cp2 * 2 + s_
                for r in range(2):
                    for c2 in range(2):
                        S.op("pe", lambda e, r=r, cp=cp, s_=s_, c2=c2: e.matmul(pt[c2 * 64:(c2 + 1) * 64, s_ * 256:(s_ + 1) * 256], zv[:, r, c2, :, cp],
                                                                                 tA_t[:, r, :], start=(r == 0), stop=(r == 1), tile_position=(0, c2 * 64)),
                             reads=[zs_b, tA_b], writes=[pb], inc=(s_ == 1 and r == 1 and c2 == 1))
            eng = "act" if cp2 % 2 == 0 else "dve"
            if eng == "act":
                S.op("act", lambda e: e.activation(out=tt_t[:, cp2 * 512:(cp2 + 1) * 512], in_=pt[:, :], func=AF.Copy), reads=[pb], writes=[tt_b])
            else:
                S.op("dve", lambda e: e.tensor_copy(out=tt_t[:, cp2 * 512:(cp2 + 1) * 512], in_=pt[:, :]), reads=[pb], writes=[tt_b])
        if only == "fft":
            d_tt = cx.out("dbg_tt", [128, 128 * 256], BF16)
            S.dma("sp", d_tt[:, :], tt_t[:, :], reads=[tt_b])
        tcv = tC_t[:, :].rearrange("p (k r j) -> p k r j", r=2, j=16)
        psCb = [Buf("psCb%d" % i) for i in range(4)]
        for c2 in range(2):
            for k1 in range(128):
                bank = k1 // 32
                col = (k1 % 32) * 16
                for r in range(2):
                    S.op("pe", lambda e, r=r, k1=k1: e.matmul(psC[bank][0][:, col:col + 16], ttv[c2 * 64:(c2 + 1) * 64, :, r * 128 + k1],
                                                             tcv[c2 * 64:(c2 + 1) * 64, k1, r, :], start=(r == 0), stop=(r == 1),
                                                             tile_position=(c2 * 64, 0)),
                         reads=[tt_b, tC_b], writes=[psCb[bank]], inc=(r == 1 and k1 % 32 == 31))
            for bank in range(4):
                dst = yf_t[:, c2, :].rearrange("p (j k) -> p k j", k=128)[:, bank * 32:(bank + 1) * 32, :]
                srcp = psC[bank][0][:, :].rearrange("p (k j) -> p k j", j=16)
                if bank % 2 == 0:
                    S.op("act", lambda e: e.activation(out=dst, in_=srcp, func=AF.Copy), reads=[psCb[bank]], writes=[yf_b])
                else:
                    S.op("dve", lambda e: e.tensor_copy(out=dst, in_=srcp), reads=[psCb[bank]], writes=[yf_b])
        if full_ctx:
            zc_t, zc_b = cx.sb("zc_t", [128, 2, 512], BF16, stack=st)
            tX_t, tX_b = cx.sb("tX_t", [128, 2, 2, NCTX], BF16, stack=st)
            S.dma("sp", zc_t[:], Zc[:, :].rearrange("(a p) c -> p a c", p=128), writes=[zc_b])
            S.dma("pool", tX_t[:], tabX[:, :, :, :], writes=[tX_b])
            for c2 in range(2):
                pt, pb = psA.next()
                i = 0
                for r in range(2):
                    for tl in range(2):
                        S.op("pe", lambda e, r=r, tl=tl, i=i: e.matmul(pt[:, 0:NCTX], zc_t[:, tl, r * 256 + c2 * 128:r * 256 + (c2 + 1) * 128],
                                                                       tX_t[:, r, tl, :], start=(i == 0), stop=(i == 3)),
                             reads=[zc_b, tX_b], writes=[pb], inc=(i == 3))
                        i += 1
                S.op("act", lambda e: e.activation(out=yfc_t[:, c2, :], in_=pt[:, 0:NCTX], func=AF.Copy), reads=[pb], writes=[yfc_b])

    S.barrier()
    if only == "fft":
        d_yf = cx.out("dbg_yf", [128, 2, T], BF16)
        S.dma("sp", d_yf[:, :, :], yf_t[:], reads=[yf_b])
        if standalone:
            S.finish(None)
        return cx
    wg_t, wg_b = cx.sb("wg_t", [128, KC, 4, D], BF16)
    for kc in range(KC):
        S.dma("pool", wg_t[:, kc, :, :], wg[kc * 128:(kc + 1) * 128, :, :], writes=[wg_b], also=True)
    with ExitStack() as st:
        kt = [cx.sb("kt%d" % i, [128, NKEY], BF16, stack=st) for i in range(2)]
        vt = [cx.sb("vt%d" % i, [128, NKC, 128], BF16, stack=st) for i in range(2)]
        qt = [cx.sb("qt%d" % i, [128, T + NCTX], BF16, stack=st) for i in range(2)]
        pT = [cx.sb("pT%d" % i, [128, 2, 512], BF16, stack=st) for i in range(3)]
        psS = [cx.ps("psS%d" % i, [128, 1024], F32, stack=st) for i in range(2)]
        psO = [cx.ps("psO%d" % i, [128, 512], F32, stack=st) for i in range(2)]
        psE = [cx.ps("psE%d" % i, [128, 512], F32, stack=st) for i in range(1)]
        psD, psD_b = cx.ps("psD", [128, 512], F32, stack=st)
        dsb = [cx.sb("dsb%d" % i, [64, 512], F32, stack=st) for i in range(2)]
        on32_t, on32_b = cx.sb("on32_t", [128, 32], BF16, stack=st)
        S.op("pool", lambda e: e.memset(on32_t[:], 1.0), writes=[on32_b])
        selr = []
        for m in range(2):
            sl_t, sl_b = cx.sb("selr%d" % m, [64, 128], F32, stack=st)
            S.op("pool", lambda e: e.memset(sl_t[:], 0.0), writes=[sl_b])
            S.op("pool", lambda e, m=m: e.memset(sl_t[m * 32:m * 32 + 1, :], 1.0), reads=[sl_b], writes=[sl_b])
            selr.append((sl_t, sl_b))
        osb = [cx.sb("osb%d" % i, [128, 2, 512], F32, stack=st) for i in range(2)]
        rc = [cx.sb("rc%d" % i, [128, 512], F32, stack=st) for i in range(2)]
        oo_t, oo_b = cx.sb("at_oo_t", [128, 512], F32, stack=st)
        o1_t, o1_b = cx.sb("at_o1_t", [128, 512], F32, stack=st)
        sq_t, sq_b = cx.sb("at_sq_t", [128, 512], F32, stack=st)
        rs_t, rs_b = cx.sb("at_rs_t", [128, 512], F32, stack=st)
        gi = 0
        pending_epi = []
        for h in range(4):
            k_t, k_b = kt[h % 2]
            v_t, v_b = vt[h % 2]
            q_t, q_b = qt[h % 2]
            S.dma("sp", k_t[:, 0:NCTX], KTc[:, h, :], writes=[k_b])
            gk = [cx.dram_bufs["KTg%d" % h]] if ("KTg%d" % h) in cx.dram_bufs else []
            gv = [cx.dram_bufs["Vg%d" % h]] if ("Vg%d" % h) in cx.dram_bufs else []
            S.dma("sp", k_t[:, NCTX:NKEY].rearrange("p (r t) -> p r t", r=4), KTgh[h][:, :].rearrange("(r p) t -> p r t", p=128), reads=gk, writes=[k_b], also=True)
            S.dma("sp", v_t[:, 0:2, :], Vc[:, h * 128:(h + 1) * 128].rearrange("(c p) d -> p c d", p=128), writes=[v_b])
            vsrc = Vgh[h][:, :].rearrange("(c p) d -> p c d", p=128)
            for g in range(2):
                S.dma("sp", v_t[:, 2 + g * 32:2 + (g + 1) * 32, :], vsrc[:, g * 32:(g + 1) * 32, :], reads=gv, writes=[v_b], also=True)
            S.dma("sp", q_t[:, 0:T], QT[:, h, :], writes=[q_b])
            if full_ctx:
                S.dma("sp", q_t[:, T:T + NCTX], QTc[:, h, :], writes=[q_b], also=True)
            groups = [(g * 512, 512, NKC, oa_t, oa_b, g * 512) for g in range(4)]
            if full_ctx:
                groups.append((T, NCTX, 2, oac_t, oac_b, 0))
            for (q0, nq, nkc, dst_t, dst_b, d0) in groups:
                psOb = [psO[0][1], psO[1][1]]

                def qk(kc, q0=q0, nq=nq):
                    ps_t, ps_b = psS[kc % 2]
                    for m in range(2):
                        S.op("pe", lambda e, m=m: e.matmul(ps_t[:, m * 512:m * 512 + nq], k_t[m * 64:(m + 1) * 64, kc * 128:(kc + 1) * 128],
                                                           q_t[m * 64:(m + 1) * 64, q0:q0 + nq], start=True, stop=True, tile_position=(m * 64, 0)),
                             reads=[k_b, q_b], writes=[ps_b], inc=(m == 1))

                def den(kc, p_t, p_b, nq=nq, nkc=nkc):
                    for m in range(2):
                        S.op("pe", lambda e, m=m: e.matmul(psD[m * 32:(m + 1) * 32, 0:nq], on32_t[:, :], p_t[:, m, 0:nq], start=(kc == 0), stop=(kc == nkc - 1),
                                                           tile_position=(0, m * 32)),
                             reads=[p_b, on32_b], writes=[psD_b], inc=(m == 1))

                qk(0)
                den_prev = None
                for kc in range(nkc):
                    if kc + 1 < nkc:
                        qk(kc + 1)
                    if den_prev is not None:
                        den(*den_prev)
                    ps_t, ps_b = psS[kc % 2]
                    p_t, p_b = pT[kc % 3]
                    S.op("act", lambda e: e.activation(out=p_t[:, :, 0:nq], in_=ps_t[:, :].rearrange("p (m q) -> p m q", m=2)[:, :, 0:nq],
                                                       func=AF.Exp, scale=0.125),
                         reads=[ps_b], writes=[p_b])
                    for m in range(2):
                        S.op("pe", lambda e, m=m: e.matmul(psO[m][0][:, 0:nq], v_t[:, kc, :], p_t[:, m, 0:nq], start=(kc == 0), stop=(kc == nkc - 1)),
                             reads=[p_b, v_b], writes=[psOb[m]], inc=(m == 1))
                    den_prev = (kc, p_t, p_b)
                    if kc == nkc - 1:
                        den(*den_prev)
                    if pending_epi and (kc in (4, 10, 18) or kc == nkc - 1):
                        while pending_epi:
                            pending_epi.pop(0)()
                            if kc != nkc - 1:
                                break
                ob_t, ob_b = osb[gi % 2]
                gi += 1
                d_t, d_b = dsb[(gi - 1) % 2]
                S.op("dve", lambda e: e.tensor_copy(out=d_t[:, 0:nq], in_=psD[0:64, 0:nq]), reads=[psD_b], writes=[d_b])
                for m in range(2):
                    S.op("dve" if m == 0 else "pool", lambda e, m=m: e.tensor_copy(out=ob_t[:, m, 0:nq], in_=psO[m][0][:, 0:nq]), reads=[psOb[m]], writes=[ob_b]) if False else \
                        S.op("dve", lambda e, m=m: e.tensor_copy(out=ob_t[:, m, 0:nq], in_=psO[m][0][:, 0:nq]), reads=[psOb[m]], writes=[ob_b])

                def epi_a(nq=nq, d_t=d_t, d_b=d_b):
                    pe_t, pe_b = psE[0]
                    S.op("pe", lambda e: e.matmul(pe_t[:, 0:nq], selr[0][0][:, :], d_t[:, 0:nq], start=True, stop=True), reads=[selr[0][1], d_b], writes=[pe_b])
                    S.op("dve", lambda e: e.reciprocal(out=rc[0][0][:, 0:nq], in_=pe_t[:, 0:nq]), reads=[pe_b], writes=[rc[0][1]])

                def epi_b(nq=nq, ob_t=ob_t, ob_b=ob_b, d_t=d_t, d_b=d_b):
                    pe_t, pe_b = psE[0]
                    S.op("pe", lambda e: e.matmul(pe_t[:, 0:nq], selr[1][0][:, :], d_t[:, 0:nq], start=True, stop=True), reads=[selr[1][1], d_b], writes=[pe_b])
                    S.op("dve", lambda e: e.reciprocal(out=rc[1][0][:, 0:nq], in_=pe_t[:, 0:nq]), reads=[pe_b], writes=[rc[1][1]])
                    S.op("dve", lambda e: e.tensor_tensor(out=oo_t[:, 0:nq], in0=ob_t[:, 0, 0:nq], in1=rc[0][0][:, 0:nq], op=ALU.mult),
                         reads=[ob_b, rc[0][1]], writes=[oo_b])
                    S.op("dve", lambda e: e.tensor_tensor(out=o1_t[:, 0:nq], in0=ob_t[:, 1, 0:nq], in1=rc[1][0][:, 0:nq], op=ALU.mult),
                         reads=[ob_b, rc[1][1]], writes=[o1_b])
                    S.op("dve", lambda e: e.scalar_tensor_tensor(out=oo_t[:, 0:nq], in0=o1_t[:, 0:nq], scalar=nlam_t[:, 0:1], in1=oo_t[:, 0:nq],
                                                                 op0=ALU.mult, op1=ALU.add),
                         reads=[oo_b, o1_b, nlam_b], writes=[oo_b])
                    S.op("pool", lambda e: e.tensor_tensor(out=sq_t[:, 0:nq], in0=oo_t[:, 0:nq], in1=oo_t[:, 0:nq], op=ALU.mult), reads=[oo_b], writes=[sq_b])

                def epi_c(nq=nq, dst_t=dst_t, dst_b=dst_b, d0=d0, h=h):
                    pe_t, pe_b = psE[0]
                    S.op("pe", lambda e: e.matmul(pe_t[:, 0:nq], ones_t[:, :], sq_t[:, 0:nq], start=True, stop=True), reads=[ones_b, sq_b], writes=[pe_b])
                    S.op("dve", lambda e: e.tensor_scalar(out=rs_t[:, 0:nq], in0=pe_t[:, 0:nq], scalar1=1.0 / 128, scalar2=EPS, op0=ALU.mult, op1=ALU.add),
                         reads=[pe_b], writes=[rs_b])
                    S.op("act", lambda e: e.activation(out=rs_t[:, 0:nq], in_=rs_t[:, 0:nq], func=AF.Ln), reads=[rs_b], writes=[rs_b])
                    S.op("act", lambda e: e.activation(out=rs_t[:, 0:nq], in_=rs_t[:, 0:nq], func=AF.Exp, scale=-0.5), reads=[rs_b], writes=[rs_b])
                    S.op("dve", lambda e: e.scalar_tensor_tensor(out=dst_t[:, h, d0:d0 + nq], in0=oo_t[:, 0:nq], scalar=subg_t[:, 0:1], in1=rs_t[:, 0:nq],
                                                                 op0=ALU.mult, op1=ALU.mult),
                         reads=[oo_b, subg_b, rs_b], writes=[dst_b])

                pending_epi.extend([epi_a, epi_b, epi_c])
        while pending_epi:
            pending_epi.pop(0)()
    S.barrier()
    if dbg:
        d_yf = cx.out("dbg_yf", [128, 2, T], BF16); d_oa = cx.out("dbg_oa", [128, 4, T], BF16)
        S.dma("sp", d_yf[:, :, :], yf_t[:], reads=[yf_b])
        S.dma("sp", d_oa[:, :, :], oa_t[:], reads=[oa_b])
        if full_ctx:
            d_yfc = cx.out("dbg_yfc", [128, 2, NCTX], BF16); d_oac = cx.out("dbg_oac", [128, 4, NCTX], BF16)
            S.dma("sp", d_yfc[:, :, :], yfc_t[:], reads=[yfc_b])
            S.dma("sp", d_oac[:, :, :], oac_t[:], reads=[oac_b])
    with ExitStack() as st:
        wo_t, wo_b = cx.sb("wo_t", [128, 10, D], BF16, stack=st)
        S.dma("pool", wo_t[:], wo[:, :].rearrange("(c p) d -> p c d", p=128), writes=[wo_b])
        wout_t, wout_b = cx.sb("wout_t", [128, KC, D], BF16, stack=st)
        S.dma("pool", wout_t[:], wout[:, :].rearrange("(c p) d -> p c d", p=128), writes=[wout_b])
        hxs = [cx.sb("hx_t%d" % i, [128, KC, 512], BF16, stack=st) for i in range(2)]
        ycs = [cx.sb("yc_t%d" % i, [128, 2, 512], BF16, stack=st) for i in range(1)]
        yps = [cx.sb("yp_t%d" % i, [128, 2, 512], BF16, stack=st) for i in range(1)]
        ys = [cx.sb("y_t%d" % i, [128, KC, 512], BF16, stack=st) for i in range(2)]
        xb = [cx.sb("xb%d" % i, [128, KC, 512], F32, stack=st) for i in range(1)]
        sig = [cx.sb("sig%d" % i, [128, 512], F32, stack=st) for i in range(2)]
        acc = [cx.sb("acc%d" % i, [128, 512], F32, stack=st) for i in range(2)]
        tmpb = [cx.sb("tmpb%d" % i, [128, 512], F32, stack=st) for i in range(2)]
        sqs = [cx.sb("sqs%d" % i, [128, 512], F32, stack=st) for i in range(2)]
        rs_t, rs_b = cx.sb("rs_t", [128, 512], F32, stack=st)
        h2 = [cx.sb("h2_%d" % i, [128, KC, 512], BF16, stack=st) for i in range(1)]
        psr = PsumRing(cx, 6, "psr", stack=st)
        pss = PsumRing(cx, 2, "pss", stack=st)
        segs = [(s0, 512, 0, hxT, ycT, ypT, yf_t, yf_b, oa_t, oa_b, xT, o_xm, o_h2) for s0 in range(0, T, 512)]
        if full_ctx:
            segs.append((0, NCTX, 1, hxTc, ycTc, ypTc, yfc_t, yfc_b, oac_t, oac_b, cxT, o_cxm, o_ch2))
        bi = 0

        def gated(si):
            (s0, ns, j, hsrc, ycsrc, ypsrc, yfs_t, yfs_b, oas_t, oas_b, xsrc, oxd, ohd) = segs[si]
            hx_t, hx_b = hxs[si % 2]
            yc_t, yc_b = ycs[0]
            yp_t, yp_b = yps[0]
            y_t, y_b = ys[si % 2]
            S.dma("sp", hx_t[:, :, 0:ns], hsrc[:, :, s0:s0 + ns], writes=[hx_b])
            S.dma("sp", yc_t[:, :, 0:ns], ycsrc[:, :, s0:s0 + ns], writes=[yc_b])
            S.dma("sp", yp_t[:, :, 0:ns], ypsrc[:, :, s0:s0 + ns], writes=[yp_b])
            nb = ns
            br = [(yfs_t, yfs_b, s0, 0, 2), (oas_t, oas_b, s0, 2, 4), (yc_t, yc_b, 0, 6, 2), (yp_t, yp_b, 0, 8, 2)]
            for m in range(KC):
                a_t, a_b = acc[m % 2]
                for jb, (bt, bb, boff, wc0, nch) in enumerate(br):
                    pg, pgb = psr.next()
                    for kc in range(KC):
                        S.op("pe", lambda e, kc=kc: e.matmul(pg[:, 0:nb], wg_t[:, kc, jb, m * 128:(m + 1) * 128], hx_t[:, kc, 0:nb],
                                                             start=(kc == 0), stop=(kc == KC - 1)),
                             reads=[wg_b, hx_b], writes=[pgb], inc=(kc == KC - 1))
                    s_t, s_b = sig[jb % 2]
                    S.op("act", lambda e: e.activation(out=s_t[:, 0:nb], in_=pg[:, 0:nb], func=AF.Sigmoid), reads=[pgb], writes=[s_b])
                    pbr, pbrb = psr.next()
                    for c in range(nch):
                        S.op("pe", lambda e, c=c: e.matmul(pbr[:, 0:nb], wo_t[:, wc0 + c, m * 128:(m + 1) * 128], bt[:, c, boff:boff + nb],
                                                           start=(c == 0), stop=(c == nch - 1)),
                             reads=[wo_b, bb], writes=[pbrb], inc=(c == nch - 1))
                    if jb == 0:
                        S.op("dve", lambda e: e.tensor_tensor(out=a_t[:, 0:nb], in0=pbr[:, 0:nb], in1=s_t[:, 0:nb], op=ALU.mult),
                             reads=[pbrb, s_b], writes=[a_b])
                    else:
                        t_t, t_b = tmpb[jb % 2]
                        S.op("dve", lambda e: e.tensor_tensor(out=t_t[:, 0:nb], in0=pbr[:, 0:nb], in1=s_t[:, 0:nb], op=ALU.mult),
                             reads=[pbrb, s_b], writes=[t_b])
                        if jb < 3:
                            S.op("pool", lambda e: e.tensor_tensor(out=a_t[:, 0:nb], in0=a_t[:, 0:nb], in1=t_t[:, 0:nb], op=ALU.add),
                                 reads=[a_b, t_b], writes=[a_b])
                        else:
                            S.op("pool", lambda e: e.tensor_tensor(out=y_t[:, m, 0:nb], in0=a_t[:, 0:nb], in1=t_t[:, 0:nb], op=ALU.add),
                                 reads=[a_b, t_b], writes=[y_b])

        def tail(si):
            nonlocal bi
            (s0, ns, j, hsrc, ycsrc, ypsrc, yfs_t, yfs_b, oas_t, oas_b, xsrc, oxd, ohd) = segs[si]
            y_t, y_b = ys[si % 2]
            nb = ns
            x_t, x_b = xb[0]
            h_t, h_b = h2[0]
            bi += 1
            S.dma("sp", x_t[:, :, 0:nb], xsrc[:, s0:s0 + nb].rearrange("(c p) n -> p c n", p=128), writes=[x_b])
            pst, psb = pss.next()
            for m2 in range(KC):
                pt, pb = psr.next()
                for m in range(KC):
                    S.op("pe", lambda e, m=m: e.matmul(pt[:, 0:nb], wout_t[:, m, m2 * 128:(m2 + 1) * 128], y_t[:, m, 0:nb],
                                                       start=(m == 0), stop=(m == KC - 1)),
                         reads=[wout_b, y_b], writes=[pb], inc=(m == KC - 1))
                S.op("dve", lambda e: e.scalar_tensor_tensor(out=x_t[:, m2, 0:nb], in0=pt[:, 0:nb], scalar=mod_t[:, 16 + m2, j:j + 1],
                                                             in1=x_t[:, m2, 0:nb], op0=ALU.mult, op1=ALU.add),
                     reads=[pb, mod_b, x_b], writes=[x_b])
                q_t2, q_b2 = sqs[m2 % 2]
                S.op("act", lambda e: e.activation(out=q_t2[:, 0:nb], in_=x_t[:, m2, 0:nb], func=AF.Square), reads=[x_b], writes=[q_b2])
                S.op("pe", lambda e: e.matmul(pst[:, 0:nb], ones_t[:, :], q_t2[:, 0:nb], start=(m2 == 0), stop=(m2 == KC - 1)),
                     reads=[ones_b, q_b2], writes=[psb], inc=True)
            S.dma("act", oxd[:, s0:s0 + nb].rearrange("(c p) n -> p c n", p=128), x_t[:, :, 0:nb], reads=[x_b])
            S.op("dve", lambda e: e.tensor_scalar(out=rs_t[:, 0:nb], in0=pst[:, 0:nb], scalar1=1.0 / D, scalar2=EPS, op0=ALU.mult, op1=ALU.add),
                 reads=[psb], writes=[rs_b])
            S.op("act", lambda e: e.activation(out=rs_t[:, 0:nb], in_=rs_t[:, 0:nb], func=AF.Sqrt), reads=[rs_b], writes=[rs_b])
            S.op("dve", lambda e: e.reciprocal(out=rs_t[:, 0:nb], in_=rs_t[:, 0:nb]), reads=[rs_b], writes=[rs_b])
            for kc in range(KC):
                t_t, t_b = tmpb[kc % 2]
                S.op("dve", lambda e: e.tensor_tensor(out=t_t[:, 0:nb], in0=x_t[:, kc, 0:nb], in1=rs_t[:, 0:nb], op=ALU.mult),
                     reads=[x_b, rs_b], writes=[t_b])
                S.op("act", lambda e: e.activation(out=h_t[:, kc, 0:nb], in_=t_t[:, 0:nb], func=AF.Identity,
                                                   bias=mod_t[:, 24 + kc, j:j + 1], scale=gs_t[:, kc, j:j + 1]),
                     reads=[t_b, mod_b, gs_b], writes=[h_b])
            S.dma("act", ohd[:, :, s0:s0 + nb], h_t[:, :, 0:nb], reads=[h_b])

        gated(0)
        for si in range(len(segs)):
            if si + 1 < len(segs):
                gated(si + 1)
            tail(si)
    if standalone:
        S.finish(None)
    return cx


def host_B(inp, l, resA, xT_all, ctxT_all, full_ctx):
    w_in = np.asarray(inp["w_in"][l])
    wg = np.ascontiguousarray(w_in[:, OFF_G:].reshape(D, 4, D))
    wo = np.ascontiguousarray(np.concatenate([inp["wo_f"][l], inp["wo_a"][l], inp["wo_c"][l], inp["wo_p"][l]], axis=0))
    wout = np.ascontiguousarray(inp["w_out"][l])
    lamv = np.stack([inp["lam_q1"][l], inp["lam_k1"][l], inp["lam_q2"][l], inp["lam_k2"][l]], 0).astype(np.float32)
    subg = np.ascontiguousarray(np.asarray(inp["subln_g"][l], np.float32).reshape(128, 1))
    n2g = fm_vec(inp["norm2_g"][l])
    ident = np.eye(128, dtype=np.float32)
    maps = []
    for i in range(NCORES):
        b, j = i // 4, i % 4
        tabA, tabC, tabX = fft_tables(j)
        if resA is None:
            m = dict(wg=wg, wo=wo, wout=wout, lamv=lamv, subg=subg, n2g=n2g, tabA=tabA, tabC=tabC, ident=ident)
            if full_ctx:
                m["tabX"] = tabX
            maps.append(m)
            continue
        grp = [resA[b * 4 + jj] for jj in range(4)]
        r = resA[i]
        m = dict(QT=r["QT"], KTc=r["KTc"], Vc=r["Vc"],
                 ycT=r["ycT"], ypT=r["ypT"], hxT=r["hxT"], xT=np.ascontiguousarray(xT_all[b][:, j * T:(j + 1) * T]), mod=r["mod"],
                 wg=wg, wo=wo, wout=wout, lamv=lamv, subg=subg, n2g=n2g, tabA=tabA, tabC=tabC, ident=ident)
        for h in range(4):
            m["KTg%d" % h] = np.ascontiguousarray(np.concatenate([g["KT%d" % h] for g in grp], axis=0))
            m["Vg%d" % h] = np.ascontiguousarray(np.concatenate([g["V%d" % h] for g in grp], axis=0))
            m["Zg%d" % h] = np.ascontiguousarray(np.concatenate([g["Z%d" % h] for g in grp], axis=0))
        if full_ctx:
            m.update(QTc=r["QTc"], Zc=r["Zc"], ycTc=r["ycTc"], ypTc=r["ypTc"], hxTc=r["hxTc"],
                     cxT=np.ascontiguousarray(ctxT_all[b]), tabX=tabX)
        maps.append(m)
    return maps


NFC = DFF // 128


def build_C(full_ctx, final, cx=None):
    standalone = cx is None
    if standalone:
        cx = Ctx("C")
    nc, S = cx.nc, cx.S
    hx2 = cx.inp("hx2T", [128, KC, T], BF16); hhalo = cx.inp("hhalo", [128, KC, 2], BF16)
    xm = cx.inp("xmT", [D, T]); mod = cx.inp("mod", [128, 48, 2])
    wup = cx.inp("wup", [D, 2 * DFF]); wdn = cx.inp("wdn", [DFF, D])
    fw = cx.inp("fw", [128, NFC, 4])
    o_x = cx.out("xoT", [D, T])
    if full_ctx:
        chx2 = cx.inp("chx2T", [128, KC, NCTX], BF16); cxm = cx.inp("cxmT", [D, NCTX])
        o_cx = cx.out("cxoT", [D, NCTX])
    if final:
        fg = cx.inp("fg", [128, KC])
    wup_t, _unused = cx.sb("wup_t", [128, KC, 2 * DFF], BF16)
    wup_bs = {}
    HP = 11 * 128
    for piece in range(2):
        for half in (1, 0):
            bb = Buf("wup_%d_%d" % (half, piece))
            c0_ = half * DFF + piece * HP
            for kc in range(KC):
                S.dma("pool", wup_t[:, kc, c0_:c0_ + HP], wup[kc * 128:(kc + 1) * 128, c0_:c0_ + HP], writes=[bb], also=True)
            wup_bs[(half, piece)] = bb
    wdn_t, wdn_b = cx.sb("wdn_t", [128, NFC, D], BF16)
    for g in range(2):
        S.dma("pool", wdn_t[:, g * 11:(g + 1) * 11, :], wdn[g * 11 * 128:(g + 1) * 11 * 128, :].rearrange("(c p) d -> p c d", p=128),
              writes=[wdn_b], also=True)
    mod_t, mod_b = load_const(cx, "mod_t", mod[:, :, :], [128, 48, 2])
    fw_t, fw_b = load_const(cx, "fw_t", fw[:, :, :], [128, NFC, 4])
    hal_t, hal_b = load_const(cx, "hal_t", hhalo[:, :, :], [128, KC, 2], BF16)
    zero_t, zero_b = cx.sb("zero_t", [128, KC, 2], BF16)
    S.op("dve", lambda e: e.memset(zero_t[:], 0.0), writes=[zero_b])
    if final:
        fg_t, fg_b = load_const(cx, "fg_t", fg[:, :], [128, KC])
        ones_t, ones_b = cx.sb("ones_t", [128, 128], F32)
        S.op("dve", lambda e: e.memset(ones_t[:], 1.0), writes=[ones_b])
    hxb = [cx.sb("hxb%d" % i, [128, KC, 512], BF16) for i in range(2)]
    x_t, x_b = cx.sb("x_t", [128, KC, 512], F32)
    u_t, u_b = cx.sb("u_t", [128, NFC, 512], BF16)
    gw = [cx.sb("gw%d" % i, [128, 512], F32) for i in range(2)]
    cv = [cx.sb("cv%d" % i, [128, 512], F32) for i in range(2)]
    ge = [cx.sb("ge%d" % i, [128, 512], F32) for i in range(2)]
    psr = PsumRing(cx, 6, "psr")
    pss = PsumRing(cx, 2, "pss")
    if final:
        sqs = [cx.sb("sqs%d" % i, [128, 512], F32) for i in range(2)]
        rs_t, rs_b = cx.sb("rs_t", [128, 512], F32)
    segs = [(hx2, xm, o_x, T, 0, hal_t, hal_b)]
    if full_ctx:
        segs.append((chx2, cxm, o_cx, NCTX, 1, zero_t, zero_b))
    bi = 0
    for (hsrc, xsrc, xdst, n, j, m_t, m_b) in segs:
        blocks = []
        b0 = 0
        while b0 < n:
            nb = min(510, n - b0)
            blocks.append((b0, nb))
            b0 += nb
        for (b0, nb) in blocks:
            h_t, h_b = hxb[bi % 2]
            bi += 1
            lo = max(b0 - 1, 0)
            hi = min(b0 + nb + 1, n)
            S.dma("sp", h_t[:, :, lo - (b0 - 1):hi - (b0 - 1)], hsrc[:, :, lo:hi], writes=[h_b])
            if b0 == 0:
                S.op("pool", lambda e: e.tensor_copy(out=h_t[:, :, 0:1], in_=m_t[:, :, 0:1]), reads=[m_b], writes=[h_b])
            if b0 + nb == n:
                S.op("pool", lambda e: e.tensor_copy(out=h_t[:, :, nb + 1:nb + 2], in_=m_t[:, :, 1:2]), reads=[m_b], writes=[h_b])
            S.dma("sp", x_t[:, :, 0:nb], xsrc[:, b0:b0 + nb].rearrange("(c p) n -> p c n", p=128), writes=[x_b])
            for c in range(NFC):
                pg, pgb = psr.next()
                for kc in range(KC):
                    S.op("pe", lambda e, kc=kc: e.matmul(pg[:, 0:nb + 2], wup_t[:, kc, DFF + c * 128:DFF + (c + 1) * 128], h_t[:, kc, 0:nb + 2],
                                                         start=(kc == 0), stop=(kc == KC - 1)),
                         reads=[wup_bs[(1, c // 11)], h_b], writes=[pgb], inc=(kc == KC - 1))
                g_t, g_b = gw[c % 2]
                S.op("act", lambda e: e.activation(out=g_t[:, 0:nb + 2], in_=pg[:, 0:nb + 2], func=AF.Copy), reads=[pgb], writes=[g_b])
                c_t, c_b = cv[c % 2]
                S.op("dve", lambda e: e.tensor_scalar(out=c_t[:, 0:nb], in0=g_t[:, 0:nb], scalar1=fw_t[:, c, 0:1], scalar2=fw_t[:, c, 3:4],
                                                      op0=ALU.mult, op1=ALU.add), reads=[g_b, fw_b], writes=[c_b])
                for k in (1, 2):
                    S.op("dve", lambda e, k=k: e.scalar_tensor_tensor(out=c_t[:, 0:nb], in0=g_t[:, k:k + nb], scalar=fw_t[:, c, k:k + 1], in1=c_t[:, 0:nb],
                                                                      op0=ALU.mult, op1=ALU.add), reads=[g_b, fw_b, c_b], writes=[c_b])
                e_t, e_b = ge[c % 2]
                S.op("act", lambda e: e.activation(out=e_t[:, 0:nb], in_=c_t[:, 0:nb], func=AF.Gelu), reads=[c_b], writes=[e_b])
                pv, pvb = psr.next()
                for kc in range(KC):
                    S.op("pe", lambda e, kc=kc: e.matmul(pv[:, 0:nb], wup_t[:, kc, c * 128:(c + 1) * 128], h_t[:, kc, 1:nb + 1],
                                                         start=(kc == 0), stop=(kc == KC - 1)),
                         reads=[wup_bs[(0, c // 11)], h_b], writes=[pvb], inc=(kc == KC - 1))
                S.op("dve", lambda e: e.tensor_tensor(out=u_t[:, c, 0:nb], in0=pv[:, 0:nb], in1=e_t[:, 0:nb], op=ALU.mult),
                     reads=[pvb, e_b], writes=[u_b])
            if final:
                pst, psb = pss.next()
            for m2 in range(KC):
                po, pob = psr.next()
                for c in range(NFC):
                    S.op("pe", lambda e, c=c: e.matmul(po[:, 0:nb], wdn_t[:, c, m2 * 128:(m2 + 1) * 128], u_t[:, c, 0:nb],
                                                       start=(c == 0), stop=(c == NFC - 1)),
                         reads=[wdn_b, u_b], writes=[pob], inc=(c == NFC - 1))
                S.op("dve", lambda e: e.scalar_tensor_tensor(out=x_t[:, m2, 0:nb], in0=po[:, 0:nb], scalar=mod_t[:, 40 + m2, j:j + 1],
                                                             in1=x_t[:, m2, 0:nb], op0=ALU.mult, op1=ALU.add),
                     reads=[pob, mod_b, x_b], writes=[x_b])
                if final:
                    q_t2, q_b2 = sqs[m2 % 2]
                    S.op("act", lambda e: e.activation(out=q_t2[:, 0:nb], in_=x_t[:, m2, 0:nb], func=AF.Square), reads=[x_b], writes=[q_b2])
                    S.op("pe", lambda e: e.matmul(pst[:, 0:nb], ones_t[:, :], q_t2[:, 0:nb], start=(m2 == 0), stop=(m2 == KC - 1)),
                         reads=[ones_b, q_b2], writes=[psb], inc=True)
            if final:
                S.op("dve", lambda e: e.tensor_scalar(out=rs_t[:, 0:nb], in0=pst[:, 0:nb], scalar1=1.0 / D, scalar2=EPS, op0=ALU.mult, op1=ALU.add),
                     reads=[psb], writes=[rs_b])
                S.op("act", lambda e: e.activation(out=rs_t[:, 0:nb], in_=rs_t[:, 0:nb], func=AF.Sqrt), reads=[rs_b], writes=[rs_b])
                S.op("dve", lambda e: e.reciprocal(out=rs_t[:, 0:nb], in_=rs_t[:, 0:nb]), reads=[rs_b], writes=[rs_b])
                for kc in range(KC):
                    S.op("dve", lambda e, kc=kc: e.scalar_tensor_tensor(out=x_t[:, kc, 0:nb], in0=x_t[:, kc, 0:nb], scalar=fg_t[:, kc:kc + 1],
                                                                        in1=rs_t[:, 0:nb], op0=ALU.mult, op1=ALU.mult),
                         reads=[x_b, fg_b, rs_b], writes=[x_b])
            S.dma("act", xdst[:, b0:b0 + nb].rearrange("(c p) n -> p c n", p=128), x_t[:, :, 0:nb], reads=[x_b])
    if standalone:
        S.finish(None)
    return cx


def host_C(inp, l, resA, resB, full_ctx, final):
    wup = np.ascontiguousarray(inp["w_up"][l]); wdn = np.ascontiguousarray(inp["w_down"][l])
    fwv = np.concatenate([np.asarray(inp["ffn_dw_w"][l]), np.asarray(inp["ffn_dw_b"][l])[None, :]], axis=0)
    fw = np.ascontiguousarray(fwv.T.reshape(NFC, 128, 4).transpose(1, 0, 2)).astype(np.float32)
    maps = []
    for i in range(NCORES):
        b, j = i // 4, i % 4
        if resA is None:
            m = dict(wup=wup, wdn=wdn, fw=fw)
            if final:
                m["fg"] = fm_vec(inp["final_g"])
            maps.append(m)
            continue
        hh = np.zeros((128, KC, 2), NPBF)
        if j > 0:
            hh[:, :, 0] = resB[i - 1]["hx2T"][:, :, T - 1]
        if j < 3:
            hh[:, :, 1] = resB[i + 1]["hx2T"][:, :, 0]
        m = dict(hx2T=resB[i]["hx2T"], hhalo=hh, xmT=resB[i]["xmT"], mod=resA[i]["mod"], wup=wup, wdn=wdn, fw=fw)
        if full_ctx:
            m.update(chx2T=resB[i]["chx2T"], cxmT=resB[i]["cxmT"])
        if final:
            m["fg"] = fm_vec(inp["final_g"])
        maps.append(m)
    return maps


def _np(results):
    return [{k: np.asarray(v) for k, v in r.items()} for r in results]


def kernel_unfused(**inputs):
    inp = {k: np.asarray(v) for k, v in inputs.items()}
    x = inp["x"].astype(np.float32, copy=False)
    B = x.shape[0]
    xT_all = [np.ascontiguousarray(x[b].T) for b in range(B)]
    ctxT_all = [np.ascontiguousarray(inp["ctx"][b].T.astype(np.float32)) for b in range(B)]
    depth = inp["w_in"].shape[0]
    for l in range(depth):
        last = l == depth - 1
        full_ctx = not last
        lam_init = 0.8 - 0.6 * math.exp(-0.3 * l)
        cxA = get_prog(("A", full_ctx), build_A, full_ctx)
        resA = _np(run_prog(cxA, host_A(inp, l, xT_all, ctxT_all)))
        cxB = get_prog(("B", full_ctx, l), build_B, full_ctx, lam_init)
        resB = _np(run_prog(cxB, host_B(inp, l, resA, xT_all, ctxT_all, full_ctx)))
        cxC = get_prog(("C", full_ctx, last), build_C, full_ctx, last)
        resC = _np(run_prog(cxC, host_C(inp, l, resA, resB, full_ctx, last)))
        xT_all = [np.concatenate([resC[b * 4 + j]["xoT"] for j in range(4)], axis=1) for b in range(B)]
        if full_ctx:
            ctxT_all = [resC[b * 4]["cxoT"] for b in range(B)]
    out = np.stack([xT_all[b].T for b in range(B)], axis=0)
    return np.ascontiguousarray(out.astype(np.float32))


def _select_halo(cx, gathered, ncol, dt, pick_l, pick_r, wl, out_aps, sel_t, sel_b, name):
    S = cx.S
    g_t, g_b = cx.sb(name + "_g", [128, 4, ncol], dt)
    S.dma("sp", g_t[:], gathered[:, :].rearrange("(r p) n -> p r n", p=128), writes=[g_b])
    res = []
    for side, pick in ((0, pick_l), (1, pick_r)):
        a_t, a_b = cx.sb(name + "_a%d" % side, [128, KC, wl], F32)
        gv = g_t[:, :, :].rearrange("p r (c n) -> p r c n", c=KC)
        S.op("dve", lambda e: e.tensor_scalar(out=a_t[:], in0=gv[:, 0, :, pick], scalar1=sel_t[:, side * 4:side * 4 + 1], scalar2=None, op0=ALU.mult),
             reads=[g_b, sel_b], writes=[a_b])
        for r in range(1, 4):
            S.op("dve", lambda e, r=r: e.scalar_tensor_tensor(out=a_t[:], in0=gv[:, r, :, pick], scalar=sel_t[:, side * 4 + r:side * 4 + r + 1],
                                                              in1=a_t[:], op0=ALU.mult, op1=ALU.add),
                 reads=[g_b, sel_b, a_b], writes=[a_b])
        res.append((a_t, a_b))
    return res


def build_M(cx, depth):
    nc, S = cx.nc, cx.S
    cx.begin_phase("M_")
    cT = cx.inp("cT", [128, KC, 2])
    cT_t, cT_b = load_const(cx, "cT_t", cT[:, :, :], [128, KC, 2])
    psr = PsumRing(cx, 4, "psm")
    mq = []
    for l in range(depth):
        awq = cx.inp("L%d_ada_wq" % l, [D, 1536]); abq = cx.inp("L%d_ada_bq" % l, [128, 12])
        abq_t, abq_b = load_const(cx, "abq%d" % l, abq[:, :], [128, 12])
        mq_t, mq_b = cx.sb("mq%d" % l, [128, 12, 2], F32)
        cx.prefix = "M%d_" % l
        _adaln_mod(cx, cT_t, cT_b, awq, abq_t, abq_b, mq_t, mq_b, psr, 3, 0)
        cx.prefix = "M_"
        mq_d = nc.dram_tensor("M_mqd%d" % l, [128, 24], F32).ap()
        S.dma("sp", mq_d[:, :], mq_t[:, :, :].rearrange("p c j -> p (c j)"), reads=[mq_b])
        mq.append(mq_d)
    S.barrier()
    mod_ap = []
    mg_l = []
    for l in range(depth):
        mg = nc.dram_tensor("M_mg%d" % l, [512, 24], F32).ap()
        S.allgather(mq[l], mg)
        mg_l.append(mg)
    S.barrier()
    for l in range(depth):
        md = nc.dram_tensor("M_mod%d" % l, [128, 48, 2], F32).ap()
        mt, mb = cx.sb("mgt%d" % l, [128, 4, 24], F32)
        S.dma("sp", mt[:], mg_l[l][:, :].rearrange("(r p) n -> p r n", p=128), writes=[mb])
        S.dma("sp", md[:, :, :].rearrange("p (r c) j -> p r (c j)", r=4), mt[:], reads=[mb])
        mod_ap.append(md)
    cx.end_phase()
    return mod_ap


def build_fused(depth=2):
    cx = Ctx("F", fused=True)
    nc, S = cx.nc, cx.S
    sel = cx.inp("sel", [128, 8])
    mod_ap = build_M(cx, depth)
    prevC = None
    xTh_ap = None
    for l in range(depth):
        last = l == depth - 1
        full_ctx = not last
        lam_init = 0.8 - 0.6 * math.exp(-0.3 * l)
        links = {}
        if l > 0:
            links = {"xT": prevC["xoT"], "cxT": prevC["cxoT"], "xTh": xTh_ap}
        links["mod_in"] = mod_ap[l]
        cx.begin_phase("L%dA_" % l, links)
        gath = {}

        for nm, shp in (("Z", [SEQ, 128]), ("KT", [512, T]), ("V", [SEQ, 128])):
            for h in range(4):
                gath["%sg%d" % (nm, h)] = nc.dram_tensor("L%dE1_%sg%d" % (l, nm, h), shp, BF16).ap()

        def gather_kvz(l=l, gath=gath):
            for h in range(4):
                S.allgather(cx.produced["Z%d" % h], gath["Zg%d" % h])

        cx.after_blocks = gather_kvz
        build_A(full_ctx, cx)
        cx.after_blocks = None
        pA = cx.end_phase()
        pA["mod"] = mod_ap[l]
        xT_ap = links["xT"] if l > 0 else cx.ins["L0A_xT"]
        cxT_ap = links["cxT"] if l > 0 else cx.ins["L0A_cxT"]
        links = {k: pA[k] for k in ("QT", "KTc", "Vc", "ycT", "ypT", "hxT", "mod")}
        links.update(gath)
        links["xT"] = xT_ap
        if full_ctx:
            links.update({k: pA[k] for k in ("QTc", "Zc", "ycTc", "ypTc", "hxTc")})
            links["cxT"] = cxT_ap
        cx.begin_phase("L%dB_" % l, links)

        def gather_kv(pA=pA, gath=gath):
            for h in range(4):
                for nm in ("KT", "V"):
                    st_ = S.allgather(pA["%s%d" % (nm, h)], gath["%sg%d" % (nm, h)])
                    gb = Buf("g_%s%d" % (nm, h))
                    gb.w = [st_]
                    cx.dram_bufs["%sg%d" % (nm, h)] = gb

        cx.after_fft_tables = gather_kv
        cx.dram_bufs = {}
        build_B(full_ctx, lam_init, cx=cx)
        cx.after_fft_tables = None
        pB = cx.end_phase()
        cx.begin_phase("L%dE2_" % l)
        sel_t, sel_b = load_const(cx, "sel_t", sel[:, :], [128, 8])
        e_t, e_b = cx.sb("e_t", [128, KC, 2], BF16)
        S.dma("sp", e_t[:, :, 0:1], pB["hx2T"][:, :, 0:1], writes=[e_b], slow=True)
        S.dma("sp", e_t[:, :, 1:2], pB["hx2T"][:, :, T - 1:T], writes=[e_b], also=True, slow=True)
        ein = nc.dram_tensor("L%dE2_in" % l, [128, KC * 2], BF16).ap()
        eout = nc.dram_tensor("L%dE2_out" % l, [512, KC * 2], BF16).ap()
        hhalo = nc.dram_tensor("L%dE2_hhalo" % l, [128, KC, 2], BF16).ap()
        S.dma("sp", ein[:, :], e_t[:, :, :].rearrange("p c n -> p (c n)"), reads=[e_b])
        S.barrier()
        S.allgather(ein, eout)
        S.barrier()
        (l_t, l_b), (r_t, r_b) = _select_halo(cx, eout, KC * 2, BF16, slice(1, 2), slice(0, 1), 1, None, sel_t, sel_b, "h2")
        hh_t, hh_b = cx.sb("hh_t", [128, KC, 2], BF16)
        S.op("dve", lambda e: e.tensor_copy(out=hh_t[:, :, 0:1], in_=l_t[:]), reads=[l_b], writes=[hh_b])
        S.op("dve", lambda e: e.tensor_copy(out=hh_t[:, :, 1:2], in_=r_t[:]), reads=[r_b, hh_b], writes=[hh_b])
        S.dma("sp", hhalo[:, :, :], hh_t[:], reads=[hh_b])
        cx.end_phase()
        links = {"hx2T": pB["hx2T"], "hhalo": hhalo, "xmT": pB["xmT"], "mod": pA["mod"]}
        if full_ctx:
            links.update({"chx2T": pB["chx2T"], "cxmT": pB["cxmT"]})
        cx.begin_phase("L%dC_" % l, links, ext_out=({"xoT": "outT"} if last else None))
        build_C(full_ctx, last, cx=cx)
        pC = cx.end_phase()
        prevC = pC
        if not last:
            cx.begin_phase("L%dE3_" % l)
            sel_t, sel_b = load_const(cx, "sel_t", sel[:, :], [128, 8])
            e_t, e_b = cx.sb("e_t", [128, KC, 2 * HALO], F32)
            S.dma("sp", e_t[:, :, 0:HALO], pC["xoT"][:, 0:HALO].rearrange("(c p) n -> p c n", p=128), writes=[e_b])
            S.dma("sp", e_t[:, :, HALO:2 * HALO], pC["xoT"][:, T - HALO:T].rearrange("(c p) n -> p c n", p=128), writes=[e_b], also=True)
            ein = nc.dram_tensor("L%dE3_in" % l, [128, KC * 2 * HALO], F32).ap()
            eout = nc.dram_tensor("L%dE3_out" % l, [512, KC * 2 * HALO], F32).ap()
            xTh_ap = nc.dram_tensor("L%dE3_xTh" % l, [D, 2 * HALO], F32).ap()
            S.dma("sp", ein[:, :], e_t[:, :, :].rearrange("p c n -> p (c n)"), reads=[e_b])
            S.barrier()
            S.allgather(ein, eout)
            S.barrier()
            (l_t, l_b), (r_t, r_b) = _select_halo(cx, eout, KC * 2 * HALO, F32, slice(HALO, 2 * HALO), slice(0, HALO), HALO, None, sel_t, sel_b, "xh")
            S.dma("sp", xTh_ap[:, 0:HALO].rearrange("(c p) n -> p c n", p=128), l_t[:], reads=[l_b])
            S.dma("sp", xTh_ap[:, HALO:2 * HALO].rearrange("(c p) n -> p c n", p=128), r_t[:], reads=[r_b])
            cx.end_phase()
    S.finish(None)
    return cx


def kernel(**inputs):
    inp = {k: np.asarray(v) for k, v in inputs.items()}
    x = inp["x"].astype(np.float32, copy=False)
    B = x.shape[0]
    depth = inp["w_in"].shape[0]
    cx = get_prog(("F", depth), build_fused, depth)
    xT_all = [np.ascontiguousarray(x[b].T) for b in range(B)]
    ctxT_all = [np.ascontiguousarray(inp["ctx"][b].T.astype(np.float32)) for b in range(B)]
    maps = [dict() for _ in range(NCORES)]
    for l in range(depth):
        last = l == depth - 1
        full_ctx = not last
        mA = host_A(inp, l, xT_all if l == 0 else None, ctxT_all if l == 0 else None)
        mB = host_B(inp, l, None, None, None, full_ctx)
        mC = host_C(inp, l, None, None, full_ctx, last)
        for i in range(NCORES):
            for pre, m in (("L%dA_" % l, mA[i]), ("L%dB_" % l, mB[i]), ("L%dC_" % l, mC[i])):
                for k, v in m.items():
                    if pre + k in cx.ins:
                        maps[i][pre + k] = v
    for i in range(NCORES):
        b, j = i // 4, i % 4
        maps[i]["M_cT"] = np.ascontiguousarray(np.stack([fm_vec(inp["c"][b]), fm_vec(inp["c_ctx"])], axis=2))
        for l in range(depth):
            maps[i]["M_L%d_ada_wq" % l] = np.ascontiguousarray(inp["ada_w"][l][:, j * 1536:(j + 1) * 1536])
            maps[i]["M_L%d_ada_bq" % l] = fm_vec(inp["ada_b"][l][j * 1536:(j + 1) * 1536])
        sel = np.zeros((128, 8), np.float32)
        if j > 0:
            sel[:, j - 1] = 1.0
        if j < 3:
            sel[:, 4 + j + 1] = 1.0
        maps[i]["sel"] = sel
        missing = set(cx.ins) - set(maps[i])
        assert not missing, missing
    res = run_prog(cx, maps)
    outT = [np.asarray(r["outT"]) for r in res]
    out = np.stack([np.concatenate([outT[b * 4 + j] for j in range(4)], axis=1).T for b in range(B)], axis=0)
    return np.ascontiguousarray(out.astype(np.float32))
```

```python
import math
from contextlib import ExitStack
import numpy as np
import ml_dtypes
import concourse.bass as bass
import concourse.mybir as mybir
from concourse.bass_utils import run_bass_kernel_spmd

F32 = mybir.dt.float32
BF16 = mybir.dt.bfloat16
AF = mybir.ActivationFunctionType
ALU = mybir.AluOpType
NPBF = ml_dtypes.bfloat16

D = 1024
KC = 8
T = 2048
NCTX = 256
SEQ = 8192
HALO = 16
DFF = 2816
EPS = 1e-6
NCORES = 8
SAME_ENGINE_SYNC = True


class Buf:
    def __init__(self, name=""):
        self.name = name
        self.w = []
        self.r = []


class Sched:
    def __init__(self, nc, es, ndma=20):
        self.nc = nc
        self.E = {"pe": nc.tensor, "act": nc.scalar, "dve": nc.vector, "pool": nc.gpsimd, "sp": nc.sync}
        self.sem = {e: es.enter_context(nc.semaphore("sem_" + e)) for e in self.E}
        self.cnt = {e: 0 for e in self.E}
        self.seen = {e: {} for e in self.E}
        self.pend = {e: [] for e in self.E}
        self.dsems = [es.enter_context(nc.semaphore("dsem%d" % i)) for i in range(ndma)]
        self.dcnt = [0] * ndma
        self.dnext = 0
        self.nwaits = 0
        self.cc_sem = es.enter_context(nc.semaphore("cc_sem"))
        self.cc_cnt = 0

    def _wait(self, e, st):
        key, sem, val = st
        if key == e and (e == "pe" or not SAME_ENGINE_SYNC):
            return
        if self.seen[e].get(key, 0) >= val:
            return
        self.E[e].wait_ge(sem, val)
        self.nwaits += 1
        self.seen[e][key] = val

    def _deps(self, e, reads, writes):
        for oe, pl in self.pend.items():
            if oe == e:
                continue
            for (R, W) in pl:
                for b in list(reads) + list(writes):
                    if b in W or (b in R and b in writes):
                        raise RuntimeError("dependency on un-stamped access of %s by %s (buf %s)" % (oe, e, b.name))
        for b in reads:
            for st in b.w:
                self._wait(e, st)
        for b in writes:
            for st in b.w:
                self._wait(e, st)
            for st in b.r:
                self._wait(e, st)

    def op(self, e, fn, reads=(), writes=(), inc=True):
        self._deps(e, reads, writes)
        ins = fn(self.E[e])
        self.pend[e].append((tuple(reads), tuple(writes)))
        if inc:
            self.cnt[e] += 1
            ins.then_inc(self.sem[e], 1)
            st = (e, self.sem[e], self.cnt[e])
            for (R, W) in self.pend[e]:
                for b in R:
                    b.r.append(st)
                for b in W:
                    b.w = [st]
                    b.r = []
            self.pend[e] = []
        return ins

    def dma(self, q, out, in_, reads=(), writes=(), also=False, slow=False):
        self._deps(q, reads, writes)
        i = self.dnext
        self.dnext = (self.dnext + 1) % len(self.dsems)
        key = "d%d" % i
        if self.dcnt[i] > 0:
            self._wait(q, (key, self.dsems[i], self.dcnt[i]))
        ins = self.E[q].dma_start(out=out, in_=in_, allow_slow_non_contiguous=True) if slow else self.E[q].dma_start(out=out, in_=in_)
        self.dcnt[i] += 16
        ins.then_inc(self.dsems[i], 16)
        st = (key, self.dsems[i], self.dcnt[i])
        for b in reads:
            b.r.append(st)
        for b in writes:
            if also:
                b.w = b.w + [st]
            else:
                b.w = [st]
                b.r = []
        return ins

    def barrier(self):
        for e, pl in self.pend.items():
            if pl:
                raise RuntimeError("barrier with un-stamped accesses on " + e)
        for e in self.E:
            for oe in self.E:
                if oe != e and self.cnt[oe] > 0:
                    self._wait(e, (oe, self.sem[oe], self.cnt[oe]))
            for i in range(len(self.dsems)):
                if self.dcnt[i] > 0:
                    self._wait(e, ("d%d" % i, self.dsems[i], self.dcnt[i]))
            if self.cc_cnt > 0:
                self._wait(e, ("cc", self.cc_sem, self.cc_cnt))

    def allgather(self, in_ap, out_ap):
        ins = self.nc.gpsimd.collective_compute("AllGather", ALU.bypass, replica_groups=[[0, 1, 2, 3], [4, 5, 6, 7]],
                                                ins=[in_ap.opt()], outs=[out_ap.opt()])
        self.cc_cnt += 1
        ins.then_inc(self.cc_sem)
        st = ("cc", self.cc_sem, self.cc_cnt)
        self._wait("pool", st)
        return st

    def finish(self, bufs):
        for i in range(len(self.dsems)):
            if self.dcnt[i] > 0:
                self._wait("sp", ("d%d" % i, self.dsems[i], self.dcnt[i]))


class Ctx:
    def __init__(self, name, fused=False):
        self.nc = bass.Bass("TRN2", target_bir_lowering=False)
        self.es_root = ExitStack()
        self.es = ExitStack()
        self.S = Sched(self.nc, self.es_root)
        self.ins = {}
        self.outs = {}
        self.fused = fused
        self.prefix = ""
        self.links = {}
        self.produced = {}
        self.ext_out = {}
        self.dram_bufs = {}
        self.after_blocks = None
        self.after_fft_tables = None

    def begin_phase(self, prefix, links=None, ext_out=None):
        self.prefix = prefix
        self.links = dict(links or {})
        self.produced = {}
        self.ext_out = dict(ext_out or {})
        self.es = ExitStack()

    def end_phase(self):
        self.S.barrier()
        self.es.close()
        self.es = ExitStack()
        return self.produced

    def inp(self, name, shape, dt=F32):
        if name in self.links:
            return self.links[name]
        t = self.nc.dram_tensor(self.prefix + name, list(shape), dt, kind="ExternalInput").ap()
        self.ins[self.prefix + name] = t
        return t

    def out(self, name, shape, dt=F32):
        if self.fused and name not in self.ext_out:
            t = self.nc.dram_tensor(self.prefix + name, list(shape), dt).ap()
            self.produced[name] = t
            return t
        oname = self.ext_out.get(name, self.prefix + name)
        t = self.nc.dram_tensor(oname, list(shape), dt, kind="ExternalOutput").ap()
        self.outs[oname] = t
        self.produced[name] = t
        return t

    def sb(self, name, shape, dt=F32, stack=None):
        t = (stack or self.es).enter_context(self.nc.sbuf_tensor(self.prefix + name, list(shape), dt))
        return t, Buf(name)

    def ps(self, name, shape, dt=F32, stack=None):
        t = (stack or self.es).enter_context(self.nc.psum_tensor(self.prefix + name, list(shape), dt))
        return t, Buf(name)


class PsumRing:
    def __init__(self, cx, n, name="ps", stack=None):
        self.tiles = [cx.ps("%s%d" % (name, i), [128, 512], F32, stack=stack) for i in range(n)]
        self.i = 0

    def next(self):
        t = self.tiles[self.i]
        self.i = (self.i + 1) % len(self.tiles)
        return t


def load_const(cx, name, dram_ap, shape, dt=F32, q="sp"):
    t, b = cx.sb(name, shape, dt)
    cx.S.dma(q, t[:], dram_ap, writes=[b])
    return t, b


def rope_tables(tok0, n):
    nf = 16
    inv = (10000.0 ** (-np.arange(nf, dtype=np.float32) / nf)).astype(np.float32)
    t = np.arange(tok0, tok0 + n)
    r = (t // 64).astype(np.float32)
    col = (t % 64).astype(np.float32)
    ar = r[:, None] * inv
    ac = col[:, None] * inv
    cr, sr, cc, sc = np.cos(ar), np.sin(ar), np.cos(ac), np.sin(ac)
    C = np.concatenate([cr, cr, cc, cc], axis=1).T
    Ssg = np.concatenate([-sr, sr, -sc, sc], axis=1).T
    return (np.ascontiguousarray(np.concatenate([C, C], 0), dtype=np.float32),
            np.ascontiguousarray(np.concatenate([Ssg, Ssg], 0), dtype=np.float32))


SWAP64 = np.concatenate([np.arange(16, 32), np.arange(0, 16), np.arange(48, 64), np.arange(32, 48)])


def pool_invcnt(tok0, n, L):
    out = np.zeros((256, n), np.float32)
    t = np.arange(tok0, tok0 + n)
    for g, win in enumerate((2, 4, 8, 16)):
        lo = np.clip(t - win // 2, 0, L - 1)
        hi = np.clip(t + win - win // 2 - 1, 0, L - 1)
        out[g * 64:(g + 1) * 64, :] = (1.0 / (hi - lo + 1).astype(np.float32))[None, :]
    return out.reshape(2, 128, n).transpose(1, 0, 2).copy()


def chan_dft_table():
    a = 2 * np.pi * np.outer(np.arange(64), np.arange(64)) / 64.0
    C, Sn = np.cos(a), np.sin(a)
    Cb = np.zeros((128, 128)); Sb = np.zeros((128, 128))
    for g in range(2):
        Cb[g * 64:(g + 1) * 64, g * 64:(g + 1) * 64] = C
        Sb[g * 64:(g + 1) * 64, g * 64:(g + 1) * 64] = Sn
    return np.concatenate([Cb, Sb], axis=1).astype(np.float32)


NWA = 24


def _adaln_mod(cx, cT_t, cT_b, ada_w, adab_t, adab_b, mod_t, mod_b, psr, ngroups, chunk0):
    S = cx.S
    sil_t, sil_b = cx.sb("sil_t", [128, KC, 2], F32)
    S.op("act", lambda e: e.activation(out=sil_t[:], in_=cT_t[:], func=AF.Silu), reads=[cT_b], writes=[sil_b])
    with ExitStack() as st:
        aw = [cx.sb("aw%d" % i, [128, KC, 512], F32, stack=st) for i in range(2)]
        for g in range(ngroups):
            awt, awb = aw[g % 2]
            for kc in range(KC):
                S.dma("sp", awt[:, kc, :], ada_w[kc * 128:(kc + 1) * 128, g * 512:(g + 1) * 512], writes=[awb], also=(kc > 0))
            pt, pb = psr.next()
            for mm in range(4):
                for kc in range(KC):
                    S.op("pe", lambda e, mm=mm, kc=kc: e.matmul(pt[:, mm * 2:mm * 2 + 2], awt[:, kc, mm * 128:(mm + 1) * 128], sil_t[:, kc, :],
                                                                 start=(kc == 0), stop=(kc == KC - 1)),
                         reads=[awb, sil_b], writes=[pb], inc=(mm == 3 and kc == KC - 1))
            c_ = chunk0 + g * 4
            S.op("dve", lambda e, g=g, c_=c_: e.tensor_tensor(out=mod_t[:, c_:c_ + 4, :],
                                                              in0=pt[:, 0:8].rearrange("p (a b) -> p a b", b=2),
                                                              in1=adab_t[:, c_:c_ + 4].unsqueeze(2).to_broadcast([128, 4, 2]), op=ALU.add),
                 reads=[pb, adab_b], writes=[mod_b])
        S.barrier()


def build_A(full_ctx, cx=None):
    standalone = cx is None
    if standalone:
        cx = Ctx("A")
    nc, S = cx.nc, cx.S
    xT = cx.inp("xT", [D, T]); xTh = cx.inp("xTh", [D, 2 * HALO]); cxT = cx.inp("cxT", [D, NCTX])
    premod = "mod_in" in cx.links
    if not premod:
        cT = cx.inp("cT", [128, KC, 2]); ada_w = cx.inp("ada_w", [D, 6 * D]); ada_b = cx.inp("ada_b", [128, 48])
    n1g = cx.inp("n1g", [128, KC])
    wA = cx.inp("wA", [D, NWA * 128]); wV = cx.inp("wV", [D, 512])
    ropeC = cx.inp("ropeC", [128, T]); ropeS = cx.inp("ropeS", [128, T])
    hmask = cx.inp("hmask", [128, 2 * HALO])
    cw = cx.inp("cw", [128, 2, 31]); cvec = cx.inp("cvec", [128, 2, 4])
    pinv = cx.inp("pinv", [128, 2, T]); pinvc = cx.inp("pinvc", [128, 2, NCTX])
    pw = cx.inp("pw", [128, 2, 128]); cdft = cx.inp("cdft", [128, 256]); identA = cx.inp("identA", [128, 128])

    if not premod:
        o_mod = cx.out("mod", [128, 48, 2])
    o_QT = cx.out("QT", [128, 4, T], BF16)
    o_KTh = [cx.out("KT%d" % h, [128, T], BF16) for h in range(4)]
    o_Vh = [cx.out("V%d" % h, [T, 128], BF16) for h in range(4)]
    o_Zq = [cx.out("Z%d" % q, [T, 128], BF16) for q in range(4)]
    o_yc = cx.out("ycT", [128, 2, T], BF16); o_yp = cx.out("ypT", [128, 2, T], BF16)
    o_hx = cx.out("hxT", [128, KC, T], BF16)
    o_KTc = cx.out("KTc", [128, 4, NCTX], BF16); o_Vc = cx.out("Vc", [NCTX, 512], BF16)
    if full_ctx:
        o_QTc = cx.out("QTc", [128, 4, NCTX], BF16); o_Zc = cx.out("Zc", [NCTX, 512], BF16)
        o_ycc = cx.out("ycTc", [128, 2, NCTX], BF16); o_ypc = cx.out("ypTc", [128, 2, NCTX], BF16)
        o_hxc = cx.out("hxTc", [128, KC, NCTX], BF16)

    wA_t, wA_b = cx.sb("wA_t", [128, KC, NWA * 128], BF16)
    for kc in range(KC):
        S.dma("pool", wA_t[:, kc, :], wA[kc * 128:(kc + 1) * 128, :], writes=[wA_b], also=True)
    wV_t, wV_b = cx.sb("wV_t", [128, KC, 512], BF16)
    for kc in range(KC):
        S.dma("pool", wV_t[:, kc, :], wV[kc * 128:(kc + 1) * 128, :], writes=[wV_b], also=True)
    if not premod:
        cT_t, cT_b = load_const(cx, "cT_t", cT[:, :, :], [128, KC, 2])
        adab_t, adab_b = load_const(cx, "adab_t", ada_b[:, :], [128, 48])
    n1g_t, n1g_b = load_const(cx, "n1g_t", n1g[:, :], [128, KC])
    hmask_t, hmask_b = load_const(cx, "hmask_t", hmask[:, :], [128, 2 * HALO])
    cw_t, cw_b = load_const(cx, "cw_t", cw[:, :, :], [128, 2, 31])
    cvec_t, cvec_b = load_const(cx, "cvec_t", cvec[:, :, :], [128, 2, 4])
    pw_t, pw_b = cx.sb("pw_t", [128, 2, 128], BF16)
    S.dma("pool", pw_t[:], pw[:, :, :], writes=[pw_b])
    cdft_t, cdft_b = cx.sb("cdft_t", [128, 256], BF16)
    S.dma("pool", cdft_t[:], cdft[:, :], writes=[cdft_b])
    ones_t, ones_b = cx.sb("ones_t", [128, 128], F32)
    S.op("dve", lambda e: e.memset(ones_t[:], 1.0), writes=[ones_b])
    idA_t, idA_b = load_const(cx, "idA_t", identA[:, :], [128, 128])
    dg_t, dg_b = cx.sb("dg_t", [128, 2, 31, 128], BF16)
    for c in range(2):
        for k in range(31):
            S.op("dve", lambda e, c=c, k=k: e.tensor_scalar(out=dg_t[:, c, k, :], in0=idA_t[:, :], scalar1=cw_t[:, c, k:k + 1], scalar2=None, op0=ALU.mult),
                 reads=[idA_b, cw_b], writes=[dg_b])

    psr = PsumRing(cx, 6)
    pss = PsumRing(cx, 2, "pss")

    mod_t, mod_b = cx.sb("mod_t", [128, 48, 2], F32)
    if "mod_in" in cx.links:
        S.dma("sp", mod_t[:], cx.links["mod_in"][:, :, :], writes=[mod_b])
    else:
        _adaln_mod(cx, cT_t, cT_b, ada_w, adab_t, adab_b, mod_t, mod_b, psr, 12, 0)
        S.barrier()
        S.dma("sp", o_mod[:, :, :], mod_t[:], reads=[mod_b])
    gs_t, gs_b = cx.sb("gs_t", [128, KC, 2], F32)
    S.op("dve", lambda e: e.tensor_scalar(out=gs_t[:], in0=mod_t[:, 8:16, :], scalar1=1.0, scalar2=None, op0=ALU.add),
         reads=[mod_b], writes=[gs_b])
    S.op("dve", lambda e: e.tensor_tensor(out=gs_t[:], in0=gs_t[:], in1=n1g_t[:].unsqueeze(2).to_broadcast([128, KC, 2]), op=ALU.mult),
         reads=[gs_b, n1g_b], writes=[gs_b])

    WZ = T + 2 * HALO
    zb_t, zb_b = cx.sb("zb_t", [128, 2, WZ], BF16)
    ub_t, ub_b = cx.sb("ub_t", [128, 2, WZ], F32)
    WZC = NCTX + 2 * HALO
    zc_t, zc_b = cx.sb("zc_t", [128, 2, WZC], BF16)
    uc_t, uc_b = cx.sb("uc_t", [128, 2, WZC], F32)
    if full_ctx:
        S.op("pool", lambda e: e.memset(zc_t[:], 0.0), writes=[zc_b])
        S.op("pool", lambda e: e.memset(uc_t[:], 0.0), writes=[uc_b])

    with ExitStack() as st:
        xb = [cx.sb("xb%d" % i, [128, KC, 512], F32, stack=st) for i in range(2)]
        sqs = [cx.sb("sq%d" % i, [128, 512], F32, stack=st) for i in range(2)]
        rC = [cx.sb("rC%d" % i, [128, 512], F32, stack=st) for i in range(2)]
        rS = [cx.sb("rS%d" % i, [128, 512], F32, stack=st) for i in range(2)]
        rs_t, rs_b = cx.sb("rs_t", [128, 512], F32, stack=st)
        tmp = [cx.sb("tmp%d" % i, [128, 512], F32, stack=st) for i in range(2)]
        hx = [cx.sb("hx%d" % i, [128, KC, 512], BF16, stack=st) for i in range(2)]
        t1 = [cx.sb("t1_%d" % i, [128, 512], F32, stack=st) for i in range(2)]
        t2 = [cx.sb("t2_%d" % i, [128, 512], F32, stack=st) for i in range(2)]
        qo = [cx.sb("qo%d" % i, [128, 512], BF16, stack=st) for i in range(2)]
        sg = [cx.sb("sg%d" % i, [128, 512], F32, stack=st) for i in range(2)]
        uf = [cx.sb("uf%d" % i, [128, 2, 512], BF16, stack=st) for i in range(1)]
        vo = [cx.sb("vo%d" % i, [128, 512], BF16, stack=st) for i in range(2)]
        zo = [cx.sb("zo%d" % i, [128, 512], BF16, stack=st) for i in range(2)]

        blocks = [("main", i * 512, 512) for i in range(4)] + [("halo", 0, 2 * HALO), ("ctx", 0, NCTX)]
        srcs = {"main": xT, "halo": xTh, "ctx": cxT}
        rsall_t, _u = cx.sb("rsall_t", [128, T + 2 * HALO + NCTX], F32, stack=st)
        rs_off = []
        rs_bufs = []
        off_ = 0
        for bi, (kind, c0, n) in enumerate(blocks):
            xt, xbb = xb[bi % 2]
            for kc in range(KC):
                S.dma("sp", xt[:, kc, 0:n], srcs[kind][kc * 128:(kc + 1) * 128, c0:c0 + n], writes=[xbb], also=(kc > 0))
            pst, psb = pss.next()
            for kc in range(KC):
                sq_t, sq_b = sqs[kc % 2]
                S.op("act", lambda e, kc=kc: e.activation(out=sq_t[:, 0:n], in_=xt[:, kc, 0:n], func=AF.Square), reads=[xbb], writes=[sq_b])
                S.op("pe", lambda e, kc=kc: e.matmul(pst[:, 0:n], ones_t[:, :], sq_t[:, 0:n], start=(kc == 0), stop=(kc == KC - 1)),
                     reads=[ones_b, sq_b], writes=[psb], inc=True)
            rb = Buf("rs%d" % bi)
            rsl = rsall_t[:, off_:off_ + n]
            S.op("dve", lambda e: e.tensor_scalar(out=rsl, in0=pst[:, 0:n], scalar1=1.0 / D, scalar2=EPS, op0=ALU.mult, op1=ALU.add),
                 reads=[psb], writes=[rb])
            S.op("act", lambda e: e.activation(out=rsl, in_=rsl, func=AF.Sqrt), reads=[rb], writes=[rb])
            S.op("dve", lambda e: e.reciprocal(out=rsl, in_=rsl), reads=[rb], writes=[rb])
            rs_off.append(off_)
            rs_bufs.append(rb)
            off_ += n
        def loads(bi):
            kind, c0, n = blocks[bi]
            xt, xbb = xb[bi % 2]
            for kc in range(KC):
                S.dma("sp", xt[:, kc, 0:n], srcs[kind][kc * 128:(kc + 1) * 128, c0:c0 + n], writes=[xbb], also=(kc > 0))
            if kind == "main":
                S.dma("sp", rC[bi % 2][0][:, :], ropeC[:, c0:c0 + n], writes=[rC[bi % 2][1]])
                S.dma("sp", rS[bi % 2][0][:, :], ropeS[:, c0:c0 + n], writes=[rS[bi % 2][1]])

        loads(0)
        for bi, (kind, c0, n) in enumerate(blocks):
            if bi + 1 < len(blocks):
                loads(bi + 1)
            xt, xbb = xb[bi % 2]
            j = 1 if kind == "ctx" else 0
            if kind == "main":
                ropeC_t, ropeC_b = rC[bi % 2]
                ropeS_t, ropeS_b = rS[bi % 2]
            rs_t = rsall_t[:, rs_off[bi]:rs_off[bi] + 512] if n == 512 else rsall_t[:, rs_off[bi]:rs_off[bi] + n]
            rs_b = rs_bufs[bi]
            hxt, hxb = hx[bi % 2]
            for kc in range(KC):
                tt, tb = tmp[kc % 2]
                S.op("dve", lambda e, kc=kc, tt=tt: e.tensor_tensor(out=tt[:, 0:n], in0=xt[:, kc, 0:n], in1=rs_t[:, 0:n], op=ALU.mult),
                     reads=[xbb, rs_b], writes=[tb])
                S.op("act", lambda e, kc=kc, tt=tt: e.activation(out=hxt[:, kc, 0:n], in_=tt[:, 0:n], func=AF.Identity,
                                                                 bias=mod_t[:, kc, j:j + 1], scale=gs_t[:, kc, j:j + 1]),
                     reads=[tb, mod_b, gs_b], writes=[hxb])
            if kind == "main":
                S.dma("act", o_hx[:, :, c0:c0 + n], hxt[:, :, 0:n], reads=[hxb])
            elif kind == "ctx" and full_ctx:
                S.dma("act", o_hxc[:, :, :], hxt[:, :, 0:n], reads=[hxb])

            def fm(ci):
                pt, pb = psr.next()
                for kc in range(KC):
                    S.op("pe", lambda e, kc=kc: e.matmul(pt[:, 0:n], wA_t[:, kc, ci * 128:(ci + 1) * 128], hxt[:, kc, 0:n],
                                                         start=(kc == 0), stop=(kc == KC - 1)),
                         reads=[wA_b, hxb], writes=[pb], inc=(kc == KC - 1))
                return pt, pb

            if kind != "halo":
                for qk in range(2):
                    if qk == 0 and kind == "ctx" and not full_ctx:
                        continue
                    for h in range(4):
                        if kind == "main":
                            dsl = o_QT[:, h, c0:c0 + n] if qk == 0 else o_KTh[h][:, c0:c0 + n]
                        else:
                            dsl = (o_QTc if qk == 0 else o_KTc)[:, h, 0:n]
                        pt, pb = fm(qk * 8 + h)
                        qt, qb = qo[h % 2]
                        if kind == "main":
                            p2, p2b = fm(qk * 8 + 4 + h)
                            a1, a1b = t1[h % 2]
                            a2, a2b = t2[h % 2]
                            S.op("dve", lambda e: e.tensor_tensor(out=a1[:, 0:n], in0=pt[:, 0:n], in1=ropeC_t[:, 0:n], op=ALU.mult),
                                 reads=[pb, ropeC_b], writes=[a1b])
                            S.op("dve", lambda e: e.tensor_tensor(out=a2[:, 0:n], in0=p2[:, 0:n], in1=ropeS_t[:, 0:n], op=ALU.mult),
                                 reads=[p2b, ropeS_b], writes=[a2b])
                            S.op("pool", lambda e: e.tensor_tensor(out=qt[:, 0:n], in0=a1[:, 0:n], in1=a2[:, 0:n], op=ALU.add),
                                 reads=[a1b, a2b], writes=[qb])
                            S.dma("pool", dsl, qt[:, 0:n], reads=[qb])
                        else:
                            S.op("act", lambda e: e.activation(out=qt[:, 0:n], in_=pt[:, 0:n], func=AF.Copy), reads=[pb], writes=[qb])
                            S.dma("act", dsl, qt[:, 0:n], reads=[qb])
            do_cp = (kind != "ctx") or full_ctx
            if do_cp:
                if kind == "main":
                    zt, zbb, ut, ubb, col = zb_t, zb_b, ub_t, ub_b, HALO + c0
                elif kind == "ctx":
                    zt, zbb, ut, ubb, col = zc_t, zc_b, uc_t, uc_b, HALO
                for c in range(2):
                    pa, pab = fm(16 + c)
                    pg, pgb = fm(18 + c)
                    s_t, s_b = sg[c % 2]
                    S.op("act", lambda e: e.activation(out=s_t[:, 0:n], in_=pg[:, 0:n], func=AF.Sigmoid), reads=[pgb], writes=[s_b])
                    pu, pub = fm(20 + c)
                    if kind == "halo":
                        for (lo, dcol) in ((0, 0), (HALO, HALO + T)):
                            S.op("dve", lambda e, lo=lo, dcol=dcol: e.tensor_tensor(out=zb_t[:, c, dcol:dcol + HALO], in0=pa[:, lo:lo + HALO],
                                                                                     in1=s_t[:, lo:lo + HALO], op=ALU.mult),
                                 reads=[pab, s_b], writes=[zb_b])
                            S.op("dve", lambda e, lo=lo, dcol=dcol: e.tensor_tensor(out=zb_t[:, c, dcol:dcol + HALO], in0=zb_t[:, c, dcol:dcol + HALO],
                                                                                     in1=hmask_t[:, lo:lo + HALO], op=ALU.mult),
                                 reads=[zb_b, hmask_b], writes=[zb_b])
                            S.op("dve", lambda e, lo=lo, dcol=dcol: e.tensor_tensor(out=ub_t[:, c, dcol:dcol + HALO], in0=pu[:, lo:lo + HALO],
                                                                                     in1=hmask_t[:, lo:lo + HALO], op=ALU.mult),
                                 reads=[pub, hmask_b], writes=[ub_b])
                    else:
                        S.op("dve", lambda e: e.tensor_tensor(out=zt[:, c, col:col + n], in0=pa[:, 0:n], in1=s_t[:, 0:n], op=ALU.mult),
                             reads=[pab, s_b], writes=[zbb])
                        S.op("act", lambda e: e.activation(out=ut[:, c, col:col + n], in_=pu[:, 0:n], func=AF.Copy), reads=[pub], writes=[ubb])
            if kind == "main" or (kind == "ctx" and full_ctx):
                uft, ufb = uf[0]
                for c in range(2):
                    pt, pb = fm(22 + c)
                    S.op("act", lambda e, c=c: e.activation(out=uft[:, c, 0:n], in_=pt[:, 0:n], func=AF.Copy), reads=[pb], writes=[ufb])
                for tt_ in range(n // 128):
                    pt, pb = psr.next()
                    for c in range(2):
                        S.op("pe", lambda e, c=c: e.matmul(pt[:, c * 256:(c + 1) * 256], uft[:, c, tt_ * 128:(tt_ + 1) * 128], cdft_t[:, :],
                                                           start=True, stop=True),
                             reads=[ufb, cdft_b], writes=[pb], inc=(c == 1))
                    z_t, z_b = zo[tt_ % 2]
                    S.op("act", lambda e: e.activation(out=z_t[:, :].rearrange("p (r c k) -> p r c k", r=2, c=2),
                                                       in_=pt[:, :].rearrange("p (c r k) -> p r c k", c=2, r=2), func=AF.Copy),
                         reads=[pb], writes=[z_b])
                    if kind == "main":
                        for q4 in range(4):
                            S.dma("act", o_Zq[q4][c0 + tt_ * 128:c0 + (tt_ + 1) * 128, :], z_t[:, q4 * 128:(q4 + 1) * 128], reads=[z_b])
                    else:
                        S.dma("act", o_Zc[c0 + tt_ * 128:c0 + (tt_ + 1) * 128, :], z_t[:, :], reads=[z_b])
            if kind != "halo":
                for tt_ in range(n // 128):
                    pt, pb = psr.next()
                    for kc in range(KC):
                        S.op("pe", lambda e, kc=kc: e.matmul(pt[:, :], hxt[:, kc, tt_ * 128:(tt_ + 1) * 128], wV_t[:, kc, :],
                                                             start=(kc == 0), stop=(kc == KC - 1)),
                             reads=[hxb, wV_b], writes=[pb], inc=(kc == KC - 1))
                    v_t, v_b = vo[tt_ % 2]
                    S.op("dve", lambda e: e.tensor_copy(out=v_t[:, :], in_=pt[:, :]), reads=[pb], writes=[v_b])
                    if kind == "main":
                        for h in range(4):
                            S.dma("sp", o_Vh[h][c0 + tt_ * 128:c0 + (tt_ + 1) * 128, :], v_t[:, h * 128:(h + 1) * 128], reads=[v_b])
                    else:
                        S.dma("sp", o_Vc[c0 + tt_ * 128:c0 + (tt_ + 1) * 128, :], v_t[:, :], reads=[v_b])

    S.barrier()
    if getattr(cx, "after_blocks", None) is not None:
        cx.after_blocks()
    segs = [(zb_t, zb_b, ub_t, ub_b, T, pinv, o_yc, o_yp)]
    if full_ctx:
        segs.append((zc_t, zc_b, uc_t, uc_b, NCTX, pinvc, o_ycc, o_ypc))
    with ExitStack() as st:
        acc_t, acc_b = cx.sb("acc_t", [128, 2, T], F32, stack=st)
        pin_t, pin_b = cx.sb("pin_t", [128, 2, T], F32, stack=st)
        w2_t, w2_b = cx.sb("w2_t", [128, 2, T + 2 * HALO], F32, stack=st)
        w4_t, w4_b = cx.sb("w4_t", [128, 2, T + 2 * HALO], F32, stack=st)
        w8_t, w8_b = cx.sb("w8_t", [128, T + 2 * HALO], F32, stack=st)
        s1 = [cx.sb("s1_%d" % i, [128, 512], F32, stack=st) for i in range(2)]
        s2 = [cx.sb("s2_%d" % i, [128, 512], F32, stack=st) for i in range(2)]
        s3 = [cx.sb("s3_%d" % i, [128, 512], F32, stack=st) for i in range(2)]
        yo = [cx.sb("yo%d" % i, [128, 2, 512], BF16, stack=st) for i in range(2)]
        dd = [cx.sb("dd%d" % i, [128, 2, 512], BF16, stack=st) for i in range(2)]
        po = [cx.sb("po%d" % i, [128, 2, 512], BF16, stack=st) for i in range(2)]
        for (zt, zbb, ut, ubb, n, pinv_d, oyc, oyp) in segs:
            off = HALO - 15
            for blk in range((n + 511) // 512):
                b0 = blk * 512
                nb = min(512, n - b0)
                for c in range(2):
                    pt, pb = psr.next()
                    for k in range(31):
                        S.op("pe", lambda e, c=c, k=k: e.matmul(pt[:, 0:nb], dg_t[:, c, k, :], zt[:, c, off + k + b0:off + k + b0 + nb],
                                                                start=(k == 0), stop=(k == 30)),
                             reads=[dg_b, zbb], writes=[pb], inc=(k == 30))
                    S.op("act", lambda e, c=c: e.activation(out=acc_t[:, c, b0:b0 + nb], in_=pt[:, 0:nb], func=AF.Identity, bias=cvec_t[:, c, 0:1]),
                         reads=[pb, cvec_b], writes=[acc_b])
            for blk in range((n + 511) // 512):
                b0 = blk * 512
                nb = min(512, n - b0)
                sq_t2, sq_b2 = s1[blk % 2]
                psum_, psumb = pss.next()
                pssq, pssqb = pss.next()
                for c in range(2):
                    S.op("pe", lambda e, c=c: e.matmul(psum_[:, 0:nb], ones_t[:, :], acc_t[:, c, b0:b0 + nb], start=(c == 0), stop=(c == 1)),
                         reads=[ones_b, acc_b], writes=[psumb], inc=(c == 1))
                for c in range(2):
                    S.op("act", lambda e, c=c: e.activation(out=sq_t2[:, 0:nb], in_=acc_t[:, c, b0:b0 + nb], func=AF.Square),
                         reads=[acc_b], writes=[sq_b2])
                    S.op("pe", lambda e, c=c: e.matmul(pssq[:, 0:nb], ones_t[:, :], sq_t2[:, 0:nb], start=(c == 0), stop=(c == 1)),
                         reads=[ones_b, sq_b2], writes=[pssqb], inc=True)
                mean_t, mean_b = s2[blk % 2]
                var_t, var_b = s3[blk % 2]
                S.op("dve", lambda e: e.tensor_scalar(out=mean_t[:, 0:nb], in0=psum_[:, 0:nb], scalar1=1.0 / 256, scalar2=None, op0=ALU.mult),
                     reads=[psumb], writes=[mean_b])
                S.op("dve", lambda e: e.tensor_tensor(out=var_t[:, 0:nb], in0=mean_t[:, 0:nb], in1=mean_t[:, 0:nb], op=ALU.mult),
                     reads=[mean_b], writes=[var_b])
                S.op("dve", lambda e: e.scalar_tensor_tensor(out=var_t[:, 0:nb], in0=pssq[:, 0:nb], scalar=1.0 / 256, in1=var_t[:, 0:nb],
                                                             op0=ALU.mult, op1=ALU.subtract),
                     reads=[pssqb, var_b], writes=[var_b])
                S.op("dve", lambda e: e.tensor_scalar(out=var_t[:, 0:nb], in0=var_t[:, 0:nb], scalar1=EPS, scalar2=None, op0=ALU.add),
                     reads=[var_b], writes=[var_b])
                S.op("act", lambda e: e.activation(out=var_t[:, 0:nb], in_=var_t[:, 0:nb], func=AF.Sqrt), reads=[var_b], writes=[var_b])
                S.op("dve", lambda e: e.reciprocal(out=var_t[:, 0:nb], in_=var_t[:, 0:nb]), reads=[var_b], writes=[var_b])
                y_t, y_b = yo[blk % 2]
                for c in range(2):
                    S.op("dve", lambda e, c=c: e.tensor_tensor(out=sq_t2[:, 0:nb], in0=acc_t[:, c, b0:b0 + nb], in1=mean_t[:, 0:nb], op=ALU.subtract),
                         reads=[acc_b, mean_b], writes=[sq_b2])
                    S.op("dve", lambda e, c=c: e.tensor_tensor(out=sq_t2[:, 0:nb], in0=sq_t2[:, 0:nb], in1=var_t[:, 0:nb], op=ALU.mult),
                         reads=[sq_b2, var_b], writes=[sq_b2])
                    S.op("act", lambda e, c=c: e.activation(out=y_t[:, c, 0:nb], in_=sq_t2[:, 0:nb], func=AF.Silu,
                                                            bias=cvec_t[:, c, 2:3], scale=cvec_t[:, c, 1:2]),
                         reads=[sq_b2, cvec_b], writes=[y_b])
                S.dma("act", oyc[:, :, b0:b0 + nb], y_t[:, :, 0:nb], reads=[y_b])
            S.dma("sp", pin_t[:, :, 0:n], pinv_d[:, :, :], writes=[pin_b])
            W = n + 2 * HALO
            S.op("dve", lambda e: e.tensor_tensor(out=w2_t[:, :, 1:W], in0=ut[:, :, 0:W - 1], in1=ut[:, :, 1:W], op=ALU.add),
                 reads=[ubb], writes=[w2_b])
            S.op("dve", lambda e: e.tensor_tensor(out=w4_t[:, :, 2:W - 1], in0=w2_t[:, :, 1:W - 2], in1=w2_t[:, :, 3:W], op=ALU.add),
                 reads=[w2_b], writes=[w4_b])
            S.op("dve", lambda e: e.tensor_tensor(out=w8_t[:, 4:W - 3], in0=w4_t[:, 1, 2:W - 5], in1=w4_t[:, 1, 6:W - 1], op=ALU.add),
                 reads=[w4_b], writes=[w8_b])
            H = HALO
            S.op("dve", lambda e: e.tensor_copy(out=acc_t[0:64, 0, 0:n], in_=w2_t[0:64, 0, H:H + n]), reads=[w2_b, acc_b], writes=[acc_b])
            S.op("dve", lambda e: e.tensor_copy(out=acc_t[64:128, 0, 0:n], in_=w4_t[64:128, 0, H:H + n]), reads=[w4_b, acc_b], writes=[acc_b])
            S.op("dve", lambda e: e.tensor_copy(out=acc_t[0:64, 1, 0:n], in_=w8_t[0:64, H:H + n]), reads=[w8_b, acc_b], writes=[acc_b])
            S.op("dve", lambda e: e.tensor_tensor(out=acc_t[64:128, 1, 0:n], in0=w8_t[64:128, H - 4:H - 4 + n], in1=w8_t[64:128, H + 4:H + 4 + n], op=ALU.add),
                 reads=[w8_b, acc_b], writes=[acc_b])
            S.op("dve", lambda e: e.tensor_tensor(out=acc_t[:, :, 0:n], in0=acc_t[:, :, 0:n], in1=pin_t[:, :, 0:n], op=ALU.mult),
                 reads=[acc_b, pin_b], writes=[acc_b])
            for blk in range((n + 511) // 512):
                b0 = blk * 512
                nb = min(512, n - b0)
                d_t, d_b = dd[blk % 2]
                S.op("dve", lambda e: e.tensor_tensor(out=d_t[:, :, 0:nb], in0=acc_t[:, :, b0:b0 + nb], in1=ut[:, :, H + b0:H + b0 + nb], op=ALU.subtract),
                     reads=[acc_b, ubb], writes=[d_b])
                p_t, p_b = po[blk % 2]
                for c in range(2):
                    pt, pb = psr.next()
                    S.op("pe", lambda e, c=c: e.matmul(pt[:, 0:nb], pw_t[:, c, :], d_t[:, c, 0:nb], start=True, stop=True),
                         reads=[pw_b, d_b], writes=[pb])
                    S.op("act", lambda e, c=c: e.activation(out=p_t[:, c, 0:nb], in_=pt[:, 0:nb], func=AF.Identity, scale=cvec_t[:, c, 3:4]),
                         reads=[pb, cvec_b], writes=[p_b])
                S.dma("act", oyp[:, :, b0:b0 + nb], p_t[:, :, 0:nb], reads=[p_b])
    if standalone:
        S.finish(None)
    return cx


def fm_vec(v):
    v = np.asarray(v, np.float32)
    return np.ascontiguousarray(v.reshape(-1, 128).T)


OFF_F, OFF_Q, OFF_K, OFF_V, OFF_C, OFF_P, OFF_G = 0, 256, 768, 1280, 1792, 2304, 2560


def wA_layout(w_in):
    cols = []
    swap128 = np.concatenate([SWAP64, 64 + SWAP64])
    for base in (OFF_Q, OFF_K):
        for h in range(4):
            cols.append(base + h * 128 + np.arange(128))
        for h in range(4):
            cols.append(base + h * 128 + swap128)
    cols.append(OFF_C + np.arange(512))
    cols.append(OFF_P + np.arange(256))
    cols.append(OFF_F + np.arange(256))
    cols = np.concatenate(cols)
    return np.ascontiguousarray(w_in[:, cols])


def host_A(inp, l, xT_all, ctxT_all):
    w_in = np.asarray(inp["w_in"][l])
    wA = wA_layout(w_in)
    wV = np.ascontiguousarray(w_in[:, OFF_V:OFF_C])
    ada_w = np.ascontiguousarray(inp["ada_w"][l])
    ada_b = fm_vec(inp["ada_b"][l])
    n1g = fm_vec(inp["norm1_g"][l])
    cw = np.ascontiguousarray(np.asarray(inp["conv_dw_w"][l]).T.reshape(2, 128, 31).transpose(1, 0, 2))
    cvec = np.stack([fm_vec(inp["conv_dw_b"][l]), fm_vec(inp["conv_ln_g"][l]), fm_vec(inp["conv_ln_b"][l]),
                     fm_vec(inp["pool_scale"][l])], axis=2).astype(np.float32)
    pw_in = np.asarray(inp["pool_w"][l])
    pw = np.zeros((128, 2, 128), np.float32)
    for c in range(2):
        for g in range(2):
            pw[g * 64:(g + 1) * 64, c, g * 64:(g + 1) * 64] = pw_in[2 * c + g]
    cdft = chan_dft_table()
    pinvc = pool_invcnt(0, NCTX, NCTX)
    maps = []
    for i in range(NCORES):
        b, j = i // 4, i % 4
        t0 = j * T
        hm = np.zeros((128, 2 * HALO), np.float32)
        if j > 0:
            hm[:, 0:HALO] = 1.0
        if j < 3:
            hm[:, HALO:] = 1.0
        cT = np.stack([fm_vec(inp["c"][b]), fm_vec(inp["c_ctx"])], axis=2)
        rC, rS = rope_tables(t0, T)
        m = dict(cT=np.ascontiguousarray(cT), ada_w=ada_w, ada_b=ada_b, identA=np.eye(128, dtype=np.float32),
                 n1g=n1g, wA=wA, wV=wV, ropeC=rC, ropeS=rS, hmask=hm, cw=cw, cvec=cvec,
                 pinv=pool_invcnt(t0, T, SEQ), pinvc=pinvc, pw=pw, cdft=cdft)
        if xT_all is not None:
            xTh = np.zeros((D, 2 * HALO), np.float32)
            if j > 0:
                xTh[:, 0:HALO] = xT_all[b][:, t0 - HALO:t0]
            if j < 3:
                xTh[:, HALO:] = xT_all[b][:, t0 + T:t0 + T + HALO]
            m.update(xT=np.ascontiguousarray(xT_all[b][:, t0:t0 + T]), xTh=xTh, cxT=np.ascontiguousarray(ctxT_all[b]))
        maps.append(m)
    return maps


_PROGS = {}


def get_prog(key, builder, *args):
    if key not in _PROGS:
        _PROGS[key] = builder(*args)
    return _PROGS[key]


def run_prog(cx, maps):
    res = run_bass_kernel_spmd(cx.nc, maps, core_ids=list(range(NCORES)))
    return res.results


NKEY = NCTX + SEQ
NKC = NKEY // 128
SCALE_F = 1.0 / math.sqrt(SEQ * 64.0)
SCALE_FC = 1.0 / math.sqrt(NCTX * 64.0)


def fft_tables(j):
    n1 = np.arange(128)
    ph = 2 * np.pi * np.outer(n1, n1) / 128.0
    C, Sn = np.cos(ph), np.sin(ph)
    tabA = np.stack([np.concatenate([C, Sn], 1), np.concatenate([-Sn, C], 1)], axis=1).astype(np.float32)
    n2 = np.arange(64)[:, None, None]
    k1 = np.arange(128)[None, :, None]
    k2 = (16 * j + np.arange(16))[None, None, :]
    th = 2 * np.pi * n2 * (k1 + 128 * k2) / float(SEQ)
    M = np.stack([np.cos(th) * SCALE_F, -np.sin(th) * SCALE_F], axis=2)
    tabC = np.concatenate([M, M], axis=0).astype(np.float32)
    l = np.arange(NCTX)
    a = 2 * np.pi * np.outer(l, l) / float(NCTX)
    Cc = (np.cos(a) * SCALE_FC).reshape(2, 128, NCTX).transpose(1, 0, 2)
    Sc = (-np.sin(a) * SCALE_FC).reshape(2, 128, NCTX).transpose(1, 0, 2)
    tabX = np.stack([Cc, Sc], axis=1).astype(np.float32)
    return tabA, tabC, np.ascontiguousarray(tabX)


def build_B(full_ctx, lam_init, dbg=False, only=None, cx=None):
    standalone = cx is None
    if standalone:
        cx = Ctx("B")
    nc, S = cx.nc, cx.S
    QT = cx.inp("QT", [128, 4, T], BF16)
    KTgh = [cx.inp("KTg%d" % h, [512, T], BF16) for h in range(4)]
    Vgh = [cx.inp("Vg%d" % h, [SEQ, 128], BF16) for h in range(4)]
    Zgq = [cx.inp("Zg%d" % q, [SEQ, 128], BF16) for q in range(4)]
    KTc = cx.inp("KTc", [128, 4, NCTX], BF16); Vc = cx.inp("Vc", [NCTX, 512], BF16)
    ycT = cx.inp("ycT", [128, 2, T], BF16); ypT = cx.inp("ypT", [128, 2, T], BF16); hxT = cx.inp("hxT", [128, KC, T], BF16)
    xT = cx.inp("xT", [D, T]); mod = cx.inp("mod", [128, 48, 2])
    wg = cx.inp("wg", [D, 4, D]); wo = cx.inp("wo", [1280, D]); wout = cx.inp("wout", [D, D])
    lamv = cx.inp("lamv", [4, 64]); subg = cx.inp("subg", [128, 1]); n2g = cx.inp("n2g", [128, KC])
    tabA = cx.inp("tabA", [128, 2, 256]); tabC = cx.inp("tabC", [128, 128, 2, 16]); ident = cx.inp("ident", [128, 128])
    o_xm = cx.out("xmT", [D, T]); o_h2 = cx.out("hx2T", [128, KC, T], BF16)
    if full_ctx:
        QTc = cx.inp("QTc", [128, 4, NCTX], BF16); Zc = cx.inp("Zc", [NCTX, 512], BF16)
        ycTc = cx.inp("ycTc", [128, 2, NCTX], BF16); ypTc = cx.inp("ypTc", [128, 2, NCTX], BF16); hxTc = cx.inp("hxTc", [128, KC, NCTX], BF16)
        cxT = cx.inp("cxT", [D, NCTX]); tabX = cx.inp("tabX", [128, 2, 2, NCTX])
        o_cxm = cx.out("cxmT", [D, NCTX]); o_ch2 = cx.out("chx2T", [128, KC, NCTX], BF16)

    mod_t, mod_b = load_const(cx, "mod_t", mod[:, :, :], [128, 48, 2])
    n2g_t, n2g_b = load_const(cx, "n2g_t", n2g[:, :], [128, KC])
    subg_t, subg_b = load_const(cx, "subg_t", subg[:, :], [128, 1])
    ones_t, ones_b = cx.sb("ones_t", [128, 128], F32)
    S.op("dve", lambda e: e.memset(ones_t[:], 1.0), writes=[ones_b])
    id_t, id_b = cx.sb("id_t", [128, 128], BF16)
    S.dma("pool", id_t[:], ident[:, :], writes=[id_b])
    lv_t, lv_b = cx.sb("lv_t", [128, 4, 64], F32)
    for r in range(4):
        S.dma("sp", lv_t[:, r, :], lamv[r:r + 1, :].partition_broadcast(128), writes=[lv_b], also=(r > 0))
    lp_t, lp_b = cx.sb("lp_t", [128, 2, 64], F32)
    S.op("dve", lambda e: e.tensor_tensor(out=lp_t[:], in0=lv_t[:, 0:4:2, :], in1=lv_t[:, 1:4:2, :], op=ALU.mult), reads=[lv_b], writes=[lp_b])
    ls_t, ls_b = cx.sb("ls_t", [128, 2], F32)
    S.op("dve", lambda e: e.reduce_sum(out=ls_t[:], in_=lp_t[:], axis=mybir.AxisListType.X), reads=[lp_b], writes=[ls_b])
    S.op("act", lambda e: e.activation(out=ls_t[:], in_=ls_t[:], func=AF.Exp), reads=[ls_b], writes=[ls_b])
    nlam_t, nlam_b = cx.sb("nlam_t", [128, 1], F32)
    S.op("dve", lambda e: e.tensor_tensor(out=nlam_t[:], in0=ls_t[:, 1:2], in1=ls_t[:, 0:1], op=ALU.subtract), reads=[ls_b], writes=[nlam_b])
    S.op("dve", lambda e: e.tensor_scalar(out=nlam_t[:], in0=nlam_t[:], scalar1=-float(lam_init), scalar2=None, op0=ALU.add),
         reads=[nlam_b], writes=[nlam_b])
    S.op("dve", lambda e: e.tensor_scalar(out=subg_t[:], in0=subg_t[:], scalar1=float(1.0 - lam_init), scalar2=None, op0=ALU.mult),
         reads=[subg_b], writes=[subg_b])
    gs_t, gs_b = cx.sb("gs_t", [128, KC, 2], F32)
    S.op("dve", lambda e: e.tensor_scalar(out=gs_t[:], in0=mod_t[:, 32:40, :], scalar1=1.0, scalar2=None, op0=ALU.add), reads=[mod_b], writes=[gs_b])
    S.op("dve", lambda e: e.tensor_tensor(out=gs_t[:], in0=gs_t[:], in1=n2g_t[:].unsqueeze(2).to_broadcast([128, KC, 2]), op=ALU.mult),
         reads=[gs_b, n2g_b], writes=[gs_b])

    yf_t, yf_b = cx.sb("yf_t", [128, 2, T], BF16)

    oa_t, oa_b = cx.sb("oa_t", [128, 4, T], BF16)
    if full_ctx:
        yfc_t, yfc_b = cx.sb("yfc_t", [128, 2, NCTX], BF16)
        oac_t, oac_b = cx.sb("oac_t", [128, 4, NCTX], BF16)

    with ExitStack() as st:
        zs_t, zs_b = cx.sb("zs_t", [128, 64 * 512], BF16, stack=st)
        tt_t, tt_b = cx.sb("tt_t", [128, 128 * 256], BF16, stack=st)
        tA_t, tA_b = cx.sb("tA_t", [128, 2, 256], BF16, stack=st)
        tC_t, tC_b = cx.sb("tC_t", [128, 128 * 32], BF16, stack=st)
        S.dma("pool", tA_t[:], tabA[:, :, :], writes=[tA_b])
        S.dma("pool", tC_t[:], tabC[:, :, :, :].rearrange("p a b c -> p (a b c)"), writes=[tC_b])
        if cx.after_fft_tables is not None:
            cx.after_fft_tables()
        zdst = zs_t[:, :].rearrange("p (n c) -> p n c", c=512)
        for q4 in range(4):
            S.dma("sp", zdst[:, :, q4 * 128:(q4 + 1) * 128], Zgq[q4][:, :].rearrange("(a b) c -> a b c", b=64), writes=[zs_b], also=(q4 > 0))
        psA = PsumRing(cx, 3, "psA", stack=st)
        psC = [cx.ps("psC%d" % i, [128, 512], F32, stack=st) for i in range(4)]
        zv = zs_t[:, :].rearrange("p (n r c k) -> p r c n k", n=64, r=2, c=2)
        ttv = tt_t[:, :].rearrange("p (k x) -> p k x", x=256)
        for cp2 in range(64):
            pt, pb = psA.next()
            for s_ in range(2):
                cp = cp2 * 2 + s_
                for r in range(2):
                    for c2 in range(2):
                        S.op("pe", lambda e, r=r, cp=cp, s_=s_, c2=c2: e.matmul(pt[c2 * 64:(c2 + 1) * 64, s_ * 256:(s_ + 1) * 256], zv[:, r, c2, :, cp],
                                                                                 tA_t[:, r, :], start=(r == 0), stop=(r == 1), tile_position=(0, c2 * 64)),
                             reads=[zs_b, tA_b], writes=[pb], inc=(s_ == 1 and r == 1 and c2 == 1))
            eng = "act" if cp2 % 2 == 0 else "dve"
            if eng == "act":
                S.op("act", lambda e: e.activation(out=tt_t[:, cp2 * 512:(cp2 + 1) * 512], in_=pt[:, :], func=AF.Copy), reads=[pb], writes=[tt_b])
            else:
                S.op("dve", lambda e: e.tensor_copy(out=tt_t[:, cp2 * 512:(cp2 + 1) * 512], in_=pt[:, :]), reads=[pb], writes=[tt_b])
        if only == "fft":
            d_tt = cx.out("dbg_tt", [128, 128 * 256], BF16)
            S.dma("sp", d_tt[:, :], tt_t[:, :], reads=[tt_b])
        tcv = tC_t[:, :].rearrange("p (k r j) -> p k r j", r=2, j=16)
        psCb = [Buf("psCb%d" % i) for i in range(4)]
        for c2 in range(2):
            for k1 in range(128):
                bank = k1 // 32
                col = (k1 % 32) * 16
                for r in range(2):
                    S.op("pe", lambda e, r=r, k1=k1: e.matmul(psC[bank][0][:, col:col + 16], ttv[c2 * 64:(c2 + 1) * 64, :, r * 128 + k1],
                                                             tcv[c2 * 64:(c2 + 1) * 64, k1, r, :], start=(r == 0), stop=(r == 1),
                                                             tile_position=(c2 * 64, 0)),
                         reads=[tt_b, tC_b], writes=[psCb[bank]], inc=(r == 1 and k1 % 32 == 31))
            for bank in range(4):
                dst = yf_t[:, c2, :].rearrange("p (j k) -> p k j", k=128)[:, bank * 32:(bank + 1) * 32, :]
                srcp = psC[bank][0][:, :].rearrange("p (k j) -> p k j", j=16)
                if bank % 2 == 0:
                    S.op("act", lambda e: e.activation(out=dst, in_=srcp, func=AF.Copy), reads=[psCb[bank]], writes=[yf_b])
                else:
                    S.op("dve", lambda e: e.tensor_copy(out=dst, in_=srcp), reads=[psCb[bank]], writes=[yf_b])
        if full_ctx:
            zc_t, zc_b = cx.sb("zc_t", [128, 2, 512], BF16, stack=st)
            tX_t, tX_b = cx.sb("tX_t", [128, 2, 2, NCTX], BF16, stack=st)
            S.dma("sp", zc_t[:], Zc[:, :].rearrange("(a p) c -> p a c", p=128), writes=[zc_b])
            S.dma("pool", tX_t[:], tabX[:, :, :, :], writes=[tX_b])
            for c2 in range(2):
                pt, pb = psA.next()
                i = 0
                for r in range(2):
                    for tl in range(2):
                        S.op("pe", lambda e, r=r, tl=tl, i=i: e.matmul(pt[:, 0:NCTX], zc_t[:, tl, r * 256 + c2 * 128:r * 256 + (c2 + 1) * 128],
                                                                       tX_t[:, r, tl, :], start=(i == 0), stop=(i == 3)),
                             reads=[zc_b, tX_b], writes=[pb], inc=(i == 3))
                        i += 1
                S.op("act", lambda e: e.activation(out=yfc_t[:, c2, :], in_=pt[:, 0:NCTX], func=AF.Copy), reads=[pb], writes=[yfc_b])

    S.barrier()
    if only == "fft":
        d_yf = cx.out("dbg_yf", [128, 2, T], BF16)
        S.dma("sp", d_yf[:, :, :], yf_t[:], reads=[yf_b])
        if standalone:
            S.finish(None)
        return cx
    wg_t, wg_b = cx.sb("wg_t", [128, KC, 4, D], BF16)
    for kc in range(KC):
        S.dma("pool", wg_t[:, kc, :, :], wg[kc * 128:(kc + 1) * 128, :, :], writes=[wg_b], also=True)
    with ExitStack() as st:
        kt = [cx.sb("kt%d" % i, [128, NKEY], BF16, stack=st) for i in range(2)]
        vt = [cx.sb("vt%d" % i, [128, NKC, 128], BF16, stack=st) for i in range(2)]
        qt = [cx.sb("qt%d" % i, [128, T + NCTX], BF16, stack=st) for i in range(2)]
        pT = [cx.sb("pT%d" % i, [128, 2, 512], BF16, stack=st) for i in range(3)]
        psS = [cx.ps("psS%d" % i, [128, 1024], F32, stack=st) for i in range(2)]
        psO = [cx.ps("psO%d" % i, [128, 512], F32, stack=st) for i in range(2)]
        psE = [cx.ps("psE%d" % i, [128, 512], F32, stack=st) for i in range(1)]
        psD, psD_b = cx.ps("psD", [128, 512], F32, stack=st)
        dsb = [cx.sb("dsb%d" % i, [64, 512], F32, stack=st) for i in range(2)]
        on32_t, on32_b = cx.sb("on32_t", [128, 32], BF16, stack=st)
        S.op("pool", lambda e: e.memset(on32_t[:], 1.0), writes=[on32_b])
        selr = []
        for m in range(2):
            sl_t, sl_b = cx.sb("selr%d" % m, [64, 128], F32, stack=st)
            S.op("pool", lambda e: e.memset(sl_t[:], 0.0), writes=[sl_b])
            S.op("pool", lambda e, m=m: e.memset(sl_t[m * 32:m * 32 + 1, :], 1.0), reads=[sl_b], writes=[sl_b])
            selr.append((sl_t, sl_b))
        osb = [cx.sb("osb%d" % i, [128, 2, 512], F32, stack=st) for i in range(2)]
        rc = [cx.sb("rc%d" % i, [128, 512], F32, stack=st) for i in range(2)]
        oo_t, oo_b = cx.sb("at_oo_t", [128, 512], F32, stack=st)
        o1_t, o1_b = cx.sb("at_o1_t", [128, 512], F32, stack=st)
        sq_t, sq_b = cx.sb("at_sq_t", [128, 512], F32, stack=st)
        rs_t, rs_b = cx.sb("at_rs_t", [128, 512], F32, stack=st)
        gi = 0
        pending_epi = []
        for h in range(4):
            k_t, k_b = kt[h % 2]
            v_t, v_b = vt[h % 2]
            q_t, q_b = qt[h % 2]
            S.dma("sp", k_t[:, 0:NCTX], KTc[:, h, :], writes=[k_b])
            gk = [cx.dram_bufs["KTg%d" % h]] if ("KTg%d" % h) in cx.dram_bufs else []
            gv = [cx.dram_bufs["Vg%d" % h]] if ("Vg%d" % h) in cx.dram_bufs else []
            S.dma("sp", k_t[:, NCTX:NKEY].rearrange("p (r t) -> p r t", r=4), KTgh[h][:, :].rearrange("(r p) t -> p r t", p=128), reads=gk, writes=[k_b], also=True)
            S.dma("sp", v_t[:, 0:2, :], Vc[:, h * 128:(h + 1) * 128].rearrange("(c p) d -> p c d", p=128), writes=[v_b])
            vsrc = Vgh[h][:, :].rearrange("(c p) d -> p c d", p=128)
            for g in range(2):
                S.dma("sp", v_t[:, 2 + g * 32:2 + (g + 1) * 32, :], vsrc[:, g * 32:(g + 1) * 32, :], reads=gv, writes=[v_b], also=True)
            S.dma("sp", q_t[:, 0:T], QT[:, h, :], writes=[q_b])
            if full_ctx:
                S.dma("sp", q_t[:, T:T + NCTX], QTc[:, h, :], writes=[q_b], also=True)
            groups = [(g * 512, 512, NKC, oa_t, oa_b, g * 512) for g in range(4)]
            if full_ctx:
                groups.append((T, NCTX, 2, oac_t, oac_b, 0))
            for (q0, nq, nkc, dst_t, dst_b, d0) in groups:
                psOb = [psO[0][1], psO[1][1]]

                def qk(kc, q0=q0, nq=nq):
                    ps_t, ps_b = psS[kc % 2]
                    for m in range(2):
                        S.op("pe", lambda e, m=m: e.matmul(ps_t[:, m * 512:m * 512 + nq], k_t[m * 64:(m + 1) * 64, kc * 128:(kc + 1) * 128],
                                                           q_t[m * 64:(m + 1) * 64, q0:q0 + nq], start=True, stop=True, tile_position=(m * 64, 0)),
                             reads=[k_b, q_b], writes=[ps_b], inc=(m == 1))

                def den(kc, p_t, p_b, nq=nq, nkc=nkc):
                    for m in range(2):
                        S.op("pe", lambda e, m=m: e.matmul(psD[m * 32:(m + 1) * 32, 0:nq], on32_t[:, :], p_t[:, m, 0:nq], start=(kc == 0), stop=(kc == nkc - 1),
                                                           tile_position=(0, m * 32)),
                             reads=[p_b, on32_b], writes=[psD_b], inc=(m == 1))

                qk(0)
                den_prev = None
                for kc in range(nkc):
                    if kc + 1 < nkc:
                        qk(kc + 1)
                    if den_prev is not None:
                        den(*den_prev)
                    ps_t, ps_b = psS[kc % 2]
                    p_t, p_b = pT[kc % 3]
                    S.op("act", lambda e: e.activation(out=p_t[:, :, 0:nq], in_=ps_t[:, :].rearrange("p (m q) -> p m q", m=2)[:, :, 0:nq],
                                                       func=AF.Exp, scale=0.125),
                         reads=[ps_b], writes=[p_b])
                    for m in range(2):
                        S.op("pe", lambda e, m=m: e.matmul(psO[m][0][:, 0:nq], v_t[:, kc, :], p_t[:, m, 0:nq], start=(kc == 0), stop=(kc == nkc - 1)),
                             reads=[p_b, v_b], writes=[psOb[m]], inc=(m == 1))
                    den_prev = (kc, p_t, p_b)
                    if kc == nkc - 1:
                        den(*den_prev)
                    if pending_epi and (kc in (4, 10, 18) or kc == nkc - 1):
                        while pending_epi:
                            pending_epi.pop(0)()
                            if kc != nkc - 1:
                                break
                ob_t, ob_b = osb[gi % 2]
                gi += 1
                d_t, d_b = dsb[(gi - 1) % 2]
                S.op("dve", lambda e: e.tensor_copy(out=d_t[:, 0:nq], in_=psD[0:64, 0:nq]), reads=[psD_b], writes=[d_b])
                for m in range(2):
                    S.op("dve" if m == 0 else "pool", lambda e, m=m: e.tensor_copy(out=ob_t[:, m, 0:nq], in_=psO[m][0][:, 0:nq]), reads=[psOb[m]], writes=[ob_b]) if False else \
                        S.op("dve", lambda e, m=m: e.tensor_copy(out=ob_t[:, m, 0:nq], in_=psO[m][0][:, 0:nq]), reads=[psOb[m]], writes=[ob_b])

                def epi_a(nq=nq, d_t=d_t, d_b=d_b):
                    pe_t, pe_b = psE[0]
                    S.op("pe", lambda e: e.matmul(pe_t[:, 0:nq], selr[0][0][:, :], d_t[:, 0:nq], start=True, stop=True), reads=[selr[0][1], d_b], writes=[pe_b])
                    S.op("dve", lambda e: e.reciprocal(out=rc[0][0][:, 0:nq], in_=pe_t[:, 0:nq]), reads=[pe_b], writes=[rc[0][1]])

                def epi_b(nq=nq, ob_t=ob_t, ob_b=ob_b, d_t=d_t, d_b=d_b):
                    pe_t, pe_b = psE[0]
                    S.op("pe", lambda e: e.matmul(pe_t[:, 0:nq], selr[1][0][:, :], d_t[:, 0:nq], start=True, stop=True), reads=[selr[1][1], d_b], writes=[pe_b])
                    S.op("dve", lambda e: e.reciprocal(out=rc[1][0][:, 0:nq], in_=pe_t[:, 0:nq]), reads=[pe_b], writes=[rc[1][1]])
                    S.op("dve", lambda e: e.tensor_tensor(out=oo_t[:, 0:nq], in0=ob_t[:, 0, 0:nq], in1=rc[0][0][:, 0:nq], op=ALU.mult),
                         reads=[ob_b, rc[0][1]], writes=[oo_b])
                    S.op("dve", lambda e: e.tensor_tensor(out=o1_t[:, 0:nq], in0=ob_t[:, 1, 0:nq], in1=rc[1][0][:, 0:nq], op=ALU.mult),
                         reads=[ob_b, rc[1][1]], writes=[o1_b])
                    S.op("dve", lambda e: e.scalar_tensor_tensor(out=oo_t[:, 0:nq], in0=o1_t[:, 0:nq], scalar=nlam_t[:, 0:1], in1=oo_t[:, 0:nq],
                                                                 op0=ALU.mult, op1=ALU.add),
                         reads=[oo_b, o1_b, nlam_b], writes=[oo_b])
                    S.op("pool", lambda e: e.tensor_tensor(out=sq_t[:, 0:nq], in0=oo_t[:, 0:nq], in1=oo_t[:, 0:nq], op=ALU.mult), reads=[oo_b], writes=[sq_b])

                def epi_c(nq=nq, dst_t=dst_t, dst_b=dst_b, d0=d0, h=h):
                    pe_t, pe_b = psE[0]
                    S.op("pe", lambda e: e.matmul(pe_t[:, 0:nq], ones_t[:, :], sq_t[:, 0:nq], start=True, stop=True), reads=[ones_b, sq_b], writes=[pe_b])
                    S.op("dve", lambda e: e.tensor_scalar(out=rs_t[:, 0:nq], in0=pe_t[:, 0:nq], scalar1=1.0 / 128, scalar2=EPS, op0=ALU.mult, op1=ALU.add),
                         reads=[pe_b], writes=[rs_b])
                    S.op("act", lambda e: e.activation(out=rs_t[:, 0:nq], in_=rs_t[:, 0:nq], func=AF.Ln), reads=[rs_b], writes=[rs_b])
                    S.op("act", lambda e: e.activation(out=rs_t[:, 0:nq], in_=rs_t[:, 0:nq], func=AF.Exp, scale=-0.5), reads=[rs_b], writes=[rs_b])
                    S.op("dve", lambda e: e.scalar_tensor_tensor(out=dst_t[:, h, d0:d0 + nq], in0=oo_t[:, 0:nq], scalar=subg_t[:, 0:1], in1=rs_t[:, 0:nq],
                                                                 op0=ALU.mult, op1=ALU.mult),
                         reads=[oo_b, subg_b, rs_b], writes=[dst_b])

                pending_epi.extend([epi_a, epi_b, epi_c])
        while pending_epi:
            pending_epi.pop(0)()
    S.barrier()
    if dbg:
        d_yf = cx.out("dbg_yf", [128, 2, T], BF16); d_oa = cx.out("dbg_oa", [128, 4, T], BF16)
        S.dma("sp", d_yf[:, :, :], yf_t[:], reads=[yf_b])
        S.dma("sp", d_oa[:, :, :], oa_t[:], reads=[oa_b])
        if full_ctx:
            d_yfc = cx.out("dbg_yfc", [128, 2, NCTX], BF16); d_oac = cx.out("dbg_oac", [128, 4, NCTX], BF16)
            S.dma("sp", d_yfc[:, :, :], yfc_t[:], reads=[yfc_b])
            S.dma("sp", d_oac[:, :, :], oac_t[:], reads=[oac_b])
    with ExitStack() as st:
        wo_t, wo_b = cx.sb("wo_t", [128, 10, D], BF16, stack=st)
        S.dma("pool", wo_t[:], wo[:, :].rearrange("(c p) d -> p c d", p=128), writes=[wo_b])
        wout_t, wout_b = cx.sb("wout_t", [128, KC, D], BF16, stack=st)
        S.dma("pool", wout_t[:], wout[:, :].rearrange("(c p) d -> p c d", p=128), writes=[wout_b])
        hxs = [cx.sb("hx_t%d" % i, [128, KC, 512], BF16, stack=st) for i in range(2)]
        ycs = [cx.sb("yc_t%d" % i, [128, 2, 512], BF16, stack=st) for i in range(1)]
        yps = [cx.sb("yp_t%d" % i, [128, 2, 512], BF16, stack=st) for i in range(1)]
        ys = [cx.sb("y_t%d" % i, [128, KC, 512], BF16, stack=st) for i in range(2)]
        xb = [cx.sb("xb%d" % i, [128, KC, 512], F32, stack=st) for i in range(1)]
        sig = [cx.sb("sig%d" % i, [128, 512], F32, stack=st) for i in range(2)]
        acc = [cx.sb("acc%d" % i, [128, 512], F32, stack=st) for i in range(2)]
        tmpb = [cx.sb("tmpb%d" % i, [128, 512], F32, stack=st) for i in range(2)]
        sqs = [cx.sb("sqs%d" % i, [128, 512], F32, stack=st) for i in range(2)]
        rs_t, rs_b = cx.sb("rs_t", [128, 512], F32, stack=st)
        h2 = [cx.sb("h2_%d" % i, [128, KC, 512], BF16, stack=st) for i in range(1)]
        psr = PsumRing(cx, 6, "psr", stack=st)
        pss = PsumRing(cx, 2, "pss", stack=st)
        segs = [(s0, 512, 0, hxT, ycT, ypT, yf_t, yf_b, oa_t, oa_b, xT, o_xm, o_h2) for s0 in range(0, T, 512)]
        if full_ctx:
            segs.append((0, NCTX, 1, hxTc, ycTc, ypTc, yfc_t, yfc_b, oac_t, oac_b, cxT, o_cxm, o_ch2))
        bi = 0

        def gated(si):
            (s0, ns, j, hsrc, ycsrc, ypsrc, yfs_t, yfs_b, oas_t, oas_b, xsrc, oxd, ohd) = segs[si]
            hx_t, hx_b = hxs[si % 2]
            yc_t, yc_b = ycs[0]
            yp_t, yp_b = yps[0]
            y_t, y_b = ys[si % 2]
            S.dma("sp", hx_t[:, :, 0:ns], hsrc[:, :, s0:s0 + ns], writes=[hx_b])
            S.dma("sp", yc_t[:, :, 0:ns], ycsrc[:, :, s0:s0 + ns], writes=[yc_b])
            S.dma("sp", yp_t[:, :, 0:ns], ypsrc[:, :, s0:s0 + ns], writes=[yp_b])
            nb = ns
            br = [(yfs_t, yfs_b, s0, 0, 2), (oas_t, oas_b, s0, 2, 4), (yc_t, yc_b, 0, 6, 2), (yp_t, yp_b, 0, 8, 2)]
            for m in range(KC):
                a_t, a_b = acc[m % 2]
                for jb, (bt, bb, boff, wc0, nch) in enumerate(br):
                    pg, pgb = psr.next()
                    for kc in range(KC):
                        S.op("pe", lambda e, kc=kc: e.matmul(pg[:, 0:nb], wg_t[:, kc, jb, m * 128:(m + 1) * 128], hx_t[:, kc, 0:nb],
                                                             start=(kc == 0), stop=(kc == KC - 1)),
                             reads=[wg_b, hx_b], writes=[pgb], inc=(kc == KC - 1))
                    s_t, s_b = sig[jb % 2]
                    S.op("act", lambda e: e.activation(out=s_t[:, 0:nb], in_=pg[:, 0:nb], func=AF.Sigmoid), reads=[pgb], writes=[s_b])
                    pbr, pbrb = psr.next()
                    for c in range(nch):
                        S.op("pe", lambda e, c=c: e.matmul(pbr[:, 0:nb], wo_t[:, wc0 + c, m * 128:(m + 1) * 128], bt[:, c, boff:boff + nb],
                                                           start=(c == 0), stop=(c == nch - 1)),
                             reads=[wo_b, bb], writes=[pbrb], inc=(c == nch - 1))
                    if jb == 0:
                        S.op("dve", lambda e: e.tensor_tensor(out=a_t[:, 0:nb], in0=pbr[:, 0:nb], in1=s_t[:, 0:nb], op=ALU.mult),
                             reads=[pbrb, s_b], writes=[a_b])
                    else:
                        t_t, t_b = tmpb[jb % 2]
                        S.op("dve", lambda e: e.tensor_tensor(out=t_t[:, 0:nb], in0=pbr[:, 0:nb], in1=s_t[:, 0:nb], op=ALU.mult),
                             reads=[pbrb, s_b], writes=[t_b])
                        if jb < 3:
                            S.op("pool", lambda e: e.tensor_tensor(out=a_t[:, 0:nb], in0=a_t[:, 0:nb], in1=t_t[:, 0:nb], op=ALU.add),
                                 reads=[a_b, t_b], writes=[a_b])
                        else:
                            S.op("pool", lambda e: e.tensor_tensor(out=y_t[:, m, 0:nb], in0=a_t[:, 0:nb], in1=t_t[:, 0:nb], op=ALU.add),
                                 reads=[a_b, t_b], writes=[y_b])

        def tail(si):
            nonlocal bi
            (s0, ns, j, hsrc, ycsrc, ypsrc, yfs_t, yfs_b, oas_t, oas_b, xsrc, oxd, ohd) = segs[si]
            y_t, y_b = ys[si % 2]
            nb = ns
            x_t, x_b = xb[0]
            h_t, h_b = h2[0]
            bi += 1
            S.dma("sp", x_t[:, :, 0:nb], xsrc[:, s0:s0 + nb].rearrange("(c p) n -> p c n", p=128), writes=[x_b])
            pst, psb = pss.next()
            for m2 in range(KC):
                pt, pb = psr.next()
                for m in range(KC):
                    S.op("pe", lambda e, m=m: e.matmul(pt[:, 0:nb], wout_t[:, m, m2 * 128:(m2 + 1) * 128], y_t[:, m, 0:nb],
                                                       start=(m == 0), stop=(m == KC - 1)),
                         reads=[wout_b, y_b], writes=[pb], inc=(m == KC - 1))
                S.op("dve", lambda e: e.scalar_tensor_tensor(out=x_t[:, m2, 0:nb], in0=pt[:, 0:nb], scalar=mod_t[:, 16 + m2, j:j + 1],
                                                             in1=x_t[:, m2, 0:nb], op0=ALU.mult, op1=ALU.add),
                     reads=[pb, mod_b, x_b], writes=[x_b])
                q_t2, q_b2 = sqs[m2 % 2]
                S.op("act", lambda e: e.activation(out=q_t2[:, 0:nb], in_=x_t[:, m2, 0:nb], func=AF.Square), reads=[x_b], writes=[q_b2])
                S.op("pe", lambda e: e.matmul(pst[:, 0:nb], ones_t[:, :], q_t2[:, 0:nb], start=(m2 == 0), stop=(m2 == KC - 1)),
                     reads=[ones_b, q_b2], writes=[psb], inc=True)
            S.dma("act", oxd[:, s0:s0 + nb].rearrange("(c p) n -> p c n", p=128), x_t[:, :, 0:nb], reads=[x_b])
            S.op("dve", lambda e: e.tensor_scalar(out=rs_t[:, 0:nb], in0=pst[:, 0:nb], scalar1=1.0 / D, scalar2=EPS, op0=ALU.mult, op1=ALU.add),
                 reads=[psb], writes=[rs_b])
            S.op("act", lambda e: e.activation(out=rs_t[:, 0:nb], in_=rs_t[:, 0:nb], func=AF.Sqrt), reads=[rs_b], writes=[rs_b])
            S.op("dve", lambda e: e.reciprocal(out=rs_t[:, 0:nb], in_=rs_t[:, 0:nb]), reads=[rs_b], writes=[rs_b])
            for kc in range(KC):
                t_t, t_b = tmpb[kc % 2]
                S.op("dve", lambda e: e.tensor_tensor(out=t_t[:, 0:nb], in0=x_t[:, kc, 0:nb], in1=rs_t[:, 0:nb], op=ALU.mult),
                     reads=[x_b, rs_b], writes=[t_b])
                S.op("act", lambda e: e.activation(out=h_t[:, kc, 0:nb], in_=t_t[:, 0:nb], func=AF.Identity,
                                                   bias=mod_t[:, 24 + kc, j:j + 1], scale=gs_t[:, kc, j:j + 1]),
                     reads=[t_b, mod_b, gs_b], writes=[h_b])
            S.dma("act", ohd[:, :, s0:s0 + nb], h_t[:, :, 0:nb], reads=[h_b])

        gated(0)
        for si in range(len(segs)):
            if si + 1 < len(segs):
                gated(si + 1)
            tail(si)
    if standalone:
        S.finish(None)
    return cx


def host_B(inp, l, resA, xT_all, ctxT_all, full_ctx):
    w_in = np.asarray(inp["w_in"][l])
    wg = np.ascontiguousarray(w_in[:, OFF_G:].reshape(D, 4, D))
    wo = np.ascontiguousarray(np.concatenate([inp["wo_f"][l], inp["wo_a"][l], inp["wo_c"][l], inp["wo_p"][l]], axis=0))
    wout = np.ascontiguousarray(inp["w_out"][l])
    lamv = np.stack([inp["lam_q1"][l], inp["lam_k1"][l], inp["lam_q2"][l], inp["lam_k2"][l]], 0).astype(np.float32)
    subg = np.ascontiguousarray(np.asarray(inp["subln_g"][l], np.float32).reshape(128, 1))
    n2g = fm_vec(inp["norm2_g"][l])
    ident = np.eye(128, dtype=np.float32)
    maps = []
    for i in range(NCORES):
        b, j = i // 4, i % 4
        tabA, tabC, tabX = fft_tables(j)
        if resA is None:
            m = dict(wg=wg, wo=wo, wout=wout, lamv=lamv, subg=subg, n2g=n2g, tabA=tabA, tabC=tabC, ident=ident)
            if full_ctx:
                m["tabX"] = tabX
            maps.append(m)
            continue
        grp = [resA[b * 4 + jj] for jj in range(4)]
        r = resA[i]
        m = dict(QT=r["QT"], KTc=r["KTc"], Vc=r["Vc"],
                 ycT=r["ycT"], ypT=r["ypT"], hxT=r["hxT"], xT=np.ascontiguousarray(xT_all[b][:, j * T:(j + 1) * T]), mod=r["mod"],
                 wg=wg, wo=wo, wout=wout, lamv=lamv, subg=subg, n2g=n2g, tabA=tabA, tabC=tabC, ident=ident)
        for h in range(4):
            m["KTg%d" % h] = np.ascontiguousarray(np.concatenate([g["KT%d" % h] for g in grp], axis=0))
            m["Vg%d" % h] = np.ascontiguousarray(np.concatenate([g["V%d" % h] for g in grp], axis=0))
            m["Zg%d" % h] = np.ascontiguousarray(np.concatenate([g["Z%d" % h] for g in grp], axis=0))
        if full_ctx:
            m.update(QTc=r["QTc"], Zc=r["Zc"], ycTc=r["ycTc"], ypTc=r["ypTc"], hxTc=r["hxTc"],
                     cxT=np.ascontiguousarray(ctxT_all[b]), tabX=tabX)
        maps.append(m)
    return maps


NFC = DFF // 128


def build_C(full_ctx, final, cx=None):
    standalone = cx is None
    if standalone:
        cx = Ctx("C")
    nc, S = cx.nc, cx.S
    hx2 = cx.inp("hx2T", [128, KC, T], BF16); hhalo = cx.inp("hhalo", [128, KC, 2], BF16)
    xm = cx.inp("xmT", [D, T]); mod = cx.inp("mod", [128, 48, 2])
    wup = cx.inp("wup", [D, 2 * DFF]); wdn = cx.inp("wdn", [DFF, D])
    fw = cx.inp("fw", [128, NFC, 4])
    o_x = cx.out("xoT", [D, T])
    if full_ctx:
        chx2 = cx.inp("chx2T", [128, KC, NCTX], BF16); cxm = cx.inp("cxmT", [D, NCTX])
        o_cx = cx.out("cxoT", [D, NCTX])
    if final:
        fg = cx.inp("fg", [128, KC])
    wup_t, _unused = cx.sb("wup_t", [128, KC, 2 * DFF], BF16)
    wup_bs = {}
    HP = 11 * 128
    for piece in range(2):
        for half in (1, 0):
            bb = Buf("wup_%d_%d" % (half, piece))
            c0_ = half * DFF + piece * HP
            for kc in range(KC):
                S.dma("pool", wup_t[:, kc, c0_:c0_ + HP], wup[kc * 128:(kc + 1) * 128, c0_:c0_ + HP], writes=[bb], also=True)
            wup_bs[(half, piece)] = bb
    wdn_t, wdn_b = cx.sb("wdn_t", [128, NFC, D], BF16)
    for g in range(2):
        S.dma("pool", wdn_t[:, g * 11:(g + 1) * 11, :], wdn[g * 11 * 128:(g + 1) * 11 * 128, :].rearrange("(c p) d -> p c d", p=128),
              writes=[wdn_b], also=True)
    mod_t, mod_b = load_const(cx, "mod_t", mod[:, :, :], [128, 48, 2])
    fw_t, fw_b = load_const(cx, "fw_t", fw[:, :, :], [128, NFC, 4])
    hal_t, hal_b = load_const(cx, "hal_t", hhalo[:, :, :], [128, KC, 2], BF16)
    zero_t, zero_b = cx.sb("zero_t", [128, KC, 2], BF16)
    S.op("dve", lambda e: e.memset(zero_t[:], 0.0), writes=[zero_b])
    if final:
        fg_t, fg_b = load_const(cx, "fg_t", fg[:, :], [128, KC])
        ones_t, ones_b = cx.sb("ones_t", [128, 128], F32)
        S.op("dve", lambda e: e.memset(ones_t[:], 1.0), writes=[ones_b])
    hxb = [cx.sb("hxb%d" % i, [128, KC, 512], BF16) for i in range(2)]
    x_t, x_b = cx.sb("x_t", [128, KC, 512], F32)
    u_t, u_b = cx.sb("u_t", [128, NFC, 512], BF16)
    gw = [cx.sb("gw%d" % i, [128, 512], F32) for i in range(2)]
    cv = [cx.sb("cv%d" % i, [128, 512], F32) for i in range(2)]
    ge = [cx.sb("ge%d" % i, [128, 512], F32) for i in range(2)]
    psr = PsumRing(cx, 6, "psr")
    pss = PsumRing(cx, 2, "pss")
    if final:
        sqs = [cx.sb("sqs%d" % i, [128, 512], F32) for i in range(2)]
        rs_t, rs_b = cx.sb("rs_t", [128, 512], F32)
    segs = [(hx2, xm, o_x, T, 0, hal_t, hal_b)]
    if full_ctx:
        segs.append((chx2, cxm, o_cx, NCTX, 1, zero_t, zero_b))
    bi = 0
    for (hsrc, xsrc, xdst, n, j, m_t, m_b) in segs:
        blocks = []
        b0 = 0
        while b0 < n:
            nb = min(510, n - b0)
            blocks.append((b0, nb))
            b0 += nb
        for (b0, nb) in blocks:
            h_t, h_b = hxb[bi % 2]
            bi += 1
            lo = max(b0 - 1, 0)
            hi = min(b0 + nb + 1, n)
            S.dma("sp", h_t[:, :, lo - (b0 - 1):hi - (b0 - 1)], hsrc[:, :, lo:hi], writes=[h_b])
            if b0 == 0:
                S.op("pool", lambda e: e.tensor_copy(out=h_t[:, :, 0:1], in_=m_t[:, :, 0:1]), reads=[m_b], writes=[h_b])
            if b0 + nb == n:
                S.op("pool", lambda e: e.tensor_copy(out=h_t[:, :, nb + 1:nb + 2], in_=m_t[:, :, 1:2]), reads=[m_b], writes=[h_b])
            S.dma("sp", x_t[:, :, 0:nb], xsrc[:, b0:b0 + nb].rearrange("(c p) n -> p c n", p=128), writes=[x_b])
            for c in range(NFC):
                pg, pgb = psr.next()
                for kc in range(KC):
                    S.op("pe", lambda e, kc=kc: e.matmul(pg[:, 0:nb + 2], wup_t[:, kc, DFF + c * 128:DFF + (c + 1) * 128], h_t[:, kc, 0:nb + 2],
                                                         start=(kc == 0), stop=(kc == KC - 1)),
                         reads=[wup_bs[(1, c // 11)], h_b], writes=[pgb], inc=(kc == KC - 1))
                g_t, g_b = gw[c % 2]
                S.op("act", lambda e: e.activation(out=g_t[:, 0:nb + 2], in_=pg[:, 0:nb + 2], func=AF.Copy), reads=[pgb], writes=[g_b])
                c_t, c_b = cv[c % 2]
                S.op("dve", lambda e: e.tensor_scalar(out=c_t[:, 0:nb], in0=g_t[:, 0:nb], scalar1=fw_t[:, c, 0:1], scalar2=fw_t[:, c, 3:4],
                                                      op0=ALU.mult, op1=ALU.add), reads=[g_b, fw_b], writes=[c_b])
                for k in (1, 2):
                    S.op("dve", lambda e, k=k: e.scalar_tensor_tensor(out=c_t[:, 0:nb], in0=g_t[:, k:k + nb], scalar=fw_t[:, c, k:k + 1], in1=c_t[:, 0:nb],
                                                                      op0=ALU.mult, op1=ALU.add), reads=[g_b, fw_b, c_b], writes=[c_b])
                e_t, e_b = ge[c % 2]
                S.op("act", lambda e: e.activation(out=e_t[:, 0:nb], in_=c_t[:, 0:nb], func=AF.Gelu), reads=[c_b], writes=[e_b])
                pv, pvb = psr.next()
                for kc in range(KC):
                    S.op("pe", lambda e, kc=kc: e.matmul(pv[:, 0:nb], wup_t[:, kc, c * 128:(c + 1) * 128], h_t[:, kc, 1:nb + 1],
                                                         start=(kc == 0), stop=(kc == KC - 1)),
                         reads=[wup_bs[(0, c // 11)], h_b], writes=[pvb], inc=(kc == KC - 1))
                S.op("dve", lambda e: e.tensor_tensor(out=u_t[:, c, 0:nb], in0=pv[:, 0:nb], in1=e_t[:, 0:nb], op=ALU.mult),
                     reads=[pvb, e_b], writes=[u_b])
            if final:
                pst, psb = pss.next()
            for m2 in range(KC):
                po, pob = psr.next()
                for c in range(NFC):
                    S.op("pe", lambda e, c=c: e.matmul(po[:, 0:nb], wdn_t[:, c, m2 * 128:(m2 + 1) * 128], u_t[:, c, 0:nb],
                                                       start=(c == 0), stop=(c == NFC - 1)),
                         reads=[wdn_b, u_b], writes=[pob], inc=(c == NFC - 1))
                S.op("dve", lambda e: e.scalar_tensor_tensor(out=x_t[:, m2, 0:nb], in0=po[:, 0:nb], scalar=mod_t[:, 40 + m2, j:j + 1],
                                                             in1=x_t[:, m2, 0:nb], op0=ALU.mult, op1=ALU.add),
                     reads=[pob, mod_b, x_b], writes=[x_b])
                if final:
                    q_t2, q_b2 = sqs[m2 % 2]
                    S.op("act", lambda e: e.activation(out=q_t2[:, 0:nb], in_=x_t[:, m2, 0:nb], func=AF.Square), reads=[x_b], writes=[q_b2])
                    S.op("pe", lambda e: e.matmul(pst[:, 0:nb], ones_t[:, :], q_t2[:, 0:nb], start=(m2 == 0), stop=(m2 == KC - 1)),
                         reads=[ones_b, q_b2], writes=[psb], inc=True)
            if final:
                S.op("dve", lambda e: e.tensor_scalar(out=rs_t[:, 0:nb], in0=pst[:, 0:nb], scalar1=1.0 / D, scalar2=EPS, op0=ALU.mult, op1=ALU.add),
                     reads=[psb], writes=[rs_b])
                S.op("act", lambda e: e.activation(out=rs_t[:, 0:nb], in_=rs_t[:, 0:nb], func=AF.Sqrt), reads=[rs_b], writes=[rs_b])
                S.op("dve", lambda e: e.reciprocal(out=rs_t[:, 0:nb], in_=rs_t[:, 0:nb]), reads=[rs_b], writes=[rs_b])
                for kc in range(KC):
                    S.op("dve", lambda e, kc=kc: e.scalar_tensor_tensor(out=x_t[:, kc, 0:nb], in0=x_t[:, kc, 0:nb], scalar=fg_t[:, kc:kc + 1],
                                                                        in1=rs_t[:, 0:nb], op0=ALU.mult, op1=ALU.mult),
                         reads=[x_b, fg_b, rs_b], writes=[x_b])
            S.dma("act", xdst[:, b0:b0 + nb].rearrange("(c p) n -> p c n", p=128), x_t[:, :, 0:nb], reads=[x_b])
    if standalone:
        S.finish(None)
    return cx


def host_C(inp, l, resA, resB, full_ctx, final):
    wup = np.ascontiguousarray(inp["w_up"][l]); wdn = np.ascontiguousarray(inp["w_down"][l])
    fwv = np.concatenate([np.asarray(inp["ffn_dw_w"][l]), np.asarray(inp["ffn_dw_b"][l])[None, :]], axis=0)
    fw = np.ascontiguousarray(fwv.T.reshape(NFC, 128, 4).transpose(1, 0, 2)).astype(np.float32)
    maps = []
    for i in range(NCORES):
        b, j = i // 4, i % 4
        if resA is None:
            m = dict(wup=wup, wdn=wdn, fw=fw)
            if final:
                m["fg"] = fm_vec(inp["final_g"])
            maps.append(m)
            continue
        hh = np.zeros((128, KC, 2), NPBF)
        if j > 0:
            hh[:, :, 0] = resB[i - 1]["hx2T"][:, :, T - 1]
        if j < 3:
            hh[:, :, 1] = resB[i + 1]["hx2T"][:, :, 0]
        m = dict(hx2T=resB[i]["hx2T"], hhalo=hh, xmT=resB[i]["xmT"], mod=resA[i]["mod"], wup=wup, wdn=wdn, fw=fw)
        if full_ctx:
            m.update(chx2T=resB[i]["chx2T"], cxmT=resB[i]["cxmT"])
        if final:
            m["fg"] = fm_vec(inp["final_g"])
        maps.append(m)
    return maps


def _np(results):
    return [{k: np.asarray(v) for k, v in r.items()} for r in results]


def kernel_unfused(**inputs):
    inp = {k: np.asarray(v) for k, v in inputs.items()}
    x = inp["x"].astype(np.float32, copy=False)
    B = x.shape[0]
    xT_all = [np.ascontiguousarray(x[b].T) for b in range(B)]
    ctxT_all = [np.ascontiguousarray(inp["ctx"][b].T.astype(np.float32)) for b in range(B)]
    depth = inp["w_in"].shape[0]
    for l in range(depth):
        last = l == depth - 1
        full_ctx = not last
        lam_init = 0.8 - 0.6 * math.exp(-0.3 * l)
        cxA = get_prog(("A", full_ctx), build_A, full_ctx)
        resA = _np(run_prog(cxA, host_A(inp, l, xT_all, ctxT_all)))
        cxB = get_prog(("B", full_ctx, l), build_B, full_ctx, lam_init)
        resB = _np(run_prog(cxB, host_B(inp, l, resA, xT_all, ctxT_all, full_ctx)))
        cxC = get_prog(("C", full_ctx, last), build_C, full_ctx, last)
        resC = _np(run_prog(cxC, host_C(inp, l, resA, resB, full_ctx, last)))
        xT_all = [np.concatenate([resC[b * 4 + j]["xoT"] for j in range(4)], axis=1) for b in range(B)]
        if full_ctx:
            ctxT_all = [resC[b * 4]["cxoT"] for b in range(B)]
    out = np.stack([xT_all[b].T for b in range(B)], axis=0)
    return np.ascontiguousarray(out.astype(np.float32))


def _select_halo(cx, gathered, ncol, dt, pick_l, pick_r, wl, out_aps, sel_t, sel_b, name):
    S = cx.S
    g_t, g_b = cx.sb(name + "_g", [128, 4, ncol], dt)
    S.dma("sp", g_t[:], gathered[:, :].rearrange("(r p) n -> p r n", p=128), writes=[g_b])
    res = []
    for side, pick in ((0, pick_l), (1, pick_r)):
        a_t, a_b = cx.sb(name + "_a%d" % side, [128, KC, wl], F32)
        gv = g_t[:, :, :].rearrange("p r (c n) -> p r c n", c=KC)
        S.op("dve", lambda e: e.tensor_scalar(out=a_t[:], in0=gv[:, 0, :, pick], scalar1=sel_t[:, side * 4:side * 4 + 1], scalar2=None, op0=ALU.mult),
             reads=[g_b, sel_b], writes=[a_b])
        for r in range(1, 4):
            S.op("dve", lambda e, r=r: e.scalar_tensor_tensor(out=a_t[:], in0=gv[:, r, :, pick], scalar=sel_t[:, side * 4 + r:side * 4 + r + 1],
                                                              in1=a_t[:], op0=ALU.mult, op1=ALU.add),
                 reads=[g_b, sel_b, a_b], writes=[a_b])
        res.append((a_t, a_b))
    return res


def build_M(cx, depth):
    nc, S = cx.nc, cx.S
    cx.begin_phase("M_")
    cT = cx.inp("cT", [128, KC, 2])
    cT_t, cT_b = load_const(cx, "cT_t", cT[:, :, :], [128, KC, 2])
    psr = PsumRing(cx, 4, "psm")
    mq = []
    for l in range(depth):
        awq = cx.inp("L%d_ada_wq" % l, [D, 1536]); abq = cx.inp("L%d_ada_bq" % l, [128, 12])
        abq_t, abq_b = load_const(cx, "abq%d" % l, abq[:, :], [128, 12])
        mq_t, mq_b = cx.sb("mq%d" % l, [128, 12, 2], F32)
        cx.prefix = "M%d_" % l
        _adaln_mod(cx, cT_t, cT_b, awq, abq_t, abq_b, mq_t, mq_b, psr, 3, 0)
        cx.prefix = "M_"
        mq_d = nc.dram_tensor("M_mqd%d" % l, [128, 24], F32).ap()
        S.dma("sp", mq_d[:, :], mq_t[:, :, :].rearrange("p c j -> p (c j)"), reads=[mq_b])
        mq.append(mq_d)
    S.barrier()
    mod_ap = []
    mg_l = []
    for l in range(depth):
        mg = nc.dram_tensor("M_mg%d" % l, [512, 24], F32).ap()
        S.allgather(mq[l], mg)
        mg_l.append(mg)
    S.barrier()
    for l in range(depth):
        md = nc.dram_tensor("M_mod%d" % l, [128, 48, 2], F32).ap()
        mt, mb = cx.sb("mgt%d" % l, [128, 4, 24], F32)
        S.dma("sp", mt[:], mg_l[l][:, :].rearrange("(r p) n -> p r n", p=128), writes=[mb])
        S.dma("sp", md[:, :, :].rearrange("p (r c) j -> p r (c j)", r=4), mt[:], reads=[mb])
        mod_ap.append(md)
    cx.end_phase()
    return mod_ap


def build_fused(depth=2):
    cx = Ctx("F", fused=True)
    nc, S = cx.nc, cx.S
    sel = cx.inp("sel", [128, 8])
    mod_ap = build_M(cx, depth)
    prevC = None
    xTh_ap = None
    for l in range(depth):
        last = l == depth - 1
        full_ctx = not last
        lam_init = 0.8 - 0.6 * math.exp(-0.3 * l)
        links = {}
        if l > 0:
            links = {"xT": prevC["xoT"], "cxT": prevC["cxoT"], "xTh": xTh_ap}
        links["mod_in"] = mod_ap[l]
        cx.begin_phase("L%dA_" % l, links)
        gath = {}

        for nm, shp in (("Z", [SEQ, 128]), ("KT", [512, T]), ("V", [SEQ, 128])):
            for h in range(4):
                gath["%sg%d" % (nm, h)] = nc.dram_tensor("L%dE1_%sg%d" % (l, nm, h), shp, BF16).ap()

        def gather_kvz(l=l, gath=gath):
            for h in range(4):
                S.allgather(cx.produced["Z%d" % h], gath["Zg%d" % h])

        cx.after_blocks = gather_kvz
        build_A(full_ctx, cx)
        cx.after_blocks = None
        pA = cx.end_phase()
        pA["mod"] = mod_ap[l]
        xT_ap = links["xT"] if l > 0 else cx.ins["L0A_xT"]
        cxT_ap = links["cxT"] if l > 0 else cx.ins["L0A_cxT"]
        links = {k: pA[k] for k in ("QT", "KTc", "Vc", "ycT", "ypT", "hxT", "mod")}
        links.update(gath)
        links["xT"] = xT_ap
        if full_ctx:
            links.update({k: pA[k] for k in ("QTc", "Zc", "ycTc", "ypTc", "hxTc")})
            links["cxT"] = cxT_ap
        cx.begin_phase("L%dB_" % l, links)

        def gather_kv(pA=pA, gath=gath):
            for h in range(4):
                for nm in ("KT", "V"):
                    st_ = S.allgather(pA["%s%d" % (nm, h)], gath["%sg%d" % (nm, h)])
                    gb = Buf("g_%s%d" % (nm, h))
                    gb.w = [st_]
                    cx.dram_bufs["%sg%d" % (nm, h)] = gb

        cx.after_fft_tables = gather_kv
        cx.dram_bufs = {}
        build_B(full_ctx, lam_init, cx=cx)
        cx.after_fft_tables = None
        pB = cx.end_phase()
        cx.begin_phase("L%dE2_" % l)
        sel_t, sel_b = load_const(cx, "sel_t", sel[:, :], [128, 8])
        e_t, e_b = cx.sb("e_t", [128, KC, 2], BF16)
        S.dma("sp", e_t[:, :, 0:1], pB["hx2T"][:, :, 0:1], writes=[e_b], slow=True)
        S.dma("sp", e_t[:, :, 1:2], pB["hx2T"][:, :, T - 1:T], writes=[e_b], also=True, slow=True)
        ein = nc.dram_tensor("L%dE2_in" % l, [128, KC * 2], BF16).ap()
        eout = nc.dram_tensor("L%dE2_out" % l, [512, KC * 2], BF16).ap()
        hhalo = nc.dram_tensor("L%dE2_hhalo" % l, [128, KC, 2], BF16).ap()
        S.dma("sp", ein[:, :], e_t[:, :, :].rearrange("p c n -> p (c n)"), reads=[e_b])
        S.barrier()
        S.allgather(ein, eout)
        S.barrier()
        (l_t, l_b), (r_t, r_b) = _select_halo(cx, eout, KC * 2, BF16, slice(1, 2), slice(0, 1), 1, None, sel_t, sel_b, "h2")
        hh_t, hh_b = cx.sb("hh_t", [128, KC, 2], BF16)
        S.op("dve", lambda e: e.tensor_copy(out=hh_t[:, :, 0:1], in_=l_t[:]), reads=[l_b], writes=[hh_b])
        S.op("dve", lambda e: e.tensor_copy(out=hh_t[:, :, 1:2], in_=r_t[:]), reads=[r_b, hh_b], writes=[hh_b])
        S.dma("sp", hhalo[:, :, :], hh_t[:], reads=[hh_b])
        cx.end_phase()
        links = {"hx2T": pB["hx2T"], "hhalo": hhalo, "xmT": pB["xmT"], "mod": pA["mod"]}
        if full_ctx:
            links.update({"chx2T": pB["chx2T"], "cxmT": pB["cxmT"]})
        cx.begin_phase("L%dC_" % l, links, ext_out=({"xoT": "outT"} if last else None))
        build_C(full_ctx, last, cx=cx)
        pC = cx.end_phase()
        prevC = pC
        if not last:
            cx.begin_phase("L%dE3_" % l)
            sel_t, sel_b = load_const(cx, "sel_t", sel[:, :], [128, 8])
            e_t, e_b = cx.sb("e_t", [128, KC, 2 * HALO], F32)
            S.dma("sp", e_t[:, :, 0:HALO], pC["xoT"][:, 0:HALO].rearrange("(c p) n -> p c n", p=128), writes=[e_b])
            S.dma("sp", e_t[:, :, HALO:2 * HALO], pC["xoT"][:, T - HALO:T].rearrange("(c p) n -> p c n", p=128), writes=[e_b], also=True)
            ein = nc.dram_tensor("L%dE3_in" % l, [128, KC * 2 * HALO], F32).ap()
            eout = nc.dram_tensor("L%dE3_out" % l, [512, KC * 2 * HALO], F32).ap()
            xTh_ap = nc.dram_tensor("L%dE3_xTh" % l, [D, 2 * HALO], F32).ap()
            S.dma("sp", ein[:, :], e_t[:, :, :].rearrange("p c n -> p (c n)"), reads=[e_b])
            S.barrier()
            S.allgather(ein, eout)
            S.barrier()
            (l_t, l_b), (r_t, r_b) = _select_halo(cx, eout, KC * 2 * HALO, F32, slice(HALO, 2 * HALO), slice(0, HALO), HALO, None, sel_t, sel_b, "xh")
            S.dma("sp", xTh_ap[:, 0:HALO].rearrange("(c p) n -> p c n", p=128), l_t[:], reads=[l_b])
            S.dma("sp", xTh_ap[:, HALO:2 * HALO].rearrange("(c p) n -> p c n", p=128), r_t[:], reads=[r_b])
            cx.end_phase()
    S.finish(None)
    return cx


def kernel(**inputs):
    inp = {k: np.asarray(v) for k, v in inputs.items()}
    x = inp["x"].astype(np.float32, copy=False)
    B = x.shape[0]
    depth = inp["w_in"].shape[0]
    cx = get_prog(("F", depth), build_fused, depth)
    xT_all = [np.ascontiguousarray(x[b].T) for b in range(B)]
    ctxT_all = [np.ascontiguousarray(inp["ctx"][b].T.astype(np.float32)) for b in range(B)]
    maps = [dict() for _ in range(NCORES)]
    for l in range(depth):
        last = l == depth - 1
        full_ctx = not last
        mA = host_A(inp, l, xT_all if l == 0 else None, ctxT_all if l == 0 else None)
        mB = host_B(inp, l, None, None, None, full_ctx)
        mC = host_C(inp, l, None, None, full_ctx, last)
        for i in range(NCORES):
            for pre, m in (("L%dA_" % l, mA[i]), ("L%dB_" % l, mB[i]), ("L%dC_" % l, mC[i])):
                for k, v in m.items():
                    if pre + k in cx.ins:
                        maps[i][pre + k] = v
    for i in range(NCORES):
        b, j = i // 4, i % 4
        maps[i]["M_cT"] = np.ascontiguousarray(np.stack([fm_vec(inp["c"][b]), fm_vec(inp["c_ctx"])], axis=2))
        for l in range(depth):
            maps[i]["M_L%d_ada_wq" % l] = np.ascontiguousarray(inp["ada_w"][l][:, j * 1536:(j + 1) * 1536])
            maps[i]["M_L%d_ada_bq" % l] = fm_vec(inp["ada_b"][l][j * 1536:(j + 1) * 1536])
        sel = np.zeros((128, 8), np.float32)
        if j > 0:
            sel[:, j - 1] = 1.0
        if j < 3:
            sel[:, 4 + j + 1] = 1.0
        maps[i]["sel"] = sel
        missing = set(cx.ins) - set(maps[i])
        assert not missing, missing
    res = run_prog(cx, maps)
    outT = [np.asarray(r["outT"]) for r in res]
    out = np.stack([np.concatenate([outT[b * 4 + j] for j in range(4)], axis=1).T for b in range(B)], axis=0)
    return np.ascontiguousarray(out.astype(np.float32))
```

```python
import math
from contextlib import ExitStack
import numpy as np
import ml_dtypes
import concourse.bass as bass
import concourse.mybir as mybir
from concourse.bass_utils import run_bass_kernel_spmd

F32 = mybir.dt.float32
BF16 = mybir.dt.bfloat16
AF = mybir.ActivationFunctionType
ALU = mybir.AluOpType
NPBF = ml_dtypes.bfloat16

D = 1024
KC = 8
T = 2048
NCTX = 256
SEQ = 8192
HALO = 16
DFF = 2816
EPS = 1e-6
NCORES = 8
SAME_ENGINE_SYNC = True


class Buf:
    def __init__(self, name=""):
        self.name = name
        self.w = []
        self.r = []


class Sched:
    def __init__(self, nc, es, ndma=64):
        self.nc = nc
        self.E = {"pe": nc.tensor, "act": nc.scalar, "dve": nc.vector, "pool": nc.gpsimd, "sp": nc.sync}
        self.sem = {e: es.enter_context(nc.semaphore("sem_" + e)) for e in self.E}
        self.cnt = {e: 0 for e in self.E}
        self.seen = {e: {} for e in self.E}
        self.pend = {e: [] for e in self.E}
        self.dsems = [es.enter_context(nc.semaphore("dsem%d" % i)) for i in range(ndma)]
        self.dcnt = [0] * ndma
        self.dnext = 0
        self.nwaits = 0
        self.cc_sem = es.enter_context(nc.semaphore("cc_sem"))
        self.cc_cnt = 0

    def _wait(self, e, st):
        key, sem, val = st
        if key == e and (e == "pe" or not SAME_ENGINE_SYNC):
            return
        if self.seen[e].get(key, 0) >= val:
            return
        self.E[e].wait_ge(sem, val)
        self.nwaits += 1
        self.seen[e][key] = val

    def _deps(self, e, reads, writes):
        for oe, pl in self.pend.items():
            if oe == e:
                continue
            for (R, W) in pl:
                for b in list(reads) + list(writes):
                    if b in W or (b in R and b in writes):
                        raise RuntimeError("dependency on un-stamped access of %s by %s (buf %s)" % (oe, e, b.name))
        for b in reads:
            for st in b.w:
                self._wait(e, st)
        for b in writes:
            for st in b.w:
                self._wait(e, st)
            for st in b.r:
                self._wait(e, st)

    def op(self, e, fn, reads=(), writes=(), inc=True):
        self._deps(e, reads, writes)
        ins = fn(self.E[e])
        self.pend[e].append((tuple(reads), tuple(writes)))
        if inc:
            self.cnt[e] += 1
            ins.then_inc(self.sem[e], 1)
            st = (e, self.sem[e], self.cnt[e])
            for (R, W) in self.pend[e]:
                for b in R:
                    b.r.append(st)
                for b in W:
                    b.w = [st]
                    b.r = []
            self.pend[e] = []
        return ins

    def dma(self, q, out, in_, reads=(), writes=(), also=False, slow=False):
        self._deps(q, reads, writes)
        i = self.dnext
        self.dnext = (self.dnext + 1) % len(self.dsems)
        key = "d%d" % i
        if self.dcnt[i] > 0:
            self._wait(q, (key, self.dsems[i], self.dcnt[i]))
        ins = self.E[q].dma_start(out=out, in_=in_, allow_slow_non_contiguous=True) if slow else self.E[q].dma_start(out=out, in_=in_)
        self.dcnt[i] += 16
        ins.then_inc(self.dsems[i], 16)
        st = (key, self.dsems[i], self.dcnt[i])
        for b in reads:
            b.r.append(st)
        for b in writes:
            if also:
                b.w = b.w + [st]
            else:
                b.w = [st]
                b.r = []
        return ins

    def barrier(self):
        for e, pl in self.pend.items():
            if pl:
                raise RuntimeError("barrier with un-stamped accesses on " + e)
        for e in self.E:
            for oe in self.E:
                if oe != e and self.cnt[oe] > 0:
                    self._wait(e, (oe, self.sem[oe], self.cnt[oe]))
            for i in range(len(self.dsems)):
                if self.dcnt[i] > 0:
                    self._wait(e, ("d%d" % i, self.dsems[i], self.dcnt[i]))
            if self.cc_cnt > 0:
                self._wait(e, ("cc", self.cc_sem, self.cc_cnt))

    def allgather(self, in_ap, out_ap):
        ins = self.nc.gpsimd.collective_compute("AllGather", ALU.bypass, replica_groups=[[0, 1, 2, 3], [4, 5, 6, 7]],
                                                ins=[in_ap.opt()], outs=[out_ap.opt()])
        self.cc_cnt += 1
        ins.then_inc(self.cc_sem)
        st = ("cc", self.cc_sem, self.cc_cnt)
        self._wait("pool", st)
        return st

    def finish(self, bufs):
        for i in range(len(self.dsems)):
            if self.dcnt[i] > 0:
                self._wait("sp", ("d%d" % i, self.dsems[i], self.dcnt[i]))


class Ctx:
    def __init__(self, name, fused=False):
        self.nc = bass.Bass("TRN2", target_bir_lowering=False)
        self.es_root = ExitStack()
        self.es = ExitStack()
        self.S = Sched(self.nc, self.es_root)
        self.ins = {}
        self.outs = {}
        self.fused = fused
        self.prefix = ""
        self.links = {}
        self.produced = {}
        self.ext_out = {}
        self.dram_bufs = {}
        self.after_blocks = None
        self.after_fft_tables = None

    def begin_phase(self, prefix, links=None, ext_out=None):
        self.prefix = prefix
        self.links = dict(links or {})
        self.produced = {}
        self.ext_out = dict(ext_out or {})
        self.es = ExitStack()

    def end_phase(self):
        self.S.barrier()
        self.es.close()
        self.es = ExitStack()
        return self.produced

    def inp(self, name, shape, dt=F32):
        if name in self.links:
            return self.links[name]
        t = self.nc.dram_tensor(self.prefix + name, list(shape), dt, kind="ExternalInput").ap()
        self.ins[self.prefix + name] = t
        return t

    def out(self, name, shape, dt=F32):
        if self.fused and name not in self.ext_out:
            t = self.nc.dram_tensor(self.prefix + name, list(shape), dt).ap()
            self.produced[name] = t
            return t
        oname = self.ext_out.get(name, self.prefix + name)
        t = self.nc.dram_tensor(oname, list(shape), dt, kind="ExternalOutput").ap()
        self.outs[oname] = t
        self.produced[name] = t
        return t

    def sb(self, name, shape, dt=F32, stack=None):
        t = (stack or self.es).enter_context(self.nc.sbuf_tensor(self.prefix + name, list(shape), dt))
        return t, Buf(name)

    def ps(self, name, shape, dt=F32, stack=None):
        t = (stack or self.es).enter_context(self.nc.psum_tensor(self.prefix + name, list(shape), dt))
        return t, Buf(name)


class PsumRing:
    def __init__(self, cx, n, name="ps", stack=None):
        self.tiles = [cx.ps("%s%d" % (name, i), [128, 512], F32, stack=stack) for i in range(n)]
        self.i = 0

    def next(self):
        t = self.tiles[self.i]
        self.i = (self.i + 1) % len(self.tiles)
        return t


def load_const(cx, name, dram_ap, shape, dt=F32, q="sp"):
    t, b = cx.sb(name, shape, dt)
    cx.S.dma(q, t[:], dram_ap, writes=[b])
    return t, b


def rope_tables(tok0, n):
    nf = 16
    inv = (10000.0 ** (-np.arange(nf, dtype=np.float32) / nf)).astype(np.float32)
    t = np.arange(tok0, tok0 + n)
    r = (t // 64).astype(np.float32)
    col = (t % 64).astype(np.float32)
    ar = r[:, None] * inv
    ac = col[:, None] * inv
    cr, sr, cc, sc = np.cos(ar), np.sin(ar), np.cos(ac), np.sin(ac)
    C = np.concatenate([cr, cr, cc, cc], axis=1).T
    Ssg = np.concatenate([-sr, sr, -sc, sc], axis=1).T
    return (np.ascontiguousarray(np.concatenate([C, C], 0), dtype=np.float32),
            np.ascontiguousarray(np.concatenate([Ssg, Ssg], 0), dtype=np.float32))


SWAP64 = np.concatenate([np.arange(16, 32), np.arange(0, 16), np.arange(48, 64), np.arange(32, 48)])


def pool_invcnt(tok0, n, L):
    out = np.zeros((256, n), np.float32)
    t = np.arange(tok0, tok0 + n)
    for g, win in enumerate((2, 4, 8, 16)):
        lo = np.clip(t - win // 2, 0, L - 1)
        hi = np.clip(t + win - win // 2 - 1, 0, L - 1)
        out[g * 64:(g + 1) * 64, :] = (1.0 / (hi - lo + 1).astype(np.float32))[None, :]
    return out.reshape(2, 128, n).transpose(1, 0, 2).copy()


def chan_dft_table():
    a = 2 * np.pi * np.outer(np.arange(64), np.arange(64)) / 64.0
    C, Sn = np.cos(a), np.sin(a)
    Cb = np.zeros((128, 128)); Sb = np.zeros((128, 128))
    for g in range(2):
        Cb[g * 64:(g + 1) * 64, g * 64:(g + 1) * 64] = C
        Sb[g * 64:(g + 1) * 64, g * 64:(g + 1) * 64] = Sn
    return np.concatenate([Cb, Sb], axis=1).astype(np.float32)


NWA = 24


def _adaln_mod(cx, cT_t, cT_b, ada_w, adab_t, adab_b, mod_t, mod_b, psr, ngroups, chunk0):
    S = cx.S
    sil_t, sil_b = cx.sb("sil_t", [128, KC, 2], F32)
    S.op("act", lambda e: e.activation(out=sil_t[:], in_=cT_t[:], func=AF.Silu), reads=[cT_b], writes=[sil_b])
    with ExitStack() as st:
        aw = [cx.sb("aw%d" % i, [128, KC, 512], F32, stack=st) for i in range(2)]
        for g in range(ngroups):
            awt, awb = aw[g % 2]
            for kc in range(KC):
                S.dma("sp", awt[:, kc, :], ada_w[kc * 128:(kc + 1) * 128, g * 512:(g + 1) * 512], writes=[awb], also=(kc > 0))
            pt, pb = psr.next()
            for mm in range(4):
                for kc in range(KC):
                    S.op("pe", lambda e, mm=mm, kc=kc: e.matmul(pt[:, mm * 2:mm * 2 + 2], awt[:, kc, mm * 128:(mm + 1) * 128], sil_t[:, kc, :],
                                                                 start=(kc == 0), stop=(kc == KC - 1)),
                         reads=[awb, sil_b], writes=[pb], inc=(mm == 3 and kc == KC - 1))
            c_ = chunk0 + g * 4
            S.op("dve", lambda e, g=g, c_=c_: e.tensor_tensor(out=mod_t[:, c_:c_ + 4, :],
                                                              in0=pt[:, 0:8].rearrange("p (a b) -> p a b", b=2),
                                                              in1=adab_t[:, c_:c_ + 4].unsqueeze(2).to_broadcast([128, 4, 2]), op=ALU.add),
                 reads=[pb, adab_b], writes=[mod_b])
        S.barrier()


def build_A(full_ctx, cx=None):
    standalone = cx is None
    if standalone:
        cx = Ctx("A")
    nc, S = cx.nc, cx.S
    xT = cx.inp("xT", [D, T]); xTh = cx.inp("xTh", [D, 2 * HALO]); cxT = cx.inp("cxT", [D, NCTX])
    premod = "mod_in" in cx.links
    if not premod:
        cT = cx.inp("cT", [128, KC, 2]); ada_w = cx.inp("ada_w", [D, 6 * D]); ada_b = cx.inp("ada_b", [128, 48])
    n1g = cx.inp("n1g", [128, KC])
    wA = cx.inp("wA", [D, NWA * 128]); wV = cx.inp("wV", [D, 512])
    ropeC = cx.inp("ropeC", [128, T]); ropeS = cx.inp("ropeS", [128, T])
    hmask = cx.inp("hmask", [128, 2 * HALO])
    cw = cx.inp("cw", [128, 2, 31]); cvec = cx.inp("cvec", [128, 2, 4])
    pinv = cx.inp("pinv", [128, 2, T]); pinvc = cx.inp("pinvc", [128, 2, NCTX])
    pw = cx.inp("pw", [128, 2, 128]); cdft = cx.inp("cdft", [128, 256]); identA = cx.inp("identA", [128, 128])

    if not premod:
        o_mod = cx.out("mod", [128, 48, 2])
    o_QT = cx.out("QT", [128, 4, T], BF16)
    o_KTh = [cx.out("KT%d" % h, [128, T], BF16) for h in range(4)]
    o_Vh = [cx.out("V%d" % h, [T, 128], BF16) for h in range(4)]
    o_Zq = [cx.out("Z%d" % q, [T, 128], BF16) for q in range(4)]
    o_yc = cx.out("ycT", [128, 2, T], BF16); o_yp = cx.out("ypT", [128, 2, T], BF16)
    o_hx = cx.out("hxT", [128, KC, T], BF16)
    o_KTc = cx.out("KTc", [128, 4, NCTX], BF16); o_Vc = cx.out("Vc", [NCTX, 512], BF16)
    if full_ctx:
        o_QTc = cx.out("QTc", [128, 4, NCTX], BF16); o_Zc = cx.out("Zc", [NCTX, 512], BF16)
        o_ycc = cx.out("ycTc", [128, 2, NCTX], BF16); o_ypc = cx.out("ypTc", [128, 2, NCTX], BF16)
        o_hxc = cx.out("hxTc", [128, KC, NCTX], BF16)

    wA_t, wA_b = cx.sb("wA_t", [128, KC, NWA * 128], BF16)
    for kc in range(KC):
        S.dma("pool", wA_t[:, kc, :], wA[kc * 128:(kc + 1) * 128, :], writes=[wA_b], also=True)
    wV_t, wV_b = cx.sb("wV_t", [128, KC, 512], BF16)
    for kc in range(KC):
        S.dma("pool", wV_t[:, kc, :], wV[kc * 128:(kc + 1) * 128, :], writes=[wV_b], also=True)
    if not premod:
        cT_t, cT_b = load_const(cx, "cT_t", cT[:, :, :], [128, KC, 2])
        adab_t, adab_b = load_const(cx, "adab_t", ada_b[:, :], [128, 48])
    n1g_t, n1g_b = load_const(cx, "n1g_t", n1g[:, :], [128, KC])
    hmask_t, hmask_b = load_const(cx, "hmask_t", hmask[:, :], [128, 2 * HALO])
    cw_t, cw_b = load_const(cx, "cw_t", cw[:, :, :], [128, 2, 31])
    cvec_t, cvec_b = load_const(cx, "cvec_t", cvec[:, :, :], [128, 2, 4])
    pw_t, pw_b = cx.sb("pw_t", [128, 2, 128], BF16)
    S.dma("pool", pw_t[:], pw[:, :, :], writes=[pw_b])
    cdft_t, cdft_b = cx.sb("cdft_t", [128, 256], BF16)
    S.dma("pool", cdft_t[:], cdft[:, :], writes=[cdft_b])
    ones_t, ones_b = cx.sb("ones_t", [128, 128], F32)
    S.op("dve", lambda e: e.memset(ones_t[:], 1.0), writes=[ones_b])
    idA_t, idA_b = load_const(cx, "idA_t", identA[:, :], [128, 128])
    dg_t, dg_b = cx.sb("dg_t", [128, 2, 31, 128], BF16)
    for c in range(2):
        for k in range(31):
            S.op("dve", lambda e, c=c, k=k: e.tensor_scalar(out=dg_t[:, c, k, :], in0=idA_t[:, :], scalar1=cw_t[:, c, k:k + 1], scalar2=None, op0=ALU.mult),
                 reads=[idA_b, cw_b], writes=[dg_b])

    psr = PsumRing(cx, 6)
    pss = PsumRing(cx, 2, "pss")

    mod_t, mod_b = cx.sb("mod_t", [128, 48, 2], F32)
    if "mod_in" in cx.links:
        S.dma("sp", mod_t[:], cx.links["mod_in"][:, :, :], writes=[mod_b])
    else:
        _adaln_mod(cx, cT_t, cT_b, ada_w, adab_t, adab_b, mod_t, mod_b, psr, 12, 0)
        S.barrier()
        S.dma("sp", o_mod[:, :, :], mod_t[:], reads=[mod_b])
    gs_t, gs_b = cx.sb("gs_t", [128, KC, 2], F32)
    S.op("dve", lambda e: e.tensor_scalar(out=gs_t[:], in0=mod_t[:, 8:16, :], scalar1=1.0, scalar2=None, op0=ALU.add),
         reads=[mod_b], writes=[gs_b])
    S.op("dve", lambda e: e.tensor_tensor(out=gs_t[:], in0=gs_t[:], in1=n1g_t[:].unsqueeze(2).to_broadcast([128, KC, 2]), op=ALU.mult),
         reads=[gs_b, n1g_b], writes=[gs_b])

    WZ = T + 2 * HALO
    zb_t, zb_b = cx.sb("zb_t", [128, 2, WZ], BF16)
    ub_t, ub_b = cx.sb("ub_t", [128, 2, WZ], F32)
    WZC = NCTX + 2 * HALO
    zc_t, zc_b = cx.sb("zc_t", [128, 2, WZC], BF16)
    uc_t, uc_b = cx.sb("uc_t", [128, 2, WZC], F32)
    if full_ctx:
        S.op("pool", lambda e: e.memset(zc_t[:], 0.0), writes=[zc_b])
        S.op("pool", lambda e: e.memset(uc_t[:], 0.0), writes=[uc_b])

    with ExitStack() as st:
        xb = [cx.sb("xb%d" % i, [128, KC, 512], F32, stack=st) for i in range(2)]
        sqs = [cx.sb("sq%d" % i, [128, 512], F32, stack=st) for i in range(2)]
        rC = [cx.sb("rC%d" % i, [128, 512], F32, stack=st) for i in range(2)]
        rS = [cx.sb("rS%d" % i, [128, 512], F32, stack=st) for i in range(2)]
        rs_t, rs_b = cx.sb("rs_t", [128, 512], F32, stack=st)
        tmp = [cx.sb("tmp%d" % i, [128, 512], F32, stack=st) for i in range(2)]
        hx = [cx.sb("hx%d" % i, [128, KC, 512], BF16, stack=st) for i in range(2)]
        t1 = [cx.sb("t1_%d" % i, [128, 512], F32, stack=st) for i in range(2)]
        t2 = [cx.sb("t2_%d" % i, [128, 512], F32, stack=st) for i in range(2)]
        qo = [cx.sb("qo%d" % i, [128, 512], BF16, stack=st) for i in range(2)]
        sg = [cx.sb("sg%d" % i, [128, 512], F32, stack=st) for i in range(2)]
        uf = [cx.sb("uf%d" % i, [128, 2, 512], BF16, stack=st) for i in range(1)]
        vo = [cx.sb("vo%d" % i, [128, 512], BF16, stack=st) for i in range(2)]
        zo = [cx.sb("zo%d" % i, [128, 512], BF16, stack=st) for i in range(2)]

        blocks = [("main", i * 512, 512) for i in range(4)] + [("halo", 0, 2 * HALO), ("ctx", 0, NCTX)]
        srcs = {"main": xT, "halo": xTh, "ctx": cxT}
        rsall_t, _u = cx.sb("rsall_t", [128, T + 2 * HALO + NCTX], F32, stack=st)
        rs_off = []
        rs_bufs = []
        off_ = 0
        for bi, (kind, c0, n) in enumerate(blocks):
            xt, xbb = xb[bi % 2]
            for kc in range(KC):
                S.dma("sp", xt[:, kc, 0:n], srcs[kind][kc * 128:(kc + 1) * 128, c0:c0 + n], writes=[xbb], also=(kc > 0))
            pst, psb = pss.next()
            for kc in range(KC):
                sq_t, sq_b = sqs[kc % 2]
                S.op("act", lambda e, kc=kc: e.activation(out=sq_t[:, 0:n], in_=xt[:, kc, 0:n], func=AF.Square), reads=[xbb], writes=[sq_b])
                S.op("pe", lambda e, kc=kc: e.matmul(pst[:, 0:n], ones_t[:, :], sq_t[:, 0:n], start=(kc == 0), stop=(kc == KC - 1)),
                     reads=[ones_b, sq_b], writes=[psb], inc=True)
            rb = Buf("rs%d" % bi)
            rsl = rsall_t[:, off_:off_ + n]
            S.op("dve", lambda e: e.tensor_scalar(out=rsl, in0=pst[:, 0:n], scalar1=1.0 / D, scalar2=EPS, op0=ALU.mult, op1=ALU.add),
                 reads=[psb], writes=[rb])
            S.op("act", lambda e: e.activation(out=rsl, in_=rsl, func=AF.Sqrt), reads=[rb], writes=[rb])
            S.op("dve", lambda e: e.reciprocal(out=rsl, in_=rsl), reads=[rb], writes=[rb])
            rs_off.append(off_)
            rs_bufs.append(rb)
            off_ += n
        def loads(bi):
            kind, c0, n = blocks[bi]
            xt, xbb = xb[bi % 2]
            for kc in range(KC):
                S.dma("sp", xt[:, kc, 0:n], srcs[kind][kc * 128:(kc + 1) * 128, c0:c0 + n], writes=[xbb], also=(kc > 0))
            if kind == "main":
                S.dma("sp", rC[bi % 2][0][:, :], ropeC[:, c0:c0 + n], writes=[rC[bi % 2][1]])
                S.dma("sp", rS[bi % 2][0][:, :], ropeS[:, c0:c0 + n], writes=[rS[bi % 2][1]])

        loads(0)
        for bi, (kind, c0, n) in enumerate(blocks):
            if bi + 1 < len(blocks):
                loads(bi + 1)
            xt, xbb = xb[bi % 2]
            j = 1 if kind == "ctx" else 0
            if kind == "main":
                ropeC_t, ropeC_b = rC[bi % 2]
                ropeS_t, ropeS_b = rS[bi % 2]
            rs_t = rsall_t[:, rs_off[bi]:rs_off[bi] + 512] if n == 512 else rsall_t[:, rs_off[bi]:rs_off[bi] + n]
            rs_b = rs_bufs[bi]
            hxt, hxb = hx[bi % 2]
            for kc in range(KC):
                tt, tb = tmp[kc % 2]
                S.op("dve", lambda e, kc=kc, tt=tt: e.tensor_tensor(out=tt[:, 0:n], in0=xt[:, kc, 0:n], in1=rs_t[:, 0:n], op=ALU.mult),
                     reads=[xbb, rs_b], writes=[tb])
                S.op("act", lambda e, kc=kc, tt=tt: e.activation(out=hxt[:, kc, 0:n], in_=tt[:, 0:n], func=AF.Identity,
                                                                 bias=mod_t[:, kc, j:j + 1], scale=gs_t[:, kc, j:j + 1]),
                     reads=[tb, mod_b, gs_b], writes=[hxb])
            if kind == "main":
                S.dma("act", o_hx[:, :, c0:c0 + n], hxt[:, :, 0:n], reads=[hxb])
            elif kind == "ctx" and full_ctx:
                S.dma("act", o_hxc[:, :, :], hxt[:, :, 0:n], reads=[hxb])

            def fm(ci):
                pt, pb = psr.next()
                for kc in range(KC):
                    S.op("pe", lambda e, kc=kc: e.matmul(pt[:, 0:n], wA_t[:, kc, ci * 128:(ci + 1) * 128], hxt[:, kc, 0:n],
                                                         start=(kc == 0), stop=(kc == KC - 1)),
                         reads=[wA_b, hxb], writes=[pb], inc=(kc == KC - 1))
                return pt, pb

            if kind != "halo":
                for qk in range(2):
                    if qk == 0 and kind == "ctx" and not full_ctx:
                        continue
                    for h in range(4):
                        if kind == "main":
                            dsl = o_QT[:, h, c0:c0 + n] if qk == 0 else o_KTh[h][:, c0:c0 + n]
                        else:
                            dsl = (o_QTc if qk == 0 else o_KTc)[:, h, 0:n]
                        pt, pb = fm(qk * 8 + h)
                        qt, qb = qo[h % 2]
                        if kind == "main":
                            p2, p2b = fm(qk * 8 + 4 + h)
                            a1, a1b = t1[h % 2]
                            a2, a2b = t2[h % 2]
                            S.op("dve", lambda e: e.tensor_tensor(out=a1[:, 0:n], in0=pt[:, 0:n], in1=ropeC_t[:, 0:n], op=ALU.mult),
                                 reads=[pb, ropeC_b], writes=[a1b])
                            S.op("dve", lambda e: e.tensor_tensor(out=a2[:, 0:n], in0=p2[:, 0:n], in1=ropeS_t[:, 0:n], op=ALU.mult),
                                 reads=[p2b, ropeS_b], writes=[a2b])
                            S.op("pool", lambda e: e.tensor_tensor(out=qt[:, 0:n], in0=a1[:, 0:n], in1=a2[:, 0:n], op=ALU.add),
                                 reads=[a1b, a2b], writes=[qb])
                            S.dma("pool", dsl, qt[:, 0:n], reads=[qb])
                        else:
                            S.op("act", lambda e: e.activation(out=qt[:, 0:n], in_=pt[:, 0:n], func=AF.Copy), reads=[pb], writes=[qb])
                            S.dma("act", dsl, qt[:, 0:n], reads=[qb])
            do_cp = (kind != "ctx") or full_ctx
            if do_cp:
                if kind == "main":
                    zt, zbb, ut, ubb, col = zb_t, zb_b, ub_t, ub_b, HALO + c0
                elif kind == "ctx":
                    zt, zbb, ut, ubb, col = zc_t, zc_b, uc_t, uc_b, HALO
                for c in range(2):
                    pa, pab = fm(16 + c)
                    pg, pgb = fm(18 + c)
                    s_t, s_b = sg[c % 2]
                    S.op("act", lambda e: e.activation(out=s_t[:, 0:n], in_=pg[:, 0:n], func=AF.Sigmoid), reads=[pgb], writes=[s_b])
                    pu, pub = fm(20 + c)
                    if kind == "halo":
                        for (lo, dcol) in ((0, 0), (HALO, HALO + T)):
                            S.op("dve", lambda e, lo=lo, dcol=dcol: e.tensor_tensor(out=zb_t[:, c, dcol:dcol + HALO], in0=pa[:, lo:lo + HALO],
                                                                                     in1=s_t[:, lo:lo + HALO], op=ALU.mult),
                                 reads=[pab, s_b], writes=[zb_b])
                            S.op("dve", lambda e, lo=lo, dcol=dcol: e.tensor_tensor(out=zb_t[:, c, dcol:dcol + HALO], in0=zb_t[:, c, dcol:dcol + HALO],
                                                                                     in1=hmask_t[:, lo:lo + HALO], op=ALU.mult),
                                 reads=[zb_b, hmask_b], writes=[zb_b])
                            S.op("dve", lambda e, lo=lo, dcol=dcol: e.tensor_tensor(out=ub_t[:, c, dcol:dcol + HALO], in0=pu[:, lo:lo + HALO],
                                                                                     in1=hmask_t[:, lo:lo + HALO], op=ALU.mult),
                                 reads=[pub, hmask_b], writes=[ub_b])
                    else:
                        S.op("dve", lambda e: e.tensor_tensor(out=zt[:, c, col:col + n], in0=pa[:, 0:n], in1=s_t[:, 0:n], op=ALU.mult),
                             reads=[pab, s_b], writes=[zbb])
                        S.op("act", lambda e: e.activation(out=ut[:, c, col:col + n], in_=pu[:, 0:n], func=AF.Copy), reads=[pub], writes=[ubb])
            if kind == "main" or (kind == "ctx" and full_ctx):
                uft, ufb = uf[0]
                for c in range(2):
                    pt, pb = fm(22 + c)
                    S.op("act", lambda e, c=c: e.activation(out=uft[:, c, 0:n], in_=pt[:, 0:n], func=AF.Copy), reads=[pb], writes=[ufb])
                for tt_ in range(n // 128):
                    pt, pb = psr.next()
                    for c in range(2):
                        S.op("pe", lambda e, c=c: e.matmul(pt[:, c * 256:(c + 1) * 256], uft[:, c, tt_ * 128:(tt_ + 1) * 128], cdft_t[:, :],
                                                           start=True, stop=True),
                             reads=[ufb, cdft_b], writes=[pb], inc=(c == 1))
                    z_t, z_b = zo[tt_ % 2]
                    S.op("act", lambda e: e.activation(out=z_t[:, :].rearrange("p (r c k) -> p r c k", r=2, c=2),
                                                       in_=pt[:, :].rearrange("p (c r k) -> p r c k", c=2, r=2), func=AF.Copy),
                         reads=[pb], writes=[z_b])
                    if kind == "main":
                        for q4 in range(4):
                            S.dma("act", o_Zq[q4][c0 + tt_ * 128:c0 + (tt_ + 1) * 128, :], z_t[:, q4 * 128:(q4 + 1) * 128], reads=[z_b])
                    else:
                        S.dma("act", o_Zc[c0 + tt_ * 128:c0 + (tt_ + 1) * 128, :], z_t[:, :], reads=[z_b])
            if kind != "halo":
                for tt_ in range(n // 128):
                    pt, pb = psr.next()
                    for kc in range(KC):
                        S.op("pe", lambda e, kc=kc: e.matmul(pt[:, :], hxt[:, kc, tt_ * 128:(tt_ + 1) * 128], wV_t[:, kc, :],
                                                             start=(kc == 0), stop=(kc == KC - 1)),
                             reads=[hxb, wV_b], writes=[pb], inc=(kc == KC - 1))
                    v_t, v_b = vo[tt_ % 2]
                    S.op("dve", lambda e: e.tensor_copy(out=v_t[:, :], in_=pt[:, :]), reads=[pb], writes=[v_b])
                    if kind == "main":
                        for h in range(4):
                            S.dma("sp", o_Vh[h][c0 + tt_ * 128:c0 + (tt_ + 1) * 128, :], v_t[:, h * 128:(h + 1) * 128], reads=[v_b])
                    else:
                        S.dma("sp", o_Vc[c0 + tt_ * 128:c0 + (tt_ + 1) * 128, :], v_t[:, :], reads=[v_b])

    S.barrier()
    if getattr(cx, "after_blocks", None) is not None:
        cx.after_blocks()
    segs = [(zb_t, zb_b, ub_t, ub_b, T, pinv, o_yc, o_yp)]
    if full_ctx:
        segs.append((zc_t, zc_b, uc_t, uc_b, NCTX, pinvc, o_ycc, o_ypc))
    with ExitStack() as st:
        acc_t, acc_b = cx.sb("acc_t", [128, 2, T], F32, stack=st)
        pin_t, pin_b = cx.sb("pin_t", [128, 2, T], F32, stack=st)
        w2_t, w2_b = cx.sb("w2_t", [128, 2, T + 2 * HALO], F32, stack=st)
        w4_t, w4_b = cx.sb("w4_t", [128, 2, T + 2 * HALO], F32, stack=st)
        w8_t, w8_b = cx.sb("w8_t", [128, T + 2 * HALO], F32, stack=st)
        s1 = [cx.sb("s1_%d" % i, [128, 512], F32, stack=st) for i in range(2)]
        s2 = [cx.sb("s2_%d" % i, [128, 512], F32, stack=st) for i in range(2)]
        s3 = [cx.sb("s3_%d" % i, [128, 512], F32, stack=st) for i in range(2)]
        yo = [cx.sb("yo%d" % i, [128, 2, 512], BF16, stack=st) for i in range(2)]
        dd = [cx.sb("dd%d" % i, [128, 2, 512], BF16, stack=st) for i in range(2)]
        po = [cx.sb("po%d" % i, [128, 2, 512], BF16, stack=st) for i in range(2)]
        for (zt, zbb, ut, ubb, n, pinv_d, oyc, oyp) in segs:
            off = HALO - 15
            for blk in range((n + 511) // 512):
                b0 = blk * 512
                nb = min(512, n - b0)
                for c in range(2):
                    pt, pb = psr.next()
                    for k in range(31):
                        S.op("pe", lambda e, c=c, k=k: e.matmul(pt[:, 0:nb], dg_t[:, c, k, :], zt[:, c, off + k + b0:off + k + b0 + nb],
                                                                start=(k == 0), stop=(k == 30)),
                             reads=[dg_b, zbb], writes=[pb], inc=(k == 30))
                    S.op("act", lambda e, c=c: e.activation(out=acc_t[:, c, b0:b0 + nb], in_=pt[:, 0:nb], func=AF.Identity, bias=cvec_t[:, c, 0:1]),
                         reads=[pb, cvec_b], writes=[acc_b])
            for blk in range((n + 511) // 512):
                b0 = blk * 512
                nb = min(512, n - b0)
                sq_t2, sq_b2 = s1[blk % 2]
                psum_, psumb = pss.next()
                pssq, pssqb = pss.next()
                for c in range(2):
                    S.op("pe", lambda e, c=c: e.matmul(psum_[:, 0:nb], ones_t[:, :], acc_t[:, c, b0:b0 + nb], start=(c == 0), stop=(c == 1)),
                         reads=[ones_b, acc_b], writes=[psumb], inc=(c == 1))
                for c in range(2):
                    S.op("act", lambda e, c=c: e.activation(out=sq_t2[:, 0:nb], in_=acc_t[:, c, b0:b0 + nb], func=AF.Square),
                         reads=[acc_b], writes=[sq_b2])
                    S.op("pe", lambda e, c=c: e.matmul(pssq[:, 0:nb], ones_t[:, :], sq_t2[:, 0:nb], start=(c == 0), stop=(c == 1)),
                         reads=[ones_b, sq_b2], writes=[pssqb], inc=True)
                mean_t, mean_b = s2[blk % 2]
                var_t, var_b = s3[blk % 2]
                S.op("dve", lambda e: e.tensor_scalar(out=mean_t[:, 0:nb], in0=psum_[:, 0:nb], scalar1=1.0 / 256, scalar2=None, op0=ALU.mult),
                     reads=[psumb], writes=[mean_b])
                S.op("dve", lambda e: e.tensor_tensor(out=var_t[:, 0:nb], in0=mean_t[:, 0:nb], in1=mean_t[:, 0:nb], op=ALU.mult),
                     reads=[mean_b], writes=[var_b])
                S.op("dve", lambda e: e.scalar_tensor_tensor(out=var_t[:, 0:nb], in0=pssq[:, 0:nb], scalar=1.0 / 256, in1=var_t[:, 0:nb],
                                                             op0=ALU.mult, op1=ALU.subtract),
                     reads=[pssqb, var_b], writes=[var_b])
                S.op("dve", lambda e: e.tensor_scalar(out=var_t[:, 0:nb], in0=var_t[:, 0:nb], scalar1=EPS, scalar2=None, op0=ALU.add),
                     reads=[var_b], writes=[var_b])
                S.op("act", lambda e: e.activation(out=var_t[:, 0:nb], in_=var_t[:, 0:nb], func=AF.Sqrt), reads=[var_b], writes=[var_b])
                S.op("dve", lambda e: e.reciprocal(out=var_t[:, 0:nb], in_=var_t[:, 0:nb]), reads=[var_b], writes=[var_b])
                y_t, y_b = yo[blk % 2]
                for c in range(2):
                    S.op("dve", lambda e, c=c: e.tensor_tensor(out=sq_t2[:, 0:nb], in0=acc_t[:, c, b0:b0 + nb], in1=mean_t[:, 0:nb], op=ALU.subtract),
                         reads=[acc_b, mean_b], writes=[sq_b2])
                    S.op("dve", lambda e, c=c: e.tensor_tensor(out=sq_t2[:, 0:nb], in0=sq_t2[:, 0:nb], in1=var_t[:, 0:nb], op=ALU.mult),
                         reads=[sq_b2, var_b], writes=[sq_b2])
                    S.op("act", lambda e, c=c: e.activation(out=y_t[:, c, 0:nb], in_=sq_t2[:, 0:nb], func=AF.Silu,
                                                            bias=cvec_t[:, c, 2:3], scale=cvec_t[:, c, 1:2]),
                         reads=[sq_b2, cvec_b], writes=[y_b])
                S.dma("act", oyc[:, :, b0:b0 + nb], y_t[:, :, 0:nb], reads=[y_b])
            S.dma("sp", pin_t[:, :, 0:n], pinv_d[:, :, :], writes=[pin_b])
            W = n + 2 * HALO
            S.op("dve", lambda e: e.tensor_tensor(out=w2_t[:, :, 1:W], in0=ut[:, :, 0:W - 1], in1=ut[:, :, 1:W], op=ALU.add),
                 reads=[ubb], writes=[w2_b])
            S.op("dve", lambda e: e.tensor_tensor(out=w4_t[:, :, 2:W - 1], in0=w2_t[:, :, 1:W - 2], in1=w2_t[:, :, 3:W], op=ALU.add),
                 reads=[w2_b], writes=[w4_b])
            S.op("dve", lambda e: e.tensor_tensor(out=w8_t[:, 4:W - 3], in0=w4_t[:, 1, 2:W - 5], in1=w4_t[:, 1, 6:W - 1], op=ALU.add),
                 reads=[w4_b], writes=[w8_b])
            H = HALO
            S.op("dve", lambda e: e.tensor_copy(out=acc_t[0:64, 0, 0:n], in_=w2_t[0:64, 0, H:H + n]), reads=[w2_b, acc_b], writes=[acc_b])
            S.op("dve", lambda e: e.tensor_copy(out=acc_t[64:128, 0, 0:n], in_=w4_t[64:128, 0, H:H + n]), reads=[w4_b, acc_b], writes=[acc_b])
            S.op("dve", lambda e: e.tensor_copy(out=acc_t[0:64, 1, 0:n], in_=w8_t[0:64, H:H + n]), reads=[w8_b, acc_b], writes=[acc_b])
            S.op("dve", lambda e: e.tensor_tensor(out=acc_t[64:128, 1, 0:n], in0=w8_t[64:128, H - 4:H - 4 + n], in1=w8_t[64:128, H + 4:H + 4 + n], op=ALU.add),
                 reads=[w8_b, acc_b], writes=[acc_b])
            S.op("dve", lambda e: e.tensor_tensor(out=acc_t[:, :, 0:n], in0=acc_t[:, :, 0:n], in1=pin_t[:, :, 0:n], op=ALU.mult),
                 reads=[acc_b, pin_b], writes=[acc_b])
            for blk in range((n + 511) // 512):
                b0 = blk * 512
                nb = min(512, n - b0)
                d_t, d_b = dd[blk % 2]
                S.op("dve", lambda e: e.tensor_tensor(out=d_t[:, :, 0:nb], in0=acc_t[:, :, b0:b0 + nb], in1=ut[:, :, H + b0:H + b0 + nb], op=ALU.subtract),
                     reads=[acc_b, ubb], writes=[d_b])
                p_t, p_b = po[blk % 2]
                for c in range(2):
                    pt, pb = psr.next()
                    S.op("pe", lambda e, c=c: e.matmul(pt[:, 0:nb], pw_t[:, c, :], d_t[:, c, 0:nb], start=True, stop=True),
                         reads=[pw_b, d_b], writes=[pb])
                    S.op("act", lambda e, c=c: e.activation(out=p_t[:, c, 0:nb], in_=pt[:, 0:nb], func=AF.Identity, scale=cvec_t[:, c, 3:4]),
                         reads=[pb, cvec_b], writes=[p_b])
                S.dma("act", oyp[:, :, b0:b0 + nb], p_t[:, :, 0:nb], reads=[p_b])
    if standalone:
        S.finish(None)
    return cx


def fm_vec(v):
    v = np.asarray(v, np.float32)
    return np.ascontiguousarray(v.reshape(-1, 128).T)


OFF_F, OFF_Q, OFF_K, OFF_V, OFF_C, OFF_P, OFF_G = 0, 256, 768, 1280, 1792, 2304, 2560


def wA_layout(w_in):
    cols = []
    swap128 = np.concatenate([SWAP64, 64 + SWAP64])
    for base in (OFF_Q, OFF_K):
        for h in range(4):
            cols.append(base + h * 128 + np.arange(128))
        for h in range(4):
            cols.append(base + h * 128 + swap128)
    cols.append(OFF_C + np.arange(512))
    cols.append(OFF_P + np.arange(256))
    cols.append(OFF_F + np.arange(256))
    cols = np.concatenate(cols)
    return np.ascontiguousarray(w_in[:, cols])


def host_A(inp, l, xT_all, ctxT_all):
    w_in = np.asarray(inp["w_in"][l])
    wA = wA_layout(w_in)
    wV = np.ascontiguousarray(w_in[:, OFF_V:OFF_C])
    ada_w = np.ascontiguousarray(inp["ada_w"][l])
    ada_b = fm_vec(inp["ada_b"][l])
    n1g = fm_vec(inp["norm1_g"][l])
    cw = np.ascontiguousarray(np.asarray(inp["conv_dw_w"][l]).T.reshape(2, 128, 31).transpose(1, 0, 2))
    cvec = np.stack([fm_vec(inp["conv_dw_b"][l]), fm_vec(inp["conv_ln_g"][l]), fm_vec(inp["conv_ln_b"][l]),
                     fm_vec(inp["pool_scale"][l])], axis=2).astype(np.float32)
    pw_in = np.asarray(inp["pool_w"][l])
    pw = np.zeros((128, 2, 128), np.float32)
    for c in range(2):
        for g in range(2):
            pw[g * 64:(g + 1) * 64, c, g * 64:(g + 1) * 64] = pw_in[2 * c + g]
    cdft = chan_dft_table()
    pinvc = pool_invcnt(0, NCTX, NCTX)
    maps = []
    for i in range(NCORES):
        b, j = i // 4, i % 4
        t0 = j * T
        hm = np.zeros((128, 2 * HALO), np.float32)
        if j > 0:
            hm[:, 0:HALO] = 1.0
        if j < 3:
            hm[:, HALO:] = 1.0
        cT = np.stack([fm_vec(inp["c"][b]), fm_vec(inp["c_ctx"])], axis=2)
        rC, rS = rope_tables(t0, T)
        m = dict(cT=np.ascontiguousarray(cT), ada_w=ada_w, ada_b=ada_b, identA=np.eye(128, dtype=np.float32),
                 n1g=n1g, wA=wA, wV=wV, ropeC=rC, ropeS=rS, hmask=hm, cw=cw, cvec=cvec,
                 pinv=pool_invcnt(t0, T, SEQ), pinvc=pinvc, pw=pw, cdft=cdft)
        if xT_all is not None:
            xTh = np.zeros((D, 2 * HALO), np.float32)
            if j > 0:
                xTh[:, 0:HALO] = xT_all[b][:, t0 - HALO:t0]
            if j < 3:
                xTh[:, HALO:] = xT_all[b][:, t0 + T:t0 + T + HALO]
            m.update(xT=np.ascontiguousarray(xT_all[b][:, t0:t0 + T]), xTh=xTh, cxT=np.ascontiguousarray(ctxT_all[b]))
        maps.append(m)
    return maps


_PROGS = {}


def get_prog(key, builder, *args):
    if key not in _PROGS:
        _PROGS[key] = builder(*args)
    return _PROGS[key]


def run_prog(cx, maps):
    res = run_bass_kernel_spmd(cx.nc, maps, core_ids=list(range(NCORES)))
    return res.results


NKEY = NCTX + SEQ
NKC = NKEY // 128
SCALE_F = 1.0 / math.sqrt(SEQ * 64.0)
SCALE_FC = 1.0 / math.sqrt(NCTX * 64.0)


def fft_tables(j):
    n1 = np.arange(128)
    ph = 2 * np.pi * np.outer(n1, n1) / 128.0
    C, Sn = np.cos(ph), np.sin(ph)
    tabA = np.stack([np.concatenate([C, Sn], 1), np.concatenate([-Sn, C], 1)], axis=1).astype(np.float32)
    n2 = np.arange(64)[:, None, None]
    k1 = np.arange(128)[None, :, None]
    k2 = (16 * j + np.arange(16))[None, None, :]
    th = 2 * np.pi * n2 * (k1 + 128 * k2) / float(SEQ)
    M = np.stack([np.cos(th) * SCALE_F, -np.sin(th) * SCALE_F], axis=2)
    tabC = np.concatenate([M, M], axis=0).astype(np.float32)
    l = np.arange(NCTX)
    a = 2 * np.pi * np.outer(l, l) / float(NCTX)
    Cc = (np.cos(a) * SCALE_FC).reshape(2, 128, NCTX).transpose(1, 0, 2)
    Sc = (-np.sin(a) * SCALE_FC).reshape(2, 128, NCTX).transpose(1, 0, 2)
    tabX = np.stack([Cc, Sc], axis=1).astype(np.float32)
    return tabA, tabC, np.ascontiguousarray(tabX)


def build_B(full_ctx, lam_init, dbg=False, only=None, cx=None):
    standalone = cx is None
    if standalone:
        cx = Ctx("B")
    nc, S = cx.nc, cx.S
    QT = cx.inp("QT", [128, 4, T], BF16)
    KTgh = [cx.inp("KTg%d" % h, [512, T], BF16) for h in range(4)]
    Vgh = [cx.inp("Vg%d" % h, [SEQ, 128], BF16) for h in range(4)]
    Zgq = [cx.inp("Zg%d" % q, [SEQ, 128], BF16) for q in range(4)]
    KTc = cx.inp("KTc", [128, 4, NCTX], BF16); Vc = cx.inp("Vc", [NCTX, 512], BF16)
    ycT = cx.inp("ycT", [128, 2, T], BF16); ypT = cx.inp("ypT", [128, 2, T], BF16); hxT = cx.inp("hxT", [128, KC, T], BF16)
    xT = cx.inp("xT", [D, T]); mod = cx.inp("mod", [128, 48, 2])
    wg = cx.inp("wg", [D, 4, D]); wo = cx.inp("wo", [1280, D]); wout = cx.inp("wout", [D, D])
    lamv = cx.inp("lamv", [4, 64]); subg = cx.inp("subg", [128, 1]); n2g = cx.inp("n2g", [128, KC])
    tabA = cx.inp("tabA", [128, 2, 256]); tabC = cx.inp("tabC", [128, 128, 2, 16]); ident = cx.inp("ident", [128, 128])
    o_xm = cx.out("xmT", [D, T]); o_h2 = cx.out("hx2T", [128, KC, T], BF16)
    if full_ctx:
        QTc = cx.inp("QTc", [128, 4, NCTX], BF16); Zc = cx.inp("Zc", [NCTX, 512], BF16)
        ycTc = cx.inp("ycTc", [128, 2, NCTX], BF16); ypTc = cx.inp("ypTc", [128, 2, NCTX], BF16); hxTc = cx.inp("hxTc", [128, KC, NCTX], BF16)
        cxT = cx.inp("cxT", [D, NCTX]); tabX = cx.inp("tabX", [128, 2, 2, NCTX])
        o_cxm = cx.out("cxmT", [D, NCTX]); o_ch2 = cx.out("chx2T", [128, KC, NCTX], BF16)

    mod_t, mod_b = load_const(cx, "mod_t", mod[:, :, :], [128, 48, 2])
    n2g_t, n2g_b = load_const(cx, "n2g_t", n2g[:, :], [128, KC])
    subg_t, subg_b = load_const(cx, "subg_t", subg[:, :], [128, 1])
    ones_t, ones_b = cx.sb("ones_t", [128, 128], F32)
    S.op("dve", lambda e: e.memset(ones_t[:], 1.0), writes=[ones_b])
    id_t, id_b = cx.sb("id_t", [128, 128], BF16)
    S.dma("pool", id_t[:], ident[:, :], writes=[id_b])
    lv_t, lv_b = cx.sb("lv_t", [128, 4, 64], F32)
    for r in range(4):
        S.dma("sp", lv_t[:, r, :], lamv[r:r + 1, :].partition_broadcast(128), writes=[lv_b], also=(r > 0))
    lp_t, lp_b = cx.sb("lp_t", [128, 2, 64], F32)
    S.op("dve", lambda e: e.tensor_tensor(out=lp_t[:], in0=lv_t[:, 0:4:2, :], in1=lv_t[:, 1:4:2, :], op=ALU.mult), reads=[lv_b], writes=[lp_b])
    ls_t, ls_b = cx.sb("ls_t", [128, 2], F32)
    S.op("dve", lambda e: e.reduce_sum(out=ls_t[:], in_=lp_t[:], axis=mybir.AxisListType.X), reads=[lp_b], writes=[ls_b])
    S.op("act", lambda e: e.activation(out=ls_t[:], in_=ls_t[:], func=AF.Exp), reads=[ls_b], writes=[ls_b])
    nlam_t, nlam_b = cx.sb("nlam_t", [128, 1], F32)
    S.op("dve", lambda e: e.tensor_tensor(out=nlam_t[:], in0=ls_t[:, 1:2], in1=ls_t[:, 0:1], op=ALU.subtract), reads=[ls_b], writes=[nlam_b])
    S.op("dve", lambda e: e.tensor_scalar(out=nlam_t[:], in0=nlam_t[:], scalar1=-float(lam_init), scalar2=None, op0=ALU.add),
         reads=[nlam_b], writes=[nlam_b])
    S.op("dve", lambda e: e.tensor_scalar(out=subg_t[:], in0=subg_t[:], scalar1=float(1.0 - lam_init), scalar2=None, op0=ALU.mult),
         reads=[subg_b], writes=[subg_b])
    gs_t, gs_b = cx.sb("gs_t", [128, KC, 2], F32)
    S.op("dve", lambda e: e.tensor_scalar(out=gs_t[:], in0=mod_t[:, 32:40, :], scalar1=1.0, scalar2=None, op0=ALU.add), reads=[mod_b], writes=[gs_b])
    S.op("dve", lambda e: e.tensor_tensor(out=gs_t[:], in0=gs_t[:], in1=n2g_t[:].unsqueeze(2).to_broadcast([128, KC, 2]), op=ALU.mult),
         reads=[gs_b, n2g_b], writes=[gs_b])

    yf_t, yf_b = cx.sb("yf_t", [128, 2, T], BF16)

    oa_t, oa_b = cx.sb("oa_t", [128, 4, T], BF16)
    if full_ctx:
        yfc_t, yfc_b = cx.sb("yfc_t", [128, 2, NCTX], BF16)
        oac_t, oac_b = cx.sb("oac_t", [128, 4, NCTX], BF16)

    with ExitStack() as st:
        zs_t, zs_b = cx.sb("zs_t", [128, 64 * 512], BF16, stack=st)
        tt_t, tt_b = cx.sb("tt_t", [128, 128 * 256], BF16, stack=st)
        tA_t, tA_b = cx.sb("tA_t", [128, 2, 256], BF16, stack=st)
        tC_t, tC_b = cx.sb("tC_t", [128, 128 * 32], BF16, stack=st)
        S.dma("pool", tA_t[:], tabA[:, :, :], writes=[tA_b])
        S.dma("pool", tC_t[:], tabC[:, :, :, :].rearrange("p a b c -> p (a b c)"), writes=[tC_b])
        if cx.after_fft_tables is not None:
            cx.after_fft_tables()
        zdst = zs_t[:, :].rearrange("p (n c) -> p n c", c=512)
        for q4 in range(4):
            S.dma("sp", zdst[:, :, q4 * 128:(q4 + 1) * 128], Zgq[q4][:, :].rearrange("(a b) c -> a b c", b=64), writes=[zs_b], also=(q4 > 0))
        psA = PsumRing(cx, 3, "psA", stack=st)
        psC = [cx.ps("psC%d" % i, [128, 512], F32, stack=st) for i in range(4)]
        zv = zs_t[:, :].rearrange("p (n r c k) -> p r c n k", n=64, r=2, c=2)
        ttv = tt_t[:, :].rearrange("p (k x) -> p k x", x=256)
        for cp2 in range(64):
            pt, pb = psA.next()
            for s_ in range(2):
                cp = cp2 * 2 + s_
                for r in range(2):
                    for c2 in range(2):
                        S.op("pe", lambda e, r=r, cp=cp, s_=s_, c2=c2: e.matmul(pt[c2 * 64:(c2 + 1) * 64, s_ * 256:(s_ + 1) * 256], zv[:, r, c2, :, cp],
                                                                                 tA_t[:, r, :], start=(r == 0), stop=(r == 1), tile_position=(0, c2 * 64)),
                             reads=[zs_b, tA_b], writes=[pb], inc=(s_ == 1 and r == 1 and c2 == 1))
            eng = "act" if cp2 % 2 == 0 else "dve"
            if eng == "act":
                S.op("act", lambda e: e.activation(out=tt_t[:, cp2 * 512:(cp2 + 1) * 512], in_=pt[:, :], func=AF.Copy), reads=[pb], writes=[tt_b])
            else:
                S.op("dve", lambda e: e.tensor_copy(out=tt_t[:, cp2 * 512:(cp2 + 1) * 512], in_=pt[:, :]), reads=[pb], writes=[tt_b])
        if only == "fft":
            d_tt = cx.out("dbg_tt", [128, 128 * 256], BF16)
            S.dma("sp", d_tt[:, :], tt_t[:, :], reads=[tt_b])
        tcv = tC_t[:, :].rearrange("p (k r j) -> p k r j", r=2, j=16)
        psCb = [Buf("psCb%d" % i) for i in range(4)]
        for c2 in range(2):
            for k1 in range(128):
                bank = k1 // 32
                col = (k1 % 32) * 16
                for r in range(2):
                    S.op("pe", lambda e, r=r, k1=k1: e.matmul(psC[bank][0][:, col:col + 16], ttv[c2 * 64:(c2 + 1) * 64, :, r * 128 + k1],
                                                             tcv[c2 * 64:(c2 + 1) * 64, k1, r, :], start=(r == 0), stop=(r == 1),
                                                             tile_position=(c2 * 64, 0)),
                         reads=[tt_b, tC_b], writes=[psCb[bank]], inc=(r == 1 and k1 % 32 == 31))
            for bank in range(4):
                dst = yf_t[:, c2, :].rearrange("p (j k) -> p k j", k=128)[:, bank * 32:(bank + 1) * 32, :]
                srcp = psC[bank][0][:, :].rearrange("p (k j) -> p k j", j=16)
                if bank % 2 == 0:
                    S.op("act", lambda e: e.activation(out=dst, in_=srcp, func=AF.Copy), reads=[psCb[bank]], writes=[yf_b])
                else:
                    S.op("dve", lambda e: e.tensor_copy(out=dst, in_=srcp), reads=[psCb[bank]], writes=[yf_b])
        if full_ctx:
            zc_t, zc_b = cx.sb("zc_t", [128, 2, 512], BF16, stack=st)
            tX_t, tX_b = cx.sb("tX_t", [128, 2, 2, NCTX], BF16, stack=st)
            S.dma("sp", zc_t[:], Zc[:, :].rearrange("(a p) c -> p a c", p=128), writes=[zc_b])
            S.dma("pool", tX_t[:], tabX[:, :, :, :], writes=[tX_b])
            for c2 in range(2):
                pt, pb = psA.next()
                i = 0
                for r in range(2):
                    for tl in range(2):
                        S.op("pe", lambda e, r=r, tl=tl, i=i: e.matmul(pt[:, 0:NCTX], zc_t[:, tl, r * 256 + c2 * 128:r * 256 + (c2 + 1) * 128],
                                                                       tX_t[:, r, tl, :], start=(i == 0), stop=(i == 3)),
                             reads=[zc_b, tX_b], writes=[pb], inc=(i == 3))
                        i += 1
                S.op("act", lambda e: e.activation(out=yfc_t[:, c2, :], in_=pt[:, 0:NCTX], func=AF.Copy), reads=[pb], writes=[yfc_b])

    S.barrier()
    if only == "fft":
        d_yf = cx.out("dbg_yf", [128, 2, T], BF16)
        S.dma("sp", d_yf[:, :, :], yf_t[:], reads=[yf_b])
        if standalone:
            S.finish(None)
        return cx
    wg_t, wg_b = cx.sb("wg_t", [128, KC, 4, D], BF16)
    for kc in range(KC):
        S.dma("pool", wg_t[:, kc, :, :], wg[kc * 128:(kc + 1) * 128, :, :], writes=[wg_b], also=True)
    with ExitStack() as st:
        kt = [cx.sb("kt%d" % i, [128, NKEY], BF16, stack=st) for i in range(2)]
        vt = [cx.sb("vt%d" % i, [128, NKC, 128], BF16, stack=st) for i in range(2)]
        qt = [cx.sb("qt%d" % i, [128, T + NCTX], BF16, stack=st) for i in range(2)]
        pT = [cx.sb("pT%d" % i, [128, 2, 512], BF16, stack=st) for i in range(3)]
        psS = [cx.ps("psS%d" % i, [128, 1024], F32, stack=st) for i in range(2)]
        psO = [cx.ps("psO%d" % i, [128, 512], F32, stack=st) for i in range(2)]
        psE = [cx.ps("psE%d" % i, [128, 512], F32, stack=st) for i in range(1)]
        psD, psD_b = cx.ps("psD", [128, 512], F32, stack=st)
        dsb = [cx.sb("dsb%d" % i, [64, 512], F32, stack=st) for i in range(2)]
        on32_t, on32_b = cx.sb("on32_t", [128, 32], BF16, stack=st)
        S.op("pool", lambda e: e.memset(on32_t[:], 1.0), writes=[on32_b])
        selr = []
        for m in range(2):
            sl_t, sl_b = cx.sb("selr%d" % m, [64, 128], F32, stack=st)
            S.op("pool", lambda e: e.memset(sl_t[:], 0.0), writes=[sl_b])
            S.op("pool", lambda e, m=m: e.memset(sl_t[m * 32:m * 32 + 1, :], 1.0), reads=[sl_b], writes=[sl_b])
            selr.append((sl_t, sl_b))
        osb = [cx.sb("osb%d" % i, [128, 2, 512], F32, stack=st) for i in range(2)]
        rc = [cx.sb("rc%d" % i, [128, 512], F32, stack=st) for i in range(2)]
        oo_t, oo_b = cx.sb("at_oo_t", [128, 512], F32, stack=st)
        o1_t, o1_b = cx.sb("at_o1_t", [128, 512], F32, stack=st)
        sq_t, sq_b = cx.sb("at_sq_t", [128, 512], F32, stack=st)
        rs_t, rs_b = cx.sb("at_rs_t", [128, 512], F32, stack=st)
        gi = 0
        pending_epi = []
        for h in range(4):
            k_t, k_b = kt[h % 2]
            v_t, v_b = vt[h % 2]
            q_t, q_b = qt[h % 2]
            S.dma("sp", k_t[:, 0:NCTX], KTc[:, h, :], writes=[k_b])
            gk = [cx.dram_bufs["KTg%d" % h]] if ("KTg%d" % h) in cx.dram_bufs else []
            gv = [cx.dram_bufs["Vg%d" % h]] if ("Vg%d" % h) in cx.dram_bufs else []
            S.dma("sp", k_t[:, NCTX:NKEY].rearrange("p (r t) -> p r t", r=4), KTgh[h][:, :].rearrange("(r p) t -> p r t", p=128), reads=gk, writes=[k_b], also=True)
            S.dma("sp", v_t[:, 0:2, :], Vc[:, h * 128:(h + 1) * 128].rearrange("(c p) d -> p c d", p=128), writes=[v_b])
            vsrc = Vgh[h][:, :].rearrange("(c p) d -> p c d", p=128)
            for g in range(2):
                S.dma("sp", v_t[:, 2 + g * 32:2 + (g + 1) * 32, :], vsrc[:, g * 32:(g + 1) * 32, :], reads=gv, writes=[v_b], also=True)
            S.dma("sp", q_t[:, 0:T], QT[:, h, :], writes=[q_b])
            if full_ctx:
                S.dma("sp", q_t[:, T:T + NCTX], QTc[:, h, :], writes=[q_b], also=True)
            groups = [(g * 512, 512, NKC, oa_t, oa_b, g * 512) for g in range(4)]
            if full_ctx:
                groups.append((T, NCTX, 2, oac_t, oac_b, 0))
            for (q0, nq, nkc, dst_t, dst_b, d0) in groups:
                psOb = [psO[0][1], psO[1][1]]

                def qk(kc, q0=q0, nq=nq):
                    ps_t, ps_b = psS[kc % 2]
                    for m in range(2):
                        S.op("pe", lambda e, m=m: e.matmul(ps_t[:, m * 512:m * 512 + nq], k_t[m * 64:(m + 1) * 64, kc * 128:(kc + 1) * 128],
                                                           q_t[m * 64:(m + 1) * 64, q0:q0 + nq], start=True, stop=True, tile_position=(m * 64, 0)),
                             reads=[k_b, q_b], writes=[ps_b], inc=(m == 1))

                def den(kc, p_t, p_b, nq=nq, nkc=nkc):
                    for m in range(2):
                        S.op("pe", lambda e, m=m: e.matmul(psD[m * 32:(m + 1) * 32, 0:nq], on32_t[:, :], p_t[:, m, 0:nq], start=(kc == 0), stop=(kc == nkc - 1),
                                                           tile_position=(0, m * 32)),
                             reads=[p_b, on32_b], writes=[psD_b], inc=(m == 1))

                qk(0)
                den_prev = None
                for kc in range(nkc):
                    if kc + 1 < nkc:
                        qk(kc + 1)
                    if den_prev is not None:
                        den(*den_prev)
                    ps_t, ps_b = psS[kc % 2]
                    p_t, p_b = pT[kc % 3]
                    S.op("act", lambda e: e.activation(out=p_t[:, :, 0:nq], in_=ps_t[:, :].rearrange("p (m q) -> p m q", m=2)[:, :, 0:nq],
                                                       func=AF.Exp, scale=0.125),
                         reads=[ps_b], writes=[p_b])
                    for m in range(2):
                        S.op("pe", lambda e, m=m: e.matmul(psO[m][0][:, 0:nq], v_t[:, kc, :], p_t[:, m, 0:nq], start=(kc == 0), stop=(kc == nkc - 1)),
                             reads=[p_b, v_b], writes=[psOb[m]], inc=(m == 1))
                    den_prev = (kc, p_t, p_b)
                    if kc == nkc - 1:
                        den(*den_prev)
                    if pending_epi and (kc in (4, 10, 18) or kc == nkc - 1):
                        while pending_epi:
                            pending_epi.pop(0)()
                            if kc != nkc - 1:
                                break
                ob_t, ob_b = osb[gi % 2]
                gi += 1
                d_t, d_b = dsb[(gi - 1) % 2]
                S.op("dve", lambda e: e.tensor_copy(out=d_t[:, 0:nq], in_=psD[0:64, 0:nq]), reads=[psD_b], writes=[d_b])
                for m in range(2):
                    S.op("dve" if m == 0 else "pool", lambda e, m=m: e.tensor_copy(out=ob_t[:, m, 0:nq], in_=psO[m][0][:, 0:nq]), reads=[psOb[m]], writes=[ob_b]) if False else \
                        S.op("dve", lambda e, m=m: e.tensor_copy(out=ob_t[:, m, 0:nq], in_=psO[m][0][:, 0:nq]), reads=[psOb[m]], writes=[ob_b])

                def epi_a(nq=nq, d_t=d_t, d_b=d_b):
                    pe_t, pe_b = psE[0]
                    S.op("pe", lambda e: e.matmul(pe_t[:, 0:nq], selr[0][0][:, :], d_t[:, 0:nq], start=True, stop=True), reads=[selr[0][1], d_b], writes=[pe_b])
                    S.op("dve", lambda e: e.reciprocal(out=rc[0][0][:, 0:nq], in_=pe_t[:, 0:nq]), reads=[pe_b], writes=[rc[0][1]])

                def epi_b(nq=nq, ob_t=ob_t, ob_b=ob_b, d_t=d_t, d_b=d_b):
                    pe_t, pe_b = psE[0]
                    S.op("pe", lambda e: e.matmul(pe_t[:, 0:nq], selr[1][0][:, :], d_t[:, 0:nq], start=True, stop=True), reads=[selr[1][1], d_b], writes=[pe_b])
                    S.op("dve", lambda e: e.reciprocal(out=rc[1][0][:, 0:nq], in_=pe_t[:, 0:nq]), reads=[pe_b], writes=[rc[1][1]])
                    S.op("dve", lambda e: e.tensor_tensor(out=oo_t[:, 0:nq], in0=ob_t[:, 0, 0:nq], in1=rc[0][0][:, 0:nq], op=ALU.mult),
                         reads=[ob_b, rc[0][1]], writes=[oo_b])
                    S.op("dve", lambda e: e.tensor_tensor(out=o1_t[:, 0:nq], in0=ob_t[:, 1, 0:nq], in1=rc[1][0][:, 0:nq], op=ALU.mult),
                         reads=[ob_b, rc[1][1]], writes=[o1_b])
                    S.op("dve", lambda e: e.scalar_tensor_tensor(out=oo_t[:, 0:nq], in0=o1_t[:, 0:nq], scalar=nlam_t[:, 0:1], in1=oo_t[:, 0:nq],
                                                                 op0=ALU.mult, op1=ALU.add),
                         reads=[oo_b, o1_b, nlam_b], writes=[oo_b])
                    S.op("pool", lambda e: e.tensor_tensor(out=sq_t[:, 0:nq], in0=oo_t[:, 0:nq], in1=oo_t[:, 0:nq], op=ALU.mult), reads=[oo_b], writes=[sq_b])

                def epi_c(nq=nq, dst_t=dst_t, dst_b=dst_b, d0=d0, h=h):
                    pe_t, pe_b = psE[0]
                    S.op("pe", lambda e: e.matmul(pe_t[:, 0:nq], ones_t[:, :], sq_t[:, 0:nq], start=True, stop=True), reads=[ones_b, sq_b], writes=[pe_b])
                    S.op("dve", lambda e: e.tensor_scalar(out=rs_t[:, 0:nq], in0=pe_t[:, 0:nq], scalar1=1.0 / 128, scalar2=EPS, op0=ALU.mult, op1=ALU.add),
                         reads=[pe_b], writes=[rs_b])
                    S.op("act", lambda e: e.activation(out=rs_t[:, 0:nq], in_=rs_t[:, 0:nq], func=AF.Ln), reads=[rs_b], writes=[rs_b])
                    S.op("act", lambda e: e.activation(out=rs_t[:, 0:nq], in_=rs_t[:, 0:nq], func=AF.Exp, scale=-0.5), reads=[rs_b], writes=[rs_b])
                    S.op("dve", lambda e: e.scalar_tensor_tensor(out=dst_t[:, h, d0:d0 + nq], in0=oo_t[:, 0:nq], scalar=subg_t[:, 0:1], in1=rs_t[:, 0:nq],
                                                                 op0=ALU.mult, op1=ALU.mult),
                         reads=[oo_b, subg_b, rs_b], writes=[dst_b])

                pending_epi.extend([epi_a, epi_b, epi_c])
        while pending_epi:
            pending_epi.pop(0)()
    S.barrier()
    if dbg:
        d_yf = cx.out("dbg_yf", [128, 2, T], BF16); d_oa = cx.out("dbg_oa", [128, 4, T], BF16)
        S.dma("sp", d_yf[:, :, :], yf_t[:], reads=[yf_b])
        S.dma("sp", d_oa[:, :, :], oa_t[:], reads=[oa_b])
        if full_ctx:
            d_yfc = cx.out("dbg_yfc", [128, 2, NCTX], BF16); d_oac = cx.out("dbg_oac", [128, 4, NCTX], BF16)
            S.dma("sp", d_yfc[:, :, :], yfc_t[:], reads=[yfc_b])
            S.dma("sp", d_oac[:, :, :], oac_t[:], reads=[oac_b])
    with ExitStack() as st:
        wo_t, wo_b = cx.sb("wo_t", [128, 10, D], BF16, stack=st)
        S.dma("pool", wo_t[:], wo[:, :].rearrange("(c p) d -> p c d", p=128), writes=[wo_b])
        wout_t, wout_b = cx.sb("wout_t", [128, KC, D], BF16, stack=st)
        S.dma("pool", wout_t[:], wout[:, :].rearrange("(c p) d -> p c d", p=128), writes=[wout_b])
        hxs = [cx.sb("hx_t%d" % i, [128, KC, 512], BF16, stack=st) for i in range(2)]
        ycs = [cx.sb("yc_t%d" % i, [128, 2, 512], BF16, stack=st) for i in range(1)]
        yps = [cx.sb("yp_t%d" % i, [128, 2, 512], BF16, stack=st) for i in range(1)]
        ys = [cx.sb("y_t%d" % i, [128, KC, 512], BF16, stack=st) for i in range(2)]
        xb = [cx.sb("xb%d" % i, [128, KC, 512], F32, stack=st) for i in range(1)]
        sig = [cx.sb("sig%d" % i, [128, 512], F32, stack=st) for i in range(2)]
        acc = [cx.sb("acc%d" % i, [128, 512], F32, stack=st) for i in range(2)]
        tmpb = [cx.sb("tmpb%d" % i, [128, 512], F32, stack=st) for i in range(2)]
        sqs = [cx.sb("sqs%d" % i, [128, 512], F32, stack=st) for i in range(2)]
        rs_t, rs_b = cx.sb("rs_t", [128, 512], F32, stack=st)
        h2 = [cx.sb("h2_%d" % i, [128, KC, 512], BF16, stack=st) for i in range(1)]
        psr = PsumRing(cx, 6, "psr", stack=st)
        pss = PsumRing(cx, 2, "pss", stack=st)
        segs = [(s0, 512, 0, hxT, ycT, ypT, yf_t, yf_b, oa_t, oa_b, xT, o_xm, o_h2) for s0 in range(0, T, 512)]
        if full_ctx:
            segs.append((0, NCTX, 1, hxTc, ycTc, ypTc, yfc_t, yfc_b, oac_t, oac_b, cxT, o_cxm, o_ch2))
        bi = 0

        def gated(si):
            (s0, ns, j, hsrc, ycsrc, ypsrc, yfs_t, yfs_b, oas_t, oas_b, xsrc, oxd, ohd) = segs[si]
            hx_t, hx_b = hxs[si % 2]
            yc_t, yc_b = ycs[0]
            yp_t, yp_b = yps[0]
            y_t, y_b = ys[si % 2]
            S.dma("sp", hx_t[:, :, 0:ns], hsrc[:, :, s0:s0 + ns], writes=[hx_b])
            S.dma("sp", yc_t[:, :, 0:ns], ycsrc[:, :, s0:s0 + ns], writes=[yc_b])
            S.dma("sp", yp_t[:, :, 0:ns], ypsrc[:, :, s0:s0 + ns], writes=[yp_b])
            nb = ns
            br = [(yfs_t, yfs_b, s0, 0, 2), (oas_t, oas_b, s0, 2, 4), (yc_t, yc_b, 0, 6, 2), (yp_t, yp_b, 0, 8, 2)]
            for m in range(KC):
                a_t, a_b = acc[m % 2]
                for jb, (bt, bb, boff, wc0, nch) in enumerate(br):
                    pg, pgb = psr.next()
                    for kc in range(KC):
                        S.op("pe", lambda e, kc=kc: e.matmul(pg[:, 0:nb], wg_t[:, kc, jb, m * 128:(m + 1) * 128], hx_t[:, kc, 0:nb],
                                                             start=(kc == 0), stop=(kc == KC - 1)),
                             reads=[wg_b, hx_b], writes=[pgb], inc=(kc == KC - 1))
                    s_t, s_b = sig[jb % 2]
                    S.op("act", lambda e: e.activation(out=s_t[:, 0:nb], in_=pg[:, 0:nb], func=AF.Sigmoid), reads=[pgb], writes=[s_b])
                    pbr, pbrb = psr.next()
                    for c in range(nch):
                        S.op("pe", lambda e, c=c: e.matmul(pbr[:, 0:nb], wo_t[:, wc0 + c, m * 128:(m + 1) * 128], bt[:, c, boff:boff + nb],
                                                           start=(c == 0), stop=(c == nch - 1)),
                             reads=[wo_b, bb], writes=[pbrb], inc=(c == nch - 1))
                    if jb == 0:
                        S.op("dve", lambda e: e.tensor_tensor(out=a_t[:, 0:nb], in0=pbr[:, 0:nb], in1=s_t[:, 0:nb], op=ALU.mult),
                             reads=[pbrb, s_b], writes=[a_b])
                    else:
                        t_t, t_b = tmpb[jb % 2]
                        S.op("dve", lambda e: e.tensor_tensor(out=t_t[:, 0:nb], in0=pbr[:, 0:nb], in1=s_t[:, 0:nb], op=ALU.mult),
                             reads=[pbrb, s_b], writes=[t_b])
                        if jb < 3:
                            S.op("pool", lambda e: e.tensor_tensor(out=a_t[:, 0:nb], in0=a_t[:, 0:nb], in1=t_t[:, 0:nb], op=ALU.add),
                                 reads=[a_b, t_b], writes=[a_b])
                        else:
                            S.op("pool", lambda e: e.tensor_tensor(out=y_t[:, m, 0:nb], in0=a_t[:, 0:nb], in1=t_t[:, 0:nb], op=ALU.add),
                                 reads=[a_b, t_b], writes=[y_b])

        def tail(si):
            nonlocal bi
            (s0, ns, j, hsrc, ycsrc, ypsrc, yfs_t, yfs_b, oas_t, oas_b, xsrc, oxd, ohd) = segs[si]
            y_t, y_b = ys[si % 2]
            nb = ns
            x_t, x_b = xb[0]
            h_t, h_b = h2[0]
            bi += 1
            S.dma("sp", x_t[:, :, 0:nb], xsrc[:, s0:s0 + nb].rearrange("(c p) n -> p c n", p=128), writes=[x_b])
            pst, psb = pss.next()
            for m2 in range(KC):
                pt, pb = psr.next()
                for m in range(KC):
                    S.op("pe", lambda e, m=m: e.matmul(pt[:, 0:nb], wout_t[:, m, m2 * 128:(m2 + 1) * 128], y_t[:, m, 0:nb],
                                                       start=(m == 0), stop=(m == KC - 1)),
                         reads=[wout_b, y_b], writes=[pb], inc=(m == KC - 1))
                S.op("dve", lambda e: e.scalar_tensor_tensor(out=x_t[:, m2, 0:nb], in0=pt[:, 0:nb], scalar=mod_t[:, 16 + m2, j:j + 1],
                                                             in1=x_t[:, m2, 0:nb], op0=ALU.mult, op1=ALU.add),
                     reads=[pb, mod_b, x_b], writes=[x_b])
                q_t2, q_b2 = sqs[m2 % 2]
                S.op("act", lambda e: e.activation(out=q_t2[:, 0:nb], in_=x_t[:, m2, 0:nb], func=AF.Square), reads=[x_b], writes=[q_b2])
                S.op("pe", lambda e: e.matmul(pst[:, 0:nb], ones_t[:, :], q_t2[:, 0:nb], start=(m2 == 0), stop=(m2 == KC - 1)),
                     reads=[ones_b, q_b2], writes=[psb], inc=True)
            S.dma("act", oxd[:, s0:s0 + nb].rearrange("(c p) n -> p c n", p=128), x_t[:, :, 0:nb], reads=[x_b])
            S.op("dve", lambda e: e.tensor_scalar(out=rs_t[:, 0:nb], in0=pst[:, 0:nb], scalar1=1.0 / D, scalar2=EPS, op0=ALU.mult, op1=ALU.add),
                 reads=[psb], writes=[rs_b])
            S.op("act", lambda e: e.activation(out=rs_t[:, 0:nb], in_=rs_t[:, 0:nb], func=AF.Sqrt), reads=[rs_b], writes=[rs_b])
            S.op("dve", lambda e: e.reciprocal(out=rs_t[:, 0:nb], in_=rs_t[:, 0:nb]), reads=[rs_b], writes=[rs_b])
            for kc in range(KC):
                t_t, t_b = tmpb[kc % 2]
                S.op("dve", lambda e: e.tensor_tensor(out=t_t[:, 0:nb], in0=x_t[:, kc, 0:nb], in1=rs_t[:, 0:nb], op=ALU.mult),
                     reads=[x_b, rs_b], writes=[t_b])
                S.op("act", lambda e: e.activation(out=h_t[:, kc, 0:nb], in_=t_t[:, 0:nb], func=AF.Identity,
                                                   bias=mod_t[:, 24 + kc, j:j + 1], scale=gs_t[:, kc, j:j + 1]),
                     reads=[t_b, mod_b, gs_b], writes=[h_b])
            S.dma("act", ohd[:, :, s0:s0 + nb], h_t[:, :, 0:nb], reads=[h_b])

        gated(0)
        for si in range(len(segs)):
            if si + 1 < len(segs):
                gated(si + 1)
            tail(si)
    if standalone:
        S.finish(None)
    return cx


def host_B(inp, l, resA, xT_all, ctxT_all, full_ctx):
    w_in = np.asarray(inp["w_in"][l])
    wg = np.ascontiguousarray(w_in[:, OFF_G:].reshape(D, 4, D))
    wo = np.ascontiguousarray(np.concatenate([inp["wo_f"][l], inp["wo_a"][l], inp["wo_c"][l], inp["wo_p"][l]], axis=0))
    wout = np.ascontiguousarray(inp["w_out"][l])
    lamv = np.stack([inp["lam_q1"][l], inp["lam_k1"][l], inp["lam_q2"][l], inp["lam_k2"][l]], 0).astype(np.float32)
    subg = np.ascontiguousarray(np.asarray(inp["subln_g"][l], np.float32).reshape(128, 1))
    n2g = fm_vec(inp["norm2_g"][l])
    ident = np.eye(128, dtype=np.float32)
    maps = []
    for i in range(NCORES):
        b, j = i // 4, i % 4
        tabA, tabC, tabX = fft_tables(j)
        if resA is None:
            m = dict(wg=wg, wo=wo, wout=wout, lamv=lamv, subg=subg, n2g=n2g, tabA=tabA, tabC=tabC, ident=ident)
            if full_ctx:
                m["tabX"] = tabX
            maps.append(m)
            continue
        grp = [resA[b * 4 + jj] for jj in range(4)]
        r = resA[i]
        m = dict(QT=r["QT"], KTc=r["KTc"], Vc=r["Vc"],
                 ycT=r["ycT"], ypT=r["ypT"], hxT=r["hxT"], xT=np.ascontiguousarray(xT_all[b][:, j * T:(j + 1) * T]), mod=r["mod"],
                 wg=wg, wo=wo, wout=wout, lamv=lamv, subg=subg, n2g=n2g, tabA=tabA, tabC=tabC, ident=ident)
        for h in range(4):
            m["KTg%d" % h] = np.ascontiguousarray(np.concatenate([g["KT%d" % h] for g in grp], axis=0))
            m["Vg%d" % h] = np.ascontiguousarray(np.concatenate([g["V%d" % h] for g in grp], axis=0))
            m["Zg%d" % h] = np.ascontiguousarray(np.concatenate([g["Z%d" % h] for g in grp], axis=0))
        if full_ctx:
            m.update(QTc=r["QTc"], Zc=r["Zc"], ycTc=r["ycTc"], ypTc=r["ypTc"], hxTc=r["hxTc"],
                     cxT=np.ascontiguousarray(ctxT_all[b]), tabX=tabX)
        maps.append(m)
    return maps


NFC = DFF // 128


def build_C(full_ctx, final, cx=None):
    standalone = cx is None
    if standalone:
        cx = Ctx("C")
    nc, S = cx.nc, cx.S
    hx2 = cx.inp("hx2T", [128, KC, T], BF16); hhalo = cx.inp("hhalo", [128, KC, 2], BF16)
    xm = cx.inp("xmT", [D, T]); mod = cx.inp("mod", [128, 48, 2])
    wup = cx.inp("wup", [D, 2 * DFF]); wdn = cx.inp("wdn", [DFF, D])
    fw = cx.inp("fw", [128, NFC, 4])
    o_x = cx.out("xoT", [D, T])
    if full_ctx:
        chx2 = cx.inp("chx2T", [128, KC, NCTX], BF16); cxm = cx.inp("cxmT", [D, NCTX])
        o_cx = cx.out("cxoT", [D, NCTX])
    if final:
        fg = cx.inp("fg", [128, KC])
    wup_t, _unused = cx.sb("wup_t", [128, KC, 2 * DFF], BF16)
    wup_bs = {}
    HP = 11 * 128
    for piece in range(2):
        for half in (1, 0):
            bb = Buf("wup_%d_%d" % (half, piece))
            c0_ = half * DFF + piece * HP
            for kc in range(KC):
                S.dma("pool", wup_t[:, kc, c0_:c0_ + HP], wup[kc * 128:(kc + 1) * 128, c0_:c0_ + HP], writes=[bb], also=True)
            wup_bs[(half, piece)] = bb
    wdn_t, wdn_b = cx.sb("wdn_t", [128, NFC, D], BF16)
    for g in range(2):
        S.dma("pool", wdn_t[:, g * 11:(g + 1) * 11, :], wdn[g * 11 * 128:(g + 1) * 11 * 128, :].rearrange("(c p) d -> p c d", p=128),
              writes=[wdn_b], also=True)
    mod_t, mod_b = load_const(cx, "mod_t", mod[:, :, :], [128, 48, 2])
    fw_t, fw_b = load_const(cx, "fw_t", fw[:, :, :], [128, NFC, 4])
    hal_t, hal_b = load_const(cx, "hal_t", hhalo[:, :, :], [128, KC, 2], BF16)
    zero_t, zero_b = cx.sb("zero_t", [128, KC, 2], BF16)
    S.op("dve", lambda e: e.memset(zero_t[:], 0.0), writes=[zero_b])
    if final:
        fg_t, fg_b = load_const(cx, "fg_t", fg[:, :], [128, KC])
        ones_t, ones_b = cx.sb("ones_t", [128, 128], F32)
        S.op("dve", lambda e: e.memset(ones_t[:], 1.0), writes=[ones_b])
    hxb = [cx.sb("hxb%d" % i, [128, KC, 512], BF16) for i in range(2)]
    x_t, x_b = cx.sb("x_t", [128, KC, 512], F32)
    u_t, u_b = cx.sb("u_t", [128, NFC, 512], BF16)
    gw = [cx.sb("gw%d" % i, [128, 512], F32) for i in range(2)]
    cv = [cx.sb("cv%d" % i, [128, 512], F32) for i in range(2)]
    ge = [cx.sb("ge%d" % i, [128, 512], F32) for i in range(2)]
    psr = PsumRing(cx, 6, "psr")
    pss = PsumRing(cx, 2, "pss")
    if final:
        sqs = [cx.sb("sqs%d" % i, [128, 512], F32) for i in range(2)]
        rs_t, rs_b = cx.sb("rs_t", [128, 512], F32)
    segs = [(hx2, xm, o_x, T, 0, hal_t, hal_b)]
    if full_ctx:
        segs.append((chx2, cxm, o_cx, NCTX, 1, zero_t, zero_b))
    bi = 0
    for (hsrc, xsrc, xdst, n, j, m_t, m_b) in segs:
        blocks = []
        b0 = 0
        while b0 < n:
            nb = min(510, n - b0)
            blocks.append((b0, nb))
            b0 += nb
        for (b0, nb) in blocks:
            h_t, h_b = hxb[bi % 2]
            bi += 1
            lo = max(b0 - 1, 0)
            hi = min(b0 + nb + 1, n)
            S.dma("sp", h_t[:, :, lo - (b0 - 1):hi - (b0 - 1)], hsrc[:, :, lo:hi], writes=[h_b])
            if b0 == 0:
                S.op("pool", lambda e: e.tensor_copy(out=h_t[:, :, 0:1], in_=m_t[:, :, 0:1]), reads=[m_b], writes=[h_b])
            if b0 + nb == n:
                S.op("pool", lambda e: e.tensor_copy(out=h_t[:, :, nb + 1:nb + 2], in_=m_t[:, :, 1:2]), reads=[m_b], writes=[h_b])
            S.dma("sp", x_t[:, :, 0:nb], xsrc[:, b0:b0 + nb].rearrange("(c p) n -> p c n", p=128), writes=[x_b])
            for c in range(NFC):
                pg, pgb = psr.next()
                for kc in range(KC):
                    S.op("pe", lambda e, kc=kc: e.matmul(pg[:, 0:nb + 2], wup_t[:, kc, DFF + c * 128:DFF + (c + 1) * 128], h_t[:, kc, 0:nb + 2],
                                                         start=(kc == 0), stop=(kc == KC - 1)),
                         reads=[wup_bs[(1, c // 11)], h_b], writes=[pgb], inc=(kc == KC - 1))
                g_t, g_b = gw[c % 2]
                S.op("act", lambda e: e.activation(out=g_t[:, 0:nb + 2], in_=pg[:, 0:nb + 2], func=AF.Copy), reads=[pgb], writes=[g_b])
                c_t, c_b = cv[c % 2]
                S.op("dve", lambda e: e.tensor_scalar(out=c_t[:, 0:nb], in0=g_t[:, 0:nb], scalar1=fw_t[:, c, 0:1], scalar2=fw_t[:, c, 3:4],
                                                      op0=ALU.mult, op1=ALU.add), reads=[g_b, fw_b], writes=[c_b])
                for k in (1, 2):
                    S.op("dve", lambda e, k=k: e.scalar_tensor_tensor(out=c_t[:, 0:nb], in0=g_t[:, k:k + nb], scalar=fw_t[:, c, k:k + 1], in1=c_t[:, 0:nb],
                                                                      op0=ALU.mult, op1=ALU.add), reads=[g_b, fw_b, c_b], writes=[c_b])
                e_t, e_b = ge[c % 2]
                S.op("act", lambda e: e.activation(out=e_t[:, 0:nb], in_=c_t[:, 0:nb], func=AF.Gelu), reads=[c_b], writes=[e_b])
                pv, pvb = psr.next()
                for kc in range(KC):
                    S.op("pe", lambda e, kc=kc: e.matmul(pv[:, 0:nb], wup_t[:, kc, c * 128:(c + 1) * 128], h_t[:, kc, 1:nb + 1],
                                                         start=(kc == 0), stop=(kc == KC - 1)),
                         reads=[wup_bs[(0, c // 11)], h_b], writes=[pvb], inc=(kc == KC - 1))
                S.op("dve", lambda e: e.tensor_tensor(out=u_t[:, c, 0:nb], in0=pv[:, 0:nb], in1=e_t[:, 0:nb], op=ALU.mult),
                     reads=[pvb, e_b], writes=[u_b])
            if final:
                pst, psb = pss.next()
            for m2 in range(KC):
                po, pob = psr.next()
                for c in range(NFC):
                    S.op("pe", lambda e, c=c: e.matmul(po[:, 0:nb], wdn_t[:, c, m2 * 128:(m2 + 1) * 128], u_t[:, c, 0:nb],
                                                       start=(c == 0), stop=(c == NFC - 1)),
                         reads=[wdn_b, u_b], writes=[pob], inc=(c == NFC - 1))
                S.op("dve", lambda e: e.scalar_tensor_tensor(out=x_t[:, m2, 0:nb], in0=po[:, 0:nb], scalar=mod_t[:, 40 + m2, j:j + 1],
                                                             in1=x_t[:, m2, 0:nb], op0=ALU.mult, op1=ALU.add),
                     reads=[pob, mod_b, x_b], writes=[x_b])
                if final:
                    q_t2, q_b2 = sqs[m2 % 2]
                    S.op("act", lambda e: e.activation(out=q_t2[:, 0:nb], in_=x_t[:, m2, 0:nb], func=AF.Square), reads=[x_b], writes=[q_b2])
                    S.op("pe", lambda e: e.matmul(pst[:, 0:nb], ones_t[:, :], q_t2[:, 0:nb], start=(m2 == 0), stop=(m2 == KC - 1)),
                         reads=[ones_b, q_b2], writes=[psb], inc=True)
            if final:
                S.op("dve", lambda e: e.tensor_scalar(out=rs_t[:, 0:nb], in0=pst[:, 0:nb], scalar1=1.0 / D, scalar2=EPS, op0=ALU.mult, op1=ALU.add),
                     reads=[psb], writes=[rs_b])
                S.op("act", lambda e: e.activation(out=rs_t[:, 0:nb], in_=rs_t[:, 0:nb], func=AF.Sqrt), reads=[rs_b], writes=[rs_b])
                S.op("dve", lambda e: e.reciprocal(out=rs_t[:, 0:nb], in_=rs_t[:, 0:nb]), reads=[rs_b], writes=[rs_b])
                for kc in range(KC):
                    S.op("dve", lambda e, kc=kc: e.scalar_tensor_tensor(out=x_t[:, kc, 0:nb], in0=x_t[:, kc, 0:nb], scalar=fg_t[:, kc:kc + 1],
                                                                        in1=rs_t[:, 0:nb], op0=ALU.mult, op1=ALU.mult),
                         reads=[x_b, fg_b, rs_b], writes=[x_b])
            S.dma("act", xdst[:, b0:b0 + nb].rearrange("(c p) n -> p c n", p=128), x_t[:, :, 0:nb], reads=[x_b])
    if standalone:
        S.finish(None)
    return cx


def host_C(inp, l, resA, resB, full_ctx, final):
    wup = np.ascontiguousarray(inp["w_up"][l]); wdn = np.ascontiguousarray(inp["w_down"][l])
    fwv = np.concatenate([np.asarray(inp["ffn_dw_w"][l]), np.asarray(inp["ffn_dw_b"][l])[None, :]], axis=0)
    fw = np.ascontiguousarray(fwv.T.reshape(NFC, 128, 4).transpose(1, 0, 2)).astype(np.float32)
    maps = []
    for i in range(NCORES):
        b, j = i // 4, i % 4
        if resA is None:
            m = dict(wup=wup, wdn=wdn, fw=fw)
            if final:
                m["fg"] = fm_vec(inp["final_g"])
            maps.append(m)
            continue
        hh = np.zeros((128, KC, 2), NPBF)
        if j > 0:
            hh[:, :, 0] = resB[i - 1]["hx2T"][:, :, T - 1]
        if j < 3:
            hh[:, :, 1] = resB[i + 1]["hx2T"][:, :, 0]
        m = dict(hx2T=resB[i]["hx2T"], hhalo=hh, xmT=resB[i]["xmT"], mod=resA[i]["mod"], wup=wup, wdn=wdn, fw=fw)
        if full_ctx:
            m.update(chx2T=resB[i]["chx2T"], cxmT=resB[i]["cxmT"])
        if final:
            m["fg"] = fm_vec(inp["final_g"])
        maps.append(m)
    return maps


def _np(results):
    return [{k: np.asarray(v) for k, v in r.items()} for r in results]


def kernel_unfused(**inputs):
    inp = {k: np.asarray(v) for k, v in inputs.items()}
    x = inp["x"].astype(np.float32, copy=False)
    B = x.shape[0]
    xT_all = [np.ascontiguousarray(x[b].T) for b in range(B)]
    ctxT_all = [np.ascontiguousarray(inp["ctx"][b].T.astype(np.float32)) for b in range(B)]
    depth = inp["w_in"].shape[0]
    for l in range(depth):
        last = l == depth - 1
        full_ctx = not last
        lam_init = 0.8 - 0.6 * math.exp(-0.3 * l)
        cxA = get_prog(("A", full_ctx), build_A, full_ctx)
        resA = _np(run_prog(cxA, host_A(inp, l, xT_all, ctxT_all)))
        cxB = get_prog(("B", full_ctx, l), build_B, full_ctx, lam_init)
        resB = _np(run_prog(cxB, host_B(inp, l, resA, xT_all, ctxT_all, full_ctx)))
        cxC = get_prog(("C", full_ctx, last), build_C, full_ctx, last)
        resC = _np(run_prog(cxC, host_C(inp, l, resA, resB, full_ctx, last)))
        xT_all = [np.concatenate([resC[b * 4 + j]["xoT"] for j in range(4)], axis=1) for b in range(B)]
        if full_ctx:
            ctxT_all = [resC[b * 4]["cxoT"] for b in range(B)]
    out = np.stack([xT_all[b].T for b in range(B)], axis=0)
    return np.ascontiguousarray(out.astype(np.float32))


def _select_halo(cx, gathered, ncol, dt, pick_l, pick_r, wl, out_aps, sel_t, sel_b, name):
    S = cx.S
    g_t, g_b = cx.sb(name + "_g", [128, 4, ncol], dt)
    S.dma("sp", g_t[:], gathered[:, :].rearrange("(r p) n -> p r n", p=128), writes=[g_b])
    res = []
    for side, pick in ((0, pick_l), (1, pick_r)):
        a_t, a_b = cx.sb(name + "_a%d" % side, [128, KC, wl], F32)
        gv = g_t[:, :, :].rearrange("p r (c n) -> p r c n", c=KC)
        S.op("dve", lambda e: e.tensor_scalar(out=a_t[:], in0=gv[:, 0, :, pick], scalar1=sel_t[:, side * 4:side * 4 + 1], scalar2=None, op0=ALU.mult),
             reads=[g_b, sel_b], writes=[a_b])
        for r in range(1, 4):
            S.op("dve", lambda e, r=r: e.scalar_tensor_tensor(out=a_t[:], in0=gv[:, r, :, pick], scalar=sel_t[:, side * 4 + r:side * 4 + r + 1],
                                                              in1=a_t[:], op0=ALU.mult, op1=ALU.add),
                 reads=[g_b, sel_b, a_b], writes=[a_b])
        res.append((a_t, a_b))
    return res


def build_M(cx, depth):
    nc, S = cx.nc, cx.S
    cx.begin_phase("M_")
    cT = cx.inp("cT", [128, KC, 2])
    cT_t, cT_b = load_const(cx, "cT_t", cT[:, :, :], [128, KC, 2])
    psr = PsumRing(cx, 4, "psm")
    mq = []
    for l in range(depth):
        awq = cx.inp("L%d_ada_wq" % l, [D, 1536]); abq = cx.inp("L%d_ada_bq" % l, [128, 12])
        abq_t, abq_b = load_const(cx, "abq%d" % l, abq[:, :], [128, 12])
        mq_t, mq_b = cx.sb("mq%d" % l, [128, 12, 2], F32)
        cx.prefix = "M%d_" % l
        _adaln_mod(cx, cT_t, cT_b, awq, abq_t, abq_b, mq_t, mq_b, psr, 3, 0)
        cx.prefix = "M_"
        mq_d = nc.dram_tensor("M_mqd%d" % l, [128, 24], F32).ap()
        S.dma("sp", mq_d[:, :], mq_t[:, :, :].rearrange("p c j -> p (c j)"), reads=[mq_b])
        mq.append(mq_d)
    S.barrier()
    mod_ap = []
    mg_l = []
    for l in range(depth):
        mg = nc.dram_tensor("M_mg%d" % l, [512, 24], F32).ap()
        S.allgather(mq[l], mg)
        mg_l.append(mg)
    S.barrier()
    for l in range(depth):
        md = nc.dram_tensor("M_mod%d" % l, [128, 48, 2], F32).ap()
        mt, mb = cx.sb("mgt%d" % l, [128, 4, 24], F32)
        S.dma("sp", mt[:], mg_l[l][:, :].rearrange("(r p) n -> p r n", p=128), writes=[mb])
        S.dma("sp", md[:, :, :].rearrange("p (r c) j -> p r (c j)", r=4), mt[:], reads=[mb])
        mod_ap.append(md)
    cx.end_phase()
    return mod_ap


def build_fused(depth=2):
    cx = Ctx("F", fused=True)
    nc, S = cx.nc, cx.S
    sel = cx.inp("sel", [128, 8])
    mod_ap = build_M(cx, depth)
    prevC = None
    xTh_ap = None
    for l in range(depth):
        last = l == depth - 1
        full_ctx = not last
        lam_init = 0.8 - 0.6 * math.exp(-0.3 * l)
        links = {}
        if l > 0:
            links = {"xT": prevC["xoT"], "cxT": prevC["cxoT"], "xTh": xTh_ap}
        links["mod_in"] = mod_ap[l]
        cx.begin_phase("L%dA_" % l, links)
        gath = {}

        for nm, shp in (("Z", [SEQ, 128]), ("KT", [512, T]), ("V", [SEQ, 128])):
            for h in range(4):
                gath["%sg%d" % (nm, h)] = nc.dram_tensor("L%dE1_%sg%d" % (l, nm, h), shp, BF16).ap()

        def gather_kvz(l=l, gath=gath):
            for h in range(4):
                S.allgather(cx.produced["Z%d" % h], gath["Zg%d" % h])

        cx.after_blocks = gather_kvz
        build_A(full_ctx, cx)
        cx.after_blocks = None
        pA = cx.end_phase()
        pA["mod"] = mod_ap[l]
        xT_ap = links["xT"] if l > 0 else cx.ins["L0A_xT"]
        cxT_ap = links["cxT"] if l > 0 else cx.ins["L0A_cxT"]
        links = {k: pA[k] for k in ("QT", "KTc", "Vc", "ycT", "ypT", "hxT", "mod")}
        links.update(gath)
        links["xT"] = xT_ap
        if full_ctx:
            links.update({k: pA[k] for k in ("QTc", "Zc", "ycTc", "ypTc", "hxTc")})
            links["cxT"] = cxT_ap
        cx.begin_phase("L%dB_" % l, links)

        def gather_kv(pA=pA, gath=gath):
            for h in range(4):
                for nm in ("KT", "V"):
                    st_ = S.allgather(pA["%s%d" % (nm, h)], gath["%sg%d" % (nm, h)])
                    gb = Buf("g_%s%d" % (nm, h))
                    gb.w = [st_]
                    cx.dram_bufs["%sg%d" % (nm, h)] = gb

        cx.after_fft_tables = gather_kv
        cx.dram_bufs = {}
        build_B(full_ctx, lam_init, cx=cx)
        cx.after_fft_tables = None
        pB = cx.end_phase()
        cx.begin_phase("L%dE2_" % l)
        sel_t, sel_b = load_const(cx, "sel_t", sel[:, :], [128, 8])
        e_t, e_b = cx.sb("e_t", [128, KC, 2], BF16)
        S.dma("sp", e_t[:, :, 0:1], pB["hx2T"][:, :, 0:1], writes=[e_b], slow=True)
        S.dma("sp", e_t[:, :, 1:2], pB["hx2T"][:, :, T - 1:T], writes=[e_b], also=True, slow=True)
        ein = nc.dram_tensor("L%dE2_in" % l, [128, KC * 2], BF16).ap()
        eout = nc.dram_tensor("L%dE2_out" % l, [512, KC * 2], BF16).ap()
        hhalo = nc.dram_tensor("L%dE2_hhalo" % l, [128, KC, 2], BF16).ap()
        S.dma("sp", ein[:, :], e_t[:, :, :].rearrange("p c n -> p (c n)"), reads=[e_b])
        S.barrier()
        S.allgather(ein, eout)
        S.barrier()
        (l_t, l_b), (r_t, r_b) = _select_halo(cx, eout, KC * 2, BF16, slice(1, 2), slice(0, 1), 1, None, sel_t, sel_b, "h2")
        hh_t, hh_b = cx.sb("hh_t", [128, KC, 2], BF16)
        S.op("dve", lambda e: e.tensor_copy(out=hh_t[:, :, 0:1], in_=l_t[:]), reads=[l_b], writes=[hh_b])
        S.op("dve", lambda e: e.tensor_copy(out=hh_t[:, :, 1:2], in_=r_t[:]), reads=[r_b, hh_b], writes=[hh_b])
        S.dma("sp", hhalo[:, :, :], hh_t[:], reads=[hh_b])
        cx.end_phase()
        links = {"hx2T": pB["hx2T"], "hhalo": hhalo, "xmT": pB["xmT"], "mod": pA["mod"]}
        if full_ctx:
            links.update({"chx2T": pB["chx2T"], "cxmT": pB["cxmT"]})
        cx.begin_phase("L%dC_" % l, links, ext_out=({"xoT": "outT"} if last else None))
        build_C(full_ctx, last, cx=cx)
        pC = cx.end_phase()
        prevC = pC
        if not last:
            cx.begin_phase("L%dE3_" % l)
            sel_t, sel_b = load_const(cx, "sel_t", sel[:, :], [128, 8])
            e_t, e_b = cx.sb("e_t", [128, KC, 2 * HALO], F32)
            S.dma("sp", e_t[:, :, 0:HALO], pC["xoT"][:, 0:HALO].rearrange("(c p) n -> p c n", p=128), writes=[e_b])
            S.dma("sp", e_t[:, :, HALO:2 * HALO], pC["xoT"][:, T - HALO:T].rearrange("(c p) n -> p c n", p=128), writes=[e_b], also=True)
            ein = nc.dram_tensor("L%dE3_in" % l, [128, KC * 2 * HALO], F32).ap()
            eout = nc.dram_tensor("L%dE3_out" % l, [512, KC * 2 * HALO], F32).ap()
            xTh_ap = nc.dram_tensor("L%dE3_xTh" % l, [D, 2 * HALO], F32).ap()
            S.dma("sp", ein[:, :], e_t[:, :, :].rearrange("p c n -> p (c n)"), reads=[e_b])
            S.barrier()
            S.allgather(ein, eout)
            S.barrier()
            (l_t, l_b), (r_t, r_b) = _select_halo(cx, eout, KC * 2 * HALO, F32, slice(HALO, 2 * HALO), slice(0, HALO), HALO, None, sel_t, sel_b, "xh")
            S.dma("sp", xTh_ap[:, 0:HALO].rearrange("(c p) n -> p c n", p=128), l_t[:], reads=[l_b])
            S.dma("sp", xTh_ap[:, HALO:2 * HALO].rearrange("(c p) n -> p c n", p=128), r_t[:], reads=[r_b])
            cx.end_phase()
    S.finish(None)
    return cx


def kernel(**inputs):
    inp = {k: np.asarray(v) for k, v in inputs.items()}
    x = inp["x"].astype(np.float32, copy=False)
    B = x.shape[0]
    depth = inp["w_in"].shape[0]
    cx = get_prog(("F", depth), build_fused, depth)
    xT_all = [np.ascontiguousarray(x[b].T) for b in range(B)]
    ctxT_all = [np.ascontiguousarray(inp["ctx"][b].T.astype(np.float32)) for b in range(B)]
    maps = [dict() for _ in range(NCORES)]
    for l in range(depth):
        last = l == depth - 1
        full_ctx = not last
        mA = host_A(inp, l, xT_all if l == 0 else None, ctxT_all if l == 0 else None)
        mB = host_B(inp, l, None, None, None, full_ctx)
        mC = host_C(inp, l, None, None, full_ctx, last)
        for i in range(NCORES):
            for pre, m in (("L%dA_" % l, mA[i]), ("L%dB_" % l, mB[i]), ("L%dC_" % l, mC[i])):
                for k, v in m.items():
                    if pre + k in cx.ins:
                        maps[i][pre + k] = v
    for i in range(NCORES):
        b, j = i // 4, i % 4
        maps[i]["M_cT"] = np.ascontiguousarray(np.stack([fm_vec(inp["c"][b]), fm_vec(inp["c_ctx"])], axis=2))
        for l in range(depth):
            maps[i]["M_L%d_ada_wq" % l] = np.ascontiguousarray(inp["ada_w"][l][:, j * 1536:(j + 1) * 1536])
            maps[i]["M_L%d_ada_bq" % l] = fm_vec(inp["ada_b"][l][j * 1536:(j + 1) * 1536])
        sel = np.zeros((128, 8), np.float32)
        if j > 0:
            sel[:, j - 1] = 1.0
        if j < 3:
            sel[:, 4 + j + 1] = 1.0
        maps[i]["sel"] = sel
        missing = set(cx.ins) - set(maps[i])
        assert not missing, missing
    res = run_prog(cx, maps)
    outT = [np.asarray(r["outT"]) for r in res]
    out = np.stack([np.concatenate([outT[b * 4 + j] for j in range(4)], axis=1).T for b in range(B)], axis=0)
    return np.ascontiguousarray(out.astype(np.float32))
```

```python
import math
from contextlib import ExitStack
import numpy as np
import ml_dtypes
import concourse.bass as bass
import concourse.mybir as mybir
from concourse.bass_utils import run_bass_kernel_spmd

F32 = mybir.dt.float32
BF16 = mybir.dt.bfloat16
AF = mybir.ActivationFunctionType
ALU = mybir.AluOpType
NPBF = ml_dtypes.bfloat16

D = 1024
KC = 8
T = 2048
NCTX = 256
SEQ = 8192
HALO = 16
DFF = 2816
EPS = 1e-6
NCORES = 8
SAME_ENGINE_SYNC = True


class Buf:
    def __init__(self, name=""):
        self.name = name
        self.w = []
        self.r = []


class Sched:
    def __init__(self, nc, es, ndma=32):
        self.nc = nc
        self.E = {"pe": nc.tensor, "act": nc.scalar, "dve": nc.vector, "pool": nc.gpsimd, "sp": nc.sync}
        self.sem = {e: es.enter_context(nc.semaphore("sem_" + e)) for e in self.E}
        self.cnt = {e: 0 for e in self.E}
        self.seen = {e: {} for e in self.E}
        self.pend = {e: [] for e in self.E}
        self.dsems = [es.enter_context(nc.semaphore("dsem%d" % i)) for i in range(ndma)]
        self.dcnt = [0] * ndma
        self.dnext = 0
        self.dnext_sw = 0
        self.nhw = 20
        self.nwaits = 0
        self.cc_sem = es.enter_context(nc.semaphore("cc_sem"))
        self.cc_cnt = 0

    def _wait(self, e, st):
        key, sem, val = st
        if key == e and (e == "pe" or not SAME_ENGINE_SYNC):
            return
        if self.seen[e].get(key, 0) >= val:
            return
        self.E[e].wait_ge(sem, val)
        self.nwaits += 1
        self.seen[e][key] = val

    def _deps(self, e, reads, writes):
        for oe, pl in self.pend.items():
            if oe == e:
                continue
            for (R, W) in pl:
                for b in list(reads) + list(writes):
                    if b in W or (b in R and b in writes):
                        raise RuntimeError("dependency on un-stamped access of %s by %s (buf %s)" % (oe, e, b.name))
        for b in reads:
            for st in b.w:
                self._wait(e, st)
        for b in writes:
            for st in b.w:
                self._wait(e, st)
            for st in b.r:
                self._wait(e, st)

    def op(self, e, fn, reads=(), writes=(), inc=True):
        self._deps(e, reads, writes)
        ins = fn(self.E[e])
        self.pend[e].append((tuple(reads), tuple(writes)))
        if inc:
            self.cnt[e] += 1
            ins.then_inc(self.sem[e], 1)
            st = (e, self.sem[e], self.cnt[e])
            for (R, W) in self.pend[e]:
                for b in R:
                    b.r.append(st)
                for b in W:
                    b.w = [st]
                    b.r = []
            self.pend[e] = []
        return ins

    def dma(self, q, out, in_, reads=(), writes=(), also=False, slow=False):
        self._deps(q, reads, writes)
        if q == "pool":
            i = self.nhw + self.dnext_sw
            self.dnext_sw = (self.dnext_sw + 1) % (len(self.dsems) - self.nhw)
        else:
            i = self.dnext
            self.dnext = (self.dnext + 1) % self.nhw
        key = "d%d" % i
        if self.dcnt[i] > 0:
            self._wait(q, (key, self.dsems[i], self.dcnt[i]))
        ins = self.E[q].dma_start(out=out, in_=in_, allow_slow_non_contiguous=True) if slow else self.E[q].dma_start(out=out, in_=in_)
        self.dcnt[i] += 16
        ins.then_inc(self.dsems[i], 16)
        st = (key, self.dsems[i], self.dcnt[i])
        for b in reads:
            b.r.append(st)
        for b in writes:
            if also:
                b.w = b.w + [st]
            else:
                b.w = [st]
                b.r = []
        return ins

    def barrier(self):
        for e, pl in self.pend.items():
            if pl:
                raise RuntimeError("barrier with un-stamped accesses on " + e)
        for e in self.E:
            for oe in self.E:
                if oe != e and self.cnt[oe] > 0:
                    self._wait(e, (oe, self.sem[oe], self.cnt[oe]))
            for i in range(len(self.dsems)):
                if self.dcnt[i] > 0:
                    self._wait(e, ("d%d" % i, self.dsems[i], self.dcnt[i]))
            if self.cc_cnt > 0:
                self._wait(e, ("cc", self.cc_sem, self.cc_cnt))

    def allgather(self, in_ap, out_ap):
        ins = self.nc.gpsimd.collective_compute("AllGather", ALU.bypass, replica_groups=[[0, 1, 2, 3], [4, 5, 6, 7]],
                                                ins=[in_ap.opt()], outs=[out_ap.opt()])
        self.cc_cnt += 1
        ins.then_inc(self.cc_sem)
        st = ("cc", self.cc_sem, self.cc_cnt)
        self._wait("pool", st)
        return st

    def finish(self, bufs):
        for i in range(len(self.dsems)):
            if self.dcnt[i] > 0:
                self._wait("sp", ("d%d" % i, self.dsems[i], self.dcnt[i]))


class Ctx:
    def __init__(self, name, fused=False):
        self.nc = bass.Bass("TRN2", target_bir_lowering=False)
        self.es_root = ExitStack()
        self.es = ExitStack()
        self.S = Sched(self.nc, self.es_root)
        self.ins = {}
        self.outs = {}
        self.fused = fused
        self.prefix = ""
        self.links = {}
        self.produced = {}
        self.ext_out = {}
        self.dram_bufs = {}
        self.after_blocks = None
        self.after_fft_tables = None

    def begin_phase(self, prefix, links=None, ext_out=None):
        self.prefix = prefix
        self.links = dict(links or {})
        self.produced = {}
        self.ext_out = dict(ext_out or {})
        self.es = ExitStack()

    def end_phase(self):
        self.S.barrier()
        self.es.close()
        self.es = ExitStack()
        return self.produced

    def inp(self, name, shape, dt=F32):
        if name in self.links:
            return self.links[name]
        t = self.nc.dram_tensor(self.prefix + name, list(shape), dt, kind="ExternalInput").ap()
        self.ins[self.prefix + name] = t
        return t

    def out(self, name, shape, dt=F32):
        if self.fused and name not in self.ext_out:
            t = self.nc.dram_tensor(self.prefix + name, list(shape), dt).ap()
            self.produced[name] = t
            return t
        oname = self.ext_out.get(name, self.prefix + name)
        t = self.nc.dram_tensor(oname, list(shape), dt, kind="ExternalOutput").ap()
        self.outs[oname] = t
        self.produced[name] = t
        return t

    def sb(self, name, shape, dt=F32, stack=None):
        t = (stack or self.es).enter_context(self.nc.sbuf_tensor(self.prefix + name, list(shape), dt))
        return t, Buf(name)

    def ps(self, name, shape, dt=F32, stack=None):
        t = (stack or self.es).enter_context(self.nc.psum_tensor(self.prefix + name, list(shape), dt))
        return t, Buf(name)


class PsumRing:
    def __init__(self, cx, n, name="ps", stack=None):
        self.tiles = [cx.ps("%s%d" % (name, i), [128, 512], F32, stack=stack) for i in range(n)]
        self.i = 0

    def next(self):
        t = self.tiles[self.i]
        self.i = (self.i + 1) % len(self.tiles)
        return t


def load_const(cx, name, dram_ap, shape, dt=F32, q="sp"):
    t, b = cx.sb(name, shape, dt)
    cx.S.dma(q, t[:], dram_ap, writes=[b])
    return t, b


def rope_tables(tok0, n):
    nf = 16
    inv = (10000.0 ** (-np.arange(nf, dtype=np.float32) / nf)).astype(np.float32)
    t = np.arange(tok0, tok0 + n)
    r = (t // 64).astype(np.float32)
    col = (t % 64).astype(np.float32)
    ar = r[:, None] * inv
    ac = col[:, None] * inv
    cr, sr, cc, sc = np.cos(ar), np.sin(ar), np.cos(ac), np.sin(ac)
    C = np.concatenate([cr, cr, cc, cc], axis=1).T
    Ssg = np.concatenate([-sr, sr, -sc, sc], axis=1).T
    return (np.ascontiguousarray(np.concatenate([C, C], 0), dtype=np.float32),
            np.ascontiguousarray(np.concatenate([Ssg, Ssg], 0), dtype=np.float32))


SWAP64 = np.concatenate([np.arange(16, 32), np.arange(0, 16), np.arange(48, 64), np.arange(32, 48)])


def pool_invcnt(tok0, n, L):
    out = np.zeros((256, n), np.float32)
    t = np.arange(tok0, tok0 + n)
    for g, win in enumerate((2, 4, 8, 16)):
        lo = np.clip(t - win // 2, 0, L - 1)
        hi = np.clip(t + win - win // 2 - 1, 0, L - 1)
        out[g * 64:(g + 1) * 64, :] = (1.0 / (hi - lo + 1).astype(np.float32))[None, :]
    return out.reshape(2, 128, n).transpose(1, 0, 2).copy()


def chan_dft_table():
    a = 2 * np.pi * np.outer(np.arange(64), np.arange(64)) / 64.0
    C, Sn = np.cos(a), np.sin(a)
    Cb = np.zeros((128, 128)); Sb = np.zeros((128, 128))
    for g in range(2):
        Cb[g * 64:(g + 1) * 64, g * 64:(g + 1) * 64] = C
        Sb[g * 64:(g + 1) * 64, g * 64:(g + 1) * 64] = Sn
    return np.concatenate([Cb, Sb], axis=1).astype(np.float32)


NWA = 24


def _adaln_mod(cx, cT_t, cT_b, ada_w, adab_t, adab_b, mod_t, mod_b, psr, ngroups, chunk0):
    S = cx.S
    sil_t, sil_b = cx.sb("sil_t", [128, KC, 2], F32)
    S.op("act", lambda e: e.activation(out=sil_t[:], in_=cT_t[:], func=AF.Silu), reads=[cT_b], writes=[sil_b])
    with ExitStack() as st:
        aw = [cx.sb("aw%d" % i, [128, KC, 512], F32, stack=st) for i in range(2)]
        for g in range(ngroups):
            awt, awb = aw[g % 2]
            for kc in range(KC):
                S.dma("sp", awt[:, kc, :], ada_w[kc * 128:(kc + 1) * 128, g * 512:(g + 1) * 512], writes=[awb], also=(kc > 0))
            pt, pb = psr.next()
            for mm in range(4):
                for kc in range(KC):
                    S.op("pe", lambda e, mm=mm, kc=kc: e.matmul(pt[:, mm * 2:mm * 2 + 2], awt[:, kc, mm * 128:(mm + 1) * 128], sil_t[:, kc, :],
                                                                 start=(kc == 0), stop=(kc == KC - 1)),
                         reads=[awb, sil_b], writes=[pb], inc=(mm == 3 and kc == KC - 1))
            c_ = chunk0 + g * 4
            S.op("dve", lambda e, g=g, c_=c_: e.tensor_tensor(out=mod_t[:, c_:c_ + 4, :],
                                                              in0=pt[:, 0:8].rearrange("p (a b) -> p a b", b=2),
                                                              in1=adab_t[:, c_:c_ + 4].unsqueeze(2).to_broadcast([128, 4, 2]), op=ALU.add),
                 reads=[pb, adab_b], writes=[mod_b])
        S.barrier()


def build_A(full_ctx, cx=None):
    standalone = cx is None
    if standalone:
        cx = Ctx("A")
    nc, S = cx.nc, cx.S
    xT = cx.inp("xT", [D, T]); xTh = cx.inp("xTh", [D, 2 * HALO]); cxT = cx.inp("cxT", [D, NCTX])
    premod = "mod_in" in cx.links
    if not premod:
        cT = cx.inp("cT", [128, KC, 2]); ada_w = cx.inp("ada_w", [D, 6 * D]); ada_b = cx.inp("ada_b", [128, 48])
    n1g = cx.inp("n1g", [128, KC])
    wA = cx.inp("wA", [D, NWA * 128]); wV = cx.inp("wV", [D, 512])
    ropeC = cx.inp("ropeC", [128, T]); ropeS = cx.inp("ropeS", [128, T])
    hmask = cx.inp("hmask", [128, 2 * HALO])
    cw = cx.inp("cw", [128, 2, 31]); cvec = cx.inp("cvec", [128, 2, 4])
    pinv = cx.inp("pinv", [128, 2, T]); pinvc = cx.inp("pinvc", [128, 2, NCTX])
    pw = cx.inp("pw", [128, 2, 128]); cdft = cx.inp("cdft", [128, 256]); identA = cx.inp("identA", [128, 128])

    if not premod:
        o_mod = cx.out("mod", [128, 48, 2])
    o_QT = cx.out("QT", [128, 4, T], BF16)
    o_KTh = [cx.out("KT%d" % h, [128, T], BF16) for h in range(4)]
    o_Vh = [cx.out("V%d" % h, [T, 128], BF16) for h in range(4)]
    o_Zq = [cx.out("Z%d" % q, [T, 128], BF16) for q in range(4)]
    o_yc = cx.out("ycT", [128, 2, T], BF16); o_yp = cx.out("ypT", [128, 2, T], BF16)
    o_hx = cx.out("hxT", [128, KC, T], BF16)
    o_KTc = cx.out("KTc", [128, 4, NCTX], BF16); o_Vc = cx.out("Vc", [NCTX, 512], BF16)
    if full_ctx:
        o_QTc = cx.out("QTc", [128, 4, NCTX], BF16); o_Zc = cx.out("Zc", [NCTX, 512], BF16)
        o_ycc = cx.out("ycTc", [128, 2, NCTX], BF16); o_ypc = cx.out("ypTc", [128, 2, NCTX], BF16)
        o_hxc = cx.out("hxTc", [128, KC, NCTX], BF16)

    wA_t, wA_b = cx.sb("wA_t", [128, KC, NWA * 128], BF16)
    for kc in range(KC):
        S.dma("pool", wA_t[:, kc, :], wA[kc * 128:(kc + 1) * 128, :], writes=[wA_b], also=True)
    wV_t, wV_b = cx.sb("wV_t", [128, KC, 512], BF16)
    for kc in range(KC):
        S.dma("pool", wV_t[:, kc, :], wV[kc * 128:(kc + 1) * 128, :], writes=[wV_b], also=True)
    if not premod:
        cT_t, cT_b = load_const(cx, "cT_t", cT[:, :, :], [128, KC, 2])
        adab_t, adab_b = load_const(cx, "adab_t", ada_b[:, :], [128, 48])
    n1g_t, n1g_b = load_const(cx, "n1g_t", n1g[:, :], [128, KC])
    hmask_t, hmask_b = load_const(cx, "hmask_t", hmask[:, :], [128, 2 * HALO])
    cw_t, cw_b = load_const(cx, "cw_t", cw[:, :, :], [128, 2, 31])
    cvec_t, cvec_b = load_const(cx, "cvec_t", cvec[:, :, :], [128, 2, 4])
    pw_t, pw_b = cx.sb("pw_t", [128, 2, 128], BF16)
    S.dma("pool", pw_t[:], pw[:, :, :], writes=[pw_b])
    cdft_t, cdft_b = cx.sb("cdft_t", [128, 256], BF16)
    S.dma("pool", cdft_t[:], cdft[:, :], writes=[cdft_b])
    ones_t, ones_b = cx.sb("ones_t", [128, 128], F32)
    S.op("dve", lambda e: e.memset(ones_t[:], 1.0), writes=[ones_b])
    idA_t, idA_b = load_const(cx, "idA_t", identA[:, :], [128, 128])
    dg_t, dg_b = cx.sb("dg_t", [128, 2, 31, 128], BF16)
    for c in range(2):
        for k in range(31):
            S.op("dve", lambda e, c=c, k=k: e.tensor_scalar(out=dg_t[:, c, k, :], in0=idA_t[:, :], scalar1=cw_t[:, c, k:k + 1], scalar2=None, op0=ALU.mult),
                 reads=[idA_b, cw_b], writes=[dg_b])

    psr = PsumRing(cx, 6)
    pss = PsumRing(cx, 2, "pss")

    mod_t, mod_b = cx.sb("mod_t", [128, 48, 2], F32)
    if "mod_in" in cx.links:
        S.dma("sp", mod_t[:], cx.links["mod_in"][:, :, :], writes=[mod_b])
    else:
        _adaln_mod(cx, cT_t, cT_b, ada_w, adab_t, adab_b, mod_t, mod_b, psr, 12, 0)
        S.barrier()
        S.dma("sp", o_mod[:, :, :], mod_t[:], reads=[mod_b])
    gs_t, gs_b = cx.sb("gs_t", [128, KC, 2], F32)
    S.op("dve", lambda e: e.tensor_scalar(out=gs_t[:], in0=mod_t[:, 8:16, :], scalar1=1.0, scalar2=None, op0=ALU.add),
         reads=[mod_b], writes=[gs_b])
    S.op("dve", lambda e: e.tensor_tensor(out=gs_t[:], in0=gs_t[:], in1=n1g_t[:].unsqueeze(2).to_broadcast([128, KC, 2]), op=ALU.mult),
         reads=[gs_b, n1g_b], writes=[gs_b])

    WZ = T + 2 * HALO
    zb_t, zb_b = cx.sb("zb_t", [128, 2, WZ], BF16)
    ub_t, ub_b = cx.sb("ub_t", [128, 2, WZ], F32)
    WZC = NCTX + 2 * HALO
    zc_t, zc_b = cx.sb("zc_t", [128, 2, WZC], BF16)
    uc_t, uc_b = cx.sb("uc_t", [128, 2, WZC], F32)
    if full_ctx:
        S.op("pool", lambda e: e.memset(zc_t[:], 0.0), writes=[zc_b])
        S.op("pool", lambda e: e.memset(uc_t[:], 0.0), writes=[uc_b])

    with ExitStack() as st:
        xb = [cx.sb("xb%d" % i, [128, KC, 512], F32, stack=st) for i in range(2)]
        sqs = [cx.sb("sq%d" % i, [128, 512], F32, stack=st) for i in range(2)]
        rC = [cx.sb("rC%d" % i, [128, 512], F32, stack=st) for i in range(2)]
        rS = [cx.sb("rS%d" % i, [128, 512], F32, stack=st) for i in range(2)]
        rs_t, rs_b = cx.sb("rs_t", [128, 512], F32, stack=st)
        tmp = [cx.sb("tmp%d" % i, [128, 512], F32, stack=st) for i in range(2)]
        hx = [cx.sb("hx%d" % i, [128, KC, 512], BF16, stack=st) for i in range(2)]
        t1 = [cx.sb("t1_%d" % i, [128, 512], F32, stack=st) for i in range(2)]
        t2 = [cx.sb("t2_%d" % i, [128, 512], F32, stack=st) for i in range(2)]
        qo = [cx.sb("qo%d" % i, [128, 512], BF16, stack=st) for i in range(2)]
        sg = [cx.sb("sg%d" % i, [128, 512], F32, stack=st) for i in range(2)]
        uf = [cx.sb("uf%d" % i, [128, 2, 512], BF16, stack=st) for i in range(1)]
        vo = [cx.sb("vo%d" % i, [128, 512], BF16, stack=st) for i in range(2)]
        zo = [cx.sb("zo%d" % i, [128, 512], BF16, stack=st) for i in range(2)]

        blocks = [("main", i * 512, 512) for i in range(4)] + [("halo", 0, 2 * HALO), ("ctx", 0, NCTX)]
        srcs = {"main": xT, "halo": xTh, "ctx": cxT}
        rsall_t, _u = cx.sb("rsall_t", [128, T + 2 * HALO + NCTX], F32, stack=st)
        rs_off = []
        rs_bufs = []
        off_ = 0
        for bi, (kind, c0, n) in enumerate(blocks):
            xt, xbb = xb[bi % 2]
            for kc in range(KC):
                S.dma("sp", xt[:, kc, 0:n], srcs[kind][kc * 128:(kc + 1) * 128, c0:c0 + n], writes=[xbb], also=(kc > 0))
            pst, psb = pss.next()
            for kc in range(KC):
                sq_t, sq_b = sqs[kc % 2]
                S.op("act", lambda e, kc=kc: e.activation(out=sq_t[:, 0:n], in_=xt[:, kc, 0:n], func=AF.Square), reads=[xbb], writes=[sq_b])
                S.op("pe", lambda e, kc=kc: e.matmul(pst[:, 0:n], ones_t[:, :], sq_t[:, 0:n], start=(kc == 0), stop=(kc == KC - 1)),
                     reads=[ones_b, sq_b], writes=[psb], inc=True)
            rb = Buf("rs%d" % bi)
            rsl = rsall_t[:, off_:off_ + n]
            S.op("dve", lambda e: e.tensor_scalar(out=rsl, in0=pst[:, 0:n], scalar1=1.0 / D, scalar2=EPS, op0=ALU.mult, op1=ALU.add),
                 reads=[psb], writes=[rb])
            S.op("act", lambda e: e.activation(out=rsl, in_=rsl, func=AF.Sqrt), reads=[rb], writes=[rb])
            S.op("dve", lambda e: e.reciprocal(out=rsl, in_=rsl), reads=[rb], writes=[rb])
            rs_off.append(off_)
            rs_bufs.append(rb)
            off_ += n
        def loads(bi):
            kind, c0, n = blocks[bi]
            xt, xbb = xb[bi % 2]
            for kc in range(KC):
                S.dma("sp", xt[:, kc, 0:n], srcs[kind][kc * 128:(kc + 1) * 128, c0:c0 + n], writes=[xbb], also=(kc > 0))
            if kind == "main":
                S.dma("sp", rC[bi % 2][0][:, :], ropeC[:, c0:c0 + n], writes=[rC[bi % 2][1]])
                S.dma("sp", rS[bi % 2][0][:, :], ropeS[:, c0:c0 + n], writes=[rS[bi % 2][1]])

        loads(0)
        for bi, (kind, c0, n) in enumerate(blocks):
            if bi + 1 < len(blocks):
                loads(bi + 1)
            xt, xbb = xb[bi % 2]
            j = 1 if kind == "ctx" else 0
            if kind == "main":
                ropeC_t, ropeC_b = rC[bi % 2]
                ropeS_t, ropeS_b = rS[bi % 2]
            rs_t = rsall_t[:, rs_off[bi]:rs_off[bi] + 512] if n == 512 else rsall_t[:, rs_off[bi]:rs_off[bi] + n]
            rs_b = rs_bufs[bi]
            hxt, hxb = hx[bi % 2]
            for kc in range(KC):
                tt, tb = tmp[kc % 2]
                S.op("dve", lambda e, kc=kc, tt=tt: e.tensor_tensor(out=tt[:, 0:n], in0=xt[:, kc, 0:n], in1=rs_t[:, 0:n], op=ALU.mult),
                     reads=[xbb, rs_b], writes=[tb])
                S.op("act", lambda e, kc=kc, tt=tt: e.activation(out=hxt[:, kc, 0:n], in_=tt[:, 0:n], func=AF.Identity,
                                                                 bias=mod_t[:, kc, j:j + 1], scale=gs_t[:, kc, j:j + 1]),
                     reads=[tb, mod_b, gs_b], writes=[hxb])
            if kind == "main":
                S.dma("act", o_hx[:, :, c0:c0 + n], hxt[:, :, 0:n], reads=[hxb])
            elif kind == "ctx" and full_ctx:
                S.dma("act", o_hxc[:, :, :], hxt[:, :, 0:n], reads=[hxb])

            def fm(ci):
                pt, pb = psr.next()
                for kc in range(KC):
                    S.op("pe", lambda e, kc=kc: e.matmul(pt[:, 0:n], wA_t[:, kc, ci * 128:(ci + 1) * 128], hxt[:, kc, 0:n],
                                                         start=(kc == 0), stop=(kc == KC - 1)),
                         reads=[wA_b, hxb], writes=[pb], inc=(kc == KC - 1))
                return pt, pb

            if kind != "halo":
                for qk in range(2):
                    if qk == 0 and kind == "ctx" and not full_ctx:
                        continue
                    for h in range(4):
                        if kind == "main":
                            dsl = o_QT[:, h, c0:c0 + n] if qk == 0 else o_KTh[h][:, c0:c0 + n]
                        else:
                            dsl = (o_QTc if qk == 0 else o_KTc)[:, h, 0:n]
                        pt, pb = fm(qk * 8 + h)
                        qt, qb = qo[h % 2]
                        if kind == "main":
                            p2, p2b = fm(qk * 8 + 4 + h)
                            a1, a1b = t1[h % 2]
                            a2, a2b = t2[h % 2]
                            S.op("dve", lambda e: e.tensor_tensor(out=a1[:, 0:n], in0=pt[:, 0:n], in1=ropeC_t[:, 0:n], op=ALU.mult),
                                 reads=[pb, ropeC_b], writes=[a1b])
                            S.op("dve", lambda e: e.tensor_tensor(out=a2[:, 0:n], in0=p2[:, 0:n], in1=ropeS_t[:, 0:n], op=ALU.mult),
                                 reads=[p2b, ropeS_b], writes=[a2b])
                            S.op("pool", lambda e: e.tensor_tensor(out=qt[:, 0:n], in0=a1[:, 0:n], in1=a2[:, 0:n], op=ALU.add),
                                 reads=[a1b, a2b], writes=[qb])
                            S.dma("pool", dsl, qt[:, 0:n], reads=[qb])
                        else:
                            S.op("act", lambda e: e.activation(out=qt[:, 0:n], in_=pt[:, 0:n], func=AF.Copy), reads=[pb], writes=[qb])
                            S.dma("act", dsl, qt[:, 0:n], reads=[qb])
            do_cp = (kind != "ctx") or full_ctx
            if do_cp:
                if kind == "main":
                    zt, zbb, ut, ubb, col = zb_t, zb_b, ub_t, ub_b, HALO + c0
                elif kind == "ctx":
                    zt, zbb, ut, ubb, col = zc_t, zc_b, uc_t, uc_b, HALO
                for c in range(2):
                    pa, pab = fm(16 + c)
                    pg, pgb = fm(18 + c)
                    s_t, s_b = sg[c % 2]
                    S.op("act", lambda e: e.activation(out=s_t[:, 0:n], in_=pg[:, 0:n], func=AF.Sigmoid), reads=[pgb], writes=[s_b])
                    pu, pub = fm(20 + c)
                    if kind == "halo":
                        for (lo, dcol) in ((0, 0), (HALO, HALO + T)):
                            S.op("dve", lambda e, lo=lo, dcol=dcol: e.tensor_tensor(out=zb_t[:, c, dcol:dcol + HALO], in0=pa[:, lo:lo + HALO],
                                                                                     in1=s_t[:, lo:lo + HALO], op=ALU.mult),
                                 reads=[pab, s_b], writes=[zb_b])
                            S.op("dve", lambda e, lo=lo, dcol=dcol: e.tensor_tensor(out=zb_t[:, c, dcol:dcol + HALO], in0=zb_t[:, c, dcol:dcol + HALO],
                                                                                     in1=hmask_t[:, lo:lo + HALO], op=ALU.mult),
                                 reads=[zb_b, hmask_b], writes=[zb_b])
                            S.op("dve", lambda e, lo=lo, dcol=dcol: e.tensor_tensor(out=ub_t[:, c, dcol:dcol + HALO], in0=pu[:, lo:lo + HALO],
                                                                                     in1=hmask_t[:, lo:lo + HALO], op=ALU.mult),
                                 reads=[pub, hmask_b], writes=[ub_b])
                    else:
                        S.op("dve", lambda e: e.tensor_tensor(out=zt[:, c, col:col + n], in0=pa[:, 0:n], in1=s_t[:, 0:n], op=ALU.mult),
                             reads=[pab, s_b], writes=[zbb])
                        S.op("act", lambda e: e.activation(out=ut[:, c, col:col + n], in_=pu[:, 0:n], func=AF.Copy), reads=[pub], writes=[ubb])
            if kind == "main" or (kind == "ctx" and full_ctx):
                uft, ufb = uf[0]
                for c in range(2):
                    pt, pb = fm(22 + c)
                    S.op("act", lambda e, c=c: e.activation(out=uft[:, c, 0:n], in_=pt[:, 0:n], func=AF.Copy), reads=[pb], writes=[ufb])
                for tt_ in range(n // 128):
                    pt, pb = psr.next()
                    for c in range(2):
                        S.op("pe", lambda e, c=c: e.matmul(pt[:, c * 256:(c + 1) * 256], uft[:, c, tt_ * 128:(tt_ + 1) * 128], cdft_t[:, :],
                                                           start=True, stop=True),
                             reads=[ufb, cdft_b], writes=[pb], inc=(c == 1))
                    z_t, z_b = zo[tt_ % 2]
                    S.op("act", lambda e: e.activation(out=z_t[:, :].rearrange("p (r c k) -> p r c k", r=2, c=2),
                                                       in_=pt[:, :].rearrange("p (c r k) -> p r c k", c=2, r=2), func=AF.Copy),
                         reads=[pb], writes=[z_b])
                    if kind == "main":
                        for q4 in range(4):
                            S.dma("act", o_Zq[q4][c0 + tt_ * 128:c0 + (tt_ + 1) * 128, :], z_t[:, q4 * 128:(q4 + 1) * 128], reads=[z_b])
                    else:
                        S.dma("act", o_Zc[c0 + tt_ * 128:c0 + (tt_ + 1) * 128, :], z_t[:, :], reads=[z_b])
            if kind != "halo":
                for tt_ in range(n // 128):
                    pt, pb = psr.next()
                    for kc in range(KC):
                        S.op("pe", lambda e, kc=kc: e.matmul(pt[:, :], hxt[:, kc, tt_ * 128:(tt_ + 1) * 128], wV_t[:, kc, :],
                                                             start=(kc == 0), stop=(kc == KC - 1)),
                             reads=[hxb, wV_b], writes=[pb], inc=(kc == KC - 1))
                    v_t, v_b = vo[tt_ % 2]
                    S.op("dve", lambda e: e.tensor_copy(out=v_t[:, :], in_=pt[:, :]), reads=[pb], writes=[v_b])
                    if kind == "main":
                        for h in range(4):
                            S.dma("sp", o_Vh[h][c0 + tt_ * 128:c0 + (tt_ + 1) * 128, :], v_t[:, h * 128:(h + 1) * 128], reads=[v_b])
                    else:
                        S.dma("sp", o_Vc[c0 + tt_ * 128:c0 + (tt_ + 1) * 128, :], v_t[:, :], reads=[v_b])

    S.barrier()
    if getattr(cx, "after_blocks", None) is not None:
        cx.after_blocks()
    segs = [(zb_t, zb_b, ub_t, ub_b, T, pinv, o_yc, o_yp)]
    if full_ctx:
        segs.append((zc_t, zc_b, uc_t, uc_b, NCTX, pinvc, o_ycc, o_ypc))
    with ExitStack() as st:
        acc_t, acc_b = cx.sb("acc_t", [128, 2, T], F32, stack=st)
        pin_t, pin_b = cx.sb("pin_t", [128, 2, T], F32, stack=st)
        w2_t, w2_b = cx.sb("w2_t", [128, 2, T + 2 * HALO], F32, stack=st)
        w4_t, w4_b = cx.sb("w4_t", [128, 2, T + 2 * HALO], F32, stack=st)
        w8_t, w8_b = cx.sb("w8_t", [128, T + 2 * HALO], F32, stack=st)
        s1 = [cx.sb("s1_%d" % i, [128, 512], F32, stack=st) for i in range(2)]
        s2 = [cx.sb("s2_%d" % i, [128, 512], F32, stack=st) for i in range(2)]
        s3 = [cx.sb("s3_%d" % i, [128, 512], F32, stack=st) for i in range(2)]
        yo = [cx.sb("yo%d" % i, [128, 2, 512], BF16, stack=st) for i in range(2)]
        dd = [cx.sb("dd%d" % i, [128, 2, 512], BF16, stack=st) for i in range(2)]
        po = [cx.sb("po%d" % i, [128, 2, 512], BF16, stack=st) for i in range(2)]
        for (zt, zbb, ut, ubb, n, pinv_d, oyc, oyp) in segs:
            off = HALO - 15
            for blk in range((n + 511) // 512):
                b0 = blk * 512
                nb = min(512, n - b0)
                for c in range(2):
                    pt, pb = psr.next()
                    for k in range(31):
                        S.op("pe", lambda e, c=c, k=k: e.matmul(pt[:, 0:nb], dg_t[:, c, k, :], zt[:, c, off + k + b0:off + k + b0 + nb],
                                                                start=(k == 0), stop=(k == 30)),
                             reads=[dg_b, zbb], writes=[pb], inc=(k == 30))
                    S.op("act", lambda e, c=c: e.activation(out=acc_t[:, c, b0:b0 + nb], in_=pt[:, 0:nb], func=AF.Identity, bias=cvec_t[:, c, 0:1]),
                         reads=[pb, cvec_b], writes=[acc_b])
            for blk in range((n + 511) // 512):
                b0 = blk * 512
                nb = min(512, n - b0)
                sq_t2, sq_b2 = s1[blk % 2]
                psum_, psumb = pss.next()
                pssq, pssqb = pss.next()
                for c in range(2):
                    S.op("pe", lambda e, c=c: e.matmul(psum_[:, 0:nb], ones_t[:, :], acc_t[:, c, b0:b0 + nb], start=(c == 0), stop=(c == 1)),
                         reads=[ones_b, acc_b], writes=[psumb], inc=(c == 1))
                for c in range(2):
                    S.op("act", lambda e, c=c: e.activation(out=sq_t2[:, 0:nb], in_=acc_t[:, c, b0:b0 + nb], func=AF.Square),
                         reads=[acc_b], writes=[sq_b2])
                    S.op("pe", lambda e, c=c: e.matmul(pssq[:, 0:nb], ones_t[:, :], sq_t2[:, 0:nb], start=(c == 0), stop=(c == 1)),
                         reads=[ones_b, sq_b2], writes=[pssqb], inc=True)
                mean_t, mean_b = s2[blk % 2]
                var_t, var_b = s3[blk % 2]
                S.op("dve", lambda e: e.tensor_scalar(out=mean_t[:, 0:nb], in0=psum_[:, 0:nb], scalar1=1.0 / 256, scalar2=None, op0=ALU.mult),
                     reads=[psumb], writes=[mean_b])
                S.op("dve", lambda e: e.tensor_tensor(out=var_t[:, 0:nb], in0=mean_t[:, 0:nb], in1=mean_t[:, 0:nb], op=ALU.mult),
                     reads=[mean_b], writes=[var_b])
                S.op("dve", lambda e: e.scalar_tensor_tensor(out=var_t[:, 0:nb], in0=pssq[:, 0:nb], scalar=1.0 / 256, in1=var_t[:, 0:nb],
                                                             op0=ALU.mult, op1=ALU.subtract),
                     reads=[pssqb, var_b], writes=[var_b])
                S.op("dve", lambda e: e.tensor_scalar(out=var_t[:, 0:nb], in0=var_t[:, 0:nb], scalar1=EPS, scalar2=None, op0=ALU.add),
                     reads=[var_b], writes=[var_b])
                S.op("act", lambda e: e.activation(out=var_t[:, 0:nb], in_=var_t[:, 0:nb], func=AF.Sqrt), reads=[var_b], writes=[var_b])
                S.op("dve", lambda e: e.reciprocal(out=var_t[:, 0:nb], in_=var_t[:, 0:nb]), reads=[var_b], writes=[var_b])
                y_t, y_b = yo[blk % 2]
                for c in range(2):
                    S.op("dve", lambda e, c=c: e.tensor_tensor(out=sq_t2[:, 0:nb], in0=acc_t[:, c, b0:b0 + nb], in1=mean_t[:, 0:nb], op=ALU.subtract),
                         reads=[acc_b, mean_b], writes=[sq_b2])
                    S.op("dve", lambda e, c=c: e.tensor_tensor(out=sq_t2[:, 0:nb], in0=sq_t2[:, 0:nb], in1=var_t[:, 0:nb], op=ALU.mult),
                         reads=[sq_b2, var_b], writes=[sq_b2])
                    S.op("act", lambda e, c=c: e.activation(out=y_t[:, c, 0:nb], in_=sq_t2[:, 0:nb], func=AF.Silu,
                                                            bias=cvec_t[:, c, 2:3], scale=cvec_t[:, c, 1:2]),
                         reads=[sq_b2, cvec_b], writes=[y_b])
                S.dma("act", oyc[:, :, b0:b0 + nb], y_t[:, :, 0:nb], reads=[y_b])
            S.dma("sp", pin_t[:, :, 0:n], pinv_d[:, :, :], writes=[pin_b])
            W = n + 2 * HALO
            S.op("dve", lambda e: e.tensor_tensor(out=w2_t[:, :, 1:W], in0=ut[:, :, 0:W - 1], in1=ut[:, :, 1:W], op=ALU.add),
                 reads=[ubb], writes=[w2_b])
            S.op("dve", lambda e: e.tensor_tensor(out=w4_t[:, :, 2:W - 1], in0=w2_t[:, :, 1:W - 2], in1=w2_t[:, :, 3:W], op=ALU.add),
                 reads=[w2_b], writes=[w4_b])
            S.op("dve", lambda e: e.tensor_tensor(out=w8_t[:, 4:W - 3], in0=w4_t[:, 1, 2:W - 5], in1=w4_t[:, 1, 6:W - 1], op=ALU.add),
                 reads=[w4_b], writes=[w8_b])
            H = HALO
            S.op("dve", lambda e: e.tensor_copy(out=acc_t[0:64, 0, 0:n], in_=w2_t[0:64, 0, H:H + n]), reads=[w2_b, acc_b], writes=[acc_b])
            S.op("dve", lambda e: e.tensor_copy(out=acc_t[64:128, 0, 0:n], in_=w4_t[64:128, 0, H:H + n]), reads=[w4_b, acc_b], writes=[acc_b])
            S.op("dve", lambda e: e.tensor_copy(out=acc_t[0:64, 1, 0:n], in_=w8_t[0:64, H:H + n]), reads=[w8_b, acc_b], writes=[acc_b])
            S.op("dve", lambda e: e.tensor_tensor(out=acc_t[64:128, 1, 0:n], in0=w8_t[64:128, H - 4:H - 4 + n], in1=w8_t[64:128, H + 4:H + 4 + n], op=ALU.add),
                 reads=[w8_b, acc_b], writes=[acc_b])
            S.op("dve", lambda e: e.tensor_tensor(out=acc_t[:, :, 0:n], in0=acc_t[:, :, 0:n], in1=pin_t[:, :, 0:n], op=ALU.mult),
                 reads=[acc_b, pin_b], writes=[acc_b])
            for blk in range((n + 511) // 512):
                b0 = blk * 512
                nb = min(512, n - b0)
                d_t, d_b = dd[blk % 2]
                S.op("dve", lambda e: e.tensor_tensor(out=d_t[:, :, 0:nb], in0=acc_t[:, :, b0:b0 + nb], in1=ut[:, :, H + b0:H + b0 + nb], op=ALU.subtract),
                     reads=[acc_b, ubb], writes=[d_b])
                p_t, p_b = po[blk % 2]
                for c in range(2):
                    pt, pb = psr.next()
                    S.op("pe", lambda e, c=c: e.matmul(pt[:, 0:nb], pw_t[:, c, :], d_t[:, c, 0:nb], start=True, stop=True),
                         reads=[pw_b, d_b], writes=[pb])
                    S.op("act", lambda e, c=c: e.activation(out=p_t[:, c, 0:nb], in_=pt[:, 0:nb], func=AF.Identity, scale=cvec_t[:, c, 3:4]),
                         reads=[pb, cvec_b], writes=[p_b])
                S.dma("act", oyp[:, :, b0:b0 + nb], p_t[:, :, 0:nb], reads=[p_b])
    if standalone:
        S.finish(None)
    return cx


def fm_vec(v):
    v = np.asarray(v, np.float32)
    return np.ascontiguousarray(v.reshape(-1, 128).T)


OFF_F, OFF_Q, OFF_K, OFF_V, OFF_C, OFF_P, OFF_G = 0, 256, 768, 1280, 1792, 2304, 2560


def wA_layout(w_in):
    cols = []
    swap128 = np.concatenate([SWAP64, 64 + SWAP64])
    for base in (OFF_Q, OFF_K):
        for h in range(4):
            cols.append(base + h * 128 + np.arange(128))
        for h in range(4):
            cols.append(base + h * 128 + swap128)
    cols.append(OFF_C + np.arange(512))
    cols.append(OFF_P + np.arange(256))
    cols.append(OFF_F + np.arange(256))
    cols = np.concatenate(cols)
    return np.ascontiguousarray(w_in[:, cols])


def host_A(inp, l, xT_all, ctxT_all):
    w_in = np.asarray(inp["w_in"][l])
    wA = wA_layout(w_in)
    wV = np.ascontiguousarray(w_in[:, OFF_V:OFF_C])
    ada_w = np.ascontiguousarray(inp["ada_w"][l])
    ada_b = fm_vec(inp["ada_b"][l])
    n1g = fm_vec(inp["norm1_g"][l])
    cw = np.ascontiguousarray(np.asarray(inp["conv_dw_w"][l]).T.reshape(2, 128, 31).transpose(1, 0, 2))
    cvec = np.stack([fm_vec(inp["conv_dw_b"][l]), fm_vec(inp["conv_ln_g"][l]), fm_vec(inp["conv_ln_b"][l]),
                     fm_vec(inp["pool_scale"][l])], axis=2).astype(np.float32)
    pw_in = np.asarray(inp["pool_w"][l])
    pw = np.zeros((128, 2, 128), np.float32)
    for c in range(2):
        for g in range(2):
            pw[g * 64:(g + 1) * 64, c, g * 64:(g + 1) * 64] = pw_in[2 * c + g]
    cdft = chan_dft_table()
    pinvc = pool_invcnt(0, NCTX, NCTX)
    maps = []
    for i in range(NCORES):
        b, j = i // 4, i % 4
        t0 = j * T
        hm = np.zeros((128, 2 * HALO), np.float32)
        if j > 0:
            hm[:, 0:HALO] = 1.0
        if j < 3:
            hm[:, HALO:] = 1.0
        cT = np.stack([fm_vec(inp["c"][b]), fm_vec(inp["c_ctx"])], axis=2)
        rC, rS = rope_tables(t0, T)
        m = dict(cT=np.ascontiguousarray(cT), ada_w=ada_w, ada_b=ada_b, identA=np.eye(128, dtype=np.float32),
                 n1g=n1g, wA=wA, wV=wV, ropeC=rC, ropeS=rS, hmask=hm, cw=cw, cvec=cvec,
                 pinv=pool_invcnt(t0, T, SEQ), pinvc=pinvc, pw=pw, cdft=cdft)
        if xT_all is not None:
            xTh = np.zeros((D, 2 * HALO), np.float32)
            if j > 0:
                xTh[:, 0:HALO] = xT_all[b][:, t0 - HALO:t0]
            if j < 3:
                xTh[:, HALO:] = xT_all[b][:, t0 + T:t0 + T + HALO]
            m.update(xT=np.ascontiguousarray(xT_all[b][:, t0:t0 + T]), xTh=xTh, cxT=np.ascontiguousarray(ctxT_all[b]))
        maps.append(m)
    return maps


_PROGS = {}


def get_prog(key, builder, *args):
    if key not in _PROGS:
        _PROGS[key] = builder(*args)
    return _PROGS[key]


def run_prog(cx, maps):
    res = run_bass_kernel_spmd(cx.nc, maps, core_ids=list(range(NCORES)))
    return res.results


NKEY = NCTX + SEQ
NKC = NKEY // 128
SCALE_F = 1.0 / math.sqrt(SEQ * 64.0)
SCALE_FC = 1.0 / math.sqrt(NCTX * 64.0)


def fft_tables(j):
    n1 = np.arange(128)
    ph = 2 * np.pi * np.outer(n1, n1) / 128.0
    C, Sn = np.cos(ph), np.sin(ph)
    tabA = np.stack([np.concatenate([C, Sn], 1), np.concatenate([-Sn, C], 1)], axis=1).astype(np.float32)
    n2 = np.arange(64)[:, None, None]
    k1 = np.arange(128)[None, :, None]
    k2 = (16 * j + np.arange(16))[None, None, :]
    th = 2 * np.pi * n2 * (k1 + 128 * k2) / float(SEQ)
    M = np.stack([np.cos(th) * SCALE_F, -np.sin(th) * SCALE_F], axis=2)
    tabC = np.concatenate([M, M], axis=0).astype(np.float32)
    l = np.arange(NCTX)
    a = 2 * np.pi * np.outer(l, l) / float(NCTX)
    Cc = (np.cos(a) * SCALE_FC).reshape(2, 128, NCTX).transpose(1, 0, 2)
    Sc = (-np.sin(a) * SCALE_FC).reshape(2, 128, NCTX).transpose(1, 0, 2)
    tabX = np.stack([Cc, Sc], axis=1).astype(np.float32)
    return tabA, tabC, np.ascontiguousarray(tabX)


def build_B(full_ctx, lam_init, dbg=False, only=None, cx=None):
    standalone = cx is None
    if standalone:
        cx = Ctx("B")
    nc, S = cx.nc, cx.S
    QT = cx.inp("QT", [128, 4, T], BF16)
    KTgh = [cx.inp("KTg%d" % h, [512, T], BF16) for h in range(4)]
    Vgh = [cx.inp("Vg%d" % h, [SEQ, 128], BF16) for h in range(4)]
    Zgq = [cx.inp("Zg%d" % q, [SEQ, 128], BF16) for q in range(4)]
    KTc = cx.inp("KTc", [128, 4, NCTX], BF16); Vc = cx.inp("Vc", [NCTX, 512], BF16)
    ycT = cx.inp("ycT", [128, 2, T], BF16); ypT = cx.inp("ypT", [128, 2, T], BF16); hxT = cx.inp("hxT", [128, KC, T], BF16)
    xT = cx.inp("xT", [D, T]); mod = cx.inp("mod", [128, 48, 2])
    wg = cx.inp("wg", [D, 4, D]); wo = cx.inp("wo", [1280, D]); wout = cx.inp("wout", [D, D])
    lamv = cx.inp("lamv", [4, 64]); subg = cx.inp("subg", [128, 1]); n2g = cx.inp("n2g", [128, KC])
    tabA = cx.inp("tabA", [128, 2, 256]); tabC = cx.inp("tabC", [128, 128, 2, 16]); ident = cx.inp("ident", [128, 128])
    o_xm = cx.out("xmT", [D, T]); o_h2 = cx.out("hx2T", [128, KC, T], BF16)
    if full_ctx:
        QTc = cx.inp("QTc", [128, 4, NCTX], BF16); Zc = cx.inp("Zc", [NCTX, 512], BF16)
        ycTc = cx.inp("ycTc", [128, 2, NCTX], BF16); ypTc = cx.inp("ypTc", [128, 2, NCTX], BF16); hxTc = cx.inp("hxTc", [128, KC, NCTX], BF16)
        cxT = cx.inp("cxT", [D, NCTX]); tabX = cx.inp("tabX", [128, 2, 2, NCTX])
        o_cxm = cx.out("cxmT", [D, NCTX]); o_ch2 = cx.out("chx2T", [128, KC, NCTX], BF16)

    mod_t, mod_b = load_const(cx, "mod_t", mod[:, :, :], [128, 48, 2])
    n2g_t, n2g_b = load_const(cx, "n2g_t", n2g[:, :], [128, KC])
    subg_t, subg_b = load_const(cx, "subg_t", subg[:, :], [128, 1])
    ones_t, ones_b = cx.sb("ones_t", [128, 128], F32)
    S.op("dve", lambda e: e.memset(ones_t[:], 1.0), writes=[ones_b])
    id_t, id_b = cx.sb("id_t", [128, 128], BF16)
    S.dma("pool", id_t[:], ident[:, :], writes=[id_b])
    lv_t, lv_b = cx.sb("lv_t", [128, 4, 64], F32)
    for r in range(4):
        S.dma("sp", lv_t[:, r, :], lamv[r:r + 1, :].partition_broadcast(128), writes=[lv_b], also=(r > 0))
    lp_t, lp_b = cx.sb("lp_t", [128, 2, 64], F32)
    S.op("dve", lambda e: e.tensor_tensor(out=lp_t[:], in0=lv_t[:, 0:4:2, :], in1=lv_t[:, 1:4:2, :], op=ALU.mult), reads=[lv_b], writes=[lp_b])
    ls_t, ls_b = cx.sb("ls_t", [128, 2], F32)
    S.op("dve", lambda e: e.reduce_sum(out=ls_t[:], in_=lp_t[:], axis=mybir.AxisListType.X), reads=[lp_b], writes=[ls_b])
    S.op("act", lambda e: e.activation(out=ls_t[:], in_=ls_t[:], func=AF.Exp), reads=[ls_b], writes=[ls_b])
    nlam_t, nlam_b = cx.sb("nlam_t", [128, 1], F32)
    S.op("dve", lambda e: e.tensor_tensor(out=nlam_t[:], in0=ls_t[:, 1:2], in1=ls_t[:, 0:1], op=ALU.subtract), reads=[ls_b], writes=[nlam_b])
    S.op("dve", lambda e: e.tensor_scalar(out=nlam_t[:], in0=nlam_t[:], scalar1=-float(lam_init), scalar2=None, op0=ALU.add),
         reads=[nlam_b], writes=[nlam_b])
    S.op("dve", lambda e: e.tensor_scalar(out=subg_t[:], in0=subg_t[:], scalar1=float(1.0 - lam_init), scalar2=None, op0=ALU.mult),
         reads=[subg_b], writes=[subg_b])
    gs_t, gs_b = cx.sb("gs_t", [128, KC, 2], F32)
    S.op("dve", lambda e: e.tensor_scalar(out=gs_t[:], in0=mod_t[:, 32:40, :], scalar1=1.0, scalar2=None, op0=ALU.add), reads=[mod_b], writes=[gs_b])
    S.op("dve", lambda e: e.tensor_tensor(out=gs_t[:], in0=gs_t[:], in1=n2g_t[:].unsqueeze(2).to_broadcast([128, KC, 2]), op=ALU.mult),
         reads=[gs_b, n2g_b], writes=[gs_b])

    yf_t, yf_b = cx.sb("yf_t", [128, 2, T], BF16)

    oa_t, oa_b = cx.sb("oa_t", [128, 4, T], BF16)
    if full_ctx:
        yfc_t, yfc_b = cx.sb("yfc_t", [128, 2, NCTX], BF16)
        oac_t, oac_b = cx.sb("oac_t", [128, 4, NCTX], BF16)

    with ExitStack() as st:
        zs_t, zs_b = cx.sb("zs_t", [128, 64 * 512], BF16, stack=st)
        tt_t, tt_b = cx.sb("tt_t", [128, 128 * 256], BF16, stack=st)
        tA_t, tA_b = cx.sb("tA_t", [128, 2, 256], BF16, stack=st)
        tC_t, tC_b = cx.sb("tC_t", [128, 128 * 32], BF16, stack=st)
        S.dma("pool", tA_t[:], tabA[:, :, :], writes=[tA_b])
        S.dma("pool", tC_t[:], tabC[:, :, :, :].rearrange("p a b c -> p (a b c)"), writes=[tC_b])
        if cx.after_fft_tables is not None:
            cx.after_fft_tables()
        zdst = zs_t[:, :].rearrange("p (n c) -> p n c", c=512)
        for q4 in range(4):
            S.dma("sp", zdst[:, :, q4 * 128:(q4 + 1) * 128], Zgq[q4][:, :].rearrange("(a b) c -> a b c", b=64), writes=[zs_b], also=(q4 > 0))
        psA = PsumRing(cx, 3, "psA", stack=st)
        psC = [cx.ps("psC%d" % i, [128, 512], F32, stack=st) for i in range(4)]
        zv = zs_t[:, :].rearrange("p (n r c k) -> p r c n k", n=64, r=2, c=2)
        ttv = tt_t[:, :].rearrange("p (k x) -> p k x", x=256)
        for cp2 in range(64):
            pt, pb = psA.next()
            for s_ in range(2):
                cp = cp2 * 2 + s_
                for r in range(2):
                    for c2 in range(2):
                        S.op("pe", lambda e, r=r, cp=cp, s_=s_, c2=c2: e.matmul(pt[c2 * 64:(c2 + 1) * 64, s_ * 256:(s_ + 1) * 256], zv[:, r, c2, :, cp],
                                                                                 tA_t[:, r, :], start=(r == 0), stop=(r == 1), tile_position=(0, c2 * 64)),
                             reads=[zs_b, tA_b], writes=[pb], inc=(s_ == 1 and r == 1 and c2 == 1))
            eng = "act" if cp2 % 2 == 0 else "dve"
            if eng == "act":
                S.op("act", lambda e: e.activation(out=tt_t[:, cp2 * 512:(cp2 + 1) * 512], in_=pt[:, :], func=AF.Copy), reads=[pb], writes=[tt_b])
            else:
                S.op("dve", lambda e: e.tensor_copy(out=tt_t[:, cp2 * 512:(cp2 + 1) * 512], in_=pt[:, :]), reads=[pb], writes=[tt_b])
        if only == "fft":
            d_tt = cx.out("dbg_tt", [128, 128 * 256], BF16)
            S.dma("sp", d_tt[:, :], tt_t[:, :], reads=[tt_b])
        tcv = tC_t[:, :].rearrange("p (k r j) -> p k r j", r=2, j=16)
        psCb = [Buf("psCb%d" % i) for i in range(4)]
        for c2 in range(2):
            for k1 in range(128):
                bank = k1 // 32
                col = (k1 % 32) * 16
                for r in range(2):
                    S.op("pe", lambda e, r=r, k1=k1: e.matmul(psC[bank][0][:, col:col + 16], ttv[c2 * 64:(c2 + 1) * 64, :, r * 128 + k1],
                                                             tcv[c2 * 64:(c2 + 1) * 64, k1, r, :], start=(r == 0), stop=(r == 1),
                                                             tile_position=(c2 * 64, 0)),
                         reads=[tt_b, tC_b], writes=[psCb[bank]], inc=(r == 1 and k1 % 32 == 31))
            for bank in range(4):
                dst = yf_t[:, c2, :].rearrange("p (j k) -> p k j", k=128)[:, bank * 32:(bank + 1) * 32, :]
                srcp = psC[bank][0][:, :].rearrange("p (k j) -> p k j", j=16)
                if bank % 2 == 0:
                    S.op("act", lambda e: e.activation(out=dst, in_=srcp, func=AF.Copy), reads=[psCb[bank]], writes=[yf_b])
                else:
                    S.op("dve", lambda e: e.tensor_copy(out=dst, in_=srcp), reads=[psCb[bank]], writes=[yf_b])
        if full_ctx:
            zc_t, zc_b = cx.sb("zc_t", [128, 2, 512], BF16, stack=st)
            tX_t, tX_b = cx.sb("tX_t", [128, 2, 2, NCTX], BF16, stack=st)
            S.dma("sp", zc_t[:], Zc[:, :].rearrange("(a p) c -> p a c", p=128), writes=[zc_b])
            S.dma("pool", tX_t[:], tabX[:, :, :, :], writes=[tX_b])
            for c2 in range(2):
                pt, pb = psA.next()
                i = 0
                for r in range(2):
                    for tl in range(2):
                        S.op("pe", lambda e, r=r, tl=tl, i=i: e.matmul(pt[:, 0:NCTX], zc_t[:, tl, r * 256 + c2 * 128:r * 256 + (c2 + 1) * 128],
                                                                       tX_t[:, r, tl, :], start=(i == 0), stop=(i == 3)),
                             reads=[zc_b, tX_b], writes=[pb], inc=(i == 3))
                        i += 1
                S.op("act", lambda e: e.activation(out=yfc_t[:, c2, :], in_=pt[:, 0:NCTX], func=AF.Copy), reads=[pb], writes=[yfc_b])

    S.barrier()
    if only == "fft":
        d_yf = cx.out("dbg_yf", [128, 2, T], BF16)
        S.dma("sp", d_yf[:, :, :], yf_t[:], reads=[yf_b])
        if standalone:
            S.finish(None)
        return cx
    wg_t, wg_b = cx.sb("wg_t", [128, KC, 4, D], BF16)
    for kc in range(KC):
        S.dma("pool", wg_t[:, kc, :, :], wg[kc * 128:(kc + 1) * 128, :, :], writes=[wg_b], also=True)
    with ExitStack() as st:
        kt = [cx.sb("kt%d" % i, [128, NKEY], BF16, stack=st) for i in range(2)]
        vt = [cx.sb("vt%d" % i, [128, NKC, 128], BF16, stack=st) for i in range(2)]
        qt = [cx.sb("qt%d" % i, [128, T + NCTX], BF16, stack=st) for i in range(2)]
        pT = [cx.sb("pT%d" % i, [128, 2, 512], BF16, stack=st) for i in range(3)]
        psS = [cx.ps("psS%d" % i, [128, 1024], F32, stack=st) for i in range(2)]
        psO = [cx.ps("psO%d" % i, [128, 512], F32, stack=st) for i in range(2)]
        psE = [cx.ps("psE%d" % i, [128, 512], F32, stack=st) for i in range(1)]
        psD, psD_b = cx.ps("psD", [128, 512], F32, stack=st)
        dsb = [cx.sb("dsb%d" % i, [64, 512], F32, stack=st) for i in range(2)]
        on32_t, on32_b = cx.sb("on32_t", [128, 32], BF16, stack=st)
        S.op("pool", lambda e: e.memset(on32_t[:], 1.0), writes=[on32_b])
        selr = []
        for m in range(2):
            sl_t, sl_b = cx.sb("selr%d" % m, [64, 128], F32, stack=st)
            S.op("pool", lambda e: e.memset(sl_t[:], 0.0), writes=[sl_b])
            S.op("pool", lambda e, m=m: e.memset(sl_t[m * 32:m * 32 + 1, :], 1.0), reads=[sl_b], writes=[sl_b])
            selr.append((sl_t, sl_b))
        osb = [cx.sb("osb%d" % i, [128, 2, 512], F32, stack=st) for i in range(2)]
        rc = [cx.sb("rc%d" % i, [128, 512], F32, stack=st) for i in range(2)]
        oo_t, oo_b = cx.sb("at_oo_t", [128, 512], F32, stack=st)
        o1_t, o1_b = cx.sb("at_o1_t", [128, 512], F32, stack=st)
        sq_t, sq_b = cx.sb("at_sq_t", [128, 512], F32, stack=st)
        rs_t, rs_b = cx.sb("at_rs_t", [128, 512], F32, stack=st)
        gi = 0
        pending_epi = []
        for h in range(4):
            k_t, k_b = kt[h % 2]
            v_t, v_b = vt[h % 2]
            q_t, q_b = qt[h % 2]
            S.dma("sp", k_t[:, 0:NCTX], KTc[:, h, :], writes=[k_b])
            gk = [cx.dram_bufs["KTg%d" % h]] if ("KTg%d" % h) in cx.dram_bufs else []
            gv = [cx.dram_bufs["Vg%d" % h]] if ("Vg%d" % h) in cx.dram_bufs else []
            S.dma("sp", k_t[:, NCTX:NKEY].rearrange("p (r t) -> p r t", r=4), KTgh[h][:, :].rearrange("(r p) t -> p r t", p=128), reads=gk, writes=[k_b], also=True)
            S.dma("sp", v_t[:, 0:2, :], Vc[:, h * 128:(h + 1) * 128].rearrange("(c p) d -> p c d", p=128), writes=[v_b])
            vsrc = Vgh[h][:, :].rearrange("(c p) d -> p c d", p=128)
            for g in range(2):
                S.dma("sp", v_t[:, 2 + g * 32:2 + (g + 1) * 32, :], vsrc[:, g * 32:(g + 1) * 32, :], reads=gv, writes=[v_b], also=True)
            S.dma("sp", q_t[:, 0:T], QT[:, h, :], writes=[q_b])
            if full_ctx:
                S.dma("sp", q_t[:, T:T + NCTX], QTc[:, h, :], writes=[q_b], also=True)
            groups = [(g * 512, 512, NKC, oa_t, oa_b, g * 512) for g in range(4)]
            if full_ctx:
                groups.append((T, NCTX, 2, oac_t, oac_b, 0))
            for (q0, nq, nkc, dst_t, dst_b, d0) in groups:
                psOb = [psO[0][1], psO[1][1]]

                def qk(kc, q0=q0, nq=nq):
                    ps_t, ps_b = psS[kc % 2]
                    for m in range(2):
                        S.op("pe", lambda e, m=m: e.matmul(ps_t[:, m * 512:m * 512 + nq], k_t[m * 64:(m + 1) * 64, kc * 128:(kc + 1) * 128],
                                                           q_t[m * 64:(m + 1) * 64, q0:q0 + nq], start=True, stop=True, tile_position=(m * 64, 0)),
                             reads=[k_b, q_b], writes=[ps_b], inc=(m == 1))

                def den(kc, p_t, p_b, nq=nq, nkc=nkc):
                    for m in range(2):
                        S.op("pe", lambda e, m=m: e.matmul(psD[m * 32:(m + 1) * 32, 0:nq], on32_t[:, :], p_t[:, m, 0:nq], start=(kc == 0), stop=(kc == nkc - 1),
                                                           tile_position=(0, m * 32)),
                             reads=[p_b, on32_b], writes=[psD_b], inc=(m == 1))

                qk(0)
                den_prev = None
                for kc in range(nkc):
                    if kc + 1 < nkc:
                        qk(kc + 1)
                    if den_prev is not None:
                        den(*den_prev)
                    ps_t, ps_b = psS[kc % 2]
                    p_t, p_b = pT[kc % 3]
                    S.op("act", lambda e: e.activation(out=p_t[:, :, 0:nq], in_=ps_t[:, :].rearrange("p (m q) -> p m q", m=2)[:, :, 0:nq],
                                                       func=AF.Exp, scale=0.125),
                         reads=[ps_b], writes=[p_b])
                    for m in range(2):
                        S.op("pe", lambda e, m=m: e.matmul(psO[m][0][:, 0:nq], v_t[:, kc, :], p_t[:, m, 0:nq], start=(kc == 0), stop=(kc == nkc - 1)),
                             reads=[p_b, v_b], writes=[psOb[m]], inc=(m == 1))
                    den_prev = (kc, p_t, p_b)
                    if kc == nkc - 1:
                        den(*den_prev)
                    if pending_epi and (kc in (4, 10, 18) or kc == nkc - 1):
                        while pending_epi:
                            pending_epi.pop(0)()
                            if kc != nkc - 1:
                                break
                ob_t, ob_b = osb[gi % 2]
                gi += 1
                d_t, d_b = dsb[(gi - 1) % 2]
                S.op("dve", lambda e: e.tensor_copy(out=d_t[:, 0:nq], in_=psD[0:64, 0:nq]), reads=[psD_b], writes=[d_b])
                for m in range(2):
                    S.op("dve" if m == 0 else "pool", lambda e, m=m: e.tensor_copy(out=ob_t[:, m, 0:nq], in_=psO[m][0][:, 0:nq]), reads=[psOb[m]], writes=[ob_b]) if False else \
                        S.op("dve", lambda e, m=m: e.tensor_copy(out=ob_t[:, m, 0:nq], in_=psO[m][0][:, 0:nq]), reads=[psOb[m]], writes=[ob_b])

                def epi_a(nq=nq, d_t=d_t, d_b=d_b):
                    pe_t, pe_b = psE[0]
                    S.op("pe", lambda e: e.matmul(pe_t[:, 0:nq], selr[0][0][:, :], d_t[:, 0:nq], start=True, stop=True), reads=[selr[0][1], d_b], writes=[pe_b])
                    S.op("dve", lambda e: e.reciprocal(out=rc[0][0][:, 0:nq], in_=pe_t[:, 0:nq]), reads=[pe_b], writes=[rc[0][1]])

                def epi_b(nq=nq, ob_t=ob_t, ob_b=ob_b, d_t=d_t, d_b=d_b):
                    pe_t, pe_b = psE[0]
                    S.op("pe", lambda e: e.matmul(pe_t[:, 0:nq], selr[1][0][:, :], d_t[:, 0:nq], start=True, stop=True), reads=[selr[1][1], d_b], writes=[pe_b])
                    S.op("dve", lambda e: e.reciprocal(out=rc[1][0][:, 0:nq], in_=pe_t[:, 0:nq]), reads=[pe_b], writes=[rc[1][1]])
                    S.op("dve", lambda e: e.tensor_tensor(out=oo_t[:, 0:nq], in0=ob_t[:, 0, 0:nq], in1=rc[0][0][:, 0:nq], op=ALU.mult),
                         reads=[ob_b, rc[0][1]], writes=[oo_b])
                    S.op("dve", lambda e: e.tensor_tensor(out=o1_t[:, 0:nq], in0=ob_t[:, 1, 0:nq], in1=rc[1][0][:, 0:nq], op=ALU.mult),
                         reads=[ob_b, rc[1][1]], writes=[o1_b])
                    S.op("dve", lambda e: e.scalar_tensor_tensor(out=oo_t[:, 0:nq], in0=o1_t[:, 0:nq], scalar=nlam_t[:, 0:1], in1=oo_t[:, 0:nq],
                                                                 op0=ALU.mult, op1=ALU.add),
                         reads=[oo_b, o1_b, nlam_b], writes=[oo_b])
                    S.op("pool", lambda e: e.tensor_tensor(out=sq_t[:, 0:nq], in0=oo_t[:, 0:nq], in1=oo_t[:, 0:nq], op=ALU.mult), reads=[oo_b], writes=[sq_b])

                def epi_c(nq=nq, dst_t=dst_t, dst_b=dst_b, d0=d0, h=h):
                    pe_t, pe_b = psE[0]
                    S.op("pe", lambda e: e.matmul(pe_t[:, 0:nq], ones_t[:, :], sq_t[:, 0:nq], start=True, stop=True), reads=[ones_b, sq_b], writes=[pe_b])
                    S.op("dve", lambda e: e.tensor_scalar(out=rs_t[:, 0:nq], in0=pe_t[:, 0:nq], scalar1=1.0 / 128, scalar2=EPS, op0=ALU.mult, op1=ALU.add),
                         reads=[pe_b], writes=[rs_b])
                    S.op("act", lambda e: e.activation(out=rs_t[:, 0:nq], in_=rs_t[:, 0:nq], func=AF.Ln), reads=[rs_b], writes=[rs_b])
                    S.op("act", lambda e: e.activation(out=rs_t[:, 0:nq], in_=rs_t[:, 0:nq], func=AF.Exp, scale=-0.5), reads=[rs_b], writes=[rs_b])
                    S.op("dve", lambda e: e.scalar_tensor_tensor(out=dst_t[:, h, d0:d0 + nq], in0=oo_t[:, 0:nq], scalar=subg_t[:, 0:1], in1=rs_t[:, 0:nq],
                                                                 op0=ALU.mult, op1=ALU.mult),
                         reads=[oo_b, subg_b, rs_b], writes=[dst_b])

                pending_epi.extend([epi_a, epi_b, epi_c])
        while pending_epi:
            pending_epi.pop(0)()
    S.barrier()
    if dbg:
        d_yf = cx.out("dbg_yf", [128, 2, T], BF16); d_oa = cx.out("dbg_oa", [128, 4, T], BF16)
        S.dma("sp", d_yf[:, :, :], yf_t[:], reads=[yf_b])
        S.dma("sp", d_oa[:, :, :], oa_t[:], reads=[oa_b])
        if full_ctx:
            d_yfc = cx.out("dbg_yfc", [128, 2, NCTX], BF16); d_oac = cx.out("dbg_oac", [128, 4, NCTX], BF16)
            S.dma("sp", d_yfc[:, :, :], yfc_t[:], reads=[yfc_b])
            S.dma("sp", d_oac[:, :, :], oac_t[:], reads=[oac_b])
    with ExitStack() as st:
        wo_t, wo_b = cx.sb("wo_t", [128, 10, D], BF16, stack=st)
        S.dma("pool", wo_t[:], wo[:, :].rearrange("(c p) d -> p c d", p=128), writes=[wo_b])
        wout_t, wout_b = cx.sb("wout_t", [128, KC, D], BF16, stack=st)
        S.dma("pool", wout_t[:], wout[:, :].rearrange("(c p) d -> p c d", p=128), writes=[wout_b])
        hxs = [cx.sb("hx_t%d" % i, [128, KC, 512], BF16, stack=st) for i in range(2)]
        ycs = [cx.sb("yc_t%d" % i, [128, 2, 512], BF16, stack=st) for i in range(1)]
        yps = [cx.sb("yp_t%d" % i, [128, 2, 512], BF16, stack=st) for i in range(1)]
        ys = [cx.sb("y_t%d" % i, [128, KC, 512], BF16, stack=st) for i in range(2)]
        xb = [cx.sb("xb%d" % i, [128, KC, 512], F32, stack=st) for i in range(1)]
        sig = [cx.sb("sig%d" % i, [128, 512], F32, stack=st) for i in range(2)]
        acc = [cx.sb("acc%d" % i, [128, 512], F32, stack=st) for i in range(2)]
        tmpb = [cx.sb("tmpb%d" % i, [128, 512], F32, stack=st) for i in range(2)]
        sqs = [cx.sb("sqs%d" % i, [128, 512], F32, stack=st) for i in range(2)]
        rs_t, rs_b = cx.sb("rs_t", [128, 512], F32, stack=st)
        h2 = [cx.sb("h2_%d" % i, [128, KC, 512], BF16, stack=st) for i in range(1)]
        psr = PsumRing(cx, 6, "psr", stack=st)
        pss = PsumRing(cx, 2, "pss", stack=st)
        segs = [(s0, 512, 0, hxT, ycT, ypT, yf_t, yf_b, oa_t, oa_b, xT, o_xm, o_h2) for s0 in range(0, T, 512)]
        if full_ctx:
            segs.append((0, NCTX, 1, hxTc, ycTc, ypTc, yfc_t, yfc_b, oac_t, oac_b, cxT, o_cxm, o_ch2))
        bi = 0

        def gated(si):
            (s0, ns, j, hsrc, ycsrc, ypsrc, yfs_t, yfs_b, oas_t, oas_b, xsrc, oxd, ohd) = segs[si]
            hx_t, hx_b = hxs[si % 2]
            yc_t, yc_b = ycs[0]
            yp_t, yp_b = yps[0]
            y_t, y_b = ys[si % 2]
            S.dma("sp", hx_t[:, :, 0:ns], hsrc[:, :, s0:s0 + ns], writes=[hx_b])
            S.dma("sp", yc_t[:, :, 0:ns], ycsrc[:, :, s0:s0 + ns], writes=[yc_b])
            S.dma("sp", yp_t[:, :, 0:ns], ypsrc[:, :, s0:s0 + ns], writes=[yp_b])
            nb = ns
            br = [(yfs_t, yfs_b, s0, 0, 2), (oas_t, oas_b, s0, 2, 4), (yc_t, yc_b, 0, 6, 2), (yp_t, yp_b, 0, 8, 2)]
            for m in range(KC):
                a_t, a_b = acc[m % 2]
                for jb, (bt, bb, boff, wc0, nch) in enumerate(br):
                    pg, pgb = psr.next()
                    for kc in range(KC):
                        S.op("pe", lambda e, kc=kc: e.matmul(pg[:, 0:nb], wg_t[:, kc, jb, m * 128:(m + 1) * 128], hx_t[:, kc, 0:nb],
                                                             start=(kc == 0), stop=(kc == KC - 1)),
                             reads=[wg_b, hx_b], writes=[pgb], inc=(kc == KC - 1))
                    s_t, s_b = sig[jb % 2]
                    S.op("act", lambda e: e.activation(out=s_t[:, 0:nb], in_=pg[:, 0:nb], func=AF.Sigmoid), reads=[pgb], writes=[s_b])
                    pbr, pbrb = psr.next()
                    for c in range(nch):
                        S.op("pe", lambda e, c=c: e.matmul(pbr[:, 0:nb], wo_t[:, wc0 + c, m * 128:(m + 1) * 128], bt[:, c, boff:boff + nb],
                                                           start=(c == 0), stop=(c == nch - 1)),
                             reads=[wo_b, bb], writes=[pbrb], inc=(c == nch - 1))
                    if jb == 0:
                        S.op("dve", lambda e: e.tensor_tensor(out=a_t[:, 0:nb], in0=pbr[:, 0:nb], in1=s_t[:, 0:nb], op=ALU.mult),
                             reads=[pbrb, s_b], writes=[a_b])
                    else:
                        t_t, t_b = tmpb[jb % 2]
                        S.op("dve", lambda e: e.tensor_tensor(out=t_t[:, 0:nb], in0=pbr[:, 0:nb], in1=s_t[:, 0:nb], op=ALU.mult),
                             reads=[pbrb, s_b], writes=[t_b])
                        if jb < 3:
                            S.op("pool", lambda e: e.tensor_tensor(out=a_t[:, 0:nb], in0=a_t[:, 0:nb], in1=t_t[:, 0:nb], op=ALU.add),
                                 reads=[a_b, t_b], writes=[a_b])
                        else:
                            S.op("pool", lambda e: e.tensor_tensor(out=y_t[:, m, 0:nb], in0=a_t[:, 0:nb], in1=t_t[:, 0:nb], op=ALU.add),
                                 reads=[a_b, t_b], writes=[y_b])

        def tail(si):
            nonlocal bi
            (s0, ns, j, hsrc, ycsrc, ypsrc, yfs_t, yfs_b, oas_t, oas_b, xsrc, oxd, ohd) = segs[si]
            y_t, y_b = ys[si % 2]
            nb = ns
            x_t, x_b = xb[0]
            h_t, h_b = h2[0]
            bi += 1
            S.dma("sp", x_t[:, :, 0:nb], xsrc[:, s0:s0 + nb].rearrange("(c p) n -> p c n", p=128), writes=[x_b])
            pst, psb = pss.next()
            for m2 in range(KC):
                pt, pb = psr.next()
                for m in range(KC):
                    S.op("pe", lambda e, m=m: e.matmul(pt[:, 0:nb], wout_t[:, m, m2 * 128:(m2 + 1) * 128], y_t[:, m, 0:nb],
                                                       start=(m == 0), stop=(m == KC - 1)),
                         reads=[wout_b, y_b], writes=[pb], inc=(m == KC - 1))
                S.op("dve", lambda e: e.scalar_tensor_tensor(out=x_t[:, m2, 0:nb], in0=pt[:, 0:nb], scalar=mod_t[:, 16 + m2, j:j + 1],
                                                             in1=x_t[:, m2, 0:nb], op0=ALU.mult, op1=ALU.add),
                     reads=[pb, mod_b, x_b], writes=[x_b])
                q_t2, q_b2 = sqs[m2 % 2]
                S.op("act", lambda e: e.activation(out=q_t2[:, 0:nb], in_=x_t[:, m2, 0:nb], func=AF.Square), reads=[x_b], writes=[q_b2])
                S.op("pe", lambda e: e.matmul(pst[:, 0:nb], ones_t[:, :], q_t2[:, 0:nb], start=(m2 == 0), stop=(m2 == KC - 1)),
                     reads=[ones_b, q_b2], writes=[psb], inc=True)
            S.dma("act", oxd[:, s0:s0 + nb].rearrange("(c p) n -> p c n", p=128), x_t[:, :, 0:nb], reads=[x_b])
            S.op("dve", lambda e: e.tensor_scalar(out=rs_t[:, 0:nb], in0=pst[:, 0:nb], scalar1=1.0 / D, scalar2=EPS, op0=ALU.mult, op1=ALU.add),
                 reads=[psb], writes=[rs_b])
            S.op("act", lambda e: e.activation(out=rs_t[:, 0:nb], in_=rs_t[:, 0:nb], func=AF.Sqrt), reads=[rs_b], writes=[rs_b])
            S.op("dve", lambda e: e.reciprocal(out=rs_t[:, 0:nb], in_=rs_t[:, 0:nb]), reads=[rs_b], writes=[rs_b])
            for kc in range(KC):
                t_t, t_b = tmpb[kc % 2]
                S.op("dve", lambda e: e.tensor_tensor(out=t_t[:, 0:nb], in0=x_t[:, kc, 0:nb], in1=rs_t[:, 0:nb], op=ALU.mult),
                     reads=[x_b, rs_b], writes=[t_b])
                S.op("act", lambda e: e.activation(out=h_t[:, kc, 0:nb], in_=t_t[:, 0:nb], func=AF.Identity,
                                                   bias=mod_t[:, 24 + kc, j:j + 1], scale=gs_t[:, kc, j:j + 1]),
                     reads=[t_b, mod_b, gs_b], writes=[h_b])
            S.dma("act", ohd[:, :, s0:s0 + nb], h_t[:, :, 0:nb], reads=[h_b])

        gated(0)
        for si in range(len(segs)):
            if si + 1 < len(segs):
                gated(si + 1)
            tail(si)
    if standalone:
        S.finish(None)
    return cx


def host_B(inp, l, resA, xT_all, ctxT_all, full_ctx):
    w_in = np.asarray(inp["w_in"][l])
    wg = np.ascontiguousarray(w_in[:, OFF_G:].reshape(D, 4, D))
    wo = np.ascontiguousarray(np.concatenate([inp["wo_f"][l], inp["wo_a"][l], inp["wo_c"][l], inp["wo_p"][l]], axis=0))
    wout = np.ascontiguousarray(inp["w_out"][l])
    lamv = np.stack([inp["lam_q1"][l], inp["lam_k1"][l], inp["lam_q2"][l], inp["lam_k2"][l]], 0).astype(np.float32)
    subg = np.ascontiguousarray(np.asarray(inp["subln_g"][l], np.float32).reshape(128, 1))
    n2g = fm_vec(inp["norm2_g"][l])
    ident = np.eye(128, dtype=np.float32)
    maps = []
    for i in range(NCORES):
        b, j = i // 4, i % 4
        tabA, tabC, tabX = fft_tables(j)
        if resA is None:
            m = dict(wg=wg, wo=wo, wout=wout, lamv=lamv, subg=subg, n2g=n2g, tabA=tabA, tabC=tabC, ident=ident)
            if full_ctx:
                m["tabX"] = tabX
            maps.append(m)
            continue
        grp = [resA[b * 4 + jj] for jj in range(4)]
        r = resA[i]
        m = dict(QT=r["QT"], KTc=r["KTc"], Vc=r["Vc"],
                 ycT=r["ycT"], ypT=r["ypT"], hxT=r["hxT"], xT=np.ascontiguousarray(xT_all[b][:, j * T:(j + 1) * T]), mod=r["mod"],
                 wg=wg, wo=wo, wout=wout, lamv=lamv, subg=subg, n2g=n2g, tabA=tabA, tabC=tabC, ident=ident)
        for h in range(4):
            m["KTg%d" % h] = np.ascontiguousarray(np.concatenate([g["KT%d" % h] for g in grp], axis=0))
            m["Vg%d" % h] = np.ascontiguousarray(np.concatenate([g["V%d" % h] for g in grp], axis=0))
            m["Zg%d" % h] = np.ascontiguousarray(np.concatenate([g["Z%d" % h] for g in grp], axis=0))
        if full_ctx:
            m.update(QTc=r["QTc"], Zc=r["Zc"], ycTc=r["ycTc"], ypTc=r["ypTc"], hxTc=r["hxTc"],
                     cxT=np.ascontiguousarray(ctxT_all[b]), tabX=tabX)
        maps.append(m)
    return maps


NFC = DFF // 128


def build_C(full_ctx, final, cx=None):
    standalone = cx is None
    if standalone:
        cx = Ctx("C")
    nc, S = cx.nc, cx.S
    hx2 = cx.inp("hx2T", [128, KC, T], BF16); hhalo = cx.inp("hhalo", [128, KC, 2], BF16)
    xm = cx.inp("xmT", [D, T]); mod = cx.inp("mod", [128, 48, 2])
    wup = cx.inp("wup", [D, 2 * DFF]); wdn = cx.inp("wdn", [DFF, D])
    fw = cx.inp("fw", [128, NFC, 4])
    o_x = cx.out("xoT", [D, T])
    if full_ctx:
        chx2 = cx.inp("chx2T", [128, KC, NCTX], BF16); cxm = cx.inp("cxmT", [D, NCTX])
        o_cx = cx.out("cxoT", [D, NCTX])
    if final:
        fg = cx.inp("fg", [128, KC])
    wup_t, _unused = cx.sb("wup_t", [128, KC, 2 * DFF], BF16)
    wup_bs = {}
    HP = 11 * 128
    for piece in range(2):
        for half in (1, 0):
            bb = Buf("wup_%d_%d" % (half, piece))
            c0_ = half * DFF + piece * HP
            for kc in range(KC):
                S.dma("pool", wup_t[:, kc, c0_:c0_ + HP], wup[kc * 128:(kc + 1) * 128, c0_:c0_ + HP], writes=[bb], also=True)
            wup_bs[(half, piece)] = bb
    wdn_t, wdn_b = cx.sb("wdn_t", [128, NFC, D], BF16)
    for g in range(2):
        S.dma("pool", wdn_t[:, g * 11:(g + 1) * 11, :], wdn[g * 11 * 128:(g + 1) * 11 * 128, :].rearrange("(c p) d -> p c d", p=128),
              writes=[wdn_b], also=True)
    mod_t, mod_b = load_const(cx, "mod_t", mod[:, :, :], [128, 48, 2])
    fw_t, fw_b = load_const(cx, "fw_t", fw[:, :, :], [128, NFC, 4])
    hal_t, hal_b = load_const(cx, "hal_t", hhalo[:, :, :], [128, KC, 2], BF16)
    zero_t, zero_b = cx.sb("zero_t", [128, KC, 2], BF16)
    S.op("dve", lambda e: e.memset(zero_t[:], 0.0), writes=[zero_b])
    if final:
        fg_t, fg_b = load_const(cx, "fg_t", fg[:, :], [128, KC])
        ones_t, ones_b = cx.sb("ones_t", [128, 128], F32)
        S.op("dve", lambda e: e.memset(ones_t[:], 1.0), writes=[ones_b])
    hxb = [cx.sb("hxb%d" % i, [128, KC, 512], BF16) for i in range(2)]
    x_t, x_b = cx.sb("x_t", [128, KC, 512], F32)
    u_t, u_b = cx.sb("u_t", [128, NFC, 512], BF16)
    gw = [cx.sb("gw%d" % i, [128, 512], F32) for i in range(2)]
    cv = [cx.sb("cv%d" % i, [128, 512], F32) for i in range(2)]
    ge = [cx.sb("ge%d" % i, [128, 512], F32) for i in range(2)]
    psr = PsumRing(cx, 6, "psr")
    pss = PsumRing(cx, 2, "pss")
    if final:
        sqs = [cx.sb("sqs%d" % i, [128, 512], F32) for i in range(2)]
        rs_t, rs_b = cx.sb("rs_t", [128, 512], F32)
    segs = [(hx2, xm, o_x, T, 0, hal_t, hal_b)]
    if full_ctx:
        segs.append((chx2, cxm, o_cx, NCTX, 1, zero_t, zero_b))
    bi = 0
    for (hsrc, xsrc, xdst, n, j, m_t, m_b) in segs:
        blocks = []
        b0 = 0
        while b0 < n:
            nb = min(510, n - b0)
            blocks.append((b0, nb))
            b0 += nb
        for (b0, nb) in blocks:
            h_t, h_b = hxb[bi % 2]
            bi += 1
            lo = max(b0 - 1, 0)
            hi = min(b0 + nb + 1, n)
            S.dma("sp", h_t[:, :, lo - (b0 - 1):hi - (b0 - 1)], hsrc[:, :, lo:hi], writes=[h_b])
            if b0 == 0:
                S.op("pool", lambda e: e.tensor_copy(out=h_t[:, :, 0:1], in_=m_t[:, :, 0:1]), reads=[m_b], writes=[h_b])
            if b0 + nb == n:
                S.op("pool", lambda e: e.tensor_copy(out=h_t[:, :, nb + 1:nb + 2], in_=m_t[:, :, 1:2]), reads=[m_b], writes=[h_b])
            S.dma("sp", x_t[:, :, 0:nb], xsrc[:, b0:b0 + nb].rearrange("(c p) n -> p c n", p=128), writes=[x_b])
            for c in range(NFC):
                pg, pgb = psr.next()
                for kc in range(KC):
                    S.op("pe", lambda e, kc=kc: e.matmul(pg[:, 0:nb + 2], wup_t[:, kc, DFF + c * 128:DFF + (c + 1) * 128], h_t[:, kc, 0:nb + 2],
                                                         start=(kc == 0), stop=(kc == KC - 1)),
                         reads=[wup_bs[(1, c // 11)], h_b], writes=[pgb], inc=(kc == KC - 1))
                g_t, g_b = gw[c % 2]
                S.op("act", lambda e: e.activation(out=g_t[:, 0:nb + 2], in_=pg[:, 0:nb + 2], func=AF.Copy), reads=[pgb], writes=[g_b])
                c_t, c_b = cv[c % 2]
                S.op("dve", lambda e: e.tensor_scalar(out=c_t[:, 0:nb], in0=g_t[:, 0:nb], scalar1=fw_t[:, c, 0:1], scalar2=fw_t[:, c, 3:4],
                                                      op0=ALU.mult, op1=ALU.add), reads=[g_b, fw_b], writes=[c_b])
                for k in (1, 2):
                    S.op("dve", lambda e, k=k: e.scalar_tensor_tensor(out=c_t[:, 0:nb], in0=g_t[:, k:k + nb], scalar=fw_t[:, c, k:k + 1], in1=c_t[:, 0:nb],
                                                                      op0=ALU.mult, op1=ALU.add), reads=[g_b, fw_b, c_b], writes=[c_b])
                e_t, e_b = ge[c % 2]
                S.op("act", lambda e: e.activation(out=e_t[:, 0:nb], in_=c_t[:, 0:nb], func=AF.Gelu), reads=[c_b], writes=[e_b])
                pv, pvb = psr.next()
                for kc in range(KC):
                    S.op("pe", lambda e, kc=kc: e.matmul(pv[:, 0:nb], wup_t[:, kc, c * 128:(c + 1) * 128], h_t[:, kc, 1:nb + 1],
                                                         start=(kc == 0), stop=(kc == KC - 1)),
                         reads=[wup_bs[(0, c // 11)], h_b], writes=[pvb], inc=(kc == KC - 1))
                S.op("dve", lambda e: e.tensor_tensor(out=u_t[:, c, 0:nb], in0=pv[:, 0:nb], in1=e_t[:, 0:nb], op=ALU.mult),
                     reads=[pvb, e_b], writes=[u_b])
            if final:
                pst, psb = pss.next()
            for m2 in range(KC):
                po, pob = psr.next()
                for c in range(NFC):
                    S.op("pe", lambda e, c=c: e.matmul(po[:, 0:nb], wdn_t[:, c, m2 * 128:(m2 + 1) * 128], u_t[:, c, 0:nb],
                                                       start=(c == 0), stop=(c == NFC - 1)),
                         reads=[wdn_b, u_b], writes=[pob], inc=(c == NFC - 1))
                S.op("dve", lambda e: e.scalar_tensor_tensor(out=x_t[:, m2, 0:nb], in0=po[:, 0:nb], scalar=mod_t[:, 40 + m2, j:j + 1],
                                                             in1=x_t[:, m2, 0:nb], op0=ALU.mult, op1=ALU.add),
                     reads=[pob, mod_b, x_b], writes=[x_b])
                if final:
                    q_t2, q_b2 = sqs[m2 % 2]
                    S.op("act", lambda e: e.activation(out=q_t2[:, 0:nb], in_=x_t[:, m2, 0:nb], func=AF.Square), reads=[x_b], writes=[q_b2])
                    S.op("pe", lambda e: e.matmul(pst[:, 0:nb], ones_t[:, :], q_t2[:, 0:nb], start=(m2 == 0), stop=(m2 == KC - 1)),
                         reads=[ones_b, q_b2], writes=[psb], inc=True)
            if final:
                S.op("dve", lambda e: e.tensor_scalar(out=rs_t[:, 0:nb], in0=pst[:, 0:nb], scalar1=1.0 / D, scalar2=EPS, op0=ALU.mult, op1=ALU.add),
                     reads=[psb], writes=[rs_b])
                S.op("act", lambda e: e.activation(out=rs_t[:, 0:nb], in_=rs_t[:, 0:nb], func=AF.Sqrt), reads=[rs_b], writes=[rs_b])
                S.op("dve", lambda e: e.reciprocal(out=rs_t[:, 0:nb], in_=rs_t[:, 0:nb]), reads=[rs_b], writes=[rs_b])
                for kc in range(KC):
                    S.op("dve", lambda e, kc=kc: e.scalar_tensor_tensor(out=x_t[:, kc, 0:nb], in0=x_t[:, kc, 0:nb], scalar=fg_t[:, kc:kc + 1],
                                                                        in1=rs_t[:, 0:nb], op0=ALU.mult, op1=ALU.mult),
                         reads=[x_b, fg_b, rs_b], writes=[x_b])
            S.dma("act", xdst[:, b0:b0 + nb].rearrange("(c p) n -> p c n", p=128), x_t[:, :, 0:nb], reads=[x_b])
    if standalone:
        S.finish(None)
    return cx


def host_C(inp, l, resA, resB, full_ctx, final):
    wup = np.ascontiguousarray(inp["w_up"][l]); wdn = np.ascontiguousarray(inp["w_down"][l])
    fwv = np.concatenate([np.asarray(inp["ffn_dw_w"][l]), np.asarray(inp["ffn_dw_b"][l])[None, :]], axis=0)
    fw = np.ascontiguousarray(fwv.T.reshape(NFC, 128, 4).transpose(1, 0, 2)).astype(np.float32)
    maps = []
    for i in range(NCORES):
        b, j = i // 4, i % 4
        if resA is None:
            m = dict(wup=wup, wdn=wdn, fw=fw)
            if final:
                m["fg"] = fm_vec(inp["final_g"])
            maps.append(m)
            continue
        hh = np.zeros((128, KC, 2), NPBF)
        if j > 0:
            hh[:, :, 0] = resB[i - 1]["hx2T"][:, :, T - 1]
        if j < 3:
            hh[:, :, 1] = resB[i + 1]["hx2T"][:, :, 0]
        m = dict(hx2T=resB[i]["hx2T"], hhalo=hh, xmT=resB[i]["xmT"], mod=resA[i]["mod"], wup=wup, wdn=wdn, fw=fw)
        if full_ctx:
            m.update(chx2T=resB[i]["chx2T"], cxmT=resB[i]["cxmT"])
        if final:
            m["fg"] = fm_vec(inp["final_g"])
        maps.append(m)
    return maps


def _np(results):
    return [{k: np.asarray(v) for k, v in r.items()} for r in results]


def kernel_unfused(**inputs):
    inp = {k: np.asarray(v) for k, v in inputs.items()}
    x = inp["x"].astype(np.float32, copy=False)
    B = x.shape[0]
    xT_all = [np.ascontiguousarray(x[b].T) for b in range(B)]
    ctxT_all = [np.ascontiguousarray(inp["ctx"][b].T.astype(np.float32)) for b in range(B)]
    depth = inp["w_in"].shape[0]
    for l in range(depth):
        last = l == depth - 1
        full_ctx = not last
        lam_init = 0.8 - 0.6 * math.exp(-0.3 * l)
        cxA = get_prog(("A", full_ctx), build_A, full_ctx)
        resA = _np(run_prog(cxA, host_A(inp, l, xT_all, ctxT_all)))
        cxB = get_prog(("B", full_ctx, l), build_B, full_ctx, lam_init)
        resB = _np(run_prog(cxB, host_B(inp, l, resA, xT_all, ctxT_all, full_ctx)))
        cxC = get_prog(("C", full_ctx, last), build_C, full_ctx, last)
        resC = _np(run_prog(cxC, host_C(inp, l, resA, resB, full_ctx, last)))
        xT_all = [np.concatenate([resC[b * 4 + j]["xoT"] for j in range(4)], axis=1) for b in range(B)]
        if full_ctx:
            ctxT_all = [resC[b * 4]["cxoT"] for b in range(B)]
    out = np.stack([xT_all[b].T for b in range(B)], axis=0)
    return np.ascontiguousarray(out.astype(np.float32))


def _select_halo(cx, gathered, ncol, dt, pick_l, pick_r, wl, out_aps, sel_t, sel_b, name):
    S = cx.S
    g_t, g_b = cx.sb(name + "_g", [128, 4, ncol], dt)
    S.dma("sp", g_t[:], gathered[:, :].rearrange("(r p) n -> p r n", p=128), writes=[g_b])
    res = []
    for side, pick in ((0, pick_l), (1, pick_r)):
        a_t, a_b = cx.sb(name + "_a%d" % side, [128, KC, wl], F32)
        gv = g_t[:, :, :].rearrange("p r (c n) -> p r c n", c=KC)
        S.op("dve", lambda e: e.tensor_scalar(out=a_t[:], in0=gv[:, 0, :, pick], scalar1=sel_t[:, side * 4:side * 4 + 1], scalar2=None, op0=ALU.mult),
             reads=[g_b, sel_b], writes=[a_b])
        for r in range(1, 4):
            S.op("dve", lambda e, r=r: e.scalar_tensor_tensor(out=a_t[:], in0=gv[:, r, :, pick], scalar=sel_t[:, side * 4 + r:side * 4 + r + 1],
                                                              in1=a_t[:], op0=ALU.mult, op1=ALU.add),
                 reads=[g_b, sel_b, a_b], writes=[a_b])
        res.append((a_t, a_b))
    return res


def build_M(cx, depth):
    nc, S = cx.nc, cx.S
    cx.begin_phase("M_")
    cT = cx.inp("cT", [128, KC, 2])
    cT_t, cT_b = load_const(cx, "cT_t", cT[:, :, :], [128, KC, 2])
    psr = PsumRing(cx, 4, "psm")
    mq = []
    for l in range(depth):
        awq = cx.inp("L%d_ada_wq" % l, [D, 1536]); abq = cx.inp("L%d_ada_bq" % l, [128, 12])
        abq_t, abq_b = load_const(cx, "abq%d" % l, abq[:, :], [128, 12])
        mq_t, mq_b = cx.sb("mq%d" % l, [128, 12, 2], F32)
        cx.prefix = "M%d_" % l
        _adaln_mod(cx, cT_t, cT_b, awq, abq_t, abq_b, mq_t, mq_b, psr, 3, 0)
        cx.prefix = "M_"
        mq_d = nc.dram_tensor("M_mqd%d" % l, [128, 24], F32).ap()
        S.dma("sp", mq_d[:, :], mq_t[:, :, :].rearrange("p c j -> p (c j)"), reads=[mq_b])
        mq.append(mq_d)
    S.barrier()
    mod_ap = []
    mg_l = []
    for l in range(depth):
        mg = nc.dram_tensor("M_mg%d" % l, [512, 24], F32).ap()
        S.allgather(mq[l], mg)
        mg_l.append(mg)
    S.barrier()
    for l in range(depth):
        md = nc.dram_tensor("M_mod%d" % l, [128, 48, 2], F32).ap()
        mt, mb = cx.sb("mgt%d" % l, [128, 4, 24], F32)
        S.dma("sp", mt[:], mg_l[l][:, :].rearrange("(r p) n -> p r n", p=128), writes=[mb])
        S.dma("sp", md[:, :, :].rearrange("p (r c) j -> p r (c j)", r=4), mt[:], reads=[mb])
        mod_ap.append(md)
    cx.end_phase()
    return mod_ap


def build_fused(depth=2):
    cx = Ctx("F", fused=True)
    nc, S = cx.nc, cx.S
    sel = cx.inp("sel", [128, 8])
    mod_ap = build_M(cx, depth)
    prevC = None
    xTh_ap = None
    for l in range(depth):
        last = l == depth - 1
        full_ctx = not last
        lam_init = 0.8 - 0.6 * math.exp(-0.3 * l)
        links = {}
        if l > 0:
            links = {"xT": prevC["xoT"], "cxT": prevC["cxoT"], "xTh": xTh_ap}
        links["mod_in"] = mod_ap[l]
        cx.begin_phase("L%dA_" % l, links)
        gath = {}

        for nm, shp in (("Z", [SEQ, 128]), ("KT", [512, T]), ("V", [SEQ, 128])):
            for h in range(4):
                gath["%sg%d" % (nm, h)] = nc.dram_tensor("L%dE1_%sg%d" % (l, nm, h), shp, BF16).ap()

        def gather_kvz(l=l, gath=gath):
            for h in range(4):
                S.allgather(cx.produced["Z%d" % h], gath["Zg%d" % h])

        cx.after_blocks = gather_kvz
        build_A(full_ctx, cx)
        cx.after_blocks = None
        pA = cx.end_phase()
        pA["mod"] = mod_ap[l]
        xT_ap = links["xT"] if l > 0 else cx.ins["L0A_xT"]
        cxT_ap = links["cxT"] if l > 0 else cx.ins["L0A_cxT"]
        links = {k: pA[k] for k in ("QT", "KTc", "Vc", "ycT", "ypT", "hxT", "mod")}
        links.update(gath)
        links["xT"] = xT_ap
        if full_ctx:
            links.update({k: pA[k] for k in ("QTc", "Zc", "ycTc", "ypTc", "hxTc")})
            links["cxT"] = cxT_ap
        cx.begin_phase("L%dB_" % l, links)

        def gather_kv(pA=pA, gath=gath):
            for h in range(4):
                for nm in ("KT", "V"):
                    st_ = S.allgather(pA["%s%d" % (nm, h)], gath["%sg%d" % (nm, h)])
                    gb = Buf("g_%s%d" % (nm, h))
                    gb.w = [st_]
                    cx.dram_bufs["%sg%d" % (nm, h)] = gb

        cx.after_fft_tables = gather_kv
        cx.dram_bufs = {}
        build_B(full_ctx, lam_init, cx=cx)
        cx.after_fft_tables = None
        pB = cx.end_phase()
        cx.begin_phase("L%dE2_" % l)
        sel_t, sel_b = load_const(cx, "sel_t", sel[:, :], [128, 8])
        e_t, e_b = cx.sb("e_t", [128, KC, 2], BF16)
        S.dma("sp", e_t[:, :, 0:1], pB["hx2T"][:, :, 0:1], writes=[e_b], slow=True)
        S.dma("sp", e_t[:, :, 1:2], pB["hx2T"][:, :, T - 1:T], writes=[e_b], also=True, slow=True)
        ein = nc.dram_tensor("L%dE2_in" % l, [128, KC * 2], BF16).ap()
        eout = nc.dram_tensor("L%dE2_out" % l, [512, KC * 2], BF16).ap()
        hhalo = nc.dram_tensor("L%dE2_hhalo" % l, [128, KC, 2], BF16).ap()
        S.dma("sp", ein[:, :], e_t[:, :, :].rearrange("p c n -> p (c n)"), reads=[e_b])
        S.barrier()
        S.allgather(ein, eout)
        S.barrier()
        (l_t, l_b), (r_t, r_b) = _select_halo(cx, eout, KC * 2, BF16, slice(1, 2), slice(0, 1), 1, None, sel_t, sel_b, "h2")
        hh_t, hh_b = cx.sb("hh_t", [128, KC, 2], BF16)
        S.op("dve", lambda e: e.tensor_copy(out=hh_t[:, :, 0:1], in_=l_t[:]), reads=[l_b], writes=[hh_b])
        S.op("dve", lambda e: e.tensor_copy(out=hh_t[:, :, 1:2], in_=r_t[:]), reads=[r_b, hh_b], writes=[hh_b])
        S.dma("sp", hhalo[:, :, :], hh_t[:], reads=[hh_b])
        cx.end_phase()
        links = {"hx2T": pB["hx2T"], "hhalo": hhalo, "xmT": pB["xmT"], "mod": pA["mod"]}
        if full_ctx:
            links.update({"chx2T": pB["chx2T"], "cxmT": pB["cxmT"]})
        cx.begin_phase("L%dC_" % l, links, ext_out=({"xoT": "outT"} if last else None))
        build_C(full_ctx, last, cx=cx)
        pC = cx.end_phase()
        prevC = pC
        if not last:
            cx.begin_phase("L%dE3_" % l)
            sel_t, sel_b = load_const(cx, "sel_t", sel[:, :], [128, 8])
            e_t, e_b = cx.sb("e_t", [128, KC, 2 * HALO], F32)
            S.dma("sp", e_t[:, :, 0:HALO], pC["xoT"][:, 0:HALO].rearrange("(c p) n -> p c n", p=128), writes=[e_b])
            S.dma("sp", e_t[:, :, HALO:2 * HALO], pC["xoT"][:, T - HALO:T].rearrange("(c p) n -> p c n", p=128), writes=[e_b], also=True)
            ein = nc.dram_tensor("L%dE3_in" % l, [128, KC * 2 * HALO], F32).ap()
            eout = nc.dram_tensor("L%dE3_out" % l, [512, KC * 2 * HALO], F32).ap()
            xTh_ap = nc.dram_tensor("L%dE3_xTh" % l, [D, 2 * HALO], F32).ap()
            S.dma("sp", ein[:, :], e_t[:, :, :].rearrange("p c n -> p (c n)"), reads=[e_b])
            S.barrier()
            S.allgather(ein, eout)
            S.barrier()
            (l_t, l_b), (r_t, r_b) = _select_halo(cx, eout, KC * 2 * HALO, F32, slice(HALO, 2 * HALO), slice(0, HALO), HALO, None, sel_t, sel_b, "xh")
            S.dma("sp", xTh_ap[:, 0:HALO].rearrange("(c p) n -> p c n", p=128), l_t[:], reads=[l_b])
            S.dma("sp", xTh_ap[:, HALO:2 * HALO].rearrange("(c p) n -> p c n", p=128), r_t[:], reads=[r_b])
            cx.end_phase()
    S.finish(None)
    return cx


def kernel(**inputs):
    inp = {k: np.asarray(v) for k, v in inputs.items()}
    x = inp["x"].astype(np.float32, copy=False)
    B = x.shape[0]
    depth = inp["w_in"].shape[0]
    cx = get_prog(("F", depth), build_fused, depth)
    xT_all = [np.ascontiguousarray(x[b].T) for b in range(B)]
    ctxT_all = [np.ascontiguousarray(inp["ctx"][b].T.astype(np.float32)) for b in range(B)]
    maps = [dict() for _ in range(NCORES)]
    for l in range(depth):
        last = l == depth - 1
        full_ctx = not last
        mA = host_A(inp, l, xT_all if l == 0 else None, ctxT_all if l == 0 else None)
        mB = host_B(inp, l, None, None, None, full_ctx)
        mC = host_C(inp, l, None, None, full_ctx, last)
        for i in range(NCORES):
            for pre, m in (("L%dA_" % l, mA[i]), ("L%dB_" % l, mB[i]), ("L%dC_" % l, mC[i])):
                for k, v in m.items():
                    if pre + k in cx.ins:
                        maps[i][pre + k] = v
    for i in range(NCORES):
        b, j = i // 4, i % 4
        maps[i]["M_cT"] = np.ascontiguousarray(np.stack([fm_vec(inp["c"][b]), fm_vec(inp["c_ctx"])], axis=2))
        for l in range(depth):
            maps[i]["M_L%d_ada_wq" % l] = np.ascontiguousarray(inp["ada_w"][l][:, j * 1536:(j + 1) * 1536])
            maps[i]["M_L%d_ada_bq" % l] = fm_vec(inp["ada_b"][l][j * 1536:(j + 1) * 1536])
        sel = np.zeros((128, 8), np.float32)
        if j > 0:
            sel[:, j - 1] = 1.0
        if j < 3:
            sel[:, 4 + j + 1] = 1.0
        maps[i]["sel"] = sel
        missing = set(cx.ins) - set(maps[i])
        assert not missing, missing
    res = run_prog(cx, maps)
    outT = [np.asarray(r["outT"]) for r in res]
    out = np.stack([np.concatenate([outT[b * 4 + j] for j in range(4)], axis=1).T for b in range(B)], axis=0)
    return np.ascontiguousarray(out.astype(np.float32))
```
